# Optimizing a Trainium2 kernel written in Bass

```python
import math
import jax, jax.numpy as jnp
from jax import lax
import numpy as np

D_MODEL = 1024
BATCH = 4
SEQ = 8192
DEPTH = 2

N_META = 16
D_MIX = D_MODEL
N_MIXERS = 4
W_GROUP = D_MIX // N_MIXERS
S5_GROUP = 16
S5_NGROUPS = W_GROUP // S5_GROUP
S5_STATE = 64
SC_WIDTH = 3
HG_HEADS = 4
HG_HEAD_DIM = W_GROUP // HG_HEADS
HG_CHUNK = 64
LRU_HEADS = 4
LRU_HEAD_DIM = W_GROUP // LRU_HEADS
LRU_CONV = 4
LRU_C = 8.0
D_FF = -(-8 * D_MODEL // (3 * 256)) * 256
ALPHA = (2 * DEPTH) ** 0.25
BETA = (8 * DEPTH) ** -0.25
EPS = 1e-5
N_SPLITS = 10
N_IN = W_GROUP * N_SPLITS

kernel_name = "hybrid_hymba_s5_conv_hgrn2_rglru"


def layer_norm(x, g, b):
    xf = x.astype(jnp.float32)
    mu = jnp.mean(xf, axis=-1, keepdims=True)
    var = jnp.mean(jnp.square(xf - mu), axis=-1, keepdims=True)
    y = (xf - mu) * lax.rsqrt(var + EPS) * g.astype(jnp.float32) + b.astype(jnp.float32)
    return y.astype(x.dtype)


def causal_depthwise_conv(x, w):
    k = w.shape[0]
    return lax.conv_general_dilated(
        x, w[:, None, :].astype(x.dtype), window_strides=(1,),
        padding=[(k - 1, 0)], dimension_numbers=("NWC", "WIO", "NWC"),
        feature_group_count=x.shape[-1])


def linear_scan(a, b):
    def combine(left, right):
        a_l, b_l = left
        a_r, b_r = right
        return a_l * a_r, a_r * b_l + b_r
    return lax.associative_scan(combine, (a, b), axis=1)[1]


def s5_mixer(u, lam_re, lam_im, b_re, b_im, c_re, c_im, d_skip, log_dt, glu_w, glu_b):
    f32 = jnp.float32
    bsz, n, _ = u.shape
    uf = u.astype(f32)
    lam = lax.complex(lam_re.astype(f32), lam_im.astype(f32))
    dt = jnp.exp(log_dt.astype(f32))[:, None]
    lam_bar = jnp.exp(lam * dt)
    bmat = lax.complex(b_re.astype(f32), b_im.astype(f32))
    b_bar = ((lam_bar - 1.0) / lam)[..., None] * bmat
    cmat = lax.complex(c_re.astype(f32), c_im.astype(f32))
    ug = uf.reshape(bsz, n, S5_NGROUPS, S5_GROUP)
    bu = jnp.einsum('gph,blgh->blgp', b_bar, ug)
    states = linear_scan(jnp.broadcast_to(lam_bar, bu.shape), bu)
    y = jnp.einsum('ghp,blgp->blgh', cmat, states).real.reshape(bsz, n, W_GROUP)
    y = y + d_skip.astype(f32) * uf
    g = jax.nn.gelu(y)
    return g * jax.nn.sigmoid(g @ glu_w.astype(f32) + glu_b.astype(f32))


def short_conv_mixer(h, gate_b, gate_c, conv_w):
    f32 = jnp.float32
    hc = gate_c.astype(f32) * h.astype(f32)
    return gate_b.astype(f32) * causal_depthwise_conv(hc, conv_w.astype(f32))


def hgrn2_mixer(q_in, f_in, i_in, g_in, lb, gnorm):
    f32 = jnp.float32
    bsz, n, _ = q_in.shape
    q = jax.nn.silu(q_in.astype(f32)) * (HG_HEAD_DIM ** -0.5)
    z = f_in.astype(f32)
    lbf = lb.astype(f32)
    log_f = jnp.logaddexp(jnp.log(lbf), jnp.log1p(-lbf) + jax.nn.log_sigmoid(z))
    k = (1.0 - lbf) * jax.nn.sigmoid(-z)
    v = i_in.astype(f32)
    pad = (-n) % HG_CHUNK
    nc = (n + pad) // HG_CHUNK

    def to_chunks(t):
        t = jnp.pad(t, ((0, 0), (pad, 0), (0, 0)))
        t = t.reshape(bsz, nc, HG_CHUNK, HG_HEADS, HG_HEAD_DIM)
        return t.transpose(1, 0, 3, 2, 4)

    mask = jnp.tril(jnp.ones((HG_CHUNK, HG_CHUNK), dtype=bool))[:, :, None]

    def step(state, inp):
        qc, kc, vc, gc = inp
        b = jnp.cumsum(gc, axis=-2)
        b_last = b[..., -1:, :]
        inter = jnp.einsum('bhtd,bhde->bhte', qc * jnp.exp(b), state)
        rel = jnp.where(mask, b[..., :, None, :] - b[..., None, :, :], -jnp.inf)
        scores = jnp.einsum('bhtd,bhsd,bhtsd->bhts', qc, kc, jnp.exp(rel))
        intra = jnp.einsum('bhts,bhse->bhte', scores, vc)
        new_state = (jnp.exp(b_last)[..., 0, :, None] * state
                     + jnp.einsum('bhsd,bhse->bhde', kc * jnp.exp(b_last - b), vc))
        return new_state, inter + intra

    state0 = jnp.zeros((bsz, HG_HEADS, HG_HEAD_DIM, HG_HEAD_DIM), f32)
    _, o = lax.scan(step, state0, (to_chunks(q), to_chunks(k), to_chunks(v), to_chunks(log_f)))
    o = o.transpose(1, 0, 3, 2, 4).reshape(bsz, nc * HG_CHUNK, HG_HEADS, HG_HEAD_DIM)[:, pad:]
    o = o * lax.rsqrt(jnp.mean(jnp.square(o), axis=-1, keepdims=True) + EPS)
    o = o * gnorm.astype(f32).reshape(HG_HEADS, HG_HEAD_DIM)
    return o.reshape(bsz, n, W_GROUP) * jax.nn.silu(g_in.astype(f32))


def rglru_mixer(xb, yb, conv_w, conv_b, wa, ba, wx, bx, a_param):
    f32 = jnp.float32
    bsz, n, _ = xb.shape
    xc = causal_depthwise_conv(xb.astype(f32), conv_w.astype(f32)) + conv_b.astype(f32)
    xh = xc.reshape(bsz, n, LRU_HEADS, LRU_HEAD_DIM)
    gate_a = jax.nn.sigmoid(jnp.einsum('blhi,hij->blhj', xh, wa.astype(f32)).reshape(bsz, n, W_GROUP) + ba.astype(f32))
    gate_x = jax.nn.sigmoid(jnp.einsum('blhi,hij->blhj', xh, wx.astype(f32)).reshape(bsz, n, W_GROUP) + bx.astype(f32))
    log_a = -LRU_C * gate_a * jax.nn.softplus(-a_param.astype(f32))
    mult = jnp.sqrt(-jnp.expm1(2.0 * log_a))
    h = linear_scan(jnp.exp(log_a), xc * gate_x * mult)
    return h * jax.nn.gelu(yb.astype(f32))


def setup_inputs(seed: int = 0) -> dict:
    key = jax.random.key(seed)
    ks = jax.random.split(key, 32)
    f32 = jnp.float32
    nrm = lambda k, s, sc: jax.random.normal(k, s, f32) * sc
    lam_im = jnp.pi * jnp.arange(S5_STATE, dtype=f32)
    lru_u = jax.random.uniform(ks[22], (DEPTH, W_GROUP), f32, 0.9, 0.999)
    return {
        "x": nrm(ks[0], (BATCH, SEQ, D_MODEL), 1.0),
        "meta_tokens": nrm(ks[1], (N_META, D_MODEL), 1.0),
        "hg_lb_raw": nrm(ks[2], (DEPTH, W_GROUP), 0.5),
        "w_in": nrm(ks[3], (DEPTH, D_MODEL, N_IN), D_MODEL ** -0.5),
        "w_out": nrm(ks[4], (DEPTH, D_MIX, D_MODEL), BETA * D_MIX ** -0.5),
        "s5_lam_re": -0.5 + nrm(ks[5], (DEPTH, S5_NGROUPS, S5_STATE), 0.01),
        "s5_lam_im": lam_im + nrm(ks[6], (DEPTH, S5_NGROUPS, S5_STATE), 0.01),
        "s5_b_re": nrm(ks[7], (DEPTH, S5_NGROUPS, S5_STATE, S5_GROUP), (2 * S5_GROUP) ** -0.5),
        "s5_b_im": nrm(ks[8], (DEPTH, S5_NGROUPS, S5_STATE, S5_GROUP), (2 * S5_GROUP) ** -0.5),
        "s5_c_re": nrm(ks[9], (DEPTH, S5_NGROUPS, S5_GROUP, S5_STATE), (2 * S5_STATE) ** -0.5),
        "s5_c_im": nrm(ks[10], (DEPTH, S5_NGROUPS, S5_GROUP, S5_STATE), (2 * S5_STATE) ** -0.5),
        "s5_d": nrm(ks[11], (DEPTH, W_GROUP), 1.0),
        "s5_log_dt": jax.random.uniform(ks[12], (DEPTH, S5_NGROUPS), f32, math.log(1e-3), math.log(1e-1)),
        "s5_glu_w": nrm(ks[13], (DEPTH, W_GROUP, W_GROUP), W_GROUP ** -0.5),
        "s5_glu_b": nrm(ks[14], (DEPTH, W_GROUP), 0.01),
        "sc_conv_w": nrm(ks[15], (DEPTH, SC_WIDTH, W_GROUP), SC_WIDTH ** -0.5),
        "hg_gnorm": 1.0 + nrm(ks[16], (DEPTH, W_GROUP), 0.01),
        "lru_conv_w": nrm(ks[17], (DEPTH, LRU_CONV, W_GROUP), LRU_CONV ** -0.5),
        "lru_conv_b": nrm(ks[18], (DEPTH, W_GROUP), 0.01),
        "lru_wa": nrm(ks[19], (DEPTH, LRU_HEADS, LRU_HEAD_DIM, LRU_HEAD_DIM), LRU_HEAD_DIM ** -0.5),
        "lru_ba": nrm(ks[20], (DEPTH, W_GROUP), 0.01),
        "lru_wx": nrm(ks[21], (DEPTH, LRU_HEADS, LRU_HEAD_DIM, LRU_HEAD_DIM), LRU_HEAD_DIM ** -0.5),
        "lru_bx": nrm(ks[23], (DEPTH, W_GROUP), 0.01),
        "lru_a_param": jnp.log(lru_u) - jnp.log1p(-lru_u),
        "ln1_g": 1.0 + nrm(ks[24], (DEPTH, D_MODEL), 0.01),
        "ln1_b": nrm(ks[25], (DEPTH, D_MODEL), 0.01),
        "w_ffn_in": nrm(ks[26], (DEPTH, D_MODEL, 2 * D_FF), D_MODEL ** -0.5),
        "w_ffn_out": nrm(ks[27], (DEPTH, D_FF, D_MODEL), BETA * D_FF ** -0.5),
        "ln2_g": 1.0 + nrm(ks[28], (DEPTH, D_MODEL), 0.01),
        "ln2_b": nrm(ks[29], (DEPTH, D_MODEL), 0.01),
    }


def reference(x, meta_tokens, hg_lb_raw, w_in, w_out, s5_lam_re, s5_lam_im, s5_b_re, s5_b_im,
              s5_c_re, s5_c_im, s5_d, s5_log_dt, s5_glu_w, s5_glu_b, sc_conv_w, hg_gnorm,
              lru_conv_w, lru_conv_b, lru_wa, lru_ba, lru_wx, lru_bx, lru_a_param,
              ln1_g, ln1_b, w_ffn_in, w_ffn_out, ln2_g, ln2_b):
    f32 = jnp.float32
    bsz = x.shape[0]
    meta = jnp.broadcast_to(meta_tokens.astype(x.dtype)[None], (bsz, N_META, D_MODEL))
    h = jnp.concatenate([meta, x], axis=1)
    lb_all = jnp.cumsum(jax.nn.softmax(hg_lb_raw.astype(f32), axis=0), axis=0)
    lb_all = lb_all - lb_all[0:1]
    for l in range(DEPTH):
        proj = h @ w_in[l]
        (s5_u, sc_h, sc_b, sc_c, hg_q, hg_f, hg_i, hg_g, lru_x, lru_y) = jnp.split(proj, N_SPLITS, axis=-1)
        y_a = s5_mixer(s5_u, s5_lam_re[l], s5_lam_im[l], s5_b_re[l], s5_b_im[l], s5_c_re[l],
                       s5_c_im[l], s5_d[l], s5_log_dt[l], s5_glu_w[l], s5_glu_b[l])
        y_b = short_conv_mixer(sc_h, sc_b, sc_c, sc_conv_w[l])
        y_c = hgrn2_mixer(hg_q, hg_f, hg_i, hg_g, lb_all[l], hg_gnorm[l])
        y_d = rglru_mixer(lru_x, lru_y, lru_conv_w[l], lru_conv_b[l], lru_wa[l], lru_ba[l],
                          lru_wx[l], lru_bx[l], lru_a_param[l])
        mix = jnp.concatenate([y_a, y_b, y_c, y_d], axis=-1).astype(h.dtype) @ w_out[l]
        h = layer_norm(ALPHA * h + mix, ln1_g[l], ln1_b[l])
        gate, up = jnp.split(h @ w_ffn_in[l], 2, axis=-1)
        ffn = (jax.nn.silu(gate) * up) @ w_ffn_out[l]
        h = layer_norm(ALPHA * h + ffn, ln2_g[l], ln2_b[l])
    return h[:, N_META:]
```

```python
import math
import os
from contextlib import ExitStack

import numpy as np
import concourse.bass as bass
import concourse.mybir as mybir
from concourse.bass_utils import run_bass_kernel_spmd

F32 = mybir.dt.float32
AF = mybir.ActivationFunctionType
ALU = mybir.AluOpType

D = 1024
KT = 8
NIN = 2560
DFF = 2816
FT = 22
NMETA = 16
SEQ = 8192
DEPTH = 2
ALPHA = (2 * DEPTH) ** 0.25
EPS = 1e-5
NT = 512
WSLOT = 4096
NWSLOT = 2


class Trk:
    def __init__(self, nc, es):
        self.nc = nc
        self.es = es
        self.E = {"pe": nc.tensor, "act": nc.scalar, "dve": nc.vector,
                  "pool": nc.gpsimd, "sp": nc.sync}
        self.cur = {}
        self.waited = {}
        self.lastw = {}
        self.rd = {}
        self.nsem = 0
        self.LIM = 30000
        self.nins = 0

    def _sem(self, stream, inc):
        s = self.cur.get(stream)
        if s is None or s[1] + inc > self.LIM:
            name = f"s{self.nsem}"
            sem = self.es.enter_context(self.nc.semaphore(f"{name}_{stream}"))
            self.nsem += 1
            s = [sem, 0, name]
            self.cur[stream] = s
        s[1] += inc
        return (s[0], s[1], s[2])

    def wait(self, engine, tok):
        sem, val, name = tok
        k = (engine, name)
        if self.waited.get(k, 0) >= val:
            return
        self.waited[k] = val
        self.E[engine].wait_ge(sem, val)

    def op(self, engine, emit, reads=(), writes=(), stream=None, inc=1):
        deps = {}

        def add(tok):
            if tok is None:
                return
            n = tok[2]
            if n not in deps or deps[n][1] < tok[1]:
                deps[n] = tok

        for k in reads:
            add(self.lastw.get(k))
        for k in writes:
            add(self.lastw.get(k))
            for t in self.rd.get(k, {}).values():
                add(t)
        st = stream or engine
        own = self.cur.get(st)
        for tok in deps.values():
            if engine == "pe" and stream is None and own is not None and tok[2] == own[2]:
                continue
            self.wait(engine, tok)
        ins = emit(self.E[engine])
        tok = self._sem(st, inc)
        ins.then_inc(tok[0], inc)
        self.nins += 1
        for k in reads:
            self.rd.setdefault(k, {})[tok[2]] = tok
        for k in writes:
            self.lastw[k] = tok
            self.rd[k] = {}
        return tok

    def drain(self, engine):
        for st, s in self.cur.items():
            self.wait(engine, (s[0], s[1], s[2]))


class Builder:
    def __init__(self, n_xtiles, stub=(), taps=()):
        self.n_xtiles = n_xtiles
        self.stub = set(stub)
        self.taps = list(taps)
        self.tiles = [(0, NMETA)] + [(NMETA + i * NT, NT) for i in range(n_xtiles)]
        self.seq_x = n_xtiles * NT

    def build(self):
        nc = bass.Bass("TRN2", target_bir_lowering=False)
        self.nc = nc
        self.es = ExitStack()
        es = self.es
        self.t = Trk(nc, es)
        t = self.t
        self.in_names = []

        def di(name, shape):
            self.in_names.append(name)
            return nc.dram_tensor(name, list(shape), F32, kind="ExternalInput").ap()
        self.x = di("x", [self.seq_x, D])
        self.meta = di("meta_tokens", [NMETA, D])
        self.w_in = di("w_in", [DEPTH, D, NIN])
        self.w_out = di("w_out", [DEPTH, D, D])
        self.w_f1 = di("w_ffn_in", [DEPTH, D, 2 * DFF])
        self.w_f2 = di("w_ffn_out", [DEPTH, DFF, D])
        self.ln = {n: di(n, [DEPTH, D]) for n in ("ln1_g", "ln1_b", "ln2_g", "ln2_b")}
        self.pin = {}
        for nm, shp in (("hg_lb_raw", [DEPTH, 256]), ("sc_conv_w", [DEPTH, 3, 256]), ("hg_gnorm", [DEPTH, 256]),
                        ("lru_conv_w", [DEPTH, 4, 256]), ("lru_conv_b", [DEPTH, 256]),
                        ("lru_wa", [DEPTH, 4, 64, 64]), ("lru_ba", [DEPTH, 256]),
                        ("lru_wx", [DEPTH, 4, 64, 64]), ("lru_bx", [DEPTH, 256]), ("lru_a_param", [DEPTH, 256]),
                        ("s5_lam_re", [DEPTH, 16, 64]), ("s5_lam_im", [DEPTH, 16, 64]),
                        ("s5_b_re", [DEPTH, 16, 64, 16]), ("s5_b_im", [DEPTH, 16, 64, 16]),
                        ("s5_c_re", [DEPTH, 16, 16, 64]), ("s5_c_im", [DEPTH, 16, 16, 64]),
                        ("s5_d", [DEPTH, 256]), ("s5_log_dt", [DEPTH, 16]),
                        ("s5_glu_w", [DEPTH, 256, 256]), ("s5_glu_b", [DEPTH, 256])):
            self.pin[nm] = di(nm, shp)
        self.out = nc.dram_tensor("out", [self.seq_x, D], F32, kind="ExternalOutput").ap()
        ntile = len(self.tiles)
        self.hmid = nc.dram_tensor("hmid", [ntile, 128, KT * NT], F32, kind="Internal").ap()
        self.tapout = {}
        for name, shape in self.taps:
            self.tapout[name] = nc.dram_tensor("tap_" + name, list(shape), F32, kind="ExternalOutput").ap()

        sb = lambda name, shape: es.enter_context(nc.sbuf_tensor(name, list(shape), F32))
        self.NSLAB = 46
        self.SL = sb("SL", [128, self.NSLAB, NT])
        self.W = [sb(f"W{i}", [128, WSLOT]) for i in range(NWSLOT)]
        self.ident = sb("ident", [128, 128])
        self.ones = sb("ones", [128, 128])
        self.PRM = sb("PRM", [128, 128])
        self.tmp = [sb(f"tmp{i}", [128, NT]) for i in range(4)]
        self.cmask = sb("cmask", [128, NT])
        self.bd64 = sb("bd64", [128, 128])
        self.lruw = sb("lruw", [128, 2, 2, 128])
        self.car_sc = sb("car_sc", [128, 2, 2])
        self.car_lx = sb("car_lx", [128, 2, 3])
        self.car_lh = sb("car_lh", [128, 2])
        self.SS = sb("SS", [128, 2, 9, 128])
        self.SM = [sb(f"SM{i}", [128, 2, 128]) for i in range(2)]
        self.mask2 = sb("mask2", [128, 128])
        self.smn = 0
        self.Q = 8
        self.BsT = sb("BsT", [128, 2, 8, 2, 128])
        self.CsT = sb("CsT", [128, 8, 8, 2, 32])
        self.TA = sb("TA", [128, 8, 2, 64])
        self.TB = sb("TB", [128, 8, 2, 65])
        self.RR = sb("RR", [128, 8])
        self.GW = sb("GW", [128, 2, 256])
        self.CARJ = sb("CARJ", [128, 8, 2])
        self.cmask8 = sb("cmask8", [128, NT])
        self.CH = sb("CH", [128, 8, 66])
        self.S5S = sb("S5S", [128, 2176])
        self.ps = [es.enter_context(nc.psum_tensor(f"ps{i}", [128, NT], F32)) for i in range(8)]
        self.psn = 0
        self.tmpn = 0
        self.S_H = list(range(0, 8))
        self.S_Y = list(range(8, 16))
        self.S_Z = list(range(16, 24))
        self.S_U = list(range(24, 46))

        t.op("pool", lambda e: e.memset(self.ident[:], 0.0), writes=["ident"])
        t.op("pool", lambda e: e.affine_select(
            out=self.ident[:], in_=self.ident[:], compare_op=ALU.not_equal, fill=1.0,
            base=0, pattern=[[-1, 128]], channel_multiplier=1), writes=["ident"])
        t.op("pool", lambda e: e.memset(self.ones[:], 1.0), writes=["ones"])
        t.op("pool", lambda e: e.memset(self.cmask[:], 1.0), writes=["cmask"])
        t.op("pool", lambda e: e.affine_select(
            out=self.cmask[:].rearrange("p (c s) -> p c s", s=64), in_=self.cmask[:].rearrange("p (c s) -> p c s", s=64),
            compare_op=ALU.not_equal, fill=0.0, base=0, pattern=[[0, NT // 64], [1, 64]], channel_multiplier=0),
            writes=["cmask"])
        t.op("pool", lambda e: e.memset(self.cmask8[:], 1.0), writes=["cmask8"])
        t.op("pool", lambda e: e.affine_select(
            out=self.cmask8[:].rearrange("p (c s) -> p c s", s=8), in_=self.cmask8[:].rearrange("p (c s) -> p c s", s=8),
            compare_op=ALU.not_equal, fill=0.0, base=0, pattern=[[0, NT // 8], [1, 8]], channel_multiplier=0),
            writes=["cmask8"])
        t.op("pool", lambda e: e.memset(self.mask2[:], 1.0), writes=["mask2"])
        t.op("pool", lambda e: e.affine_select(
            out=self.mask2[:, :], in_=self.mask2[:, :],
            compare_op=ALU.is_ge, fill=0.0, base=0, pattern=[[1, 128]], channel_multiplier=-1),
            writes=["mask2"])
        t.op("pool", lambda e: e.memset(self.mask2[0:64, 64:128], 0.0), writes=["mask2"])
        t.op("pool", lambda e: e.memset(self.bd64[:], 0.0), writes=["bd64"])
        for hh in range(2):
            t.op("pool", lambda e, hh=hh: e.memset(self.bd64[hh * 64:(hh + 1) * 64, hh * 64:(hh + 1) * 64], 1.0 / 64),
                 writes=["bd64"])

        self.wq = []
        self.wq_issued = 0
        self.wq_used = 0
        for l in range(DEPTH):
            for ti in range(ntile):
                self.wq.extend(self.chunk_seq(l))

        for l in range(DEPTH):
            self.layer_setup(l)
            for ti, (t0, n) in enumerate(self.tiles):
                self.tile_pass(l, ti, t0, n)
        t.drain("pool")
        es.close()
        return nc

    def sl(self, idx, n=NT):
        return self.SL[:, idx, 0:n]

    def slk(self, idx):
        return ("sl", idx)

    def nextps(self):
        i = self.psn
        self.psn = (self.psn + 1) % 6
        return i

    def nexttmp(self):
        i = self.tmpn
        self.tmpn = (self.tmpn + 1) % 4
        return i

    def tap(self, name, src_ap, reads):
        if name in self.tapout:
            self.t.op("pool", lambda e: e.dma_start(out=self.tapout[name], in_=src_ap),
                      reads=reads, stream="dma_io", inc=16)

    def chunk_seq(self, l):
        seq = []
        wi = self.w_in[l].rearrange("(kt p) n -> p kt n", p=128)
        for c in range(5):
            seq.append(("win", [(wi[:, :, c * 512:(c + 1) * 512], 0, KT, 512)]))
        wo = self.w_out[l].rearrange("(kt p) n -> p kt n", p=128)
        for c in range(2):
            seq.append(("wout", [(wo[:, :, c * 512:(c + 1) * 512], 0, KT, 512)]))
        w1 = self.w_f1[l].rearrange("(kt p) n -> p kt n", p=128)
        for c in range(11):
            seq.append(("f1", [(w1[:, :, c * 256:(c + 1) * 256], 0, KT, 256),
                               (w1[:, :, DFF + c * 256:DFF + (c + 1) * 256], KT * 256, KT, 256)]))
        w2 = self.w_f2[l].rearrange("(kt p) n -> p kt n", p=128)
        for c in range(8):
            seq.append(("f2", [(w2[:, :, c * 128:(c + 1) * 128], 0, FT, 128)]))
        return seq

    def _issue_chunk(self, idx):
        kind, parts = self.wq[idx]
        slot = idx % NWSLOT
        for pidx, (src, off, nk, ncol) in enumerate(parts):
            dst = self.W[slot][:, off:off + nk * ncol].rearrange("p (k n) -> p k n", k=nk)
            wk = [("w", slot, pidx)] if len(parts) == 2 else [("w", slot, 0), ("w", slot, 1)]
            self.t.op("sp", lambda e, dst=dst, src=src: e.dma_start(out=dst, in_=src),
                      writes=wk, stream=f"dma_w{slot}", inc=16)

    def wacquire(self, kind):
        idx = self.wq_used
        assert self.wq[idx][0] == kind, (self.wq[idx][0], kind)
        while self.wq_issued <= idx:
            self._issue_chunk(self.wq_issued)
            self.wq_issued += 1
        self.wq_used += 1
        return idx % NWSLOT

    def wprefetch(self):
        lim = min(self.wq_used + NWSLOT - 1, len(self.wq))
        while self.wq_issued < lim:
            self._issue_chunk(self.wq_issued)
            self.wq_issued += 1

    def wview(self, slot, off, nk, ncol):
        return self.W[slot][:, off:off + nk * ncol].rearrange("p (k n) -> p k n", k=nk)

    def layer_setup(self, l):
        t = self.t
        for i, n in enumerate(("ln1_g", "ln1_b", "ln2_g", "ln2_b")):
            src = self.ln[n][l].rearrange("(m p) -> p m", p=128)
            t.op("pool", lambda e, src=src, i=i: e.dma_start(
                out=self.PRM[:, i * 8:(i + 1) * 8], in_=src, allow_slow_non_contiguous=True),
                writes=[("prm", i)], stream="dma_io", inc=16)
        P = self.PRM
        pin = self.pin

        def vec(name, col, key):
            src = pin[name][l].rearrange("(c p) -> p c", p=128)
            t.op("pool", lambda e: e.dma_start(out=P[:, col:col + 2], in_=src, allow_slow_non_contiguous=True),
                 writes=[key], stream="dma_io", inc=16)

        for c in range(2):
            t.op("pool", lambda e, c=c: e.dma_start(
                out=P[:, 32 + 3 * c:35 + 3 * c],
                in_=pin["sc_conv_w"][l][:, c * 128:(c + 1) * 128].rearrange("k p -> p k"),
                allow_slow_non_contiguous=True), writes=["p_scw"], stream="dma_io", inc=16)
            t.op("pool", lambda e, c=c: e.dma_start(
                out=P[:, 38 + 4 * c:42 + 4 * c],
                in_=pin["lru_conv_w"][l][:, c * 128:(c + 1) * 128].rearrange("k p -> p k"),
                allow_slow_non_contiguous=True), writes=["p_lcw"], stream="dma_io", inc=16)
        vec("lru_conv_b", 46, "p_lcb")
        vec("lru_ba", 48, "p_lba")
        vec("lru_bx", 50, "p_lbx")
        vec("lru_a_param", 52, "p_cp")
        vec("hg_gnorm", 60, "p_gn")
        vec("s5_d", 62, "p_s5d")
        vec("s5_glu_b", 64, "p_glub")
        t.op("act", lambda e: e.activation(out=P[:, 52:54], in_=P[:, 52:54], func=AF.Exp, scale=-1.0),
             reads=["p_cp"], writes=["p_cp"])
        t.op("act", lambda e: e.activation(out=P[:, 52:54], in_=P[:, 52:54], func=AF.Ln, bias=1.0, scale=1.0),
             reads=["p_cp"], writes=["p_cp"])
        t.op("dve", lambda e: e.tensor_single_scalar(P[:, 54:56], P[:, 52:54], -16.0, ALU.mult),
             reads=["p_cp"], writes=["p_cp2"])
        t.op("dve", lambda e: e.tensor_single_scalar(P[:, 52:54], P[:, 52:54], -8.0, ALU.mult),
             reads=["p_cp", "p_cp2"], writes=["p_cp"])
        if l == 0:
            t.op("dve", lambda e: e.memset(P[:, 56:58], 0.0), writes=["p_lb"])
        else:
            for i in range(2):
                src = pin["hg_lb_raw"][i].rearrange("(c p) -> p c", p=128)
                t.op("pool", lambda e, i=i, src=src: e.dma_start(out=P[:, 66 + 2 * i:68 + 2 * i], in_=src,
                                                                 allow_slow_non_contiguous=True),
                     writes=[("p_lbraw", i)], stream="dma_io", inc=16)
            t.op("dve", lambda e: e.tensor_tensor(P[:, 56:58], P[:, 68:70], P[:, 66:68], ALU.subtract),
                 reads=[("p_lbraw", 0), ("p_lbraw", 1)], writes=["p_lb"])
            t.op("act", lambda e: e.activation(out=P[:, 56:58], in_=P[:, 56:58], func=AF.Sigmoid),
                 reads=["p_lb"], writes=["p_lb"])
        t.op("dve", lambda e: e.tensor_scalar(P[:, 58:60], P[:, 56:58], -1.0, 1.0, ALU.mult, ALU.add),
             reads=["p_lb"], writes=["p_oml"])
        t.op("dve", lambda e: e.memset(self.lruw[:], 0.0), writes=["lruw"])
        for gi, nm in enumerate(("lru_wa", "lru_wx")):
            for h in range(4):
                c, hh = h // 2, h % 2
                t.op("pool", lambda e, gi=gi, nm=nm, h=h, c=c, hh=hh: e.dma_start(
                    out=self.lruw[hh * 64:(hh + 1) * 64, gi, c, hh * 64:(hh + 1) * 64], in_=pin[nm][l, h]),
                    writes=["lruw"], stream="dma_io", inc=16)
        t.op("dve", lambda e: e.memset(self.car_sc[:], 0.0), writes=["car_sc"])
        t.op("dve", lambda e: e.memset(self.car_lx[:], 0.0), writes=["car_lx"])
        t.op("dve", lambda e: e.memset(self.car_lh[:], 0.0), writes=["car_lh"])
        t.op("dve", lambda e: e.memset(self.SS[:], 0.0), writes=[("SS", 0), ("SS", 1)])
        if "s5" not in self.stub:
            self.s5_setup(l)

    def evac(self, eng, out_ap, in_ap, reads, writes):
        if eng == "act":
            return self.t.op("act", lambda e: e.activation(out=out_ap, in_=in_ap, func=AF.Copy),
                             reads=reads, writes=writes)
        return self.t.op("dve", lambda e: e.tensor_copy(out_ap, in_ap), reads=reads, writes=writes)

    def mm(self, ps_ap, pskey, pairs, reads):
        def emit(pe):
            n = len(pairs)
            ins = None
            for i, (lh, rh) in enumerate(pairs):
                ins = pe.matmul(ps_ap, lh, rh, start=(i == 0), stop=(i == n - 1))
            return ins
        return self.t.op("pe", emit, reads=reads, writes=[pskey])

    def load_input(self, l, ti, t0, n):
        t = self.t
        if l == 0:
            nb = (n + 127) // 128
            pb = min(n, 128)
            stage = self.SL[:, self.S_Y[0]:self.S_Y[0] + 8, :].rearrange("p a b -> p (a b)")
            stage = stage[:, 0:nb * D].rearrange("p (b f) -> p b f", b=nb)
            if ti == 0:
                src = self.meta.rearrange("(b p) f -> p b f", p=pb)
            else:
                src = self.x[t0 - NMETA:t0 - NMETA + n, :].rearrange("(b p) f -> p b f", p=pb)
            t.op("pool", lambda e: e.dma_start(out=stage[0:pb], in_=src),
                 writes=[self.slk(i) for i in self.S_Y], stream="dma_io", inc=16)
            for k in range(KT):
                pi = self.nextps()

                def emit(pe, k=k, pi=pi):
                    ins = None
                    for b in range(nb):
                        ins = pe.transpose(self.ps[pi][:, b * pb:(b + 1) * pb],
                                           stage[0:pb, b, k * 128:(k + 1) * 128],
                                           self.ident[0:pb, 0:pb])
                    return ins
                t.op("pe", emit, reads=[self.slk(i) for i in self.S_Y] + ["ident"], writes=[("ps", pi)])
                self.evac("act" if k % 2 else "dve", self.sl(self.S_H[k], n), self.ps[pi][:, 0:n],
                          reads=[("ps", pi)], writes=[self.slk(self.S_H[k])])
        else:
            src = self.hmid[ti].rearrange("p (k n) -> p k n", k=KT)[:, :, 0:n]
            dst = self.SL[:, self.S_H[0]:self.S_H[0] + 8, 0:n]
            t.op("pool", lambda e: e.dma_start(out=dst, in_=src),
                 writes=[self.slk(i) for i in self.S_H], stream="dma_io", inc=16)

    def store_output(self, l, ti, t0, n):
        t = self.t
        if l == 0:
            dst = self.hmid[ti].rearrange("p (k n) -> p k n", k=KT)[:, :, 0:n]
            src = self.SL[:, self.S_Z[0]:self.S_Z[0] + 8, 0:n]
            t.op("pool", lambda e: e.dma_start(out=dst, in_=src),
                 reads=[self.slk(i) for i in self.S_Z], stream="dma_io", inc=16)
        else:
            if ti == 0:
                return
            nb = n // 128
            stage = self.SL[:, self.S_Y[0]:self.S_Y[0] + 8, :].rearrange("p a b -> p (a b)")
            stage = stage[:, 0:nb * D].rearrange("p (b f) -> p b f", b=nb)
            for b in range(nb):
                for half in range(2):
                    pi = self.nextps()

                    def emit(pe, b=b, half=half, pi=pi):
                        ins = None
                        for kk in range(4):
                            k = half * 4 + kk
                            ins = pe.transpose(self.ps[pi][:, kk * 128:(kk + 1) * 128],
                                               self.SL[:, self.S_Z[k], b * 128:(b + 1) * 128],
                                               self.ident[:, :])
                        return ins
                    t.op("pe", emit, reads=[self.slk(self.S_Z[half * 4 + kk]) for kk in range(4)] + ["ident"],
                         writes=[("ps", pi)])
                    self.evac("act" if half else "dve", stage[:, b, half * 512:(half + 1) * 512],
                              self.ps[pi][:, :], reads=[("ps", pi)], writes=[self.slk(self.S_Y[2 * b + half])])
            dst = self.out[t0 - NMETA:t0 - NMETA + n, :].rearrange("(b p) f -> p b f", p=128)
            t.op("pool", lambda e: e.dma_start(out=dst, in_=stage),
                 reads=[self.slk(i) for i in self.S_Y], stream="dma_io", inc=16)

    def layernorm(self, n, gcol, bcol):
        t = self.t
        Z = self.S_Z
        ps_s = self.nextps()
        ps_q = self.nextps()
        self.mm(self.ps[ps_s][:, 0:n], ("ps", ps_s),
                [(self.ones[:, :], self.sl(Z[m], n)) for m in range(KT)],
                reads=[self.slk(Z[m]) for m in range(KT)] + ["ones"])
        sq = []
        for m in range(KT):
            ti = self.nexttmp()
            t.op("act", lambda e, m=m, ti=ti: e.activation(out=self.tmp[ti][:, 0:n], in_=self.sl(Z[m], n),
                                                          func=AF.Square),
                 reads=[self.slk(Z[m])], writes=[("tmp", ti)])
            first = (m == 0)
            last = (m == KT - 1)
            t.op("pe", lambda e, ti=ti, first=first, last=last: e.matmul(
                self.ps[ps_q][:, 0:n], self.ones[:, :], self.tmp[ti][:, 0:n], start=first, stop=last),
                reads=[("tmp", ti), "ones"], writes=[("ps", ps_q)])
        mi = self.nexttmp()
        mean = self.tmp[mi]
        mk = ("tmp", mi)
        ri = self.nexttmp()
        rstd = self.tmp[ri]
        rk = ("tmp", ri)
        t.op("dve", lambda e: e.tensor_single_scalar(mean[:, 0:n], self.ps[ps_s][:, 0:n], 1.0 / D, ALU.mult),
             reads=[("ps", ps_s)], writes=[mk])
        t.op("dve", lambda e: e.tensor_tensor(rstd[:, 0:n], mean[:, 0:n], mean[:, 0:n], ALU.mult),
             reads=[mk], writes=[rk])
        t.op("dve", lambda e: e.scalar_tensor_tensor(rstd[:, 0:n], self.ps[ps_q][:, 0:n], 1.0 / D,
                                                     rstd[:, 0:n], ALU.mult, ALU.subtract),
             reads=[("ps", ps_q), rk], writes=[rk])
        t.op("act", lambda e: e.activation(out=rstd[:, 0:n], in_=rstd[:, 0:n], func=AF.Sqrt, bias=EPS, scale=1.0),
             reads=[rk], writes=[rk])
        t.op("dve", lambda e: e.reciprocal(rstd[:, 0:n], rstd[:, 0:n]), reads=[rk], writes=[rk])
        for m in range(KT):
            zk = self.slk(Z[m])
            t.op("dve", lambda e, m=m: e.tensor_tensor(self.sl(Z[m], n), self.sl(Z[m], n), mean[:, 0:n], ALU.subtract),
                 reads=[zk, mk], writes=[zk])
            t.op("dve", lambda e, m=m: e.tensor_tensor(self.sl(Z[m], n), self.sl(Z[m], n), rstd[:, 0:n], ALU.mult),
                 reads=[zk, rk], writes=[zk])
            t.op("act", lambda e, m=m: e.activation(out=self.sl(Z[m], n), in_=self.sl(Z[m], n), func=AF.Identity,
                                                    scale=self.PRM[:, gcol + m:gcol + m + 1],
                                                    bias=self.PRM[:, bcol + m:bcol + m + 1]),
                 reads=[zk, ("prm", gcol // 8), ("prm", bcol // 8)], writes=[zk])

    def tile_pass(self, l, ti, t0, n):
        t = self.t
        H, Y, Z, U = self.S_H, self.S_Y, self.S_Z, self.S_U
        self.load_input(l, ti, t0, n)
        hreads = [self.slk(H[k]) for k in range(KT)]
        for c in range(5):
            slot = self.wacquire("win")
            wv = self.wview(slot, 0, KT, 512)
            for jj in range(4):
                j = c * 4 + jj
                if j == 12 and "hg" not in self.stub:
                    nb = (n + 127) // 128
                    VT = self.SL[:, U[12]:U[12] + 2, :].rearrange("p a b -> p (a b)").rearrange(
                        "p (b f) -> p b f", f=256)
                    for blk in range(nb):
                        pb = min(128, n - blk * 128)
                        pi = self.nextps()
                        self.mm(self.ps[pi][0:pb, 0:256], ("ps", pi),
                                [(self.SL[:, H[k], blk * 128:blk * 128 + pb], wv[:, k, 0:256]) for k in range(KT)],
                                reads=hreads + [("w", slot, 0), ("w", slot, 1)])
                        self.evac("act" if blk % 2 else "dve", VT[0:pb, blk, :], self.ps[pi][0:pb, 0:256],
                                  reads=[("ps", pi)], writes=[self.slk(U[12]), self.slk(U[13])])
                    continue
                if j == 13 and "hg" not in self.stub:
                    continue
                pi = self.nextps()
                self.mm(self.ps[pi][:, 0:n], ("ps", pi),
                        [(wv[:, k, jj * 128:(jj + 1) * 128], self.sl(H[k], n)) for k in range(KT)],
                        reads=hreads + [("w", slot, 0), ("w", slot, 1)])
                self.evac("act" if j % 2 else "dve", self.sl(U[j], n), self.ps[pi][:, 0:n],
                          reads=[("ps", pi)], writes=[self.slk(U[j])])
            self.wprefetch()
        self.mixers(l, ti, t0, n)
        yreads = [self.slk(Y[k]) for k in range(KT)]
        for c in range(2):
            slot = self.wacquire("wout")
            wv = self.wview(slot, 0, KT, 512)
            for jj in range(4):
                m = c * 4 + jj
                pi = self.nextps()
                self.mm(self.ps[pi][:, 0:n], ("ps", pi),
                        [(wv[:, k, jj * 128:(jj + 1) * 128], self.sl(Y[k], n)) for k in range(KT)],
                        reads=yreads + [("w", slot, 0), ("w", slot, 1)])
                t.op("dve", lambda e, m=m, pi=pi: e.scalar_tensor_tensor(
                    self.sl(Z[m], n), self.sl(H[m], n), ALPHA, self.ps[pi][:, 0:n], ALU.mult, ALU.add),
                    reads=[("ps", pi), self.slk(H[m])], writes=[self.slk(Z[m])])
            self.wprefetch()
        self.layernorm(n, 0, 8)
        if "tap_h1" in self.tapout or "h1" in self.tapout:
            pass
        zreads = [self.slk(Z[k]) for k in range(KT)]
        for c in range(11):
            slot = self.wacquire("f1")
            wg = self.wview(slot, 0, KT, 256)
            wu = self.wview(slot, KT * 256, KT, 256)
            for jj in range(2):
                i = c * 2 + jj
                pg = self.nextps()
                pu = self.nextps()
                self.mm(self.ps[pg][:, 0:n], ("ps", pg),
                        [(wg[:, k, jj * 128:(jj + 1) * 128], self.sl(Z[k], n)) for k in range(KT)],
                        reads=zreads + [("w", slot, 0), ("w", slot, 1)])
                self.mm(self.ps[pu][:, 0:n], ("ps", pu),
                        [(wu[:, k, jj * 128:(jj + 1) * 128], self.sl(Z[k], n)) for k in range(KT)],
                        reads=zreads + [("w", slot, 0), ("w", slot, 1)])
                tk = self.nexttmp()
                t.op("act", lambda e, tk=tk, pg=pg: e.activation(out=self.tmp[tk][:, 0:n], in_=self.ps[pg][:, 0:n],
                                                                func=AF.Silu),
                     reads=[("ps", pg)], writes=[("tmp", tk)])
                t.op("dve", lambda e, tk=tk, pu=pu, i=i: e.tensor_tensor(
                    self.sl(U[i], n), self.tmp[tk][:, 0:n], self.ps[pu][:, 0:n], ALU.mult),
                    reads=[("tmp", tk), ("ps", pu)], writes=[self.slk(U[i])])
            self.wprefetch()
        ureads = [self.slk(U[k]) for k in range(FT)]
        for m in range(KT):
            slot = self.wacquire("f2")
            wv = self.wview(slot, 0, FT, 128)
            pi = self.nextps()
            self.mm(self.ps[pi][:, 0:n], ("ps", pi),
                    [(wv[:, k, :], self.sl(U[k], n)) for k in range(FT)],
                    reads=ureads + [("w", slot, 0), ("w", slot, 1)])
            t.op("dve", lambda e, m=m, pi=pi: e.scalar_tensor_tensor(
                self.sl(Z[m], n), self.sl(Z[m], n), ALPHA, self.ps[pi][:, 0:n], ALU.mult, ALU.add),
                reads=[("ps", pi), self.slk(Z[m])], writes=[self.slk(Z[m])])
            self.wprefetch()
        self.layernorm(n, 16, 24)
        self.store_output(l, ti, t0, n)

    def prm(self, col):
        return self.PRM[:, col:col + 1]

    def conv_acc(self, acc, x, carry, wcol, K, n, xk, acck, cark, wkey, first_bias=None):
        t = self.t
        if first_bias is None:
            t.op("dve", lambda e: e.tensor_single_scalar(acc[:, 0:n], x[:, 0:n], self.prm(wcol + K - 1), ALU.mult),
                 reads=[xk, wkey], writes=[acck])
        else:
            t.op("dve", lambda e: e.tensor_scalar(acc[:, 0:n], x[:, 0:n], self.prm(wcol + K - 1), self.prm(first_bias),
                                                  ALU.mult, ALU.add),
                 reads=[xk, wkey, "p_lcb"], writes=[acck])
        for k in range(K - 1):
            sh = K - 1 - k
            t.op("dve", lambda e, sh=sh, k=k: e.scalar_tensor_tensor(
                acc[:, sh:n], x[:, 0:n - sh], self.prm(wcol + k), acc[:, sh:n], ALU.mult, ALU.add),
                reads=[xk, wkey, acck], writes=[acck])
            t.op("dve", lambda e, sh=sh, k=k: e.scalar_tensor_tensor(
                acc[:, 0:sh], carry[:, K - 1 - sh:K - 1], self.prm(wcol + k), acc[:, 0:sh], ALU.mult, ALU.add),
                reads=[cark, wkey, acck], writes=[acck])
        t.op("dve", lambda e: e.tensor_copy(carry[:, 0:K - 1], x[:, n - (K - 1):n]),
             reads=[xk], writes=[cark])

    def mixers(self, l, ti, t0, n):
        t = self.t
        Y, U, Z = self.S_Y, self.S_U, self.S_Z
        if "s5" in self.stub:
            for k in range(2):
                self.evac("act" if k % 2 else "dve", self.sl(Y[k], n), self.sl(U[k], n),
                          reads=[self.slk(U[k])], writes=[self.slk(Y[k])])
        else:
            self.mix_s5(l, ti, n)
        if "sc" in self.stub:
            for k in range(2):
                self.evac("act", self.sl(Y[2 + k], n), self.sl(U[2 + k], n),
                          reads=[self.slk(U[2 + k])], writes=[self.slk(Y[2 + k])])
        else:
            self.mix_sc(n)
        if "hg" in self.stub:
            for k in range(2):
                self.evac("dve", self.sl(Y[4 + k], n), self.sl(U[4 + k], n),
                          reads=[self.slk(U[4 + k])], writes=[self.slk(Y[4 + k])])
        else:
            self.mix_hg(n)
        if "lru" in self.stub:
            for k in range(2):
                self.evac("act", self.sl(Y[6 + k], n), self.sl(U[6 + k], n),
                          reads=[self.slk(U[6 + k])], writes=[self.slk(Y[6 + k])])
        else:
            self.mix_lru(n)

    def mix_sc(self, n):
        t = self.t
        Y, U, Z = self.S_Y, self.S_U, self.S_Z
        for c in range(2):
            hs, bs, cs = U[2 + c], U[4 + c], U[6 + c]
            acc = Z[c]
            t.op("dve", lambda e: e.tensor_tensor(self.sl(hs, n), self.sl(hs, n), self.sl(cs, n), ALU.mult),
                 reads=[self.slk(hs), self.slk(cs)], writes=[self.slk(hs)])
            self.conv_acc(self.SL[:, acc, :], self.SL[:, hs, :], self.car_sc[:, c, :], 32 + 3 * c, 3, n,
                          self.slk(hs), self.slk(acc), ("car_sc", c), "p_scw")
            t.op("dve", lambda e: e.tensor_tensor(self.sl(Y[2 + c], n), self.sl(acc, n), self.sl(bs, n), ALU.mult),
                 reads=[self.slk(acc), self.slk(bs)], writes=[self.slk(Y[2 + c])])

    def mix_lru(self, n):
        t = self.t
        Y, U, Z = self.S_Y, self.S_U, self.S_Z
        for c in range(2):
            xs, ys = U[16 + c], U[18 + c]
            xc, ga, gx, aa = Z[0], Z[1], Z[2], Z[3]
            k = self.slk
            self.conv_acc(self.SL[:, xc, :], self.SL[:, xs, :], self.car_lx[:, c, :], 38 + 4 * c, 4, n,
                          k(xs), k(xc), ("car_lx", c), "p_lcw", first_bias=46 + c)
            pa, px = self.nextps(), self.nextps()
            self.mm(self.ps[pa][:, 0:n], ("ps", pa), [(self.lruw[:, 0, c, :], self.sl(xc, n))], reads=[k(xc), "lruw"])
            self.mm(self.ps[px][:, 0:n], ("ps", px), [(self.lruw[:, 1, c, :], self.sl(xc, n))], reads=[k(xc), "lruw"])
            t.op("act", lambda e: e.activation(out=self.sl(ga, n), in_=self.ps[pa][:, 0:n], func=AF.Sigmoid,
                                               bias=self.prm(48 + c), scale=1.0),
                 reads=[("ps", pa), "p_lba"], writes=[k(ga)])
            t.op("act", lambda e: e.activation(out=self.sl(gx, n), in_=self.ps[px][:, 0:n], func=AF.Sigmoid,
                                               bias=self.prm(50 + c), scale=1.0),
                 reads=[("ps", px), "p_lbx"], writes=[k(gx)])
            t.op("act", lambda e: e.activation(out=self.sl(aa, n), in_=self.sl(ga, n), func=AF.Exp,
                                               scale=self.prm(52 + c)),
                 reads=[k(ga), "p_cp"], writes=[k(aa)])
            t.op("act", lambda e: e.activation(out=self.sl(ga, n), in_=self.sl(ga, n), func=AF.Exp,
                                               scale=self.prm(54 + c)),
                 reads=[k(ga), "p_cp2"], writes=[k(ga)])
            t.op("act", lambda e: e.activation(out=self.sl(ga, n), in_=self.sl(ga, n), func=AF.Sqrt,
                                               scale=-1.0, bias=1.0),
                 reads=[k(ga)], writes=[k(ga)])
            t.op("dve", lambda e: e.tensor_tensor(self.sl(gx, n), self.sl(gx, n), self.sl(xc, n), ALU.mult),
                 reads=[k(gx), k(xc)], writes=[k(gx)])
            t.op("dve", lambda e: e.tensor_tensor(self.sl(gx, n), self.sl(gx, n), self.sl(ga, n), ALU.mult),
                 reads=[k(gx), k(ga)], writes=[k(gx)])
            t.op("dve", lambda e: e.tensor_tensor_scan(self.sl(xc, n), self.sl(aa, n), self.sl(gx, n),
                                                       self.car_lh[:, c:c + 1], ALU.mult, ALU.add),
                 reads=[k(aa), k(gx), ("car_lh", c)], writes=[k(xc)])
            t.op("dve", lambda e: e.tensor_copy(self.car_lh[:, c:c + 1], self.SL[:, xc, n - 1:n]),
                 reads=[k(xc)], writes=[("car_lh", c)])
            t.op("act", lambda e: e.activation(out=self.sl(aa, n), in_=self.sl(ys, n), func=AF.Gelu_apprx_tanh),
                 reads=[k(ys)], writes=[k(aa)])
            t.op("dve", lambda e: e.tensor_tensor(self.sl(Y[6 + c], n), self.sl(xc, n), self.sl(aa, n), ALU.mult),
                 reads=[k(xc), k(aa)], writes=[k(Y[6 + c])])

    def mix_hg(self, n):
        t = self.t
        Y, U, Z = self.S_Y, self.S_U, self.S_Z
        k = self.slk
        CL = 64 if n >= 64 else n
        nch = n // CL
        mid = CL // 2
        nb = (n + 127) // 128
        VT = self.SL[:, U[12]:U[12] + 2, :].rearrange("p a b -> p (a b)").rearrange("p (b f) -> p b f", f=256)
        vtk = [k(U[12]), k(U[13])]
        X = [Z[0], Z[1], Z[2], Z[3], Z[4], Z[5], Z[6], Z[7]]
        c3 = lambda ap: ap.rearrange("p (c s) -> p c s", s=CL)
        for pr in range(2):
            qs, fs, gs = U[8 + pr], U[10 + pr], U[14 + pr]
            g_, b_, d_, e1, e2, e3, qp, ktk = X
            t.op("act", lambda e: e.activation(out=self.sl(qs, n), in_=self.sl(qs, n), func=AF.Silu),
                 reads=[k(qs)], writes=[k(qs)])
            t.op("act", lambda e: e.activation(out=self.sl(fs, n), in_=self.sl(fs, n), func=AF.Sigmoid),
                 reads=[k(fs)], writes=[k(fs)])
            t.op("dve", lambda e: e.tensor_scalar(self.sl(fs, n), self.sl(fs, n), self.prm(58 + pr), self.prm(56 + pr),
                                                  ALU.mult, ALU.add),
                 reads=[k(fs), "p_lb", "p_oml"], writes=[k(fs)])
            t.op("act", lambda e: e.activation(out=self.sl(g_, n), in_=self.sl(fs, n), func=AF.Ln),
                 reads=[k(fs)], writes=[k(g_)])
            t.op("dve", lambda e: e.tensor_scalar(self.sl(fs, n), self.sl(fs, n), -1.0, 1.0, ALU.mult, ALU.add),
                 reads=[k(fs), k(g_)], writes=[k(fs)])
            t.op("dve", lambda e: e.tensor_tensor_scan(self.sl(b_, n), self.cmask[:, 0:n], self.sl(g_, n), 0.0,
                                                       ALU.mult, ALU.add),
                 reads=[k(g_), "cmask"], writes=[k(b_)])
            b3 = c3(self.sl(b_, n))
            t.op("dve", lambda e: e.tensor_tensor(c3(self.sl(d_, n)), b3, b3[:, :, mid:mid + 1].to_broadcast([128, nch, CL]),
                                                  ALU.subtract),
                 reads=[k(b_)], writes=[k(d_)])
            t.op("act", lambda e: e.activation(out=self.sl(e1, n), in_=self.sl(d_, n), func=AF.Exp),
                 reads=[k(d_)], writes=[k(e1)])
            t.op("act", lambda e: e.activation(out=self.sl(e2, n), in_=self.sl(d_, n), func=AF.Exp, scale=-1.0),
                 reads=[k(d_)], writes=[k(e2)])
            t.op("act", lambda e: e.activation(out=self.sl(e3, n), in_=self.sl(b_, n), func=AF.Exp),
                 reads=[k(b_)], writes=[k(e3)])
            t.op("dve", lambda e: e.tensor_tensor(c3(self.sl(d_, n)), b3, b3[:, :, CL - 1:CL].to_broadcast([128, nch, CL]),
                                                  ALU.subtract),
                 reads=[k(b_), k(e1), k(e2)], writes=[k(d_)])
            t.op("act", lambda e: e.activation(out=self.sl(d_, n), in_=self.sl(d_, n), func=AF.Exp, scale=-1.0),
                 reads=[k(d_)], writes=[k(d_)])
            t.op("dve", lambda e: e.scalar_tensor_tensor(self.sl(e1, n), self.sl(qs, n), 0.125, self.sl(e1, n),
                                                         ALU.mult, ALU.mult),
                 reads=[k(qs), k(e1)], writes=[k(e1)])
            t.op("dve", lambda e: e.tensor_tensor(self.sl(e2, n), self.sl(fs, n), self.sl(e2, n), ALU.mult),
                 reads=[k(fs), k(e2)], writes=[k(e2)])
            t.op("dve", lambda e: e.scalar_tensor_tensor(self.sl(qp, n), self.sl(qs, n), 0.125, self.sl(e3, n),
                                                         ALU.mult, ALU.mult),
                 reads=[k(qs), k(e3)], writes=[k(qp)])
            t.op("dve", lambda e: e.tensor_tensor(self.sl(d_, n), self.sl(fs, n), self.sl(d_, n), ALU.mult),
                 reads=[k(fs), k(d_)], writes=[k(d_)])
            KTv = self.SL[:, ktk, :].rearrange("p (b f) -> p b f", f=128)
            for blk in range(nb):
                pb = min(128, n - blk * 128)
                pi = self.nextps()
                t.op("pe", lambda e, blk=blk, pb=pb, pi=pi: e.transpose(
                    self.ps[pi][0:pb, 0:128], self.SL[:, d_, blk * 128:blk * 128 + pb], self.ident[:, :]),
                    reads=[k(d_), "ident"], writes=[("ps", pi)])
                self.evac("act", KTv[0:pb, blk, :], self.ps[pi][0:pb, 0:128], reads=[("ps", pi)], writes=[k(ktk)])
            for c in range(nch):
                blk, r0 = (c * CL) // 128, (c * CL) % 128
                pi = self.nextps()
                self.mm(self.ps[pi][:, 0:128], ("ps", pi),
                        [(KTv[r0:r0 + CL, blk, :], VT[r0:r0 + CL, blk, pr * 128:(pr + 1) * 128])],
                        reads=[k(ktk)] + vtk)
                for hh in range(2):
                    rs = slice(hh * 64, (hh + 1) * 64)
                    t.op("dve", lambda e, c=c, rs=rs, pi=pi: e.scalar_tensor_tensor(
                        self.SS[rs, pr, c + 1, rs], self.SS[rs, pr, c, rs],
                        self.SL[rs, e3, (c + 1) * CL - 1:(c + 1) * CL], self.ps[pi][rs, rs],
                        ALU.mult, ALU.add),
                        reads=[("ps", pi), k(e3), ("SS", pr)], writes=[("SS", pr)])
            OPS = self.ps[6]
            for blk in range(nb):
                pb = min(128, n - blk * 128)
                bs = slice(blk * 128, blk * 128 + pb)
                smi = self.smn
                self.smn = (self.smn + 1) % 2
                SMv = self.SM[smi]
                for hh in range(2):
                    rs = slice(hh * 64, (hh + 1) * 64)
                    pi = self.nextps()
                    t.op("pe", lambda e, pi=pi, rs=rs, bs=bs, pb=pb, hh=hh: e.matmul(
                        self.ps[pi][0:pb, 0:pb], self.SL[rs, e2, bs], self.SL[rs, e1, bs],
                        start=True, stop=True, tile_position=(hh * 64, 0)),
                        reads=[k(e1), k(e2)], writes=[("ps", pi)])
                    t.op("dve", lambda e, pi=pi, pb=pb, hh=hh, SMv=SMv: e.tensor_tensor(
                        SMv[0:pb, hh, 0:pb], self.ps[pi][0:pb, 0:pb], self.mask2[0:pb, 0:pb], ALU.mult),
                        reads=[("ps", pi), "mask2"], writes=[("SM", smi, hh)])

                def emit_o(pe, blk=blk, bs=bs, pb=pb, SMv=SMv):
                    ins = None
                    for hh in range(2):
                        rs = slice(hh * 64, (hh + 1) * 64)
                        pe.matmul(OPS[rs, bs], VT[0:pb, blk, pr * 128 + hh * 64:pr * 128 + (hh + 1) * 64],
                                  SMv[0:pb, hh, 0:pb], start=True, stop=False, tile_position=(0, hh * 64))
                    for c in range(blk * 128 // CL, (blk * 128 + pb) // CL):
                        cs = slice(c * CL, (c + 1) * CL)
                        ins = pe.matmul(OPS[:, cs], self.SS[:, pr, c, :], self.SL[:, qp, cs],
                                        start=False, stop=True)
                    return ins
                t.op("pe", emit_o, reads=[("SM", smi, 0), ("SM", smi, 1), ("SS", pr), k(qp)] + vtk,
                     writes=[("ps", 6)])
            for hh in range(2):
                rs = slice(hh * 64, (hh + 1) * 64)
                t.op("dve", lambda e, rs=rs: e.tensor_copy(self.SS[rs, pr, 0, rs], self.SS[rs, pr, nch, rs]),
                     reads=[("SS", pr)], writes=[("SS", pr)])
            osq, rst = g_, b_
            t.op("act", lambda e: e.activation(out=self.sl(osq, n), in_=OPS[:, 0:n], func=AF.Square),
                 reads=[("ps", 6)], writes=[k(osq)])
            pm = self.nextps()
            self.mm(self.ps[pm][:, 0:n], ("ps", pm), [(self.bd64[:, :], self.sl(osq, n))], reads=[k(osq), "bd64"])
            t.op("act", lambda e: e.activation(out=self.sl(rst, n), in_=self.ps[pm][:, 0:n], func=AF.Sqrt,
                                               bias=EPS, scale=1.0),
                 reads=[("ps", pm)], writes=[k(rst)])
            t.op("dve", lambda e: e.reciprocal(self.sl(rst, n), self.sl(rst, n)), reads=[k(rst)], writes=[k(rst)])
            t.op("dve", lambda e: e.tensor_tensor(self.sl(rst, n), self.sl(rst, n), OPS[:, 0:n], ALU.mult),
                 reads=[k(rst), ("ps", 6)], writes=[k(rst)])
            t.op("act", lambda e: e.activation(out=self.sl(gs, n), in_=self.sl(gs, n), func=AF.Silu),
                 reads=[k(gs)], writes=[k(gs)])
            t.op("dve", lambda e: e.scalar_tensor_tensor(self.sl(Y[4 + pr], n), self.sl(rst, n), self.prm(60 + pr),
                                                         self.sl(gs, n), ALU.mult, ALU.mult),
                 reads=[k(rst), k(gs), "p_gn"], writes=[k(Y[4 + pr])])

    def s5_setup(self, l):
        t = self.t
        pin = self.pin
        Q = self.Q
        S = self.S5S
        KEY = "s5setup"
        col = [0]

        def alloc(ncol):
            c0 = col[0]
            col[0] += ncol
            assert col[0] <= 2176
            return S[:, c0:c0 + ncol]

        def dv(fn):
            t.op("dve", fn, reads=[KEY], writes=[KEY])

        def ac(fn):
            t.op("act", fn, reads=[KEY], writes=[KEY])

        def dma(out, in_):
            t.op("pool", lambda e: e.dma_start(out=out, in_=in_, allow_slow_non_contiguous=True),
                 reads=[KEY], writes=[KEY], stream="dma_io", inc=16)

        TT = lambda o, a, b, op: dv(lambda e: e.tensor_tensor(o, a, b, op))
        TS = lambda o, a, sc, op: dv(lambda e: e.tensor_single_scalar(o, a, sc, op))

        def cmul(o_r, o_i, a_r, a_i, b_r, b_i, t1, t2):
            TT(t1, a_r, b_r, ALU.mult)
            TT(t2, a_i, b_i, ALU.mult)
            TT(o_r, t1, t2, ALU.subtract)
            TT(t1, a_r, b_i, ALU.mult)
            TT(t2, a_i, b_r, ALU.mult)
            TT(o_i, t1, t2, ALU.add)

        V = lambda: alloc(8)
        LR, LI, DT, X, TH, Cc, Ss, T1, T2, T3, AR, AI, WR, WI, MM, IAR, IAI, RI = [V() for _ in range(18)]
        lre = pin["s5_lam_re"][l]
        lim = pin["s5_lam_im"][l]
        ldt = pin["s5_log_dt"][l]
        dma(LR, bass.AP(lre.tensor, lre.offset, [[1, 128], [128, 8]]))
        dma(LI, bass.AP(lim.tensor, lim.offset, [[1, 128], [128, 8]]))
        for gl in range(2):
            dma(DT[gl * 64:(gl + 1) * 64, :], bass.AP(ldt.tensor, ldt.offset + gl, [[0, 64], [2, 8]]))
        ac(lambda e: e.activation(out=DT, in_=DT, func=AF.Exp))
        TT(X, LR, DT, ALU.mult)
        TT(TH, LI, DT, ALU.mult)
        ac(lambda e: e.activation(out=Ss, in_=TH, func=AF.Sin, scale=1.0 / 16))
        TS(T3, TH, 1.0 / 16, ALU.mult)
        TS(T3, T3, math.pi / 2, ALU.add)
        ac(lambda e: e.activation(out=Cc, in_=T3, func=AF.Sin))
        for _ in range(4):
            TT(T1, Cc, Cc, ALU.mult)
            TT(T2, Ss, Ss, ALU.mult)
            TT(T3, Cc, Ss, ALU.mult)
            TT(Cc, T1, T2, ALU.subtract)
            TS(Ss, T3, 2.0, ALU.mult)
        ac(lambda e: e.activation(out=T3, in_=X, func=AF.Exp))
        TT(AR, T3, Cc, ALU.mult)
        TT(AI, T3, Ss, ALU.mult)
        ac(lambda e: e.activation(out=self.RR[:, :], in_=X, func=AF.Exp, scale=float(Q)))
        dv(lambda e: e.reciprocal(RI, self.RR[:, :]))
        TT(T1, T3, T3, ALU.mult)
        dv(lambda e: e.reciprocal(T1, T1))
        TT(IAR, AR, T1, ALU.mult)
        TT(IAI, AI, T1, ALU.mult)
        TS(IAI, IAI, -1.0, ALU.mult)
        TS(T3, AR, -1.0, ALU.add)
        TT(T1, LR, LR, ALU.mult)
        TT(T2, LI, LI, ALU.mult)
        TT(MM, T1, T2, ALU.add)
        dv(lambda e: e.reciprocal(MM, MM))
        TT(T1, T3, LR, ALU.mult)
        TT(T2, AI, LI, ALU.mult)
        TT(WR, T1, T2, ALU.add)
        TT(WR, WR, MM, ALU.mult)
        TT(T1, AI, LR, ALU.mult)
        TT(T2, T3, LI, ALU.mult)
        TT(WI, T1, T2, ALU.subtract)
        TT(WI, WI, MM, ALU.mult)
        PWr = alloc(8 * (Q + 1)).rearrange("p (a s) -> p a s", s=Q + 1)
        PWi = alloc(8 * (Q + 1)).rearrange("p (a s) -> p a s", s=Q + 1)
        IPr = alloc(8 * Q).rearrange("p (a s) -> p a s", s=Q)
        IPi = alloc(8 * Q).rearrange("p (a s) -> p a s", s=Q)
        dv(lambda e: e.memset(PWr[:, :, 0], 1.0))
        dv(lambda e: e.memset(PWi[:, :, 0], 0.0))
        dv(lambda e: e.memset(IPr[:, :, 0], 1.0))
        dv(lambda e: e.memset(IPi[:, :, 0], 0.0))
        for sidx in range(Q):
            cmul(PWr[:, :, sidx + 1], PWi[:, :, sidx + 1], PWr[:, :, sidx], PWi[:, :, sidx], AR, AI, T1, T2)
        for sidx in range(Q - 1):
            cmul(IPr[:, :, sidx + 1], IPi[:, :, sidx + 1], IPr[:, :, sidx], IPi[:, :, sidx], IAR, IAI, T1, T2)
        FBr = alloc(8 * Q).rearrange("p (a s) -> p a s", s=Q)
        FBi = alloc(8 * Q).rearrange("p (a s) -> p a s", s=Q)
        for sidx in range(Q):
            cmul(FBr[:, :, sidx], FBi[:, :, sidx], IPr[:, :, sidx], IPi[:, :, sidx], WR, WI, T1, T2)
        TBr, TBi = self.TB[:, :, 0, :], self.TB[:, :, 1, :]
        dv(lambda e: e.memset(TBr[:, :, 0], 1.0))
        dv(lambda e: e.memset(TBi[:, :, 0], 0.0))
        TT(TBr[:, :, 1], PWr[:, :, Q], RI, ALU.mult)
        TT(TBi[:, :, 1], PWi[:, :, Q], RI, ALU.mult)
        big = lambda: alloc(256).rearrange("p (a h) -> p a h", h=32)
        Br, Bi, Sr, Si, U1, U2 = [big() for _ in range(6)]
        TWs = (U1.rearrange("p a h -> p (a h)"), U2.rearrange("p a h -> p (a h)"))
        tw = lambda i, m: TWs[i][:, 0:8 * m].rearrange("p (a c) -> p a c", c=m)
        m = 1
        while m < 64:
            if m >= 2:
                h = m // 2
                cmul(TBr[:, :, m], TBi[:, :, m], TBr[:, :, h], TBi[:, :, h], TBr[:, :, h], TBi[:, :, h], T1, T2)
            if m >= 2:
                br = TBr[:, :, m:m + 1].to_broadcast([128, 8, m - 1])
                bi = TBi[:, :, m:m + 1].to_broadcast([128, 8, m - 1])
                cmul(TBr[:, :, m + 1:2 * m], TBi[:, :, m + 1:2 * m], TBr[:, :, 1:m], TBi[:, :, 1:m], br, bi,
                     tw(0, m - 1), tw(1, m - 1))
            m *= 2
        cmul(TBr[:, :, 64], TBi[:, :, 64], TBr[:, :, 32], TBi[:, :, 32], TBr[:, :, 32], TBi[:, :, 32], T1, T2)
        rb = self.RR[:, :].unsqueeze(2).to_broadcast([128, 8, 64])
        TT(self.TA[:, :, 0, :], TBr[:, :, 0:64], rb, ALU.mult)
        TT(self.TA[:, :, 1, :], TBi[:, :, 0:64], rb, ALU.mult)
        TS(self.TA[:, :, 1, :], self.TA[:, :, 1, :], -1.0, ALU.mult)
        for ap_ in (Br, Bi):
            dv(lambda e, ap_=ap_: e.memset(ap_, 0.0))
        for g in range(16):
            P_, gl = g // 2, g % 2
            dma(Br[gl * 64:(gl + 1) * 64, P_, gl * 16:(gl + 1) * 16], pin["s5_b_re"][l, g])
            dma(Bi[gl * 64:(gl + 1) * 64, P_, gl * 16:(gl + 1) * 16], pin["s5_b_im"][l, g])
        for sidx in range(Q):
            fr = FBr[:, :, sidx:sidx + 1].to_broadcast([128, 8, 32])
            fi = FBi[:, :, sidx:sidx + 1].to_broadcast([128, 8, 32])
            cmul(Sr, Si, Br, Bi, fr, fi, U1, U2)
            for ri, src in enumerate((Sr, Si)):
                for half in range(2):
                    pi = self.nextps()
                    t.op("pe", lambda e, src=src, half=half, pi=pi: e.transpose(
                        self.ps[pi][:, 0:128], src[:, half * 4:(half + 1) * 4, :], self.ident[:, :]),
                        reads=[KEY, "ident"], writes=[("ps", pi)])
                    t.op("act", lambda e, half=half, sidx=sidx, ri=ri, pi=pi: e.activation(
                        out=self.BsT[:, half, sidx, ri, :], in_=self.ps[pi][:, 0:128], func=AF.Copy),
                        reads=[("ps", pi)], writes=["BsT"])
        CNr = Br.rearrange("p a h -> p (a h)").rearrange("p (a q) -> p a q", q=128)
        CNi = Bi.rearrange("p a h -> p (a h)").rearrange("p (a q) -> p a q", q=128)
        for ap_ in (CNr, CNi):
            dv(lambda e, ap_=ap_: e.memset(ap_, 0.0))
        for g in range(16):
            P_, gl = g // 2, g % 2
            half, j = P_ // 4, P_ % 4
            r0 = 32 * j + 16 * gl
            dma(CNr[r0:r0 + 16, half, gl * 64:(gl + 1) * 64], pin["s5_c_re"][l, g])
            dma(CNi[r0:r0 + 16, half, gl * 64:(gl + 1) * 64], pin["s5_c_im"][l, g])
        for src, dst in ((CNr, Sr), (CNi, Si)):
            for half in range(2):
                pi = self.nextps()
                t.op("pe", lambda e, src=src, half=half, pi=pi: e.transpose(
                    self.ps[pi][:, 0:128], src[:, half, :], self.ident[:, :]),
                    reads=[KEY, "ident"], writes=[("ps", pi)])
                t.op("act", lambda e, dst=dst, half=half, pi=pi: e.activation(
                    out=dst[:, half * 4:(half + 1) * 4, :], in_=self.ps[pi][:, 0:128].rearrange("p (a h) -> p a h", h=32),
                    func=AF.Copy), reads=[("ps", pi), KEY], writes=[KEY])
        for sidx in range(Q):
            pr_ = PWr[:, :, sidx:sidx + 1].to_broadcast([128, 8, 32])
            pi_ = PWi[:, :, sidx:sidx + 1].to_broadcast([128, 8, 32])
            cmul(self.CsT[:, :, sidx, 0, :], self.CsT[:, :, sidx, 1, :], Sr, Si, pr_, pi_, U1, U2)
            dv(lambda e, sidx=sidx: e.tensor_single_scalar(self.CsT[:, :, sidx, 1, :], self.CsT[:, :, sidx, 1, :],
                                                           -1.0, ALU.mult))
        t.op("dve", lambda e: e.tensor_copy(self.CH[:, 0, 0:1], self.CH[:, 0, 0:1]), reads=[KEY], writes=["CsT", "TAB"])
        t.op("pool", lambda e: e.dma_start(out=self.GW[:, :, :],
                                           in_=pin["s5_glu_w"][l].rearrange("(kt p) n -> p kt n", p=128)),
             writes=["GW"], stream="dma_io", inc=16)
        t.op("dve", lambda e: e.memset(self.CARJ[:], 0.0), writes=[("CARJ", P_) for P_ in range(8)])

    def mix_s5(self, l, ti, n):
        t = self.t
        Y, U, Z = self.S_Y, self.S_U, self.S_Z
        k = self.slk
        Q = self.Q
        ncn = n // Q
        tm = lambda ap: ap.rearrange("p (c s) -> p s c", s=Q)
        sm = lambda ap: ap.rearrange("p (s c) -> p s c", s=Q)
        for half in range(2):
            t.op("act", lambda e, half=half: e.activation(out=sm(self.sl(Z[half], n)), in_=tm(self.sl(U[half], n)),
                                                          func=AF.Copy),
                 reads=[k(U[half])], writes=[k(Z[half])])
        YB = self.ps[7]
        for P_ in range(8):
            half, j = P_ // 4, P_ % 4
            rj = slice(32 * j, 32 * j + 32)
            Wk, Gk, Gp = (Z[2], Z[3]), (Z[4], Z[5]), (Z[6], Z[7])
            for ri in range(2):
                pv = self.nextps()

                def emit_b(pe, ri=ri, pv=pv):
                    ins = None
                    for sidx in range(Q):
                        ins = pe.matmul(self.ps[pv][:, sidx * ncn:(sidx + 1) * ncn],
                                        self.BsT[rj, half, sidx, ri, :],
                                        self.SL[rj, Z[half], sidx * ncn:(sidx + 1) * ncn],
                                        start=True, stop=True, tile_position=(32 * j, 0))
                    return ins
                t.op("pe", emit_b, reads=["BsT", k(Z[half])], writes=[("ps", pv)])
                t.op("act", lambda e, ri=ri, pv=pv: e.activation(out=tm(self.sl(Wk[ri], n)), in_=sm(self.ps[pv][:, 0:n]),
                                                                func=AF.Copy),
                     reads=[("ps", pv)], writes=[k(Wk[ri])])
                t.op("dve", lambda e, ri=ri: e.tensor_tensor_scan(self.sl(Gk[ri], n), self.cmask8[:, 0:n],
                                                                  self.sl(Wk[ri], n), 0.0, ALU.mult, ALU.add),
                     reads=[k(Wk[ri]), "cmask8"], writes=[k(Gk[ri])])
            CH = self.CH
            ck = ("CH",)
            gl_ = [self.SL[:, Gk[ri], 0:n].rearrange("p (c s) -> p c s", s=Q)[:, :, Q - 1] for ri in range(2)]
            TAr, TAi = self.TA[:, P_, 0, 0:ncn], self.TA[:, P_, 1, 0:ncn]
            TBr, TBi = self.TB[:, P_, 0, 0:ncn + 1], self.TB[:, P_, 1, 0:ncn + 1]
            vr, vi, t1, t2 = CH[:, 0, 0:ncn], CH[:, 1, 0:ncn], CH[:, 2, 0:ncn + 1], CH[:, 3, 0:ncn + 1]
            jt = [CH[:, 4, 0:ncn + 1], CH[:, 5, 0:ncn + 1]]
            jj = [CH[:, 6, 0:ncn + 1], CH[:, 7, 0:ncn + 1]]
            gk = [k(Gk[0]), k(Gk[1])]

            def dv(fn, reads=(), writes=()):
                t.op("dve", fn, reads=list(reads) + [ck, "TAB"], writes=list(writes) + [ck])
            dv(lambda e: e.tensor_tensor(t1[:, 0:ncn], gl_[0], TAr, ALU.mult), reads=gk)
            dv(lambda e: e.tensor_tensor(t2[:, 0:ncn], gl_[1], TAi, ALU.mult), reads=gk)
            dv(lambda e: e.tensor_tensor(vr, t1[:, 0:ncn], t2[:, 0:ncn], ALU.subtract))
            dv(lambda e: e.tensor_tensor(t1[:, 0:ncn], gl_[0], TAi, ALU.mult), reads=gk)
            dv(lambda e: e.tensor_tensor(t2[:, 0:ncn], gl_[1], TAr, ALU.mult), reads=gk)
            dv(lambda e: e.tensor_tensor(vi, t1[:, 0:ncn], t2[:, 0:ncn], ALU.add))
            for ri, v_ in enumerate((vr, vi)):
                dv(lambda e, ri=ri: e.tensor_copy(jt[ri][:, 0:1], self.CARJ[:, P_, ri:ri + 1]), reads=[("CARJ", P_)])
                dv(lambda e, ri=ri, v_=v_: e.tensor_tensor_scan(
                    jt[ri][:, 1:ncn + 1], self.RR[:, P_:P_ + 1].to_broadcast([128, ncn]), v_,
                    self.CARJ[:, P_, ri:ri + 1], ALU.mult, ALU.add), reads=[("CARJ", P_)])
            dv(lambda e: e.tensor_tensor(t1, jt[0], TBr, ALU.mult))
            dv(lambda e: e.tensor_tensor(t2, jt[1], TBi, ALU.mult))
            dv(lambda e: e.tensor_tensor(jj[0], t1, t2, ALU.subtract))
            dv(lambda e: e.tensor_tensor(t1, jt[0], TBi, ALU.mult))
            dv(lambda e: e.tensor_tensor(t2, jt[1], TBr, ALU.mult))
            dv(lambda e: e.tensor_tensor(jj[1], t1, t2, ALU.add))
            for ri in range(2):
                dv(lambda e, ri=ri: e.tensor_copy(self.CARJ[:, P_, ri:ri + 1], jj[ri][:, ncn:ncn + 1]),
                   writes=[("CARJ", P_)])
            for ri in range(2):
                t.op("dve", lambda e, ri=ri: e.tensor_tensor(
                    sm(self.sl(Gp[ri], n)), tm(self.sl(Gk[ri], n)),
                    jj[ri][:, 0:ncn].unsqueeze(1).to_broadcast([128, Q, ncn]), ALU.add),
                    reads=[k(Gk[ri]), ck], writes=[k(Gp[ri])])

            def emit_c(pe):
                ins = None
                for sidx in range(Q):
                    cs = slice(sidx * ncn, (sidx + 1) * ncn)
                    pe.matmul(YB[rj, cs], self.CsT[:, P_, sidx, 0, :], self.SL[:, Gp[0], cs],
                              start=True, stop=False, tile_position=(0, 32 * j))
                    ins = pe.matmul(YB[rj, cs], self.CsT[:, P_, sidx, 1, :], self.SL[:, Gp[1], cs],
                                    start=False, stop=True, tile_position=(0, 32 * j))
                return ins
            t.op("pe", emit_c, reads=["CsT", k(Gp[0]), k(Gp[1])], writes=[("ps", 7)])
            if j == 3:
                ys = U[20 + half]
                t.op("dve", lambda e, half=half, ys=ys: e.scalar_tensor_tensor(
                    tm(self.sl(ys, n)), tm(self.sl(U[half], n)), self.prm(62 + half), sm(YB[:, 0:n]),
                    ALU.mult, ALU.add),
                    reads=[k(U[half]), ("ps", 7), "p_s5d"], writes=[k(ys)])
                t.op("act", lambda e, ys=ys: e.activation(out=self.sl(ys, n), in_=self.sl(ys, n),
                                                          func=AF.Gelu_apprx_tanh),
                     reads=[k(ys)], writes=[k(ys)])
        for m in range(2):
            pi = self.nextps()
            self.mm(self.ps[pi][:, 0:n], ("ps", pi),
                    [(self.GW[:, kt, m * 128:(m + 1) * 128], self.sl(U[20 + kt], n)) for kt in range(2)],
                    reads=[k(U[20]), k(U[21]), "GW"])
            tk = self.nexttmp()
            t.op("act", lambda e, tk=tk, pi=pi, m=m: e.activation(out=self.tmp[tk][:, 0:n], in_=self.ps[pi][:, 0:n],
                                                                 func=AF.Sigmoid, bias=self.prm(64 + m), scale=1.0),
                 reads=[("ps", pi), "p_glub"], writes=[("tmp", tk)])
            t.op("dve", lambda e, tk=tk, m=m: e.tensor_tensor(self.sl(Y[m], n), self.sl(U[20 + m], n),
                                                              self.tmp[tk][:, 0:n], ALU.mult),
                 reads=[("tmp", tk), k(U[20 + m])], writes=[k(Y[m])])


_CACHE = {}
_BUILDERS = {}


def _get_nc(n_xtiles):
    if n_xtiles not in _CACHE:
        _BUILDERS[n_xtiles] = Builder(n_xtiles)
        _CACHE[n_xtiles] = _BUILDERS[n_xtiles].build()
    return _CACHE[n_xtiles]


WNAMES = ["meta_tokens", "w_in", "w_out", "w_ffn_in", "w_ffn_out", "ln1_g", "ln1_b", "ln2_g", "ln2_b"]


def kernel(**inputs):
    x = np.ascontiguousarray(inputs["x"], dtype=np.float32)
    bsz, seq, _ = x.shape
    n_xtiles = seq // NT
    nc = _get_nc(n_xtiles)
    shared = {k: np.ascontiguousarray(inputs[k], dtype=np.float32) for k in _BUILDERS[n_xtiles].in_names if k != "x"}
    in_maps = [dict(shared, x=x[b]) for b in range(bsz)]
    res = run_bass_kernel_spmd(nc, in_maps, core_ids=list(range(bsz)))
    return np.stack([r["out"] for r in res.results], axis=0)
```

```python
import math
import os
from contextlib import ExitStack

import numpy as np
import concourse.bass as bass
import concourse.mybir as mybir
from concourse.bass_utils import run_bass_kernel_spmd

F32 = mybir.dt.float32
AF = mybir.ActivationFunctionType
ALU = mybir.AluOpType

D = 1024
KT = 8
NIN = 2560
DFF = 2816
FT = 22
NMETA = 16
SEQ = 8192
DEPTH = 2
ALPHA = (2 * DEPTH) ** 0.25
EPS = 1e-5
NT = 512
SKEW = 1
WSLOT = 4096
NWSLOT = 2


class Trk:
    def __init__(self, nc, es):
        self.nc = nc
        self.es = es
        self.E = {"pe": nc.tensor, "act": nc.scalar, "dve": nc.vector,
                  "pool": nc.gpsimd, "sp": nc.sync}
        self.cur = {}
        self.waited = {}
        self.lastw = {}
        self.rd = {}
        self.nsem = 0
        self.LIM = 30000
        self.nins = 0

    def _sem(self, stream, inc):
        s = self.cur.get(stream)
        if s is None or s[1] + inc > self.LIM:
            name = f"s{self.nsem}"
            sem = self.es.enter_context(self.nc.semaphore(f"{name}_{stream}"))
            self.nsem += 1
            s = [sem, 0, name]
            self.cur[stream] = s
        s[1] += inc
        return (s[0], s[1], s[2])

    def wait(self, engine, tok):
        sem, val, name = tok
        k = (engine, name)
        if self.waited.get(k, 0) >= val:
            return
        self.waited[k] = val
        self.E[engine].wait_ge(sem, val)

    def op(self, engine, emit, reads=(), writes=(), stream=None, inc=1):
        deps = {}

        def add(tok):
            if tok is None:
                return
            n = tok[2]
            if n not in deps or deps[n][1] < tok[1]:
                deps[n] = tok

        for k in reads:
            add(self.lastw.get(k))
        for k in writes:
            add(self.lastw.get(k))
            for t in self.rd.get(k, {}).values():
                add(t)
        st = stream or engine
        own = self.cur.get(st)
        for tok in deps.values():
            if engine == "pe" and stream is None and own is not None and tok[2] == own[2]:
                continue
            self.wait(engine, tok)
        ins = emit(self.E[engine])
        tok = self._sem(st, inc)
        ins.then_inc(tok[0], inc)
        self.nins += 1
        for k in reads:
            self.rd.setdefault(k, {})[tok[2]] = tok
        for k in writes:
            self.lastw[k] = tok
            self.rd[k] = {}
        return tok

    def drain(self, engine):
        for st, s in self.cur.items():
            self.wait(engine, (s[0], s[1], s[2]))


class Builder:
    def __init__(self, n_xtiles, stub=(), taps=(), npairs=4):
        self.npairs = npairs
        self.n_xtiles = n_xtiles
        self.stub = set(stub)
        self.taps = list(taps)
        self.tiles = [(0, NMETA)] + [(NMETA + i * NT, NT) for i in range(n_xtiles)]
        self.seq_x = n_xtiles * NT

    def build(self):
        nc = bass.Bass("TRN2", target_bir_lowering=False)
        self.nc = nc
        self.es = ExitStack()
        es = self.es
        self.t = Trk(nc, es)
        t = self.t
        self.in_names = []

        def di(name, shape):
            self.in_names.append(name)
            return nc.dram_tensor(name, list(shape), F32, kind="ExternalInput").ap()
        self.x = di("x", [self.seq_x, D])
        self.meta = di("meta_tokens", [NMETA, D])
        self.w_in = di("w_in", [DEPTH, D, NIN])
        self.w_out = di("w_out", [DEPTH, D, D])
        self.w_f1 = di("w_ffn_in", [DEPTH, D, 2 * DFF])
        self.w_f2 = di("w_ffn_out", [DEPTH, DFF, D])
        self.ln = {n: di(n, [DEPTH, D]) for n in ("ln1_g", "ln1_b", "ln2_g", "ln2_b")}
        self.pin = {}
        for nm, shp in (("hg_lb_raw", [DEPTH, 256]), ("sc_conv_w", [DEPTH, 3, 256]), ("hg_gnorm", [DEPTH, 256]),
                        ("lru_conv_w", [DEPTH, 4, 256]), ("lru_conv_b", [DEPTH, 256]),
                        ("lru_wa", [DEPTH, 4, 64, 64]), ("lru_ba", [DEPTH, 256]),
                        ("lru_wx", [DEPTH, 4, 64, 64]), ("lru_bx", [DEPTH, 256]), ("lru_a_param", [DEPTH, 256]),
                        ("s5_lam_re", [DEPTH, 16, 64]), ("s5_lam_im", [DEPTH, 16, 64]),
                        ("s5_b_re", [DEPTH, 16, 64, 16]), ("s5_b_im", [DEPTH, 16, 64, 16]),
                        ("s5_c_re", [DEPTH, 16, 16, 64]), ("s5_c_im", [DEPTH, 16, 16, 64]),
                        ("s5_d", [DEPTH, 256]), ("s5_log_dt", [DEPTH, 16]),
                        ("s5_glu_w", [DEPTH, 256, 256]), ("s5_glu_b", [DEPTH, 256])):
            self.pin[nm] = di(nm, shp)
        self.role_in = di("role", [128, 2])
        self.out = nc.dram_tensor("out", [self.seq_x, D], F32, kind="ExternalOutput").ap()
        ntile = len(self.tiles)
        self.send = [nc.dram_tensor(f"send{i}", [128, KT * NT], F32, kind="Internal").ap() for i in range(2)]
        self.recv = [nc.dram_tensor(f"recv{i}", [128, KT * NT], F32, kind="Internal").ap() for i in range(4)]
        self.groups = [[i, i + self.npairs] for i in range(self.npairs)]
        self.tapout = {}
        for name, shape in self.taps:
            self.tapout[name] = nc.dram_tensor("tap_" + name, list(shape), F32, kind="ExternalOutput").ap()

        sb = lambda name, shape: es.enter_context(nc.sbuf_tensor(name, list(shape), F32))
        self.NSLAB = 46
        self.SL = sb("SL", [128, self.NSLAB, NT])
        self.W = [sb(f"W{i}", [128, WSLOT]) for i in range(NWSLOT)]
        self.ident = sb("ident", [128, 128])
        self.ones = sb("ones", [128, 128])
        self.PRM = sb("PRM", [128, 128])
        self.tmp = [sb(f"tmp{i}", [128, NT]) for i in range(4)]
        self.cmask = sb("cmask", [128, NT])
        self.bd64 = sb("bd64", [128, 128])
        self.lruw = sb("lruw", [128, 2, 2, 128])
        self.car_sc = sb("car_sc", [128, 2, 2])
        self.car_lx = sb("car_lx", [128, 2, 3])
        self.car_lh = sb("car_lh", [128, 2])
        self.SS = sb("SS", [128, 2, 9, 128])
        self.SM = [sb(f"SM{i}", [128, 2, 128]) for i in range(2)]
        self.mask2 = sb("mask2", [128, 128])
        self.smn = 0
        self.Q = 8
        self.BsT = sb("BsT", [128, 2, 8, 2, 128])
        self.CsT = sb("CsT", [128, 8, 8, 2, 32])
        self.TA = sb("TA", [128, 8, 2, 64])
        self.TB = sb("TB", [128, 8, 2, 65])
        self.RR = sb("RR", [128, 8])
        self.GW = sb("GW", [128, 2, 256])
        self.CARJ = sb("CARJ", [128, 8, 2])
        self.cmask8 = sb("cmask8", [128, NT])
        self.CH = sb("CH", [128, 8, 66])
        self.S5S = sb("S5S", [128, 2176])
        self.ROLE = sb("ROLE", [128, 2])
        self.HM0 = sb("HM0", [128, KT, NMETA])
        self.SNAP = sb("SNAP", [128, 2 * 2 + 2 * 3 + 2 + 2 * 128 + 8 * 2])
        self.ps = [es.enter_context(nc.psum_tensor(f"ps{i}", [128, NT], F32)) for i in range(8)]
        self.psn = 0
        self.tmpn = 0
        self.S_H = list(range(0, 8))
        self.S_Y = list(range(8, 16))
        self.S_Z = list(range(16, 24))
        self.S_U = list(range(24, 46))

        t.op("pool", lambda e: e.memset(self.ident[:], 0.0), writes=["ident"])
        t.op("pool", lambda e: e.affine_select(
            out=self.ident[:], in_=self.ident[:], compare_op=ALU.not_equal, fill=1.0,
            base=0, pattern=[[-1, 128]], channel_multiplier=1), writes=["ident"])
        t.op("pool", lambda e: e.memset(self.ones[:], 1.0), writes=["ones"])
        t.op("pool", lambda e: e.memset(self.cmask[:], 1.0), writes=["cmask"])
        t.op("pool", lambda e: e.affine_select(
            out=self.cmask[:].rearrange("p (c s) -> p c s", s=64), in_=self.cmask[:].rearrange("p (c s) -> p c s", s=64),
            compare_op=ALU.not_equal, fill=0.0, base=0, pattern=[[0, NT // 64], [1, 64]], channel_multiplier=0),
            writes=["cmask"])
        t.op("pool", lambda e: e.memset(self.cmask8[:], 1.0), writes=["cmask8"])
        t.op("pool", lambda e: e.affine_select(
            out=self.cmask8[:].rearrange("p (c s) -> p c s", s=8), in_=self.cmask8[:].rearrange("p (c s) -> p c s", s=8),
            compare_op=ALU.not_equal, fill=0.0, base=0, pattern=[[0, NT // 8], [1, 8]], channel_multiplier=0),
            writes=["cmask8"])
        t.op("pool", lambda e: e.memset(self.mask2[:], 1.0), writes=["mask2"])
        t.op("pool", lambda e: e.affine_select(
            out=self.mask2[:, :], in_=self.mask2[:, :],
            compare_op=ALU.is_ge, fill=0.0, base=0, pattern=[[1, 128]], channel_multiplier=-1),
            writes=["mask2"])
        t.op("pool", lambda e: e.memset(self.mask2[0:64, 64:128], 0.0), writes=["mask2"])
        t.op("pool", lambda e: e.memset(self.bd64[:], 0.0), writes=["bd64"])
        for hh in range(2):
            t.op("pool", lambda e, hh=hh: e.memset(self.bd64[hh * 64:(hh + 1) * 64, hh * 64:(hh + 1) * 64], 1.0 / 64),
                 writes=["bd64"])

        t.op("pool", lambda e: e.dma_start(out=self.ROLE[:, :], in_=self.role_in), writes=["role"],
             stream="dma_io", inc=16)
        self.fA = self.ROLE[:, 0:1]
        self.fB = self.ROLE[:, 1:2]
        nsteps = self.n_xtiles + SKEW
        self.nsteps = nsteps
        self.wq = []
        self.wq_issued = 0
        self.wq_used = 0
        self.wq.extend(self.chunk_seq(0))
        for _ in range(1 + nsteps):
            self.wq.extend(self.chunk_seq(1))

        self.layer_setup(0, mine=False)
        self.tile_pass(0, "p0", NMETA)
        self.layer_setup(1, mine=True)
        self.tile_pass(1, "p1", NMETA)
        self.snapshot()
        for step in range(nsteps):
            self.tile_pass(1, "main", NT, step)
            if step == SKEW - 1:
                self.restore()
        t.drain("pool")
        es.close()
        return nc

    def sl(self, idx, n=NT):
        return self.SL[:, idx, 0:n]

    def slk(self, idx):
        return ("sl", idx)

    def nextps(self):
        i = self.psn
        self.psn = (self.psn + 1) % 6
        return i

    def nexttmp(self):
        i = self.tmpn
        self.tmpn = (self.tmpn + 1) % 4
        return i

    def tap(self, name, src_ap, reads):
        if name in self.tapout:
            self.t.op("pool", lambda e: e.dma_start(out=self.tapout[name], in_=src_ap),
                      reads=reads, stream="dma_io", inc=16)

    def chunk_seq(self, l):
        seq = []
        wi = self.w_in[l].rearrange("(kt p) n -> p kt n", p=128)
        for c in range(5):
            seq.append(("win", [(wi[:, :, c * 512:(c + 1) * 512], 0, KT, 512)]))
        wo = self.w_out[l].rearrange("(kt p) n -> p kt n", p=128)
        for c in range(2):
            seq.append(("wout", [(wo[:, :, c * 512:(c + 1) * 512], 0, KT, 512)]))
        w1 = self.w_f1[l].rearrange("(kt p) n -> p kt n", p=128)
        for c in range(11):
            seq.append(("f1", [(w1[:, :, c * 256:(c + 1) * 256], 0, KT, 256),
                               (w1[:, :, DFF + c * 256:DFF + (c + 1) * 256], KT * 256, KT, 256)]))
        w2 = self.w_f2[l].rearrange("(kt p) n -> p kt n", p=128)
        for c in range(8):
            seq.append(("f2", [(w2[:, :, c * 128:(c + 1) * 128], 0, FT, 128)]))
        return seq

    def _issue_chunk(self, idx):
        kind, parts = self.wq[idx]
        slot = idx % NWSLOT
        for pidx, (src, off, nk, ncol) in enumerate(parts):
            dst = self.W[slot][:, off:off + nk * ncol].rearrange("p (k n) -> p k n", k=nk)
            wk = [("w", slot, pidx)] if len(parts) == 2 else [("w", slot, 0), ("w", slot, 1)]
            self.t.op("sp", lambda e, dst=dst, src=src: e.dma_start(out=dst, in_=src),
                      writes=wk, stream=f"dma_w{slot}", inc=16)

    def wacquire(self, kind):
        idx = self.wq_used
        assert self.wq[idx][0] == kind, (self.wq[idx][0], kind)
        while self.wq_issued <= idx:
            self._issue_chunk(self.wq_issued)
            self.wq_issued += 1
        self.wq_used += 1
        return idx % NWSLOT

    def wprefetch(self):
        lim = min(self.wq_used + NWSLOT - 1, len(self.wq))
        while self.wq_issued < lim:
            self._issue_chunk(self.wq_issued)
            self.wq_issued += 1

    def wview(self, slot, off, nk, ncol):
        return self.W[slot][:, off:off + nk * ncol].rearrange("p (k n) -> p k n", k=nk)

    def layer_setup(self, l, mine):
        t = self.t
        for i, n in enumerate(("ln1_g", "ln1_b", "ln2_g", "ln2_b")):
            src = self.ln[n][l].rearrange("(m p) -> p m", p=128)
            t.op("pool", lambda e, src=src, i=i: e.dma_start(
                out=self.PRM[:, i * 8:(i + 1) * 8], in_=src, allow_slow_non_contiguous=True),
                writes=[("prm", i)], stream="dma_io", inc=16)
        P = self.PRM
        pin = self.pin

        def vec(name, col, key):
            src = pin[name][l].rearrange("(c p) -> p c", p=128)
            t.op("pool", lambda e: e.dma_start(out=P[:, col:col + 2], in_=src, allow_slow_non_contiguous=True),
                 writes=[key], stream="dma_io", inc=16)

        for c in range(2):
            t.op("pool", lambda e, c=c: e.dma_start(
                out=P[:, 32 + 3 * c:35 + 3 * c],
                in_=pin["sc_conv_w"][l][:, c * 128:(c + 1) * 128].rearrange("k p -> p k"),
                allow_slow_non_contiguous=True), writes=["p_scw"], stream="dma_io", inc=16)
            t.op("pool", lambda e, c=c: e.dma_start(
                out=P[:, 38 + 4 * c:42 + 4 * c],
                in_=pin["lru_conv_w"][l][:, c * 128:(c + 1) * 128].rearrange("k p -> p k"),
                allow_slow_non_contiguous=True), writes=["p_lcw"], stream="dma_io", inc=16)
        vec("lru_conv_b", 46, "p_lcb")
        vec("lru_ba", 48, "p_lba")
        vec("lru_bx", 50, "p_lbx")
        vec("lru_a_param", 52, "p_cp")
        vec("hg_gnorm", 60, "p_gn")
        vec("s5_d", 62, "p_s5d")
        vec("s5_glu_b", 64, "p_glub")
        t.op("act", lambda e: e.activation(out=P[:, 52:54], in_=P[:, 52:54], func=AF.Exp, scale=-1.0),
             reads=["p_cp"], writes=["p_cp"])
        t.op("act", lambda e: e.activation(out=P[:, 52:54], in_=P[:, 52:54], func=AF.Ln, bias=1.0, scale=1.0),
             reads=["p_cp"], writes=["p_cp"])
        t.op("dve", lambda e: e.tensor_single_scalar(P[:, 54:56], P[:, 52:54], -16.0, ALU.mult),
             reads=["p_cp"], writes=["p_cp2"])
        t.op("dve", lambda e: e.tensor_single_scalar(P[:, 52:54], P[:, 52:54], -8.0, ALU.mult),
             reads=["p_cp", "p_cp2"], writes=["p_cp"])
        if not mine:
            t.op("dve", lambda e: e.memset(P[:, 56:58], 0.0), writes=["p_lb"])
        else:
            for i in range(2):
                src = pin["hg_lb_raw"][i].rearrange("(c p) -> p c", p=128)
                t.op("pool", lambda e, i=i, src=src: e.dma_start(out=P[:, 66 + 2 * i:68 + 2 * i], in_=src,
                                                                 allow_slow_non_contiguous=True),
                     writes=[("p_lbraw", i)], stream="dma_io", inc=16)
            t.op("dve", lambda e: e.tensor_tensor(P[:, 56:58], P[:, 68:70], P[:, 66:68], ALU.subtract),
                 reads=[("p_lbraw", 0), ("p_lbraw", 1)], writes=["p_lb"])
            t.op("act", lambda e: e.activation(out=P[:, 56:58], in_=P[:, 56:58], func=AF.Sigmoid),
                 reads=["p_lb"], writes=["p_lb"])
            t.op("dve", lambda e: e.tensor_single_scalar(P[:, 56:58], P[:, 56:58], self.fB, ALU.mult),
                 reads=["p_lb", "role"], writes=["p_lb"])
        t.op("dve", lambda e: e.tensor_scalar(P[:, 58:60], P[:, 56:58], -1.0, 1.0, ALU.mult, ALU.add),
             reads=["p_lb"], writes=["p_oml"])
        t.op("dve", lambda e: e.memset(self.lruw[:], 0.0), writes=["lruw"])
        for gi, nm in enumerate(("lru_wa", "lru_wx")):
            for h in range(4):
                c, hh = h // 2, h % 2
                t.op("pool", lambda e, gi=gi, nm=nm, h=h, c=c, hh=hh: e.dma_start(
                    out=self.lruw[hh * 64:(hh + 1) * 64, gi, c, hh * 64:(hh + 1) * 64], in_=pin[nm][l, h]),
                    writes=["lruw"], stream="dma_io", inc=16)
        t.op("dve", lambda e: e.memset(self.car_sc[:], 0.0), writes=[("car_sc", 0), ("car_sc", 1)])
        t.op("dve", lambda e: e.memset(self.car_lx[:], 0.0), writes=[("car_lx", 0), ("car_lx", 1)])
        t.op("dve", lambda e: e.memset(self.car_lh[:], 0.0), writes=[("car_lh", 0), ("car_lh", 1)])
        t.op("dve", lambda e: e.memset(self.SS[:], 0.0), writes=[("SS", 0), ("SS", 1)])
        if "s5" not in self.stub:
            self.s5_setup(l)

    def evac(self, eng, out_ap, in_ap, reads, writes):
        if eng == "act":
            return self.t.op("act", lambda e: e.activation(out=out_ap, in_=in_ap, func=AF.Copy),
                             reads=reads, writes=writes)
        return self.t.op("dve", lambda e: e.tensor_copy(out_ap, in_ap), reads=reads, writes=writes)

    def mm(self, ps_ap, pskey, pairs, reads):
        def emit(pe):
            n = len(pairs)
            ins = None
            for i, (lh, rh) in enumerate(pairs):
                ins = pe.matmul(ps_ap, lh, rh, start=(i == 0), stop=(i == n - 1))
            return ins
        return self.t.op("pe", emit, reads=reads, writes=[pskey])

    def states(self):
        o = [0]

        def sn(ncol, shape=None):
            ap = self.SNAP[:, o[0]:o[0] + ncol]
            o[0] += ncol
            return ap
        return [
            (self.car_sc[:].rearrange("p a b -> p (a b)"), sn(4), [("car_sc", 0), ("car_sc", 1)]),
            (self.car_lx[:].rearrange("p a b -> p (a b)"), sn(6), [("car_lx", 0), ("car_lx", 1)]),
            (self.car_lh[:, :], sn(2), [("car_lh", 0), ("car_lh", 1)]),
            (self.SS[:, :, 0, :], sn(256).rearrange("p (a b) -> p a b", a=2), [("SS", 0), ("SS", 1)]),
            (self.CARJ[:].rearrange("p a b -> p (a b)"), sn(16), [("CARJ", i) for i in range(8)]),
        ]

    def snapshot(self):
        for st, snp, keys in self.states():
            self.t.op("dve", lambda e, st=st, snp=snp: e.tensor_copy(snp, st), reads=keys, writes=["snap"])

    def restore(self):
        for st, snp, keys in self.states():
            self.t.op("dve", lambda e, st=st: e.tensor_single_scalar(st, st, self.fA, ALU.mult),
                      reads=keys + ["role"], writes=keys)
            self.t.op("dve", lambda e, st=st, snp=snp: e.scalar_tensor_tensor(st, snp, self.fB, st, ALU.mult, ALU.add),
                      reads=keys + ["role", "snap"], writes=keys)

    def load_input(self, mode, n, step):
        t = self.t
        H, U = self.S_H, self.S_U
        nb = (n + 127) // 128
        pb = min(n, 128)
        stage = self.SL[:, self.S_Y[0]:self.S_Y[0] + 8, :].rearrange("p a b -> p (a b)")
        stage = stage[:, 0:nb * D].rearrange("p (b f) -> p b f", b=nb)
        if mode in ("p0", "p1"):
            src = self.meta.rearrange("(b p) f -> p b f", p=pb)
        else:
            xi = min(step, self.n_xtiles - 1)
            src = self.x[xi * NT:(xi + 1) * NT, :].rearrange("(b p) f -> p b f", p=pb)
        t.op("pool", lambda e: e.dma_start(out=stage[0:pb], in_=src),
             writes=[self.slk(i) for i in self.S_Y], stream="dma_io", inc=16)
        if mode == "main" and step >= SKEW:
            par = (step - SKEW) % 4
            rsrc = self.recv[par][:, :].rearrange("p (k n) -> p k n", k=KT)
            t.op("pool", lambda e: e.dma_start(out=self.SL[:, U[0]:U[0] + 8, :], in_=rsrc),
                 reads=[("recv", par)], writes=[self.slk(U[i]) for i in range(8)], stream="dma_io", inc=16)
        for k in range(KT):
            pi = self.nextps()

            def emit(pe, k=k, pi=pi):
                ins = None
                for b in range(nb):
                    ins = pe.transpose(self.ps[pi][:, b * pb:(b + 1) * pb],
                                       stage[0:pb, b, k * 128:(k + 1) * 128],
                                       self.ident[0:pb, 0:pb])
                return ins
            t.op("pe", emit, reads=[self.slk(i) for i in self.S_Y] + ["ident"], writes=[("ps", pi)])
            hk = self.slk(H[k])
            if mode == "p0":
                self.evac("act" if k % 2 else "dve", self.sl(H[k], n), self.ps[pi][:, 0:n],
                          reads=[("ps", pi)], writes=[hk])
                continue
            if k % 2:
                t.op("act", lambda e, k=k, pi=pi: e.activation(out=self.sl(H[k], n), in_=self.ps[pi][:, 0:n],
                                                               func=AF.Copy, scale=self.fA),
                     reads=[("ps", pi), "role"], writes=[hk])
            else:
                t.op("dve", lambda e, k=k, pi=pi: e.tensor_single_scalar(self.sl(H[k], n), self.ps[pi][:, 0:n],
                                                                         self.fA, ALU.mult),
                     reads=[("ps", pi), "role"], writes=[hk])
            if mode == "p1":
                t.op("dve", lambda e, k=k: e.scalar_tensor_tensor(self.sl(H[k], n), self.HM0[:, k, :], self.fB,
                                                                  self.sl(H[k], n), ALU.mult, ALU.add),
                     reads=[hk, "role", "HM0"], writes=[hk])
            elif step >= SKEW:
                t.op("dve", lambda e, k=k: e.scalar_tensor_tensor(self.sl(H[k], n), self.sl(U[k], n), self.fB,
                                                                  self.sl(H[k], n), ALU.mult, ALU.add),
                     reads=[hk, "role", self.slk(U[k])], writes=[hk])

    def store_output(self, mode, n, step):
        t = self.t
        Z = self.S_Z
        if mode == "p0":
            t.op("act", lambda e: e.activation(out=self.HM0[:, :, :], in_=self.SL[:, Z[0]:Z[0] + 8, 0:NMETA],
                                               func=AF.Copy),
                 reads=[self.slk(i) for i in Z], writes=["HM0"])
            return
        if mode == "p1":
            return
        if True:
            par = step % 2
            rpar = step % 4
            U = self.S_U
            for k in range(KT):
                if k % 2:
                    t.op("act", lambda e, k=k: e.activation(out=self.sl(U[8 + k]), in_=self.sl(Z[k]), func=AF.Copy,
                                                            scale=self.fA),
                         reads=[self.slk(Z[k]), "role"], writes=[self.slk(U[8 + k])])
                else:
                    t.op("dve", lambda e, k=k: e.tensor_single_scalar(self.sl(U[8 + k]), self.sl(Z[k]), self.fA,
                                                                      ALU.mult),
                         reads=[self.slk(Z[k]), "role"], writes=[self.slk(U[8 + k])])
            t.op("pool", lambda e: e.dma_start(out=self.send[par].rearrange("p (k n) -> p k n", k=KT),
                                               in_=self.SL[:, U[8]:U[8] + 8, :]),
                 reads=[self.slk(U[8 + i]) for i in range(8)], writes=[("send", par)], stream="dma_io", inc=16)
            t.op("pool", lambda e: e.collective_compute("AllReduce", ALU.add, replica_groups=self.groups,
                                                        ins=[self.send[par]], outs=[self.recv[rpar]]),
                 reads=[("send", par)], writes=[("recv", rpar)], stream="cc", inc=1)
        if step < SKEW:
            return
        nb = n // 128
        stage = self.SL[:, self.S_Y[0]:self.S_Y[0] + 8, :].rearrange("p a b -> p (a b)")
        stage = stage[:, 0:nb * D].rearrange("p (b f) -> p b f", b=nb)
        for b in range(nb):
            for half in range(2):
                pi = self.nextps()

                def emit(pe, b=b, half=half, pi=pi):
                    ins = None
                    for kk in range(4):
                        k = half * 4 + kk
                        ins = pe.transpose(self.ps[pi][:, kk * 128:(kk + 1) * 128],
                                           self.SL[:, Z[k], b * 128:(b + 1) * 128],
                                           self.ident[:, :])
                    return ins
                t.op("pe", emit, reads=[self.slk(Z[half * 4 + kk]) for kk in range(4)] + ["ident"],
                     writes=[("ps", pi)])
                self.evac("act" if half else "dve", stage[:, b, half * 512:(half + 1) * 512],
                          self.ps[pi][:, :], reads=[("ps", pi)], writes=[self.slk(self.S_Y[2 * b + half])])
        r0 = (step - SKEW) * NT
        dst = self.out[r0:r0 + n, :].rearrange("(b p) f -> p b f", p=128)
        t.op("pool", lambda e: e.dma_start(out=dst, in_=stage),
             reads=[self.slk(i) for i in self.S_Y], stream="dma_io", inc=16)

    def layernorm(self, n, gcol, bcol):
        t = self.t
        Z = self.S_Z
        ps_s = self.nextps()
        ps_q = self.nextps()
        self.mm(self.ps[ps_s][:, 0:n], ("ps", ps_s),
                [(self.ones[:, :], self.sl(Z[m], n)) for m in range(KT)],
                reads=[self.slk(Z[m]) for m in range(KT)] + ["ones"])
        sq = []
        for m in range(KT):
            ti = self.nexttmp()
            t.op("act", lambda e, m=m, ti=ti: e.activation(out=self.tmp[ti][:, 0:n], in_=self.sl(Z[m], n),
                                                          func=AF.Square),
                 reads=[self.slk(Z[m])], writes=[("tmp", ti)])
            first = (m == 0)
            last = (m == KT - 1)
            t.op("pe", lambda e, ti=ti, first=first, last=last: e.matmul(
                self.ps[ps_q][:, 0:n], self.ones[:, :], self.tmp[ti][:, 0:n], start=first, stop=last),
                reads=[("tmp", ti), "ones"], writes=[("ps", ps_q)])
        mi = self.nexttmp()
        mean = self.tmp[mi]
        mk = ("tmp", mi)
        ri = self.nexttmp()
        rstd = self.tmp[ri]
        rk = ("tmp", ri)
        t.op("dve", lambda e: e.tensor_single_scalar(mean[:, 0:n], self.ps[ps_s][:, 0:n], 1.0 / D, ALU.mult),
             reads=[("ps", ps_s)], writes=[mk])
        t.op("dve", lambda e: e.tensor_tensor(rstd[:, 0:n], mean[:, 0:n], mean[:, 0:n], ALU.mult),
             reads=[mk], writes=[rk])
        t.op("dve", lambda e: e.scalar_tensor_tensor(rstd[:, 0:n], self.ps[ps_q][:, 0:n], 1.0 / D,
                                                     rstd[:, 0:n], ALU.mult, ALU.subtract),
             reads=[("ps", ps_q), rk], writes=[rk])
        t.op("act", lambda e: e.activation(out=rstd[:, 0:n], in_=rstd[:, 0:n], func=AF.Sqrt, bias=EPS, scale=1.0),
             reads=[rk], writes=[rk])
        t.op("dve", lambda e: e.reciprocal(rstd[:, 0:n], rstd[:, 0:n]), reads=[rk], writes=[rk])
        for m in range(KT):
            zk = self.slk(Z[m])
            t.op("dve", lambda e, m=m: e.tensor_tensor(self.sl(Z[m], n), self.sl(Z[m], n), mean[:, 0:n], ALU.subtract),
                 reads=[zk, mk], writes=[zk])
            t.op("dve", lambda e, m=m: e.tensor_tensor(self.sl(Z[m], n), self.sl(Z[m], n), rstd[:, 0:n], ALU.mult),
                 reads=[zk, rk], writes=[zk])
            t.op("act", lambda e, m=m: e.activation(out=self.sl(Z[m], n), in_=self.sl(Z[m], n), func=AF.Identity,
                                                    scale=self.PRM[:, gcol + m:gcol + m + 1],
                                                    bias=self.PRM[:, bcol + m:bcol + m + 1]),
                 reads=[zk, ("prm", gcol // 8), ("prm", bcol // 8)], writes=[zk])

    def tile_pass(self, l, mode, n, step=0):
        t = self.t
        H, Y, Z, U = self.S_H, self.S_Y, self.S_Z, self.S_U
        self.load_input(mode, n, step)
        hreads = [self.slk(H[k]) for k in range(KT)]
        for c in range(5):
            slot = self.wacquire("win")
            wv = self.wview(slot, 0, KT, 512)
            for jj in range(4):
                j = c * 4 + jj
                if j == 12 and "hg" not in self.stub:
                    nb = (n + 127) // 128
                    VT = self.SL[:, U[12]:U[12] + 2, :].rearrange("p a b -> p (a b)").rearrange(
                        "p (b f) -> p b f", f=256)
                    for blk in range(nb):
                        pb = min(128, n - blk * 128)
                        pi = self.nextps()
                        self.mm(self.ps[pi][0:pb, 0:256], ("ps", pi),
                                [(self.SL[:, H[k], blk * 128:blk * 128 + pb], wv[:, k, 0:256]) for k in range(KT)],
                                reads=hreads + [("w", slot, 0), ("w", slot, 1)])
                        self.evac("act" if blk % 2 else "dve", VT[0:pb, blk, :], self.ps[pi][0:pb, 0:256],
                                  reads=[("ps", pi)], writes=[self.slk(U[12]), self.slk(U[13])])
                    continue
                if j == 13 and "hg" not in self.stub:
                    continue
                pi = self.nextps()
                self.mm(self.ps[pi][:, 0:n], ("ps", pi),
                        [(wv[:, k, jj * 128:(jj + 1) * 128], self.sl(H[k], n)) for k in range(KT)],
                        reads=hreads + [("w", slot, 0), ("w", slot, 1)])
                self.evac("act" if j % 2 else "dve", self.sl(U[j], n), self.ps[pi][:, 0:n],
                          reads=[("ps", pi)], writes=[self.slk(U[j])])
            self.wprefetch()
        self.mixers(l, 0, 0, n)
        yreads = [self.slk(Y[k]) for k in range(KT)]
        for c in range(2):
            slot = self.wacquire("wout")
            wv = self.wview(slot, 0, KT, 512)
            for jj in range(4):
                m = c * 4 + jj
                pi = self.nextps()
                self.mm(self.ps[pi][:, 0:n], ("ps", pi),
                        [(wv[:, k, jj * 128:(jj + 1) * 128], self.sl(Y[k], n)) for k in range(KT)],
                        reads=yreads + [("w", slot, 0), ("w", slot, 1)])
                t.op("dve", lambda e, m=m, pi=pi: e.scalar_tensor_tensor(
                    self.sl(Z[m], n), self.sl(H[m], n), ALPHA, self.ps[pi][:, 0:n], ALU.mult, ALU.add),
                    reads=[("ps", pi), self.slk(H[m])], writes=[self.slk(Z[m])])
            self.wprefetch()
        self.layernorm(n, 0, 8)
        if "tap_h1" in self.tapout or "h1" in self.tapout:
            pass
        zreads = [self.slk(Z[k]) for k in range(KT)]
        for c in range(11):
            slot = self.wacquire("f1")
            wg = self.wview(slot, 0, KT, 256)
            wu = self.wview(slot, KT * 256, KT, 256)
            for jj in range(2):
                i = c * 2 + jj
                pg = self.nextps()
                pu = self.nextps()
                self.mm(self.ps[pg][:, 0:n], ("ps", pg),
                        [(wg[:, k, jj * 128:(jj + 1) * 128], self.sl(Z[k], n)) for k in range(KT)],
                        reads=zreads + [("w", slot, 0), ("w", slot, 1)])
                self.mm(self.ps[pu][:, 0:n], ("ps", pu),
                        [(wu[:, k, jj * 128:(jj + 1) * 128], self.sl(Z[k], n)) for k in range(KT)],
                        reads=zreads + [("w", slot, 0), ("w", slot, 1)])
                tk = self.nexttmp()
                t.op("act", lambda e, tk=tk, pg=pg: e.activation(out=self.tmp[tk][:, 0:n], in_=self.ps[pg][:, 0:n],
                                                                func=AF.Silu),
                     reads=[("ps", pg)], writes=[("tmp", tk)])
                t.op("dve", lambda e, tk=tk, pu=pu, i=i: e.tensor_tensor(
                    self.sl(U[i], n), self.tmp[tk][:, 0:n], self.ps[pu][:, 0:n], ALU.mult),
                    reads=[("tmp", tk), ("ps", pu)], writes=[self.slk(U[i])])
            self.wprefetch()
        ureads = [self.slk(U[k]) for k in range(FT)]
        for m in range(KT):
            slot = self.wacquire("f2")
            wv = self.wview(slot, 0, FT, 128)
            pi = self.nextps()
            self.mm(self.ps[pi][:, 0:n], ("ps", pi),
                    [(wv[:, k, :], self.sl(U[k], n)) for k in range(FT)],
                    reads=ureads + [("w", slot, 0), ("w", slot, 1)])
            t.op("dve", lambda e, m=m, pi=pi: e.scalar_tensor_tensor(
                self.sl(Z[m], n), self.sl(Z[m], n), ALPHA, self.ps[pi][:, 0:n], ALU.mult, ALU.add),
                reads=[("ps", pi), self.slk(Z[m])], writes=[self.slk(Z[m])])
            self.wprefetch()
        self.layernorm(n, 16, 24)
        self.store_output(mode, n, step)

    def prm(self, col):
        return self.PRM[:, col:col + 1]

    def conv_acc(self, acc, x, carry, wcol, K, n, xk, acck, cark, wkey, first_bias=None):
        t = self.t
        if first_bias is None:
            t.op("dve", lambda e: e.tensor_single_scalar(acc[:, 0:n], x[:, 0:n], self.prm(wcol + K - 1), ALU.mult),
                 reads=[xk, wkey], writes=[acck])
        else:
            t.op("dve", lambda e: e.tensor_scalar(acc[:, 0:n], x[:, 0:n], self.prm(wcol + K - 1), self.prm(first_bias),
                                                  ALU.mult, ALU.add),
                 reads=[xk, wkey, "p_lcb"], writes=[acck])
        for k in range(K - 1):
            sh = K - 1 - k
            t.op("dve", lambda e, sh=sh, k=k: e.scalar_tensor_tensor(
                acc[:, sh:n], x[:, 0:n - sh], self.prm(wcol + k), acc[:, sh:n], ALU.mult, ALU.add),
                reads=[xk, wkey, acck], writes=[acck])
            t.op("dve", lambda e, sh=sh, k=k: e.scalar_tensor_tensor(
                acc[:, 0:sh], carry[:, K - 1 - sh:K - 1], self.prm(wcol + k), acc[:, 0:sh], ALU.mult, ALU.add),
                reads=[cark, wkey, acck], writes=[acck])
        t.op("dve", lambda e: e.tensor_copy(carry[:, 0:K - 1], x[:, n - (K - 1):n]),
             reads=[xk], writes=[cark])

    def mixers(self, l, ti, t0, n):
        t = self.t
        Y, U, Z = self.S_Y, self.S_U, self.S_Z
        if "s5" in self.stub:
            for k in range(2):
                self.evac("act" if k % 2 else "dve", self.sl(Y[k], n), self.sl(U[k], n),
                          reads=[self.slk(U[k])], writes=[self.slk(Y[k])])
        else:
            self.mix_s5(l, ti, n)
        if "sc" in self.stub:
            for k in range(2):
                self.evac("act", self.sl(Y[2 + k], n), self.sl(U[2 + k], n),
                          reads=[self.slk(U[2 + k])], writes=[self.slk(Y[2 + k])])
        else:
            self.mix_sc(n)
        if "hg" in self.stub:
            for k in range(2):
                self.evac("dve", self.sl(Y[4 + k], n), self.sl(U[4 + k], n),
                          reads=[self.slk(U[4 + k])], writes=[self.slk(Y[4 + k])])
        else:
            self.mix_hg(n)
        if "lru" in self.stub:
            for k in range(2):
                self.evac("act", self.sl(Y[6 + k], n), self.sl(U[6 + k], n),
                          reads=[self.slk(U[6 + k])], writes=[self.slk(Y[6 + k])])
        else:
            self.mix_lru(n)

    def mix_sc(self, n):
        t = self.t
        Y, U, Z = self.S_Y, self.S_U, self.S_Z
        for c in range(2):
            hs, bs, cs = U[2 + c], U[4 + c], U[6 + c]
            acc = Z[c]
            t.op("dve", lambda e: e.tensor_tensor(self.sl(hs, n), self.sl(hs, n), self.sl(cs, n), ALU.mult),
                 reads=[self.slk(hs), self.slk(cs)], writes=[self.slk(hs)])
            self.conv_acc(self.SL[:, acc, :], self.SL[:, hs, :], self.car_sc[:, c, :], 32 + 3 * c, 3, n,
                          self.slk(hs), self.slk(acc), ("car_sc", c), "p_scw")
            t.op("dve", lambda e: e.tensor_tensor(self.sl(Y[2 + c], n), self.sl(acc, n), self.sl(bs, n), ALU.mult),
                 reads=[self.slk(acc), self.slk(bs)], writes=[self.slk(Y[2 + c])])

    def mix_lru(self, n):
        t = self.t
        Y, U, Z = self.S_Y, self.S_U, self.S_Z
        for c in range(2):
            xs, ys = U[16 + c], U[18 + c]
            xc, ga, gx, aa = Z[0], Z[1], Z[2], Z[3]
            k = self.slk
            self.conv_acc(self.SL[:, xc, :], self.SL[:, xs, :], self.car_lx[:, c, :], 38 + 4 * c, 4, n,
                          k(xs), k(xc), ("car_lx", c), "p_lcw", first_bias=46 + c)
            pa, px = self.nextps(), self.nextps()
            self.mm(self.ps[pa][:, 0:n], ("ps", pa), [(self.lruw[:, 0, c, :], self.sl(xc, n))], reads=[k(xc), "lruw"])
            self.mm(self.ps[px][:, 0:n], ("ps", px), [(self.lruw[:, 1, c, :], self.sl(xc, n))], reads=[k(xc), "lruw"])
            t.op("act", lambda e: e.activation(out=self.sl(ga, n), in_=self.ps[pa][:, 0:n], func=AF.Sigmoid,
                                               bias=self.prm(48 + c), scale=1.0),
                 reads=[("ps", pa), "p_lba"], writes=[k(ga)])
            t.op("act", lambda e: e.activation(out=self.sl(gx, n), in_=self.ps[px][:, 0:n], func=AF.Sigmoid,
                                               bias=self.prm(50 + c), scale=1.0),
                 reads=[("ps", px), "p_lbx"], writes=[k(gx)])
            t.op("act", lambda e: e.activation(out=self.sl(aa, n), in_=self.sl(ga, n), func=AF.Exp,
                                               scale=self.prm(52 + c)),
                 reads=[k(ga), "p_cp"], writes=[k(aa)])
            t.op("act", lambda e: e.activation(out=self.sl(ga, n), in_=self.sl(ga, n), func=AF.Exp,
                                               scale=self.prm(54 + c)),
                 reads=[k(ga), "p_cp2"], writes=[k(ga)])
            t.op("act", lambda e: e.activation(out=self.sl(ga, n), in_=self.sl(ga, n), func=AF.Sqrt,
                                               scale=-1.0, bias=1.0),
                 reads=[k(ga)], writes=[k(ga)])
            t.op("dve", lambda e: e.tensor_tensor(self.sl(gx, n), self.sl(gx, n), self.sl(xc, n), ALU.mult),
                 reads=[k(gx), k(xc)], writes=[k(gx)])
            t.op("dve", lambda e: e.tensor_tensor(self.sl(gx, n), self.sl(gx, n), self.sl(ga, n), ALU.mult),
                 reads=[k(gx), k(ga)], writes=[k(gx)])
            t.op("dve", lambda e: e.tensor_tensor_scan(self.sl(xc, n), self.sl(aa, n), self.sl(gx, n),
                                                       self.car_lh[:, c:c + 1], ALU.mult, ALU.add),
                 reads=[k(aa), k(gx), ("car_lh", c)], writes=[k(xc)])
            t.op("dve", lambda e: e.tensor_copy(self.car_lh[:, c:c + 1], self.SL[:, xc, n - 1:n]),
                 reads=[k(xc)], writes=[("car_lh", c)])
            t.op("act", lambda e: e.activation(out=self.sl(aa, n), in_=self.sl(ys, n), func=AF.Gelu_apprx_tanh),
                 reads=[k(ys)], writes=[k(aa)])
            t.op("dve", lambda e: e.tensor_tensor(self.sl(Y[6 + c], n), self.sl(xc, n), self.sl(aa, n), ALU.mult),
                 reads=[k(xc), k(aa)], writes=[k(Y[6 + c])])

    def mix_hg(self, n):
        t = self.t
        Y, U, Z = self.S_Y, self.S_U, self.S_Z
        k = self.slk
        CL = 64 if n >= 64 else n
        nch = n // CL
        mid = CL // 2
        nb = (n + 127) // 128
        VT = self.SL[:, U[12]:U[12] + 2, :].rearrange("p a b -> p (a b)").rearrange("p (b f) -> p b f", f=256)
        vtk = [k(U[12]), k(U[13])]
        X = [Z[0], Z[1], Z[2], Z[3], Z[4], Z[5], Z[6], Z[7]]
        c3 = lambda ap: ap.rearrange("p (c s) -> p c s", s=CL)
        for pr in range(2):
            qs, fs, gs = U[8 + pr], U[10 + pr], U[14 + pr]
            g_, b_, d_, e1, e2, e3, qp, ktk = X
            t.op("act", lambda e: e.activation(out=self.sl(qs, n), in_=self.sl(qs, n), func=AF.Silu),
                 reads=[k(qs)], writes=[k(qs)])
            t.op("act", lambda e: e.activation(out=self.sl(fs, n), in_=self.sl(fs, n), func=AF.Sigmoid),
                 reads=[k(fs)], writes=[k(fs)])
            t.op("dve", lambda e: e.tensor_scalar(self.sl(fs, n), self.sl(fs, n), self.prm(58 + pr), self.prm(56 + pr),
                                                  ALU.mult, ALU.add),
                 reads=[k(fs), "p_lb", "p_oml"], writes=[k(fs)])
            t.op("act", lambda e: e.activation(out=self.sl(g_, n), in_=self.sl(fs, n), func=AF.Ln),
                 reads=[k(fs)], writes=[k(g_)])
            t.op("dve", lambda e: e.tensor_scalar(self.sl(fs, n), self.sl(fs, n), -1.0, 1.0, ALU.mult, ALU.add),
                 reads=[k(fs), k(g_)], writes=[k(fs)])
            t.op("dve", lambda e: e.tensor_tensor_scan(self.sl(b_, n), self.cmask[:, 0:n], self.sl(g_, n), 0.0,
                                                       ALU.mult, ALU.add),
                 reads=[k(g_), "cmask"], writes=[k(b_)])
            b3 = c3(self.sl(b_, n))
            t.op("dve", lambda e: e.tensor_tensor(c3(self.sl(d_, n)), b3, b3[:, :, mid:mid + 1].to_broadcast([128, nch, CL]),
                                                  ALU.subtract),
                 reads=[k(b_)], writes=[k(d_)])
            t.op("act", lambda e: e.activation(out=self.sl(e1, n), in_=self.sl(d_, n), func=AF.Exp),
                 reads=[k(d_)], writes=[k(e1)])
            t.op("act", lambda e: e.activation(out=self.sl(e2, n), in_=self.sl(d_, n), func=AF.Exp, scale=-1.0),
                 reads=[k(d_)], writes=[k(e2)])
            t.op("act", lambda e: e.activation(out=self.sl(e3, n), in_=self.sl(b_, n), func=AF.Exp),
                 reads=[k(b_)], writes=[k(e3)])
            t.op("dve", lambda e: e.tensor_tensor(c3(self.sl(d_, n)), b3, b3[:, :, CL - 1:CL].to_broadcast([128, nch, CL]),
                                                  ALU.subtract),
                 reads=[k(b_), k(e1), k(e2)], writes=[k(d_)])
            t.op("act", lambda e: e.activation(out=self.sl(d_, n), in_=self.sl(d_, n), func=AF.Exp, scale=-1.0),
                 reads=[k(d_)], writes=[k(d_)])
            t.op("dve", lambda e: e.scalar_tensor_tensor(self.sl(e1, n), self.sl(qs, n), 0.125, self.sl(e1, n),
                                                         ALU.mult, ALU.mult),
                 reads=[k(qs), k(e1)], writes=[k(e1)])
            t.op("dve", lambda e: e.tensor_tensor(self.sl(e2, n), self.sl(fs, n), self.sl(e2, n), ALU.mult),
                 reads=[k(fs), k(e2)], writes=[k(e2)])
            t.op("dve", lambda e: e.scalar_tensor_tensor(self.sl(qp, n), self.sl(qs, n), 0.125, self.sl(e3, n),
                                                         ALU.mult, ALU.mult),
                 reads=[k(qs), k(e3)], writes=[k(qp)])
            t.op("dve", lambda e: e.tensor_tensor(self.sl(d_, n), self.sl(fs, n), self.sl(d_, n), ALU.mult),
                 reads=[k(fs), k(d_)], writes=[k(d_)])
            KTv = self.SL[:, ktk, :].rearrange("p (b f) -> p b f", f=128)
            for blk in range(nb):
                pb = min(128, n - blk * 128)
                pi = self.nextps()
                t.op("pe", lambda e, blk=blk, pb=pb, pi=pi: e.transpose(
                    self.ps[pi][0:pb, 0:128], self.SL[:, d_, blk * 128:blk * 128 + pb], self.ident[:, :]),
                    reads=[k(d_), "ident"], writes=[("ps", pi)])
                self.evac("act", KTv[0:pb, blk, :], self.ps[pi][0:pb, 0:128], reads=[("ps", pi)], writes=[k(ktk)])
            for c in range(nch):
                blk, r0 = (c * CL) // 128, (c * CL) % 128
                pi = self.nextps()
                self.mm(self.ps[pi][:, 0:128], ("ps", pi),
                        [(KTv[r0:r0 + CL, blk, :], VT[r0:r0 + CL, blk, pr * 128:(pr + 1) * 128])],
                        reads=[k(ktk)] + vtk)
                for hh in range(2):
                    rs = slice(hh * 64, (hh + 1) * 64)
                    t.op("dve", lambda e, c=c, rs=rs, pi=pi: e.scalar_tensor_tensor(
                        self.SS[rs, pr, c + 1, rs], self.SS[rs, pr, c, rs],
                        self.SL[rs, e3, (c + 1) * CL - 1:(c + 1) * CL], self.ps[pi][rs, rs],
                        ALU.mult, ALU.add),
                        reads=[("ps", pi), k(e3), ("SS", pr)], writes=[("SS", pr)])
            OPS = self.ps[6]
            for blk in range(nb):
                pb = min(128, n - blk * 128)
                bs = slice(blk * 128, blk * 128 + pb)
                smi = self.smn
                self.smn = (self.smn + 1) % 2
                SMv = self.SM[smi]
                for hh in range(2):
                    rs = slice(hh * 64, (hh + 1) * 64)
                    pi = self.nextps()
                    t.op("pe", lambda e, pi=pi, rs=rs, bs=bs, pb=pb, hh=hh: e.matmul(
                        self.ps[pi][0:pb, 0:pb], self.SL[rs, e2, bs], self.SL[rs, e1, bs],
                        start=True, stop=True, tile_position=(hh * 64, 0)),
                        reads=[k(e1), k(e2)], writes=[("ps", pi)])
                    t.op("dve", lambda e, pi=pi, pb=pb, hh=hh, SMv=SMv: e.tensor_tensor(
                        SMv[0:pb, hh, 0:pb], self.ps[pi][0:pb, 0:pb], self.mask2[0:pb, 0:pb], ALU.mult),
                        reads=[("ps", pi), "mask2"], writes=[("SM", smi, hh)])

                def emit_o(pe, blk=blk, bs=bs, pb=pb, SMv=SMv):
                    ins = None
                    for hh in range(2):
                        rs = slice(hh * 64, (hh + 1) * 64)
                        pe.matmul(OPS[rs, bs], VT[0:pb, blk, pr * 128 + hh * 64:pr * 128 + (hh + 1) * 64],
                                  SMv[0:pb, hh, 0:pb], start=True, stop=False, tile_position=(0, hh * 64))
                    for c in range(blk * 128 // CL, (blk * 128 + pb) // CL):
                        cs = slice(c * CL, (c + 1) * CL)
                        ins = pe.matmul(OPS[:, cs], self.SS[:, pr, c, :], self.SL[:, qp, cs],
                                        start=False, stop=True)
                    return ins
                t.op("pe", emit_o, reads=[("SM", smi, 0), ("SM", smi, 1), ("SS", pr), k(qp)] + vtk,
                     writes=[("ps", 6)])
            for hh in range(2):
                rs = slice(hh * 64, (hh + 1) * 64)
                t.op("dve", lambda e, rs=rs: e.tensor_copy(self.SS[rs, pr, 0, rs], self.SS[rs, pr, nch, rs]),
                     reads=[("SS", pr)], writes=[("SS", pr)])
            osq, rst = g_, b_
            t.op("act", lambda e: e.activation(out=self.sl(osq, n), in_=OPS[:, 0:n], func=AF.Square),
                 reads=[("ps", 6)], writes=[k(osq)])
            pm = self.nextps()
            self.mm(self.ps[pm][:, 0:n], ("ps", pm), [(self.bd64[:, :], self.sl(osq, n))], reads=[k(osq), "bd64"])
            t.op("act", lambda e: e.activation(out=self.sl(rst, n), in_=self.ps[pm][:, 0:n], func=AF.Sqrt,
                                               bias=EPS, scale=1.0),
                 reads=[("ps", pm)], writes=[k(rst)])
            t.op("dve", lambda e: e.reciprocal(self.sl(rst, n), self.sl(rst, n)), reads=[k(rst)], writes=[k(rst)])
            t.op("dve", lambda e: e.tensor_tensor(self.sl(rst, n), self.sl(rst, n), OPS[:, 0:n], ALU.mult),
                 reads=[k(rst), ("ps", 6)], writes=[k(rst)])
            t.op("act", lambda e: e.activation(out=self.sl(gs, n), in_=self.sl(gs, n), func=AF.Silu),
                 reads=[k(gs)], writes=[k(gs)])
            t.op("dve", lambda e: e.scalar_tensor_tensor(self.sl(Y[4 + pr], n), self.sl(rst, n), self.prm(60 + pr),
                                                         self.sl(gs, n), ALU.mult, ALU.mult),
                 reads=[k(rst), k(gs), "p_gn"], writes=[k(Y[4 + pr])])

    def s5_setup(self, l):
        t = self.t
        pin = self.pin
        Q = self.Q
        S = self.S5S
        KEY = "s5setup"
        col = [0]

        def alloc(ncol):
            c0 = col[0]
            col[0] += ncol
            assert col[0] <= 2176
            return S[:, c0:c0 + ncol]

        def dv(fn):
            t.op("dve", fn, reads=[KEY], writes=[KEY])

        def ac(fn):
            t.op("act", fn, reads=[KEY], writes=[KEY])

        def dma(out, in_):
            t.op("pool", lambda e: e.dma_start(out=out, in_=in_, allow_slow_non_contiguous=True),
                 reads=[KEY], writes=[KEY], stream="dma_io", inc=16)

        TT = lambda o, a, b, op: dv(lambda e: e.tensor_tensor(o, a, b, op))
        TS = lambda o, a, sc, op: dv(lambda e: e.tensor_single_scalar(o, a, sc, op))

        def cmul(o_r, o_i, a_r, a_i, b_r, b_i, t1, t2):
            TT(t1, a_r, b_r, ALU.mult)
            TT(t2, a_i, b_i, ALU.mult)
            TT(o_r, t1, t2, ALU.subtract)
            TT(t1, a_r, b_i, ALU.mult)
            TT(t2, a_i, b_r, ALU.mult)
            TT(o_i, t1, t2, ALU.add)

        V = lambda: alloc(8)
        LR, LI, DT, X, TH, Cc, Ss, T1, T2, T3, AR, AI, WR, WI, MM, IAR, IAI, RI = [V() for _ in range(18)]
        lre = pin["s5_lam_re"][l]
        lim = pin["s5_lam_im"][l]
        ldt = pin["s5_log_dt"][l]
        dma(LR, bass.AP(lre.tensor, lre.offset, [[1, 128], [128, 8]]))
        dma(LI, bass.AP(lim.tensor, lim.offset, [[1, 128], [128, 8]]))
        for gl in range(2):
            dma(DT[gl * 64:(gl + 1) * 64, :], bass.AP(ldt.tensor, ldt.offset + gl, [[0, 64], [2, 8]]))
        ac(lambda e: e.activation(out=DT, in_=DT, func=AF.Exp))
        TT(X, LR, DT, ALU.mult)
        TT(TH, LI, DT, ALU.mult)
        ac(lambda e: e.activation(out=Ss, in_=TH, func=AF.Sin, scale=1.0 / 16))
        TS(T3, TH, 1.0 / 16, ALU.mult)
        TS(T3, T3, math.pi / 2, ALU.add)
        ac(lambda e: e.activation(out=Cc, in_=T3, func=AF.Sin))
        for _ in range(4):
            TT(T1, Cc, Cc, ALU.mult)
            TT(T2, Ss, Ss, ALU.mult)
            TT(T3, Cc, Ss, ALU.mult)
            TT(Cc, T1, T2, ALU.subtract)
            TS(Ss, T3, 2.0, ALU.mult)
        ac(lambda e: e.activation(out=T3, in_=X, func=AF.Exp))
        TT(AR, T3, Cc, ALU.mult)
        TT(AI, T3, Ss, ALU.mult)
        ac(lambda e: e.activation(out=self.RR[:, :], in_=X, func=AF.Exp, scale=float(Q)))
        dv(lambda e: e.reciprocal(RI, self.RR[:, :]))
        TT(T1, T3, T3, ALU.mult)
        dv(lambda e: e.reciprocal(T1, T1))
        TT(IAR, AR, T1, ALU.mult)
        TT(IAI, AI, T1, ALU.mult)
        TS(IAI, IAI, -1.0, ALU.mult)
        TS(T3, AR, -1.0, ALU.add)
        TT(T1, LR, LR, ALU.mult)
        TT(T2, LI, LI, ALU.mult)
        TT(MM, T1, T2, ALU.add)
        dv(lambda e: e.reciprocal(MM, MM))
        TT(T1, T3, LR, ALU.mult)
        TT(T2, AI, LI, ALU.mult)
        TT(WR, T1, T2, ALU.add)
        TT(WR, WR, MM, ALU.mult)
        TT(T1, AI, LR, ALU.mult)
        TT(T2, T3, LI, ALU.mult)
        TT(WI, T1, T2, ALU.subtract)
        TT(WI, WI, MM, ALU.mult)
        PWr = alloc(8 * (Q + 1)).rearrange("p (a s) -> p a s", s=Q + 1)
        PWi = alloc(8 * (Q + 1)).rearrange("p (a s) -> p a s", s=Q + 1)
        IPr = alloc(8 * Q).rearrange("p (a s) -> p a s", s=Q)
        IPi = alloc(8 * Q).rearrange("p (a s) -> p a s", s=Q)
        dv(lambda e: e.memset(PWr[:, :, 0], 1.0))
        dv(lambda e: e.memset(PWi[:, :, 0], 0.0))
        dv(lambda e: e.memset(IPr[:, :, 0], 1.0))
        dv(lambda e: e.memset(IPi[:, :, 0], 0.0))
        for sidx in range(Q):
            cmul(PWr[:, :, sidx + 1], PWi[:, :, sidx + 1], PWr[:, :, sidx], PWi[:, :, sidx], AR, AI, T1, T2)
        for sidx in range(Q - 1):
            cmul(IPr[:, :, sidx + 1], IPi[:, :, sidx + 1], IPr[:, :, sidx], IPi[:, :, sidx], IAR, IAI, T1, T2)
        FBr = alloc(8 * Q).rearrange("p (a s) -> p a s", s=Q)
        FBi = alloc(8 * Q).rearrange("p (a s) -> p a s", s=Q)
        for sidx in range(Q):
            cmul(FBr[:, :, sidx], FBi[:, :, sidx], IPr[:, :, sidx], IPi[:, :, sidx], WR, WI, T1, T2)
        TBr, TBi = self.TB[:, :, 0, :], self.TB[:, :, 1, :]
        dv(lambda e: e.memset(TBr[:, :, 0], 1.0))
        dv(lambda e: e.memset(TBi[:, :, 0], 0.0))
        TT(TBr[:, :, 1], PWr[:, :, Q], RI, ALU.mult)
        TT(TBi[:, :, 1], PWi[:, :, Q], RI, ALU.mult)
        big = lambda: alloc(256).rearrange("p (a h) -> p a h", h=32)
        Br, Bi, Sr, Si, U1, U2 = [big() for _ in range(6)]
        TWs = (U1.rearrange("p a h -> p (a h)"), U2.rearrange("p a h -> p (a h)"))
        tw = lambda i, m: TWs[i][:, 0:8 * m].rearrange("p (a c) -> p a c", c=m)
        m = 1
        while m < 64:
            if m >= 2:
                h = m // 2
                cmul(TBr[:, :, m], TBi[:, :, m], TBr[:, :, h], TBi[:, :, h], TBr[:, :, h], TBi[:, :, h], T1, T2)
            if m >= 2:
                br = TBr[:, :, m:m + 1].to_broadcast([128, 8, m - 1])
                bi = TBi[:, :, m:m + 1].to_broadcast([128, 8, m - 1])
                cmul(TBr[:, :, m + 1:2 * m], TBi[:, :, m + 1:2 * m], TBr[:, :, 1:m], TBi[:, :, 1:m], br, bi,
                     tw(0, m - 1), tw(1, m - 1))
            m *= 2
        cmul(TBr[:, :, 64], TBi[:, :, 64], TBr[:, :, 32], TBi[:, :, 32], TBr[:, :, 32], TBi[:, :, 32], T1, T2)
        rb = self.RR[:, :].unsqueeze(2).to_broadcast([128, 8, 64])
        TT(self.TA[:, :, 0, :], TBr[:, :, 0:64], rb, ALU.mult)
        TT(self.TA[:, :, 1, :], TBi[:, :, 0:64], rb, ALU.mult)
        TS(self.TA[:, :, 1, :], self.TA[:, :, 1, :], -1.0, ALU.mult)
        for ap_ in (Br, Bi):
            dv(lambda e, ap_=ap_: e.memset(ap_, 0.0))
        for g in range(16):
            P_, gl = g // 2, g % 2
            dma(Br[gl * 64:(gl + 1) * 64, P_, gl * 16:(gl + 1) * 16], pin["s5_b_re"][l, g])
            dma(Bi[gl * 64:(gl + 1) * 64, P_, gl * 16:(gl + 1) * 16], pin["s5_b_im"][l, g])
        for sidx in range(Q):
            fr = FBr[:, :, sidx:sidx + 1].to_broadcast([128, 8, 32])
            fi = FBi[:, :, sidx:sidx + 1].to_broadcast([128, 8, 32])
            cmul(Sr, Si, Br, Bi, fr, fi, U1, U2)
            for ri, src in enumerate((Sr, Si)):
                for half in range(2):
                    pi = self.nextps()
                    t.op("pe", lambda e, src=src, half=half, pi=pi: e.transpose(
                        self.ps[pi][:, 0:128], src[:, half * 4:(half + 1) * 4, :], self.ident[:, :]),
                        reads=[KEY, "ident"], writes=[("ps", pi)])
                    t.op("act", lambda e, half=half, sidx=sidx, ri=ri, pi=pi: e.activation(
                        out=self.BsT[:, half, sidx, ri, :], in_=self.ps[pi][:, 0:128], func=AF.Copy),
                        reads=[("ps", pi)], writes=["BsT"])
        CNr = Br.rearrange("p a h -> p (a h)").rearrange("p (a q) -> p a q", q=128)
        CNi = Bi.rearrange("p a h -> p (a h)").rearrange("p (a q) -> p a q", q=128)
        for ap_ in (CNr, CNi):
            dv(lambda e, ap_=ap_: e.memset(ap_, 0.0))
        for g in range(16):
            P_, gl = g // 2, g % 2
            half, j = P_ // 4, P_ % 4
            r0 = 32 * j + 16 * gl
            dma(CNr[r0:r0 + 16, half, gl * 64:(gl + 1) * 64], pin["s5_c_re"][l, g])
            dma(CNi[r0:r0 + 16, half, gl * 64:(gl + 1) * 64], pin["s5_c_im"][l, g])
        for src, dst in ((CNr, Sr), (CNi, Si)):
            for half in range(2):
                pi = self.nextps()
                t.op("pe", lambda e, src=src, half=half, pi=pi: e.transpose(
                    self.ps[pi][:, 0:128], src[:, half, :], self.ident[:, :]),
                    reads=[KEY, "ident"], writes=[("ps", pi)])
                t.op("act", lambda e, dst=dst, half=half, pi=pi: e.activation(
                    out=dst[:, half * 4:(half + 1) * 4, :], in_=self.ps[pi][:, 0:128].rearrange("p (a h) -> p a h", h=32),
                    func=AF.Copy), reads=[("ps", pi), KEY], writes=[KEY])
        for sidx in range(Q):
            pr_ = PWr[:, :, sidx:sidx + 1].to_broadcast([128, 8, 32])
            pi_ = PWi[:, :, sidx:sidx + 1].to_broadcast([128, 8, 32])
            cmul(self.CsT[:, :, sidx, 0, :], self.CsT[:, :, sidx, 1, :], Sr, Si, pr_, pi_, U1, U2)
            dv(lambda e, sidx=sidx: e.tensor_single_scalar(self.CsT[:, :, sidx, 1, :], self.CsT[:, :, sidx, 1, :],
                                                           -1.0, ALU.mult))
        t.op("dve", lambda e: e.tensor_copy(self.CH[:, 0, 0:1], self.CH[:, 0, 0:1]), reads=[KEY], writes=["CsT", "TAB"])
        t.op("pool", lambda e: e.dma_start(out=self.GW[:, :, :],
                                           in_=pin["s5_glu_w"][l].rearrange("(kt p) n -> p kt n", p=128)),
             writes=["GW"], stream="dma_io", inc=16)
        t.op("dve", lambda e: e.memset(self.CARJ[:], 0.0), writes=[("CARJ", P_) for P_ in range(8)])

    def mix_s5(self, l, ti, n):
        t = self.t
        Y, U, Z = self.S_Y, self.S_U, self.S_Z
        k = self.slk
        Q = self.Q
        ncn = n // Q
        tm = lambda ap: ap.rearrange("p (c s) -> p s c", s=Q)
        sm = lambda ap: ap.rearrange("p (s c) -> p s c", s=Q)
        for half in range(2):
            t.op("act", lambda e, half=half: e.activation(out=sm(self.sl(Z[half], n)), in_=tm(self.sl(U[half], n)),
                                                          func=AF.Copy),
                 reads=[k(U[half])], writes=[k(Z[half])])
        YB = self.ps[7]
        for P_ in range(8):
            half, j = P_ // 4, P_ % 4
            rj = slice(32 * j, 32 * j + 32)
            Wk, Gk, Gp = (Z[2], Z[3]), (Z[4], Z[5]), (Z[6], Z[7])
            for ri in range(2):
                pv = self.nextps()

                def emit_b(pe, ri=ri, pv=pv):
                    ins = None
                    for sidx in range(Q):
                        ins = pe.matmul(self.ps[pv][:, sidx * ncn:(sidx + 1) * ncn],
                                        self.BsT[rj, half, sidx, ri, :],
                                        self.SL[rj, Z[half], sidx * ncn:(sidx + 1) * ncn],
                                        start=True, stop=True, tile_position=(32 * j, 0))
                    return ins
                t.op("pe", emit_b, reads=["BsT", k(Z[half])], writes=[("ps", pv)])
                t.op("act", lambda e, ri=ri, pv=pv: e.activation(out=tm(self.sl(Wk[ri], n)), in_=sm(self.ps[pv][:, 0:n]),
                                                                func=AF.Copy),
                     reads=[("ps", pv)], writes=[k(Wk[ri])])
                t.op("dve", lambda e, ri=ri: e.tensor_tensor_scan(self.sl(Gk[ri], n), self.cmask8[:, 0:n],
                                                                  self.sl(Wk[ri], n), 0.0, ALU.mult, ALU.add),
                     reads=[k(Wk[ri]), "cmask8"], writes=[k(Gk[ri])])
            CH = self.CH
            ck = ("CH",)
            gl_ = [self.SL[:, Gk[ri], 0:n].rearrange("p (c s) -> p c s", s=Q)[:, :, Q - 1] for ri in range(2)]
            TAr, TAi = self.TA[:, P_, 0, 0:ncn], self.TA[:, P_, 1, 0:ncn]
            TBr, TBi = self.TB[:, P_, 0, 0:ncn + 1], self.TB[:, P_, 1, 0:ncn + 1]
            vr, vi, t1, t2 = CH[:, 0, 0:ncn], CH[:, 1, 0:ncn], CH[:, 2, 0:ncn + 1], CH[:, 3, 0:ncn + 1]
            jt = [CH[:, 4, 0:ncn + 1], CH[:, 5, 0:ncn + 1]]
            jj = [CH[:, 6, 0:ncn + 1], CH[:, 7, 0:ncn + 1]]
            gk = [k(Gk[0]), k(Gk[1])]

            def dv(fn, reads=(), writes=()):
                t.op("dve", fn, reads=list(reads) + [ck, "TAB"], writes=list(writes) + [ck])
            dv(lambda e: e.tensor_tensor(t1[:, 0:ncn], gl_[0], TAr, ALU.mult), reads=gk)
            dv(lambda e: e.tensor_tensor(t2[:, 0:ncn], gl_[1], TAi, ALU.mult), reads=gk)
            dv(lambda e: e.tensor_tensor(vr, t1[:, 0:ncn], t2[:, 0:ncn], ALU.subtract))
            dv(lambda e: e.tensor_tensor(t1[:, 0:ncn], gl_[0], TAi, ALU.mult), reads=gk)
            dv(lambda e: e.tensor_tensor(t2[:, 0:ncn], gl_[1], TAr, ALU.mult), reads=gk)
            dv(lambda e: e.tensor_tensor(vi, t1[:, 0:ncn], t2[:, 0:ncn], ALU.add))
            for ri, v_ in enumerate((vr, vi)):
                dv(lambda e, ri=ri: e.tensor_copy(jt[ri][:, 0:1], self.CARJ[:, P_, ri:ri + 1]), reads=[("CARJ", P_)])
                dv(lambda e, ri=ri, v_=v_: e.tensor_tensor_scan(
                    jt[ri][:, 1:ncn + 1], self.RR[:, P_:P_ + 1].to_broadcast([128, ncn]), v_,
                    self.CARJ[:, P_, ri:ri + 1], ALU.mult, ALU.add), reads=[("CARJ", P_)])
            dv(lambda e: e.tensor_tensor(t1, jt[0], TBr, ALU.mult))
            dv(lambda e: e.tensor_tensor(t2, jt[1], TBi, ALU.mult))
            dv(lambda e: e.tensor_tensor(jj[0], t1, t2, ALU.subtract))
            dv(lambda e: e.tensor_tensor(t1, jt[0], TBi, ALU.mult))
            dv(lambda e: e.tensor_tensor(t2, jt[1], TBr, ALU.mult))
            dv(lambda e: e.tensor_tensor(jj[1], t1, t2, ALU.add))
            for ri in range(2):
                dv(lambda e, ri=ri: e.tensor_copy(self.CARJ[:, P_, ri:ri + 1], jj[ri][:, ncn:ncn + 1]),
                   writes=[("CARJ", P_)])
            for ri in range(2):
                t.op("dve", lambda e, ri=ri: e.tensor_tensor(
                    sm(self.sl(Gp[ri], n)), tm(self.sl(Gk[ri], n)),
                    jj[ri][:, 0:ncn].unsqueeze(1).to_broadcast([128, Q, ncn]), ALU.add),
                    reads=[k(Gk[ri]), ck], writes=[k(Gp[ri])])

            def emit_c(pe):
                ins = None
                for sidx in range(Q):
                    cs = slice(sidx * ncn, (sidx + 1) * ncn)
                    pe.matmul(YB[rj, cs], self.CsT[:, P_, sidx, 0, :], self.SL[:, Gp[0], cs],
                              start=True, stop=False, tile_position=(0, 32 * j))
                    ins = pe.matmul(YB[rj, cs], self.CsT[:, P_, sidx, 1, :], self.SL[:, Gp[1], cs],
                                    start=False, stop=True, tile_position=(0, 32 * j))
                return ins
            t.op("pe", emit_c, reads=["CsT", k(Gp[0]), k(Gp[1])], writes=[("ps", 7)])
            if j == 3:
                ys = U[20 + half]
                t.op("dve", lambda e, half=half, ys=ys: e.scalar_tensor_tensor(
                    tm(self.sl(ys, n)), tm(self.sl(U[half], n)), self.prm(62 + half), sm(YB[:, 0:n]),
                    ALU.mult, ALU.add),
                    reads=[k(U[half]), ("ps", 7), "p_s5d"], writes=[k(ys)])
                t.op("act", lambda e, ys=ys: e.activation(out=self.sl(ys, n), in_=self.sl(ys, n),
                                                          func=AF.Gelu_apprx_tanh),
                     reads=[k(ys)], writes=[k(ys)])
        for m in range(2):
            pi = self.nextps()
            self.mm(self.ps[pi][:, 0:n], ("ps", pi),
                    [(self.GW[:, kt, m * 128:(m + 1) * 128], self.sl(U[20 + kt], n)) for kt in range(2)],
                    reads=[k(U[20]), k(U[21]), "GW"])
            tk = self.nexttmp()
            t.op("act", lambda e, tk=tk, pi=pi, m=m: e.activation(out=self.tmp[tk][:, 0:n], in_=self.ps[pi][:, 0:n],
                                                                 func=AF.Sigmoid, bias=self.prm(64 + m), scale=1.0),
                 reads=[("ps", pi), "p_glub"], writes=[("tmp", tk)])
            t.op("dve", lambda e, tk=tk, m=m: e.tensor_tensor(self.sl(Y[m], n), self.sl(U[20 + m], n),
                                                              self.tmp[tk][:, 0:n], ALU.mult),
                 reads=[("tmp", tk), k(U[20 + m])], writes=[k(Y[m])])


_CACHE = {}
_BUILDERS = {}


def _get_nc(n_xtiles):
    if n_xtiles not in _CACHE:
        _BUILDERS[n_xtiles] = Builder(n_xtiles)
        _CACHE[n_xtiles] = _BUILDERS[n_xtiles].build()
    return _CACHE[n_xtiles]


def make_in_maps(inputs, names, xs):
    npairs = len(xs)
    role_a = np.zeros((128, 2), np.float32)
    role_a[:, 0] = 1.0
    role_b = np.zeros((128, 2), np.float32)
    role_b[:, 1] = 1.0
    pa, pb = {}, {}
    for k in names:
        if k in ("x", "role"):
            continue
        arr = np.ascontiguousarray(inputs[k], dtype=np.float32)
        if k in ("meta_tokens", "hg_lb_raw"):
            pa[k] = arr
            pb[k] = arr
        else:
            pa[k] = np.ascontiguousarray(arr[[0, 0]])
            pb[k] = arr
    maps = [dict(pa, x=xs[i], role=role_a) for i in range(npairs)]
    maps += [dict(pb, x=xs[i], role=role_b) for i in range(npairs)]
    return maps


def kernel(**inputs):
    x = np.ascontiguousarray(inputs["x"], dtype=np.float32)
    bsz, seq, _ = x.shape
    n_xtiles = seq // NT
    nc = _get_nc(n_xtiles)
    names = _BUILDERS[n_xtiles].in_names
    in_maps = make_in_maps(inputs, names, [x[b] for b in range(bsz)])
    res = run_bass_kernel_spmd(nc, in_maps, core_ids=list(range(2 * bsz)))
    return np.stack([res.results[bsz + b]["out"] for b in range(bsz)], axis=0)
```

```python
import math
import os
from contextlib import ExitStack

import numpy as np
import concourse.bass as bass
import concourse.mybir as mybir
from concourse.bass_utils import run_bass_kernel_spmd

F32 = mybir.dt.float32
BF16 = mybir.dt.bfloat16
AF = mybir.ActivationFunctionType
ALU = mybir.AluOpType

D = 1024
KT = 8
NIN = 2560
DFF = 2816
FT = 22
NMETA = 16
SEQ = 8192
DEPTH = 2
ALPHA = (2 * DEPTH) ** 0.25
EPS = 1e-5
NT = 512
SKEW = 1
WSLOT = 4096
NWSLOT = 2
NBSLOT = 4
NCHUNK = 26


class Trk:
    def __init__(self, nc, es):
        self.nc = nc
        self.es = es
        self.E = {"pe": nc.tensor, "act": nc.scalar, "dve": nc.vector,
                  "pool": nc.gpsimd, "sp": nc.sync}
        self.cur = {}
        self.waited = {}
        self.lastw = {}
        self.rd = {}
        self.nsem = 0
        self.LIM = 30000
        self.nins = 0

    def _sem(self, stream, inc):
        s = self.cur.get(stream)
        if s is None or s[1] + inc > self.LIM:
            name = f"s{self.nsem}"
            sem = self.es.enter_context(self.nc.semaphore(f"{name}_{stream}"))
            self.nsem += 1
            s = [sem, 0, name]
            self.cur[stream] = s
        s[1] += inc
        return (s[0], s[1], s[2])

    def wait(self, engine, tok):
        sem, val, name = tok
        k = (engine, name)
        if self.waited.get(k, 0) >= val:
            return
        self.waited[k] = val
        self.E[engine].wait_ge(sem, val)

    def op(self, engine, emit, reads=(), writes=(), stream=None, inc=1):
        deps = {}

        def add(tok):
            if tok is None:
                return
            n = tok[2]
            if n not in deps or deps[n][1] < tok[1]:
                deps[n] = tok

        for k in reads:
            add(self.lastw.get(k))
        for k in writes:
            add(self.lastw.get(k))
            for t in self.rd.get(k, {}).values():
                add(t)
        st = stream or engine
        own = self.cur.get(st)
        for tok in deps.values():
            if engine == "pe" and stream is None and own is not None and tok[2] == own[2]:
                continue
            self.wait(engine, tok)
        ins = emit(self.E[engine])
        tok = self._sem(st, inc)
        ins.then_inc(tok[0], inc)
        self.nins += 1
        for k in reads:
            self.rd.setdefault(k, {})[tok[2]] = tok
        for k in writes:
            self.lastw[k] = tok
            self.rd[k] = {}
        return tok

    def drain(self, engine):
        for st, s in self.cur.items():
            self.wait(engine, (s[0], s[1], s[2]))


class _Stop(Exception):
    pass


class Builder:
    def __init__(self, n_xtiles, stub=(), taps=(), npairs=4):
        self.npairs = npairs
        self.n_xtiles = n_xtiles
        self.stub = set(stub)
        self.taps = list(taps)
        self.tiles = [(0, NMETA)] + [(NMETA + i * NT, NT) for i in range(n_xtiles)]
        self.seq_x = n_xtiles * NT

    def build(self):
        nc = bass.Bass("TRN2", target_bir_lowering=False)
        self.nc = nc
        self.es = ExitStack()
        es = self.es
        self.t = Trk(nc, es)
        t = self.t
        self.in_names = []

        def di(name, shape):
            self.in_names.append(name)
            return nc.dram_tensor(name, list(shape), F32, kind="ExternalInput").ap()
        self.x = di("x", [self.seq_x, D])
        self.meta = di("meta_tokens", [NMETA, D])
        self.w_in = di("w_in", [DEPTH, D, NIN])
        self.w_out = di("w_out", [DEPTH, D, D])
        self.w_f1 = di("w_ffn_in", [DEPTH, D, 2 * DFF])
        self.w_f2 = di("w_ffn_out", [DEPTH, DFF, D])
        self.ln = {n: di(n, [DEPTH, D]) for n in ("ln1_g", "ln1_b", "ln2_g", "ln2_b")}
        self.pin = {}
        for nm, shp in (("hg_lb_raw", [DEPTH, 256]), ("sc_conv_w", [DEPTH, 3, 256]), ("hg_gnorm", [DEPTH, 256]),
                        ("lru_conv_w", [DEPTH, 4, 256]), ("lru_conv_b", [DEPTH, 256]),
                        ("lru_wa", [DEPTH, 4, 64, 64]), ("lru_ba", [DEPTH, 256]),
                        ("lru_wx", [DEPTH, 4, 64, 64]), ("lru_bx", [DEPTH, 256]), ("lru_a_param", [DEPTH, 256]),
                        ("s5_lam_re", [DEPTH, 16, 64]), ("s5_lam_im", [DEPTH, 16, 64]),
                        ("s5_b_re", [DEPTH, 16, 64, 16]), ("s5_b_im", [DEPTH, 16, 64, 16]),
                        ("s5_c_re", [DEPTH, 16, 16, 64]), ("s5_c_im", [DEPTH, 16, 16, 64]),
                        ("s5_d", [DEPTH, 256]), ("s5_log_dt", [DEPTH, 16]),
                        ("s5_glu_w", [DEPTH, 256, 256]), ("s5_glu_b", [DEPTH, 256])):
            self.pin[nm] = di(nm, shp)
        self.role_in = di("role", [128, 2])
        self.out = nc.dram_tensor("out", [self.seq_x, D], F32, kind="ExternalOutput").ap()
        ntile = len(self.tiles)
        self.send = [nc.dram_tensor(f"send{i}", [128, KT * NT], F32, kind="Internal").ap() for i in range(2)]
        self.recv = [nc.dram_tensor(f"recv{i}", [128, KT * NT], F32, kind="Internal").ap() for i in range(4)]
        self.groups = [[i, i + self.npairs] for i in range(self.npairs)]
        self.wbf = nc.dram_tensor("wbf", [NCHUNK, 128, WSLOT], BF16, kind="Internal").ap()
        self.tapout = {}
        for name, shape in self.taps:
            self.tapout[name] = nc.dram_tensor("tap_" + name, list(shape), F32, kind="ExternalOutput").ap()

        sb = lambda name, shape: es.enter_context(nc.sbuf_tensor(name, list(shape), F32))
        self.NSLAB = 46
        self.SL = sb("SL", [128, self.NSLAB, NT])
        self.W = [sb(f"W{i}", [128, WSLOT]) for i in range(NWSLOT)]
        self.ident = sb("ident", [128, 128])
        self.ones = sb("ones", [128, 128])
        self.PRM = sb("PRM", [128, 128])
        self.tmp = [sb(f"tmp{i}", [128, NT]) for i in range(4)]
        self.cmask = sb("cmask", [128, NT])
        self.bd64 = sb("bd64", [128, 128])
        self.lruw = sb("lruw", [128, 2, 2, 128])
        self.car_sc = sb("car_sc", [128, 2, 2])
        self.car_lx = sb("car_lx", [128, 2, 3])
        self.car_lh = sb("car_lh", [128, 2])
        self.SS = sb("SS", [128, 2, 9, 128])
        self.SM = [sb(f"SM{i}", [128, 2, 128]) for i in range(2)]
        self.mask2 = sb("mask2", [128, 128])
        self.smn = 0
        self.Q = 8
        self.BsT = sb("BsT", [128, 2, 8, 2, 128])
        self.CsT = sb("CsT", [128, 8, 8, 2, 32])
        self.TA = sb("TA", [128, 8, 2, 64])
        self.TB = sb("TB", [128, 8, 2, 65])
        self.RR = sb("RR", [128, 8])
        self.GW = sb("GW", [128, 2, 256])
        self.CARJ = sb("CARJ", [128, 8, 2])
        self.cmask8 = sb("cmask8", [128, NT])
        self.CH = sb("CH", [128, 8, 66])
        self.ROLE = sb("ROLE", [128, 2])
        self.HM0 = sb("HM0", [128, KT, NMETA])
        self.SNAP = sb("SNAP", [128, 2 * 2 + 2 * 3 + 2 + 2 * 128 + 8 * 2])
        self.ps = [es.enter_context(nc.psum_tensor(f"ps{i}", [128, NT], F32)) for i in range(8)]
        self.psn = 0
        self.tmpn = 0
        self.S_H = list(range(0, 8))
        self.S_Y = list(range(8, 16))
        self.S_Z = list(range(16, 24))
        self.S_U = list(range(24, 46))
        self.S5S = self.SL[:, 24:29, :].rearrange("p a b -> p (a b)")[:, 0:2176]
        self.S5S_keys = [("sl", i) for i in range(24, 29)]
        self.CB = es.enter_context(nc.sbuf_tensor("CB", [128, WSLOT], BF16))
        self.WBv = [self.W[i // 2][:, :].bitcast(BF16)[:, (i % 2) * WSLOT:(i % 2 + 1) * WSLOT] for i in range(NBSLOT)]
        self.bf = False

        t.op("pool", lambda e: e.memset(self.ident[:], 0.0), writes=["ident"])
        t.op("pool", lambda e: e.affine_select(
            out=self.ident[:], in_=self.ident[:], compare_op=ALU.not_equal, fill=1.0,
            base=0, pattern=[[-1, 128]], channel_multiplier=1), writes=["ident"])
        t.op("pool", lambda e: e.memset(self.ones[:], 1.0), writes=["ones"])
        t.op("pool", lambda e: e.memset(self.cmask[:], 1.0), writes=["cmask"])
        t.op("pool", lambda e: e.affine_select(
            out=self.cmask[:].rearrange("p (c s) -> p c s", s=64), in_=self.cmask[:].rearrange("p (c s) -> p c s", s=64),
            compare_op=ALU.not_equal, fill=0.0, base=0, pattern=[[0, NT // 64], [1, 64]], channel_multiplier=0),
            writes=["cmask"])
        t.op("pool", lambda e: e.memset(self.cmask8[:], 1.0), writes=["cmask8"])
        t.op("pool", lambda e: e.affine_select(
            out=self.cmask8[:].rearrange("p (c s) -> p c s", s=8), in_=self.cmask8[:].rearrange("p (c s) -> p c s", s=8),
            compare_op=ALU.not_equal, fill=0.0, base=0, pattern=[[0, NT // 8], [1, 8]], channel_multiplier=0),
            writes=["cmask8"])
        t.op("pool", lambda e: e.memset(self.mask2[:], 1.0), writes=["mask2"])
        t.op("pool", lambda e: e.affine_select(
            out=self.mask2[:, :], in_=self.mask2[:, :],
            compare_op=ALU.is_ge, fill=0.0, base=0, pattern=[[1, 128]], channel_multiplier=-1),
            writes=["mask2"])
        t.op("pool", lambda e: e.memset(self.mask2[0:64, 64:128], 0.0), writes=["mask2"])
        t.op("pool", lambda e: e.memset(self.bd64[:], 0.0), writes=["bd64"])
        for hh in range(2):
            t.op("pool", lambda e, hh=hh: e.memset(self.bd64[hh * 64:(hh + 1) * 64, hh * 64:(hh + 1) * 64], 1.0 / 64),
                 writes=["bd64"])

        t.op("pool", lambda e: e.dma_start(out=self.ROLE[:, :], in_=self.role_in), writes=["role"],
             stream="dma_io", inc=16)
        self.fA = self.ROLE[:, 0:1]
        self.fB = self.ROLE[:, 1:2]
        nsteps = self.n_xtiles + SKEW
        self.nsteps = nsteps
        self.wq = []
        self.wq_issued = 0
        self.wq_used = 0
        self.wq.extend(self.chunk_seq(0))
        self.wq.extend(self.chunk_seq(1))
        for _ in range(nsteps):
            self.wq.extend(self.bf_chunk_seq())
        self._index_queue()

        try:
            self._program(nsteps)
        except _Stop:
            pass
        t.drain("pool")
        es.close()
        return nc

    def dbg(self, lvl):
        if int(os.environ.get("DBG_STOP", "99")) <= lvl:
            raise _Stop()

    def _program(self, nsteps):
        self.layer_setup(0, mine=False)
        self.tile_pass(0, "p0", NMETA)
        self.layer_setup(1, mine=True)
        self.tile_pass(1, "p1", NMETA)
        self.snapshot()
        self.dbg(1)
        self.bf = True
        for step in range(nsteps):
            self.tile_pass(1, "main", NT, step)
            if step == SKEW - 1:
                self.restore()

    def sl(self, idx, n=NT):
        return self.SL[:, idx, 0:n]

    def slk(self, idx):
        return ("sl", idx)

    def nextps(self):
        i = self.psn
        self.psn = (self.psn + 1) % 6
        return i

    def nexttmp(self):
        i = self.tmpn
        self.tmpn = (self.tmpn + 1) % 4
        return i

    def tap(self, name, src_ap, reads):
        if name in self.tapout:
            self.t.op("pool", lambda e: e.dma_start(out=self.tapout[name], in_=src_ap),
                      reads=reads, stream="dma_io", inc=16)

    def chunk_seq(self, l):
        seq = []
        wi = self.w_in[l].rearrange("(kt p) n -> p kt n", p=128)
        for c in range(5):
            seq.append(("win", [(wi[:, :, c * 512:(c + 1) * 512], 0, KT, 512)]))
        wo = self.w_out[l].rearrange("(kt p) n -> p kt n", p=128)
        for c in range(2):
            seq.append(("wout", [(wo[:, :, c * 512:(c + 1) * 512], 0, KT, 512)]))
        w1 = self.w_f1[l].rearrange("(kt p) n -> p kt n", p=128)
        for c in range(11):
            seq.append(("f1", [(w1[:, :, c * 256:(c + 1) * 256], 0, KT, 256),
                               (w1[:, :, DFF + c * 256:DFF + (c + 1) * 256], KT * 256, KT, 256)]))
        w2 = self.w_f2[l].rearrange("(kt p) n -> p kt n", p=128)
        for c in range(8):
            seq.append(("f2", [(w2[:, :, c * 128:(c + 1) * 128], 0, FT, 128)]))
        return seq

    def bf_chunk_seq(self):
        seq = []
        kinds = ["win"] * 5 + ["wout"] * 2 + ["f1"] * 11 + ["f2"] * 8
        for c, kd in enumerate(kinds):
            nel = FT * 128 if kd == "f2" else WSLOT
            seq.append((kd, [(self.wbf[c][:, 0:nel], 0, nel, 1)], c))
        return seq

    def _slot_of(self, idx):
        ent = self.wq[idx]
        if len(ent) == 3:
            return ("b", self.bcount[idx] % NBSLOT)
        return ("f", self.fcount[idx] % NWSLOT)

    def _slot_ap(self, slotid):
        return self.WBv[slotid[1]] if slotid[0] == "b" else self.W[slotid[1]]

    def _phys_keys(self, slotid):
        if slotid[0] == "b":
            return [("w", slotid[1] // 2, slotid[1] % 2)]
        return [("w", slotid[1], 0), ("w", slotid[1], 1)]

    def _issue_chunk(self, idx):
        ent = self.wq[idx]
        slotid = self._slot_of(idx)
        base = self._slot_ap(slotid)
        if len(ent) == 3:
            kind, parts, c = ent
            src, off, nel, _ = parts[0]
            self.t.op("sp", lambda e: e.dma_start(out=base[:, 0:nel], in_=src),
                      reads=[("wbf", c)], writes=self._phys_keys(slotid), stream=f"dma_w{slotid[1] // 2}", inc=16)
            return
        kind, parts = ent
        for pidx, (src, off, nk, ncol) in enumerate(parts):
            dst = base[:, off:off + nk * ncol].rearrange("p (k n) -> p k n", k=nk)
            wk = [("w", slotid[1], pidx)] if len(parts) == 2 else [("w", slotid[1], 0), ("w", slotid[1], 1)]
            self.t.op("sp", lambda e, dst=dst, src=src: e.dma_start(out=dst, in_=src),
                      writes=wk, stream=f"dma_w{slotid[1]}", inc=16)

    def _index_queue(self):
        self.bcount, self.fcount = {}, {}
        nb = nf = 0
        for i, ent in enumerate(self.wq):
            if len(ent) == 3:
                self.bcount[i] = nb
                nb += 1
            else:
                self.fcount[i] = nf
                nf += 1

    def wacquire(self, kind):
        idx = self.wq_used
        assert self.wq[idx][0] == kind, (self.wq[idx][0], kind)
        while self.wq_issued <= idx:
            self._issue_chunk(self.wq_issued)
            self.wq_issued += 1
        self.wq_used += 1
        self.cur_chunk = idx
        return self._slot_of(idx)

    def wprefetch(self):
        depth = (NBSLOT - 1) if len(self.wq[self.wq_used - 1]) == 3 else (NWSLOT - 1)
        lim = min(self.wq_used + depth, len(self.wq))
        while self.wq_issued < lim:
            nxt = self.wq[self.wq_issued]
            if (len(nxt) == 3) != (len(self.wq[self.wq_used - 1]) == 3):
                break
            self._issue_chunk(self.wq_issued)
            self.wq_issued += 1

    def wview(self, slotid, off, nk, ncol):
        return self._slot_ap(slotid)[:, off:off + nk * ncol].rearrange("p (k n) -> p k n", k=nk)

    def wkeys(self, slotid):
        return self._phys_keys(slotid)

    def cast_store(self, slotid, kind):
        c = self.cur_chunk - NCHUNK
        nel = FT * 128 if kind == "f2" else WSLOT
        self.t.op("act", lambda e: e.activation(out=self.CB[:, 0:nel], in_=self.W[slotid[1]][:, 0:nel], func=AF.Copy),
                  reads=self._phys_keys(slotid), writes=["CB"])
        self.t.op("pool", lambda e: e.dma_start(out=self.wbf[c][:, 0:nel], in_=self.CB[:, 0:nel]),
                  reads=["CB"], writes=[("wbf", c)], stream="dma_io", inc=16)

    def layer_setup(self, l, mine):
        t = self.t
        for i, n in enumerate(("ln1_g", "ln1_b", "ln2_g", "ln2_b")):
            src = self.ln[n][l].rearrange("(m p) -> p m", p=128)
            t.op("pool", lambda e, src=src, i=i: e.dma_start(
                out=self.PRM[:, i * 8:(i + 1) * 8], in_=src, allow_slow_non_contiguous=True),
                writes=[("prm", i)], stream="dma_io", inc=16)
        P = self.PRM
        pin = self.pin

        def vec(name, col, key):
            src = pin[name][l].rearrange("(c p) -> p c", p=128)
            t.op("pool", lambda e: e.dma_start(out=P[:, col:col + 2], in_=src, allow_slow_non_contiguous=True),
                 writes=[key], stream="dma_io", inc=16)

        for c in range(2):
            t.op("pool", lambda e, c=c: e.dma_start(
                out=P[:, 32 + 3 * c:35 + 3 * c],
                in_=pin["sc_conv_w"][l][:, c * 128:(c + 1) * 128].rearrange("k p -> p k"),
                allow_slow_non_contiguous=True), writes=["p_scw"], stream="dma_io", inc=16)
            t.op("pool", lambda e, c=c: e.dma_start(
                out=P[:, 38 + 4 * c:42 + 4 * c],
                in_=pin["lru_conv_w"][l][:, c * 128:(c + 1) * 128].rearrange("k p -> p k"),
                allow_slow_non_contiguous=True), writes=["p_lcw"], stream="dma_io", inc=16)
        vec("lru_conv_b", 46, "p_lcb")
        vec("lru_ba", 48, "p_lba")
        vec("lru_bx", 50, "p_lbx")
        vec("lru_a_param", 52, "p_cp")
        vec("hg_gnorm", 60, "p_gn")
        vec("s5_d", 62, "p_s5d")
        vec("s5_glu_b", 64, "p_glub")
        t.op("act", lambda e: e.activation(out=P[:, 52:54], in_=P[:, 52:54], func=AF.Exp, scale=-1.0),
             reads=["p_cp"], writes=["p_cp"])
        t.op("act", lambda e: e.activation(out=P[:, 52:54], in_=P[:, 52:54], func=AF.Ln, bias=1.0, scale=1.0),
             reads=["p_cp"], writes=["p_cp"])
        t.op("dve", lambda e: e.tensor_single_scalar(P[:, 54:56], P[:, 52:54], -16.0, ALU.mult),
             reads=["p_cp"], writes=["p_cp2"])
        t.op("dve", lambda e: e.tensor_single_scalar(P[:, 52:54], P[:, 52:54], -8.0, ALU.mult),
             reads=["p_cp", "p_cp2"], writes=["p_cp"])
        if not mine:
            t.op("dve", lambda e: e.memset(P[:, 56:58], 0.0), writes=["p_lb"])
        else:
            for i in range(2):
                src = pin["hg_lb_raw"][i].rearrange("(c p) -> p c", p=128)
                t.op("pool", lambda e, i=i, src=src: e.dma_start(out=P[:, 66 + 2 * i:68 + 2 * i], in_=src,
                                                                 allow_slow_non_contiguous=True),
                     writes=[("p_lbraw", i)], stream="dma_io", inc=16)
            t.op("dve", lambda e: e.tensor_tensor(P[:, 56:58], P[:, 68:70], P[:, 66:68], ALU.subtract),
                 reads=[("p_lbraw", 0), ("p_lbraw", 1)], writes=["p_lb"])
            t.op("act", lambda e: e.activation(out=P[:, 56:58], in_=P[:, 56:58], func=AF.Sigmoid),
                 reads=["p_lb"], writes=["p_lb"])
            t.op("dve", lambda e: e.tensor_single_scalar(P[:, 56:58], P[:, 56:58], self.fB, ALU.mult),
                 reads=["p_lb", "role"], writes=["p_lb"])
        t.op("dve", lambda e: e.tensor_scalar(P[:, 58:60], P[:, 56:58], -1.0, 1.0, ALU.mult, ALU.add),
             reads=["p_lb"], writes=["p_oml"])
        t.op("dve", lambda e: e.memset(self.lruw[:], 0.0), writes=["lruw"])
        for gi, nm in enumerate(("lru_wa", "lru_wx")):
            for h in range(4):
                c, hh = h // 2, h % 2
                t.op("pool", lambda e, gi=gi, nm=nm, h=h, c=c, hh=hh: e.dma_start(
                    out=self.lruw[hh * 64:(hh + 1) * 64, gi, c, hh * 64:(hh + 1) * 64], in_=pin[nm][l, h]),
                    writes=["lruw"], stream="dma_io", inc=16)
        t.op("dve", lambda e: e.memset(self.car_sc[:], 0.0), writes=[("car_sc", 0), ("car_sc", 1)])
        t.op("dve", lambda e: e.memset(self.car_lx[:], 0.0), writes=[("car_lx", 0), ("car_lx", 1)])
        t.op("dve", lambda e: e.memset(self.car_lh[:], 0.0), writes=[("car_lh", 0), ("car_lh", 1)])
        t.op("dve", lambda e: e.memset(self.SS[:], 0.0), writes=[("SS", 0), ("SS", 1)])
        if "s5" not in self.stub:
            self.s5_setup(l)

    def evac(self, eng, out_ap, in_ap, reads, writes):
        if eng == "act":
            return self.t.op("act", lambda e: e.activation(out=out_ap, in_=in_ap, func=AF.Copy),
                             reads=reads, writes=writes)
        return self.t.op("dve", lambda e: e.tensor_copy(out_ap, in_ap), reads=reads, writes=writes)

    def mm(self, ps_ap, pskey, pairs, reads):
        def emit(pe):
            n = len(pairs)
            ins = None
            for i, (lh, rh) in enumerate(pairs):
                ins = pe.matmul(ps_ap, lh, rh, start=(i == 0), stop=(i == n - 1))
            return ins
        return self.t.op("pe", emit, reads=reads, writes=[pskey])

    def states(self):
        o = [0]

        def sn(ncol, shape=None):
            ap = self.SNAP[:, o[0]:o[0] + ncol]
            o[0] += ncol
            return ap
        return [
            (self.car_sc[:].rearrange("p a b -> p (a b)"), sn(4), [("car_sc", 0), ("car_sc", 1)]),
            (self.car_lx[:].rearrange("p a b -> p (a b)"), sn(6), [("car_lx", 0), ("car_lx", 1)]),
            (self.car_lh[:, :], sn(2), [("car_lh", 0), ("car_lh", 1)]),
            (self.SS[:, :, 0, :], sn(256).rearrange("p (a b) -> p a b", a=2), [("SS", 0), ("SS", 1)]),
            (self.CARJ[:].rearrange("p a b -> p (a b)"), sn(16), [("CARJ", i) for i in range(8)]),
        ]

    def snapshot(self):
        for st, snp, keys in self.states():
            self.t.op("dve", lambda e, st=st, snp=snp: e.tensor_copy(snp, st), reads=keys, writes=["snap"])

    def restore(self):
        for st, snp, keys in self.states():
            self.t.op("dve", lambda e, st=st: e.tensor_single_scalar(st, st, self.fA, ALU.mult),
                      reads=keys + ["role"], writes=keys)
            self.t.op("dve", lambda e, st=st, snp=snp: e.scalar_tensor_tensor(st, snp, self.fB, st, ALU.mult, ALU.add),
                      reads=keys + ["role", "snap"], writes=keys)

    def load_input(self, mode, n, step):
        t = self.t
        H, U = self.S_H, self.S_U
        nb = (n + 127) // 128
        pb = min(n, 128)
        stage = self.SL[:, self.S_Y[0]:self.S_Y[0] + 8, :].rearrange("p a b -> p (a b)")
        stage = stage[:, 0:nb * D].rearrange("p (b f) -> p b f", b=nb)
        if mode in ("p0", "p1"):
            src = self.meta.rearrange("(b p) f -> p b f", p=pb)
        else:
            xi = min(step, self.n_xtiles - 1)
            src = self.x[xi * NT:(xi + 1) * NT, :].rearrange("(b p) f -> p b f", p=pb)
        t.op("pool", lambda e: e.dma_start(out=stage[0:pb], in_=src),
             writes=[self.slk(i) for i in self.S_Y], stream="dma_io", inc=16)
        if mode == "main" and step >= SKEW:
            par = (step - SKEW) % 4
            rsrc = self.recv[par][:, :].rearrange("p (k n) -> p k n", k=KT)
            t.op("pool", lambda e: e.dma_start(out=self.SL[:, U[0]:U[0] + 8, :], in_=rsrc),
                 reads=[("recv", par)], writes=[self.slk(U[i]) for i in range(8)], stream="dma_io", inc=16)
        for k in range(KT):
            pi = self.nextps()

            def emit(pe, k=k, pi=pi):
                ins = None
                for b in range(nb):
                    ins = pe.transpose(self.ps[pi][:, b * pb:(b + 1) * pb],
                                       stage[0:pb, b, k * 128:(k + 1) * 128],
                                       self.ident[0:pb, 0:pb])
                return ins
            t.op("pe", emit, reads=[self.slk(i) for i in self.S_Y] + ["ident"], writes=[("ps", pi)])
            hk = self.slk(H[k])
            if mode == "p0":
                self.evac("act" if k % 2 else "dve", self.sl(H[k], n), self.ps[pi][:, 0:n],
                          reads=[("ps", pi)], writes=[hk])
                continue
            if k % 2:
                t.op("act", lambda e, k=k, pi=pi: e.activation(out=self.sl(H[k], n), in_=self.ps[pi][:, 0:n],
                                                               func=AF.Copy, scale=self.fA),
                     reads=[("ps", pi), "role"], writes=[hk])
            else:
                t.op("dve", lambda e, k=k, pi=pi: e.tensor_single_scalar(self.sl(H[k], n), self.ps[pi][:, 0:n],
                                                                         self.fA, ALU.mult),
                     reads=[("ps", pi), "role"], writes=[hk])
            if mode == "p1":
                t.op("dve", lambda e, k=k: e.scalar_tensor_tensor(self.sl(H[k], n), self.HM0[:, k, :], self.fB,
                                                                  self.sl(H[k], n), ALU.mult, ALU.add),
                     reads=[hk, "role", "HM0"], writes=[hk])
            elif step >= SKEW:
                t.op("dve", lambda e, k=k: e.scalar_tensor_tensor(self.sl(H[k], n), self.sl(U[k], n), self.fB,
                                                                  self.sl(H[k], n), ALU.mult, ALU.add),
                     reads=[hk, "role", self.slk(U[k])], writes=[hk])

    def store_output(self, mode, n, step):
        t = self.t
        Z = self.S_Z
        if mode == "p0":
            t.op("act", lambda e: e.activation(out=self.HM0[:, :, :], in_=self.SL[:, Z[0]:Z[0] + 8, 0:NMETA],
                                               func=AF.Copy),
                 reads=[self.slk(i) for i in Z], writes=["HM0"])
            return
        if mode == "p1":
            return
        if True:
            par = step % 2
            rpar = step % 4
            U = self.S_U
            for k in range(KT):
                if k % 2:
                    t.op("act", lambda e, k=k: e.activation(out=self.sl(U[8 + k]), in_=self.sl(Z[k]), func=AF.Copy,
                                                            scale=self.fA),
                         reads=[self.slk(Z[k]), "role"], writes=[self.slk(U[8 + k])])
                else:
                    t.op("dve", lambda e, k=k: e.tensor_single_scalar(self.sl(U[8 + k]), self.sl(Z[k]), self.fA,
                                                                      ALU.mult),
                         reads=[self.slk(Z[k]), "role"], writes=[self.slk(U[8 + k])])
            t.op("pool", lambda e: e.dma_start(out=self.send[par].rearrange("p (k n) -> p k n", k=KT),
                                               in_=self.SL[:, U[8]:U[8] + 8, :]),
                 reads=[self.slk(U[8 + i]) for i in range(8)], writes=[("send", par)], stream="dma_io", inc=16)
            t.op("pool", lambda e: e.collective_compute("AllReduce", ALU.add, replica_groups=self.groups,
                                                        ins=[self.send[par]], outs=[self.recv[rpar]]),
                 reads=[("send", par)], writes=[("recv", rpar)], stream="cc", inc=1)
        if step < SKEW:
            return
        nb = n // 128
        stage = self.SL[:, self.S_Y[0]:self.S_Y[0] + 8, :].rearrange("p a b -> p (a b)")
        stage = stage[:, 0:nb * D].rearrange("p (b f) -> p b f", b=nb)
        for b in range(nb):
            for half in range(2):
                pi = self.nextps()

                def emit(pe, b=b, half=half, pi=pi):
                    ins = None
                    for kk in range(4):
                        k = half * 4 + kk
                        ins = pe.transpose(self.ps[pi][:, kk * 128:(kk + 1) * 128],
                                           self.SL[:, Z[k], b * 128:(b + 1) * 128],
                                           self.ident[:, :])
                    return ins
                t.op("pe", emit, reads=[self.slk(Z[half * 4 + kk]) for kk in range(4)] + ["ident"],
                     writes=[("ps", pi)])
                self.evac("act" if half else "dve", stage[:, b, half * 512:(half + 1) * 512],
                          self.ps[pi][:, :], reads=[("ps", pi)], writes=[self.slk(self.S_Y[2 * b + half])])
        r0 = (step - SKEW) * NT
        dst = self.out[r0:r0 + n, :].rearrange("(b p) f -> p b f", p=128)
        t.op("pool", lambda e: e.dma_start(out=dst, in_=stage),
             reads=[self.slk(i) for i in self.S_Y], stream="dma_io", inc=16)

    def layernorm(self, n, gcol, bcol):
        t = self.t
        Z = self.S_Z
        ps_s = self.nextps()
        ps_q = self.nextps()
        self.mm(self.ps[ps_s][:, 0:n], ("ps", ps_s),
                [(self.ones[:, :], self.sl(Z[m], n)) for m in range(KT)],
                reads=[self.slk(Z[m]) for m in range(KT)] + ["ones"])
        sq = []
        for m in range(KT):
            ti = self.nexttmp()
            t.op("act", lambda e, m=m, ti=ti: e.activation(out=self.tmp[ti][:, 0:n], in_=self.sl(Z[m], n),
                                                          func=AF.Square),
                 reads=[self.slk(Z[m])], writes=[("tmp", ti)])
            first = (m == 0)
            last = (m == KT - 1)
            t.op("pe", lambda e, ti=ti, first=first, last=last: e.matmul(
                self.ps[ps_q][:, 0:n], self.ones[:, :], self.tmp[ti][:, 0:n], start=first, stop=last),
                reads=[("tmp", ti), "ones"], writes=[("ps", ps_q)])
        mi = self.nexttmp()
        mean = self.tmp[mi]
        mk = ("tmp", mi)
        ri = self.nexttmp()
        rstd = self.tmp[ri]
        rk = ("tmp", ri)
        t.op("dve", lambda e: e.tensor_single_scalar(mean[:, 0:n], self.ps[ps_s][:, 0:n], 1.0 / D, ALU.mult),
             reads=[("ps", ps_s)], writes=[mk])
        t.op("dve", lambda e: e.tensor_tensor(rstd[:, 0:n], mean[:, 0:n], mean[:, 0:n], ALU.mult),
             reads=[mk], writes=[rk])
        t.op("dve", lambda e: e.scalar_tensor_tensor(rstd[:, 0:n], self.ps[ps_q][:, 0:n], 1.0 / D,
                                                     rstd[:, 0:n], ALU.mult, ALU.subtract),
             reads=[("ps", ps_q), rk], writes=[rk])
        t.op("act", lambda e: e.activation(out=rstd[:, 0:n], in_=rstd[:, 0:n], func=AF.Sqrt, bias=EPS, scale=1.0),
             reads=[rk], writes=[rk])
        t.op("dve", lambda e: e.reciprocal(rstd[:, 0:n], rstd[:, 0:n]), reads=[rk], writes=[rk])
        for m in range(KT):
            zk = self.slk(Z[m])
            t.op("dve", lambda e, m=m: e.tensor_tensor(self.sl(Z[m], n), self.sl(Z[m], n), mean[:, 0:n], ALU.subtract),
                 reads=[zk, mk], writes=[zk])
            t.op("dve", lambda e, m=m: e.tensor_tensor(self.sl(Z[m], n), self.sl(Z[m], n), rstd[:, 0:n], ALU.mult),
                 reads=[zk, rk], writes=[zk])
            t.op("act", lambda e, m=m: e.activation(out=self.sl(Z[m], n), in_=self.sl(Z[m], n), func=AF.Identity,
                                                    scale=self.PRM[:, gcol + m:gcol + m + 1],
                                                    bias=self.PRM[:, bcol + m:bcol + m + 1]),
                 reads=[zk, ("prm", gcol // 8), ("prm", bcol // 8)], writes=[zk])

    def bfslab(self, slab_idx, half, n):
        return self.SL[:, slab_idx, :].bitcast(BF16)[:, half * NT:half * NT + n]

    def hsrc(self, k, n0, n1):
        if self.bf:
            return self.tmp[k // 2][:, :].bitcast(BF16)[:, (k % 2) * NT + n0:(k % 2) * NT + n1], ("tmp", k // 2)
        return self.SL[:, self.S_H[k], n0:n1], self.slk(self.S_H[k])

    def ysl(self, k, n):
        if self.bf:
            return self.bfslab(self.S_Y[k // 2], k % 2, n)
        return self.sl(self.S_Y[k], n)

    def yk(self, k):
        return self.slk(self.S_Y[k // 2]) if self.bf else self.slk(self.S_Y[k])

    def zsrc(self, k, n):
        if self.bf:
            return self.bfslab(self.S_U[12 + k // 2], k % 2, n), self.slk(self.S_U[12 + k // 2])
        return self.sl(self.S_Z[k], n), self.slk(self.S_Z[k])

    def actsl(self, i, n):
        if self.bf:
            return self.bfslab(self.S_U[i // 2], i % 2, n), self.slk(self.S_U[i // 2])
        return self.sl(self.S_U[i], n), self.slk(self.S_U[i])

    def tile_pass(self, l, mode, n, step=0):
        t = self.t
        H, Y, Z, U = self.S_H, self.S_Y, self.S_Z, self.S_U
        self.load_input(mode, n, step)
        if self.bf:
            for k in range(KT):
                dst, dk = self.hsrc(k, 0, n)
                t.op("pool", lambda e, k=k, dst=dst: e.tensor_copy(dst, self.sl(H[k], n)),
                     reads=[self.slk(H[k])], writes=[dk])
        hsr = [self.hsrc(k, 0, n) for k in range(KT)]
        hreads = list({kk for _, kk in hsr})
        for c in range(5):
            slot = self.wacquire("win")
            if mode == "p1":
                self.cast_store(slot, "win")
            wv = self.wview(slot, 0, KT, 512)
            for jj in range(4):
                j = c * 4 + jj
                if j == 12 and "hg" not in self.stub:
                    nb = (n + 127) // 128
                    VT = self.SL[:, U[12]:U[12] + 2, :].rearrange("p a b -> p (a b)").rearrange(
                        "p (b f) -> p b f", f=256)
                    for blk in range(nb):
                        pb = min(128, n - blk * 128)
                        pi = self.nextps()
                        self.mm(self.ps[pi][0:pb, 0:256], ("ps", pi),
                                [(self.hsrc(k, blk * 128, blk * 128 + pb)[0], wv[:, k, 0:256]) for k in range(KT)],
                                reads=hreads + self.wkeys(slot))
                        self.evac("act" if blk % 2 else "dve", VT[0:pb, blk, :], self.ps[pi][0:pb, 0:256],
                                  reads=[("ps", pi)], writes=[self.slk(U[12]), self.slk(U[13])])
                    continue
                if j == 13 and "hg" not in self.stub:
                    continue
                pi = self.nextps()
                self.mm(self.ps[pi][:, 0:n], ("ps", pi),
                        [(wv[:, k, jj * 128:(jj + 1) * 128], hsr[k][0]) for k in range(KT)],
                        reads=hreads + self.wkeys(slot))
                self.evac("act" if j % 2 else "dve", self.sl(U[j], n), self.ps[pi][:, 0:n],
                          reads=[("ps", pi)], writes=[self.slk(U[j])])
            self.wprefetch()
        if mode == "main":
            self.dbg(2)
        self.mixers(l, 0, 0, n)
        if mode == "main":
            self.dbg(3)
        yreads = list({self.yk(k) for k in range(KT)})
        for c in range(2):
            slot = self.wacquire("wout")
            if mode == "p1":
                self.cast_store(slot, "wout")
            wv = self.wview(slot, 0, KT, 512)
            for jj in range(4):
                m = c * 4 + jj
                pi = self.nextps()
                self.mm(self.ps[pi][:, 0:n], ("ps", pi),
                        [(wv[:, k, jj * 128:(jj + 1) * 128], self.ysl(k, n)) for k in range(KT)],
                        reads=yreads + self.wkeys(slot))
                t.op("dve", lambda e, m=m, pi=pi: e.scalar_tensor_tensor(
                    self.sl(Z[m], n), self.sl(H[m], n), ALPHA, self.ps[pi][:, 0:n], ALU.mult, ALU.add),
                    reads=[("ps", pi), self.slk(H[m])], writes=[self.slk(Z[m])])
            self.wprefetch()
        self.layernorm(n, 0, 8)
        if self.bf:
            for k in range(KT):
                dst, dk = self.zsrc(k, n)
                t.op("pool", lambda e, k=k, dst=dst: e.tensor_copy(dst, self.sl(Z[k], n)),
                     reads=[self.slk(Z[k])], writes=[dk])
        if mode == "main":
            self.dbg(4)
        zsr = [self.zsrc(k, n) for k in range(KT)]
        zreads = list({kk for _, kk in zsr})
        for c in range(11):
            slot = self.wacquire("f1")
            if mode == "p1":
                self.cast_store(slot, "f1")
            wg = self.wview(slot, 0, KT, 256)
            wu = self.wview(slot, KT * 256, KT, 256)
            for jj in range(2):
                i = c * 2 + jj
                pg = self.nextps()
                pu = self.nextps()
                self.mm(self.ps[pg][:, 0:n], ("ps", pg),
                        [(wg[:, k, jj * 128:(jj + 1) * 128], zsr[k][0]) for k in range(KT)],
                        reads=zreads + self.wkeys(slot))
                self.mm(self.ps[pu][:, 0:n], ("ps", pu),
                        [(wu[:, k, jj * 128:(jj + 1) * 128], zsr[k][0]) for k in range(KT)],
                        reads=zreads + self.wkeys(slot))
                tk = self.nexttmp()
                t.op("act", lambda e, tk=tk, pg=pg: e.activation(out=self.tmp[tk][:, 0:n], in_=self.ps[pg][:, 0:n],
                                                                func=AF.Silu),
                     reads=[("ps", pg)], writes=[("tmp", tk)])
                adst, ak = self.actsl(i, n)
                t.op("dve", lambda e, tk=tk, pu=pu, adst=adst: e.tensor_tensor(
                    adst, self.tmp[tk][:, 0:n], self.ps[pu][:, 0:n], ALU.mult),
                    reads=[("tmp", tk), ("ps", pu)], writes=[ak])
            self.wprefetch()
        asr = [self.actsl(k, n) for k in range(FT)]
        ureads = list({kk for _, kk in asr})
        for m in range(KT):
            slot = self.wacquire("f2")
            if mode == "p1":
                self.cast_store(slot, "f2")
            wv = self.wview(slot, 0, FT, 128)
            pi = self.nextps()
            self.mm(self.ps[pi][:, 0:n], ("ps", pi),
                    [(wv[:, k, :], asr[k][0]) for k in range(FT)],
                    reads=ureads + self.wkeys(slot))
            t.op("dve", lambda e, m=m, pi=pi: e.scalar_tensor_tensor(
                self.sl(Z[m], n), self.sl(Z[m], n), ALPHA, self.ps[pi][:, 0:n], ALU.mult, ALU.add),
                reads=[("ps", pi), self.slk(Z[m])], writes=[self.slk(Z[m])])
            self.wprefetch()
        if mode == "main":
            self.dbg(5)
        self.layernorm(n, 16, 24)
        if mode == "main":
            self.dbg(6)
        self.store_output(mode, n, step)

    def prm(self, col):
        return self.PRM[:, col:col + 1]

    def conv_acc(self, acc, x, carry, wcol, K, n, xk, acck, cark, wkey, first_bias=None):
        t = self.t
        if first_bias is None:
            t.op("dve", lambda e: e.tensor_single_scalar(acc[:, 0:n], x[:, 0:n], self.prm(wcol + K - 1), ALU.mult),
                 reads=[xk, wkey], writes=[acck])
        else:
            t.op("dve", lambda e: e.tensor_scalar(acc[:, 0:n], x[:, 0:n], self.prm(wcol + K - 1), self.prm(first_bias),
                                                  ALU.mult, ALU.add),
                 reads=[xk, wkey, "p_lcb"], writes=[acck])
        for k in range(K - 1):
            sh = K - 1 - k
            t.op("dve", lambda e, sh=sh, k=k: e.scalar_tensor_tensor(
                acc[:, sh:n], x[:, 0:n - sh], self.prm(wcol + k), acc[:, sh:n], ALU.mult, ALU.add),
                reads=[xk, wkey, acck], writes=[acck])
            t.op("dve", lambda e, sh=sh, k=k: e.scalar_tensor_tensor(
                acc[:, 0:sh], carry[:, K - 1 - sh:K - 1], self.prm(wcol + k), acc[:, 0:sh], ALU.mult, ALU.add),
                reads=[cark, wkey, acck], writes=[acck])
        t.op("dve", lambda e: e.tensor_copy(carry[:, 0:K - 1], x[:, n - (K - 1):n]),
             reads=[xk], writes=[cark])

    def mixers(self, l, ti, t0, n):
        t = self.t
        Y, U, Z = self.S_Y, self.S_U, self.S_Z
        if "s5" in self.stub:
            for k in range(2):
                self.evac("act" if k % 2 else "dve", self.ysl(k, n), self.sl(U[k], n),
                          reads=[self.slk(U[k])], writes=[self.yk(k)])
        else:
            self.mix_s5(l, ti, n)
        if "sc" in self.stub:
            for k in range(2):
                self.evac("act", self.ysl(2 + k, n), self.sl(U[2 + k], n),
                          reads=[self.slk(U[2 + k])], writes=[self.yk(2 + k)])
        else:
            self.mix_sc(n)
        if "hg" in self.stub:
            for k in range(2):
                self.evac("dve", self.ysl(4 + k, n), self.sl(U[4 + k], n),
                          reads=[self.slk(U[4 + k])], writes=[self.yk(4 + k)])
        else:
            self.mix_hg(n)
        if "lru" in self.stub:
            for k in range(2):
                self.evac("act", self.ysl(6 + k, n), self.sl(U[6 + k], n),
                          reads=[self.slk(U[6 + k])], writes=[self.yk(6 + k)])
        else:
            self.mix_lru(n)

    def mix_sc(self, n):
        t = self.t
        Y, U, Z = self.S_Y, self.S_U, self.S_Z
        for c in range(2):
            hs, bs, cs = U[2 + c], U[4 + c], U[6 + c]
            acc = Z[c]
            t.op("dve", lambda e: e.tensor_tensor(self.sl(hs, n), self.sl(hs, n), self.sl(cs, n), ALU.mult),
                 reads=[self.slk(hs), self.slk(cs)], writes=[self.slk(hs)])
            self.conv_acc(self.SL[:, acc, :], self.SL[:, hs, :], self.car_sc[:, c, :], 32 + 3 * c, 3, n,
                          self.slk(hs), self.slk(acc), ("car_sc", c), "p_scw")
            t.op("dve", lambda e: e.tensor_tensor(self.ysl(2 + c, n), self.sl(acc, n), self.sl(bs, n), ALU.mult),
                 reads=[self.slk(acc), self.slk(bs)], writes=[self.yk(2 + c)])

    def mix_lru(self, n):
        t = self.t
        Y, U, Z = self.S_Y, self.S_U, self.S_Z
        for c in range(2):
            xs, ys = U[16 + c], U[18 + c]
            xc, ga, gx, aa = Z[0], Z[1], Z[2], Z[3]
            k = self.slk
            self.conv_acc(self.SL[:, xc, :], self.SL[:, xs, :], self.car_lx[:, c, :], 38 + 4 * c, 4, n,
                          k(xs), k(xc), ("car_lx", c), "p_lcw", first_bias=46 + c)
            pa, px = self.nextps(), self.nextps()
            self.mm(self.ps[pa][:, 0:n], ("ps", pa), [(self.lruw[:, 0, c, :], self.sl(xc, n))], reads=[k(xc), "lruw"])
            self.mm(self.ps[px][:, 0:n], ("ps", px), [(self.lruw[:, 1, c, :], self.sl(xc, n))], reads=[k(xc), "lruw"])
            t.op("act", lambda e: e.activation(out=self.sl(ga, n), in_=self.ps[pa][:, 0:n], func=AF.Sigmoid,
                                               bias=self.prm(48 + c), scale=1.0),
                 reads=[("ps", pa), "p_lba"], writes=[k(ga)])
            t.op("act", lambda e: e.activation(out=self.sl(gx, n), in_=self.ps[px][:, 0:n], func=AF.Sigmoid,
                                               bias=self.prm(50 + c), scale=1.0),
                 reads=[("ps", px), "p_lbx"], writes=[k(gx)])
            t.op("act", lambda e: e.activation(out=self.sl(aa, n), in_=self.sl(ga, n), func=AF.Exp,
                                               scale=self.prm(52 + c)),
                 reads=[k(ga), "p_cp"], writes=[k(aa)])
            t.op("act", lambda e: e.activation(out=self.sl(ga, n), in_=self.sl(ga, n), func=AF.Exp,
                                               scale=self.prm(54 + c)),
                 reads=[k(ga), "p_cp2"], writes=[k(ga)])
            t.op("act", lambda e: e.activation(out=self.sl(ga, n), in_=self.sl(ga, n), func=AF.Sqrt,
                                               scale=-1.0, bias=1.0),
                 reads=[k(ga)], writes=[k(ga)])
            t.op("dve", lambda e: e.tensor_tensor(self.sl(gx, n), self.sl(gx, n), self.sl(xc, n), ALU.mult),
                 reads=[k(gx), k(xc)], writes=[k(gx)])
            t.op("dve", lambda e: e.tensor_tensor(self.sl(gx, n), self.sl(gx, n), self.sl(ga, n), ALU.mult),
                 reads=[k(gx), k(ga)], writes=[k(gx)])
            t.op("dve", lambda e: e.tensor_tensor_scan(self.sl(xc, n), self.sl(aa, n), self.sl(gx, n),
                                                       self.car_lh[:, c:c + 1], ALU.mult, ALU.add),
                 reads=[k(aa), k(gx), ("car_lh", c)], writes=[k(xc)])
            t.op("dve", lambda e: e.tensor_copy(self.car_lh[:, c:c + 1], self.SL[:, xc, n - 1:n]),
                 reads=[k(xc)], writes=[("car_lh", c)])
            t.op("act", lambda e: e.activation(out=self.sl(aa, n), in_=self.sl(ys, n), func=AF.Gelu_apprx_tanh),
                 reads=[k(ys)], writes=[k(aa)])
            t.op("dve", lambda e: e.tensor_tensor(self.ysl(6 + c, n), self.sl(xc, n), self.sl(aa, n), ALU.mult),
                 reads=[k(xc), k(aa)], writes=[self.yk(6 + c)])

    def mix_hg(self, n):
        t = self.t
        Y, U, Z = self.S_Y, self.S_U, self.S_Z
        k = self.slk
        CL = 64 if n >= 64 else n
        nch = n // CL
        mid = CL // 2
        nb = (n + 127) // 128
        VT = self.SL[:, U[12]:U[12] + 2, :].rearrange("p a b -> p (a b)").rearrange("p (b f) -> p b f", f=256)
        vtk = [k(U[12]), k(U[13])]
        X = [Z[0], Z[1], Z[2], Z[3], Z[4], Z[5], Z[6], Z[7]]
        c3 = lambda ap: ap.rearrange("p (c s) -> p c s", s=CL)
        for pr in range(2):
            qs, fs, gs = U[8 + pr], U[10 + pr], U[14 + pr]
            g_, b_, d_, e1, e2, e3, qp, ktk = X
            t.op("act", lambda e: e.activation(out=self.sl(qs, n), in_=self.sl(qs, n), func=AF.Silu),
                 reads=[k(qs)], writes=[k(qs)])
            t.op("act", lambda e: e.activation(out=self.sl(fs, n), in_=self.sl(fs, n), func=AF.Sigmoid),
                 reads=[k(fs)], writes=[k(fs)])
            t.op("dve", lambda e: e.tensor_scalar(self.sl(fs, n), self.sl(fs, n), self.prm(58 + pr), self.prm(56 + pr),
                                                  ALU.mult, ALU.add),
                 reads=[k(fs), "p_lb", "p_oml"], writes=[k(fs)])
            t.op("act", lambda e: e.activation(out=self.sl(g_, n), in_=self.sl(fs, n), func=AF.Ln),
                 reads=[k(fs)], writes=[k(g_)])
            t.op("dve", lambda e: e.tensor_scalar(self.sl(fs, n), self.sl(fs, n), -1.0, 1.0, ALU.mult, ALU.add),
                 reads=[k(fs), k(g_)], writes=[k(fs)])
            t.op("dve", lambda e: e.tensor_tensor_scan(self.sl(b_, n), self.cmask[:, 0:n], self.sl(g_, n), 0.0,
                                                       ALU.mult, ALU.add),
                 reads=[k(g_), "cmask"], writes=[k(b_)])
            b3 = c3(self.sl(b_, n))
            t.op("dve", lambda e: e.tensor_tensor(c3(self.sl(d_, n)), b3, b3[:, :, mid:mid + 1].to_broadcast([128, nch, CL]),
                                                  ALU.subtract),
                 reads=[k(b_)], writes=[k(d_)])
            t.op("act", lambda e: e.activation(out=self.sl(e1, n), in_=self.sl(d_, n), func=AF.Exp),
                 reads=[k(d_)], writes=[k(e1)])
            t.op("act", lambda e: e.activation(out=self.sl(e2, n), in_=self.sl(d_, n), func=AF.Exp, scale=-1.0),
                 reads=[k(d_)], writes=[k(e2)])
            t.op("act", lambda e: e.activation(out=self.sl(e3, n), in_=self.sl(b_, n), func=AF.Exp),
                 reads=[k(b_)], writes=[k(e3)])
            t.op("dve", lambda e: e.tensor_tensor(c3(self.sl(d_, n)), b3, b3[:, :, CL - 1:CL].to_broadcast([128, nch, CL]),
                                                  ALU.subtract),
                 reads=[k(b_), k(e1), k(e2)], writes=[k(d_)])
            t.op("act", lambda e: e.activation(out=self.sl(d_, n), in_=self.sl(d_, n), func=AF.Exp, scale=-1.0),
                 reads=[k(d_)], writes=[k(d_)])
            t.op("dve", lambda e: e.scalar_tensor_tensor(self.sl(e1, n), self.sl(qs, n), 0.125, self.sl(e1, n),
                                                         ALU.mult, ALU.mult),
                 reads=[k(qs), k(e1)], writes=[k(e1)])
            t.op("dve", lambda e: e.tensor_tensor(self.sl(e2, n), self.sl(fs, n), self.sl(e2, n), ALU.mult),
                 reads=[k(fs), k(e2)], writes=[k(e2)])
            t.op("dve", lambda e: e.scalar_tensor_tensor(self.sl(qp, n), self.sl(qs, n), 0.125, self.sl(e3, n),
                                                         ALU.mult, ALU.mult),
                 reads=[k(qs), k(e3)], writes=[k(qp)])
            t.op("dve", lambda e: e.tensor_tensor(self.sl(d_, n), self.sl(fs, n), self.sl(d_, n), ALU.mult),
                 reads=[k(fs), k(d_)], writes=[k(d_)])
            KTv = self.SL[:, ktk, :].rearrange("p (b f) -> p b f", f=128)
            for blk in range(nb):
                pb = min(128, n - blk * 128)
                pi = self.nextps()
                t.op("pe", lambda e, blk=blk, pb=pb, pi=pi: e.transpose(
                    self.ps[pi][0:pb, 0:128], self.SL[:, d_, blk * 128:blk * 128 + pb], self.ident[:, :]),
                    reads=[k(d_), "ident"], writes=[("ps", pi)])
                self.evac("act", KTv[0:pb, blk, :], self.ps[pi][0:pb, 0:128], reads=[("ps", pi)], writes=[k(ktk)])
            for c in range(nch):
                blk, r0 = (c * CL) // 128, (c * CL) % 128
                pi = self.nextps()
                self.mm(self.ps[pi][:, 0:128], ("ps", pi),
                        [(KTv[r0:r0 + CL, blk, :], VT[r0:r0 + CL, blk, pr * 128:(pr + 1) * 128])],
                        reads=[k(ktk)] + vtk)
                for hh in range(2):
                    rs = slice(hh * 64, (hh + 1) * 64)
                    t.op("dve", lambda e, c=c, rs=rs, pi=pi: e.scalar_tensor_tensor(
                        self.SS[rs, pr, c + 1, rs], self.SS[rs, pr, c, rs],
                        self.SL[rs, e3, (c + 1) * CL - 1:(c + 1) * CL], self.ps[pi][rs, rs],
                        ALU.mult, ALU.add),
                        reads=[("ps", pi), k(e3), ("SS", pr)], writes=[("SS", pr)])
            OPS = self.ps[6]
            for blk in range(nb):
                pb = min(128, n - blk * 128)
                bs = slice(blk * 128, blk * 128 + pb)
                smi = self.smn
                self.smn = (self.smn + 1) % 2
                SMv = self.SM[smi]
                for hh in range(2):
                    rs = slice(hh * 64, (hh + 1) * 64)
                    pi = self.nextps()
                    t.op("pe", lambda e, pi=pi, rs=rs, bs=bs, pb=pb, hh=hh: e.matmul(
                        self.ps[pi][0:pb, 0:pb], self.SL[rs, e2, bs], self.SL[rs, e1, bs],
                        start=True, stop=True, tile_position=(hh * 64, 0)),
                        reads=[k(e1), k(e2)], writes=[("ps", pi)])
                    t.op("dve", lambda e, pi=pi, pb=pb, hh=hh, SMv=SMv: e.tensor_tensor(
                        SMv[0:pb, hh, 0:pb], self.ps[pi][0:pb, 0:pb], self.mask2[0:pb, 0:pb], ALU.mult),
                        reads=[("ps", pi), "mask2"], writes=[("SM", smi, hh)])

                def emit_o(pe, blk=blk, bs=bs, pb=pb, SMv=SMv):
                    ins = None
                    for hh in range(2):
                        rs = slice(hh * 64, (hh + 1) * 64)
                        pe.matmul(OPS[rs, bs], VT[0:pb, blk, pr * 128 + hh * 64:pr * 128 + (hh + 1) * 64],
                                  SMv[0:pb, hh, 0:pb], start=True, stop=False, tile_position=(0, hh * 64))
                    for c in range(blk * 128 // CL, (blk * 128 + pb) // CL):
                        cs = slice(c * CL, (c + 1) * CL)
                        ins = pe.matmul(OPS[:, cs], self.SS[:, pr, c, :], self.SL[:, qp, cs],
                                        start=False, stop=True)
                    return ins
                t.op("pe", emit_o, reads=[("SM", smi, 0), ("SM", smi, 1), ("SS", pr), k(qp)] + vtk,
                     writes=[("ps", 6)])
            for hh in range(2):
                rs = slice(hh * 64, (hh + 1) * 64)
                t.op("dve", lambda e, rs=rs: e.tensor_copy(self.SS[rs, pr, 0, rs], self.SS[rs, pr, nch, rs]),
                     reads=[("SS", pr)], writes=[("SS", pr)])
            osq, rst = g_, b_
            t.op("act", lambda e: e.activation(out=self.sl(osq, n), in_=OPS[:, 0:n], func=AF.Square),
                 reads=[("ps", 6)], writes=[k(osq)])
            pm = self.nextps()
            self.mm(self.ps[pm][:, 0:n], ("ps", pm), [(self.bd64[:, :], self.sl(osq, n))], reads=[k(osq), "bd64"])
            t.op("act", lambda e: e.activation(out=self.sl(rst, n), in_=self.ps[pm][:, 0:n], func=AF.Sqrt,
                                               bias=EPS, scale=1.0),
                 reads=[("ps", pm)], writes=[k(rst)])
            t.op("dve", lambda e: e.reciprocal(self.sl(rst, n), self.sl(rst, n)), reads=[k(rst)], writes=[k(rst)])
            t.op("dve", lambda e: e.tensor_tensor(self.sl(rst, n), self.sl(rst, n), OPS[:, 0:n], ALU.mult),
                 reads=[k(rst), ("ps", 6)], writes=[k(rst)])
            t.op("act", lambda e: e.activation(out=self.sl(gs, n), in_=self.sl(gs, n), func=AF.Silu),
                 reads=[k(gs)], writes=[k(gs)])
            t.op("dve", lambda e: e.scalar_tensor_tensor(self.ysl(4 + pr, n), self.sl(rst, n), self.prm(60 + pr),
                                                         self.sl(gs, n), ALU.mult, ALU.mult),
                 reads=[k(rst), k(gs), "p_gn"], writes=[self.yk(4 + pr)])

    def s5_setup(self, l):
        t = self.t
        pin = self.pin
        Q = self.Q
        S = self.S5S
        KEY = "s5setup"
        col = [0]

        def alloc(ncol):
            c0 = col[0]
            col[0] += ncol
            assert col[0] <= 2176
            return S[:, c0:c0 + ncol]

        def dv(fn):
            t.op("dve", fn, reads=[KEY], writes=[KEY])

        def ac(fn):
            t.op("act", fn, reads=[KEY], writes=[KEY])

        def dma(out, in_):
            t.op("pool", lambda e: e.dma_start(out=out, in_=in_, allow_slow_non_contiguous=True),
                 reads=[KEY], writes=[KEY], stream="dma_io", inc=16)

        TT = lambda o, a, b, op: dv(lambda e: e.tensor_tensor(o, a, b, op))
        TS = lambda o, a, sc, op: dv(lambda e: e.tensor_single_scalar(o, a, sc, op))

        def cmul(o_r, o_i, a_r, a_i, b_r, b_i, t1, t2):
            TT(t1, a_r, b_r, ALU.mult)
            TT(t2, a_i, b_i, ALU.mult)
            TT(o_r, t1, t2, ALU.subtract)
            TT(t1, a_r, b_i, ALU.mult)
            TT(t2, a_i, b_r, ALU.mult)
            TT(o_i, t1, t2, ALU.add)

        t.op("dve", lambda e: e.memset(S[:, 0:8], 0.0), reads=[KEY], writes=[KEY] + self.S5S_keys)
        V = lambda: alloc(8)
        LR, LI, DT, X, TH, Cc, Ss, T1, T2, T3, AR, AI, WR, WI, MM, IAR, IAI, RI = [V() for _ in range(18)]
        lre = pin["s5_lam_re"][l]
        lim = pin["s5_lam_im"][l]
        ldt = pin["s5_log_dt"][l]
        dma(LR, bass.AP(lre.tensor, lre.offset, [[1, 128], [128, 8]]))
        dma(LI, bass.AP(lim.tensor, lim.offset, [[1, 128], [128, 8]]))
        for gl in range(2):
            dma(DT[gl * 64:(gl + 1) * 64, :], bass.AP(ldt.tensor, ldt.offset + gl, [[0, 64], [2, 8]]))
        ac(lambda e: e.activation(out=DT, in_=DT, func=AF.Exp))
        TT(X, LR, DT, ALU.mult)
        TT(TH, LI, DT, ALU.mult)
        ac(lambda e: e.activation(out=Ss, in_=TH, func=AF.Sin, scale=1.0 / 16))
        TS(T3, TH, 1.0 / 16, ALU.mult)
        TS(T3, T3, math.pi / 2, ALU.add)
        ac(lambda e: e.activation(out=Cc, in_=T3, func=AF.Sin))
        for _ in range(4):
            TT(T1, Cc, Cc, ALU.mult)
            TT(T2, Ss, Ss, ALU.mult)
            TT(T3, Cc, Ss, ALU.mult)
            TT(Cc, T1, T2, ALU.subtract)
            TS(Ss, T3, 2.0, ALU.mult)
        ac(lambda e: e.activation(out=T3, in_=X, func=AF.Exp))
        TT(AR, T3, Cc, ALU.mult)
        TT(AI, T3, Ss, ALU.mult)
        ac(lambda e: e.activation(out=self.RR[:, :], in_=X, func=AF.Exp, scale=float(Q)))
        dv(lambda e: e.reciprocal(RI, self.RR[:, :]))
        TT(T1, T3, T3, ALU.mult)
        dv(lambda e: e.reciprocal(T1, T1))
        TT(IAR, AR, T1, ALU.mult)
        TT(IAI, AI, T1, ALU.mult)
        TS(IAI, IAI, -1.0, ALU.mult)
        TS(T3, AR, -1.0, ALU.add)
        TT(T1, LR, LR, ALU.mult)
        TT(T2, LI, LI, ALU.mult)
        TT(MM, T1, T2, ALU.add)
        dv(lambda e: e.reciprocal(MM, MM))
        TT(T1, T3, LR, ALU.mult)
        TT(T2, AI, LI, ALU.mult)
        TT(WR, T1, T2, ALU.add)
        TT(WR, WR, MM, ALU.mult)
        TT(T1, AI, LR, ALU.mult)
        TT(T2, T3, LI, ALU.mult)
        TT(WI, T1, T2, ALU.subtract)
        TT(WI, WI, MM, ALU.mult)
        PWr = alloc(8 * (Q + 1)).rearrange("p (a s) -> p a s", s=Q + 1)
        PWi = alloc(8 * (Q + 1)).rearrange("p (a s) -> p a s", s=Q + 1)
        IPr = alloc(8 * Q).rearrange("p (a s) -> p a s", s=Q)
        IPi = alloc(8 * Q).rearrange("p (a s) -> p a s", s=Q)
        dv(lambda e: e.memset(PWr[:, :, 0], 1.0))
        dv(lambda e: e.memset(PWi[:, :, 0], 0.0))
        dv(lambda e: e.memset(IPr[:, :, 0], 1.0))
        dv(lambda e: e.memset(IPi[:, :, 0], 0.0))
        for sidx in range(Q):
            cmul(PWr[:, :, sidx + 1], PWi[:, :, sidx + 1], PWr[:, :, sidx], PWi[:, :, sidx], AR, AI, T1, T2)
        for sidx in range(Q - 1):
            cmul(IPr[:, :, sidx + 1], IPi[:, :, sidx + 1], IPr[:, :, sidx], IPi[:, :, sidx], IAR, IAI, T1, T2)
        FBr = alloc(8 * Q).rearrange("p (a s) -> p a s", s=Q)
        FBi = alloc(8 * Q).rearrange("p (a s) -> p a s", s=Q)
        for sidx in range(Q):
            cmul(FBr[:, :, sidx], FBi[:, :, sidx], IPr[:, :, sidx], IPi[:, :, sidx], WR, WI, T1, T2)
        TBr, TBi = self.TB[:, :, 0, :], self.TB[:, :, 1, :]
        dv(lambda e: e.memset(TBr[:, :, 0], 1.0))
        dv(lambda e: e.memset(TBi[:, :, 0], 0.0))
        TT(TBr[:, :, 1], PWr[:, :, Q], RI, ALU.mult)
        TT(TBi[:, :, 1], PWi[:, :, Q], RI, ALU.mult)
        big = lambda: alloc(256).rearrange("p (a h) -> p a h", h=32)
        Br, Bi, Sr, Si, U1, U2 = [big() for _ in range(6)]
        TWs = (U1.rearrange("p a h -> p (a h)"), U2.rearrange("p a h -> p (a h)"))
        tw = lambda i, m: TWs[i][:, 0:8 * m].rearrange("p (a c) -> p a c", c=m)
        m = 1
        while m < 64:
            if m >= 2:
                h = m // 2
                cmul(TBr[:, :, m], TBi[:, :, m], TBr[:, :, h], TBi[:, :, h], TBr[:, :, h], TBi[:, :, h], T1, T2)
            if m >= 2:
                br = TBr[:, :, m:m + 1].to_broadcast([128, 8, m - 1])
                bi = TBi[:, :, m:m + 1].to_broadcast([128, 8, m - 1])
                cmul(TBr[:, :, m + 1:2 * m], TBi[:, :, m + 1:2 * m], TBr[:, :, 1:m], TBi[:, :, 1:m], br, bi,
                     tw(0, m - 1), tw(1, m - 1))
            m *= 2
        cmul(TBr[:, :, 64], TBi[:, :, 64], TBr[:, :, 32], TBi[:, :, 32], TBr[:, :, 32], TBi[:, :, 32], T1, T2)
        rb = self.RR[:, :].unsqueeze(2).to_broadcast([128, 8, 64])
        TT(self.TA[:, :, 0, :], TBr[:, :, 0:64], rb, ALU.mult)
        TT(self.TA[:, :, 1, :], TBi[:, :, 0:64], rb, ALU.mult)
        TS(self.TA[:, :, 1, :], self.TA[:, :, 1, :], -1.0, ALU.mult)
        for ap_ in (Br, Bi):
            dv(lambda e, ap_=ap_: e.memset(ap_, 0.0))
        for g in range(16):
            P_, gl = g // 2, g % 2
            dma(Br[gl * 64:(gl + 1) * 64, P_, gl * 16:(gl + 1) * 16], pin["s5_b_re"][l, g])
            dma(Bi[gl * 64:(gl + 1) * 64, P_, gl * 16:(gl + 1) * 16], pin["s5_b_im"][l, g])
        for sidx in range(Q):
            fr = FBr[:, :, sidx:sidx + 1].to_broadcast([128, 8, 32])
            fi = FBi[:, :, sidx:sidx + 1].to_broadcast([128, 8, 32])
            cmul(Sr, Si, Br, Bi, fr, fi, U1, U2)
            for ri, src in enumerate((Sr, Si)):
                for half in range(2):
                    pi = self.nextps()
                    t.op("pe", lambda e, src=src, half=half, pi=pi: e.transpose(
                        self.ps[pi][:, 0:128], src[:, half * 4:(half + 1) * 4, :], self.ident[:, :]),
                        reads=[KEY, "ident"], writes=[("ps", pi)])
                    t.op("act", lambda e, half=half, sidx=sidx, ri=ri, pi=pi: e.activation(
                        out=self.BsT[:, half, sidx, ri, :], in_=self.ps[pi][:, 0:128], func=AF.Copy),
                        reads=[("ps", pi)], writes=["BsT"])
        CNr = Br.rearrange("p a h -> p (a h)").rearrange("p (a q) -> p a q", q=128)
        CNi = Bi.rearrange("p a h -> p (a h)").rearrange("p (a q) -> p a q", q=128)
        for ap_ in (CNr, CNi):
            dv(lambda e, ap_=ap_: e.memset(ap_, 0.0))
        for g in range(16):
            P_, gl = g // 2, g % 2
            half, j = P_ // 4, P_ % 4
            r0 = 32 * j + 16 * gl
            dma(CNr[r0:r0 + 16, half, gl * 64:(gl + 1) * 64], pin["s5_c_re"][l, g])
            dma(CNi[r0:r0 + 16, half, gl * 64:(gl + 1) * 64], pin["s5_c_im"][l, g])
        for src, dst in ((CNr, Sr), (CNi, Si)):
            for half in range(2):
                pi = self.nextps()
                t.op("pe", lambda e, src=src, half=half, pi=pi: e.transpose(
                    self.ps[pi][:, 0:128], src[:, half, :], self.ident[:, :]),
                    reads=[KEY, "ident"], writes=[("ps", pi)])
                t.op("act", lambda e, dst=dst, half=half, pi=pi: e.activation(
                    out=dst[:, half * 4:(half + 1) * 4, :], in_=self.ps[pi][:, 0:128].rearrange("p (a h) -> p a h", h=32),
                    func=AF.Copy), reads=[("ps", pi), KEY], writes=[KEY])
        for sidx in range(Q):
            pr_ = PWr[:, :, sidx:sidx + 1].to_broadcast([128, 8, 32])
            pi_ = PWi[:, :, sidx:sidx + 1].to_broadcast([128, 8, 32])
            cmul(self.CsT[:, :, sidx, 0, :], self.CsT[:, :, sidx, 1, :], Sr, Si, pr_, pi_, U1, U2)
            dv(lambda e, sidx=sidx: e.tensor_single_scalar(self.CsT[:, :, sidx, 1, :], self.CsT[:, :, sidx, 1, :],
                                                           -1.0, ALU.mult))
        t.op("dve", lambda e: e.tensor_copy(self.CH[:, 0, 0:1], self.CH[:, 0, 0:1]), reads=[KEY],
             writes=["CsT", "TAB"] + self.S5S_keys)
        t.op("pool", lambda e: e.dma_start(out=self.GW[:, :, :],
                                           in_=pin["s5_glu_w"][l].rearrange("(kt p) n -> p kt n", p=128)),
             writes=["GW"], stream="dma_io", inc=16)
        t.op("dve", lambda e: e.memset(self.CARJ[:], 0.0), writes=[("CARJ", P_) for P_ in range(8)])

    def mix_s5(self, l, ti, n):
        t = self.t
        Y, U, Z = self.S_Y, self.S_U, self.S_Z
        k = self.slk
        Q = self.Q
        ncn = n // Q
        tm = lambda ap: ap.rearrange("p (c s) -> p s c", s=Q)
        sm = lambda ap: ap.rearrange("p (s c) -> p s c", s=Q)
        for half in range(2):
            t.op("act", lambda e, half=half: e.activation(out=sm(self.sl(Z[half], n)), in_=tm(self.sl(U[half], n)),
                                                          func=AF.Copy),
                 reads=[k(U[half])], writes=[k(Z[half])])
        YB = self.ps[7]
        for P_ in range(8):
            half, j = P_ // 4, P_ % 4
            rj = slice(32 * j, 32 * j + 32)
            Wk, Gk, Gp = (Z[2], Z[3]), (Z[4], Z[5]), (Z[6], Z[7])
            for ri in range(2):
                pv = self.nextps()

                def emit_b(pe, ri=ri, pv=pv):
                    ins = None
                    for sidx in range(Q):
                        ins = pe.matmul(self.ps[pv][:, sidx * ncn:(sidx + 1) * ncn],
                                        self.BsT[rj, half, sidx, ri, :],
                                        self.SL[rj, Z[half], sidx * ncn:(sidx + 1) * ncn],
                                        start=True, stop=True, tile_position=(32 * j, 0))
                    return ins
                t.op("pe", emit_b, reads=["BsT", k(Z[half])], writes=[("ps", pv)])
                t.op("act", lambda e, ri=ri, pv=pv: e.activation(out=tm(self.sl(Wk[ri], n)), in_=sm(self.ps[pv][:, 0:n]),
                                                                func=AF.Copy),
                     reads=[("ps", pv)], writes=[k(Wk[ri])])
                t.op("dve", lambda e, ri=ri: e.tensor_tensor_scan(self.sl(Gk[ri], n), self.cmask8[:, 0:n],
                                                                  self.sl(Wk[ri], n), 0.0, ALU.mult, ALU.add),
                     reads=[k(Wk[ri]), "cmask8"], writes=[k(Gk[ri])])
            CH = self.CH
            ck = ("CH",)
            gl_ = [self.SL[:, Gk[ri], 0:n].rearrange("p (c s) -> p c s", s=Q)[:, :, Q - 1] for ri in range(2)]
            TAr, TAi = self.TA[:, P_, 0, 0:ncn], self.TA[:, P_, 1, 0:ncn]
            TBr, TBi = self.TB[:, P_, 0, 0:ncn + 1], self.TB[:, P_, 1, 0:ncn + 1]
            vr, vi, t1, t2 = CH[:, 0, 0:ncn], CH[:, 1, 0:ncn], CH[:, 2, 0:ncn + 1], CH[:, 3, 0:ncn + 1]
            jt = [CH[:, 4, 0:ncn + 1], CH[:, 5, 0:ncn + 1]]
            jj = [CH[:, 6, 0:ncn + 1], CH[:, 7, 0:ncn + 1]]
            gk = [k(Gk[0]), k(Gk[1])]

            def dv(fn, reads=(), writes=()):
                t.op("dve", fn, reads=list(reads) + [ck, "TAB"], writes=list(writes) + [ck])
            dv(lambda e: e.tensor_tensor(t1[:, 0:ncn], gl_[0], TAr, ALU.mult), reads=gk)
            dv(lambda e: e.tensor_tensor(t2[:, 0:ncn], gl_[1], TAi, ALU.mult), reads=gk)
            dv(lambda e: e.tensor_tensor(vr, t1[:, 0:ncn], t2[:, 0:ncn], ALU.subtract))
            dv(lambda e: e.tensor_tensor(t1[:, 0:ncn], gl_[0], TAi, ALU.mult), reads=gk)
            dv(lambda e: e.tensor_tensor(t2[:, 0:ncn], gl_[1], TAr, ALU.mult), reads=gk)
            dv(lambda e: e.tensor_tensor(vi, t1[:, 0:ncn], t2[:, 0:ncn], ALU.add))
            for ri, v_ in enumerate((vr, vi)):
                dv(lambda e, ri=ri: e.tensor_copy(jt[ri][:, 0:1], self.CARJ[:, P_, ri:ri + 1]), reads=[("CARJ", P_)])
                dv(lambda e, ri=ri, v_=v_: e.tensor_tensor_scan(
                    jt[ri][:, 1:ncn + 1], self.RR[:, P_:P_ + 1].to_broadcast([128, ncn]), v_,
                    self.CARJ[:, P_, ri:ri + 1], ALU.mult, ALU.add), reads=[("CARJ", P_)])
            dv(lambda e: e.tensor_tensor(t1, jt[0], TBr, ALU.mult))
            dv(lambda e: e.tensor_tensor(t2, jt[1], TBi, ALU.mult))
            dv(lambda e: e.tensor_tensor(jj[0], t1, t2, ALU.subtract))
            dv(lambda e: e.tensor_tensor(t1, jt[0], TBi, ALU.mult))
            dv(lambda e: e.tensor_tensor(t2, jt[1], TBr, ALU.mult))
            dv(lambda e: e.tensor_tensor(jj[1], t1, t2, ALU.add))
            for ri in range(2):
                dv(lambda e, ri=ri: e.tensor_copy(self.CARJ[:, P_, ri:ri + 1], jj[ri][:, ncn:ncn + 1]),
                   writes=[("CARJ", P_)])
            for ri in range(2):
                t.op("dve", lambda e, ri=ri: e.tensor_tensor(
                    sm(self.sl(Gp[ri], n)), tm(self.sl(Gk[ri], n)),
                    jj[ri][:, 0:ncn].unsqueeze(1).to_broadcast([128, Q, ncn]), ALU.add),
                    reads=[k(Gk[ri]), ck], writes=[k(Gp[ri])])

            def emit_c(pe):
                ins = None
                for sidx in range(Q):
                    cs = slice(sidx * ncn, (sidx + 1) * ncn)
                    pe.matmul(YB[rj, cs], self.CsT[:, P_, sidx, 0, :], self.SL[:, Gp[0], cs],
                              start=True, stop=False, tile_position=(0, 32 * j))
                    ins = pe.matmul(YB[rj, cs], self.CsT[:, P_, sidx, 1, :], self.SL[:, Gp[1], cs],
                                    start=False, stop=True, tile_position=(0, 32 * j))
                return ins
            t.op("pe", emit_c, reads=["CsT", k(Gp[0]), k(Gp[1])], writes=[("ps", 7)])
            if j == 3:
                ys = U[20 + half]
                t.op("dve", lambda e, half=half, ys=ys: e.scalar_tensor_tensor(
                    tm(self.sl(ys, n)), tm(self.sl(U[half], n)), self.prm(62 + half), sm(YB[:, 0:n]),
                    ALU.mult, ALU.add),
                    reads=[k(U[half]), ("ps", 7), "p_s5d"], writes=[k(ys)])
                t.op("act", lambda e, ys=ys: e.activation(out=self.sl(ys, n), in_=self.sl(ys, n),
                                                          func=AF.Gelu_apprx_tanh),
                     reads=[k(ys)], writes=[k(ys)])
        for m in range(2):
            pi = self.nextps()
            self.mm(self.ps[pi][:, 0:n], ("ps", pi),
                    [(self.GW[:, kt, m * 128:(m + 1) * 128], self.sl(U[20 + kt], n)) for kt in range(2)],
                    reads=[k(U[20]), k(U[21]), "GW"])
            tk = self.nexttmp()
            t.op("act", lambda e, tk=tk, pi=pi, m=m: e.activation(out=self.tmp[tk][:, 0:n], in_=self.ps[pi][:, 0:n],
                                                                 func=AF.Sigmoid, bias=self.prm(64 + m), scale=1.0),
                 reads=[("ps", pi), "p_glub"], writes=[("tmp", tk)])
            t.op("dve", lambda e, tk=tk, m=m: e.tensor_tensor(self.ysl(m, n), self.sl(U[20 + m], n),
                                                              self.tmp[tk][:, 0:n], ALU.mult),
                 reads=[("tmp", tk), k(U[20 + m])], writes=[self.yk(m)])


_CACHE = {}
_BUILDERS = {}


def _get_nc(n_xtiles):
    if n_xtiles not in _CACHE:
        _BUILDERS[n_xtiles] = Builder(n_xtiles)
        _CACHE[n_xtiles] = _BUILDERS[n_xtiles].build()
    return _CACHE[n_xtiles]


def make_in_maps(inputs, names, xs):
    npairs = len(xs)
    role_a = np.zeros((128, 2), np.float32)
    role_a[:, 0] = 1.0
    role_b = np.zeros((128, 2), np.float32)
    role_b[:, 1] = 1.0
    pa, pb = {}, {}
    for k in names:
        if k in ("x", "role"):
            continue
        arr = np.ascontiguousarray(inputs[k], dtype=np.float32)
        if k in ("meta_tokens", "hg_lb_raw"):
            pa[k] = arr
            pb[k] = arr
        else:
            pa[k] = np.ascontiguousarray(arr[[0, 0]])
            pb[k] = arr
    maps = [dict(pa, x=xs[i], role=role_a) for i in range(npairs)]
    maps += [dict(pb, x=xs[i], role=role_b) for i in range(npairs)]
    return maps


def kernel(**inputs):
    x = np.ascontiguousarray(inputs["x"], dtype=np.float32)
    bsz, seq, _ = x.shape
    n_xtiles = seq // NT
    nc = _get_nc(n_xtiles)
    names = _BUILDERS[n_xtiles].in_names
    in_maps = make_in_maps(inputs, names, [x[b] for b in range(bsz)])
    res = run_bass_kernel_spmd(nc, in_maps, core_ids=list(range(2 * bsz)))
    return np.stack([res.results[bsz + b]["out"] for b in range(bsz)], axis=0)
```

```python
import math
import os
from contextlib import ExitStack

import numpy as np
import concourse.bass as bass
import concourse.mybir as mybir
from concourse.bass_utils import run_bass_kernel_spmd

F32 = mybir.dt.float32
BF16 = mybir.dt.bfloat16
AF = mybir.ActivationFunctionType
ALU = mybir.AluOpType

D = 1024
KT = 8
NIN = 2560
DFF = 2816
FT = 22
NMETA = 16
SEQ = 8192
DEPTH = 2
ALPHA = (2 * DEPTH) ** 0.25
EPS = 1e-5
NT = 512
SKEW = 1
WSLOT = 4096
NWSLOT = 2
NBSLOT = 4
NCHUNK = 26


class Trk:
    def __init__(self, nc, es):
        self.nc = nc
        self.es = es
        self.E = {"pe": nc.tensor, "act": nc.scalar, "dve": nc.vector,
                  "pool": nc.gpsimd, "sp": nc.sync}
        self.cur = {}
        self.waited = {}
        self.lastw = {}
        self.rd = {}
        self.nsem = 0
        self.LIM = 30000
        self.nins = 0

    def _sem(self, stream, inc):
        s = self.cur.get(stream)
        if s is None or s[1] + inc > self.LIM:
            name = f"s{self.nsem}"
            sem = self.es.enter_context(self.nc.semaphore(f"{name}_{stream}"))
            self.nsem += 1
            s = [sem, 0, name]
            self.cur[stream] = s
        s[1] += inc
        return (s[0], s[1], s[2])

    def wait(self, engine, tok):
        sem, val, name = tok
        k = (engine, name)
        if self.waited.get(k, 0) >= val:
            return
        self.waited[k] = val
        self.E[engine].wait_ge(sem, val)

    def op(self, engine, emit, reads=(), writes=(), stream=None, inc=1):
        deps = {}

        def add(tok):
            if tok is None:
                return
            n = tok[2]
            if n not in deps or deps[n][1] < tok[1]:
                deps[n] = tok

        for k in reads:
            add(self.lastw.get(k))
        for k in writes:
            add(self.lastw.get(k))
            for t in self.rd.get(k, {}).values():
                add(t)
        st = stream or engine
        own = self.cur.get(st)
        for tok in deps.values():
            if engine == "pe" and stream is None and own is not None and tok[2] == own[2]:
                continue
            self.wait(engine, tok)
        ins = emit(self.E[engine])
        tok = self._sem(st, inc)
        ins.then_inc(tok[0], inc)
        self.nins += 1
        for k in reads:
            self.rd.setdefault(k, {})[tok[2]] = tok
        for k in writes:
            self.lastw[k] = tok
            self.rd[k] = {}
        return tok

    def drain(self, engine):
        for st, s in self.cur.items():
            self.wait(engine, (s[0], s[1], s[2]))


class _Stop(Exception):
    pass


class Builder:
    def __init__(self, n_xtiles, stub=(), taps=(), npairs=4):
        self.npairs = npairs
        self.n_xtiles = n_xtiles
        self.stub = set(stub)
        self.taps = list(taps)
        self.tiles = [(0, NMETA)] + [(NMETA + i * NT, NT) for i in range(n_xtiles)]
        self.seq_x = n_xtiles * NT

    def build(self):
        nc = bass.Bass("TRN2", target_bir_lowering=False)
        self.nc = nc
        self.es = ExitStack()
        es = self.es
        self.t = Trk(nc, es)
        t = self.t
        self.in_names = []

        def di(name, shape):
            self.in_names.append(name)
            return nc.dram_tensor(name, list(shape), F32, kind="ExternalInput").ap()
        self.x = di("x", [self.seq_x, D])
        self.meta = di("meta_tokens", [NMETA, D])
        self.w_in = di("w_in", [DEPTH, D, NIN])
        self.w_out = di("w_out", [DEPTH, D, D])
        self.w_f1 = di("w_ffn_in", [DEPTH, D, 2 * DFF])
        self.w_f2 = di("w_ffn_out", [DEPTH, DFF, D])
        self.ln = {n: di(n, [DEPTH, D]) for n in ("ln1_g", "ln1_b", "ln2_g", "ln2_b")}
        self.pin = {}
        for nm, shp in (("hg_lb_raw", [DEPTH, 256]), ("sc_conv_w", [DEPTH, 3, 256]), ("hg_gnorm", [DEPTH, 256]),
                        ("lru_conv_w", [DEPTH, 4, 256]), ("lru_conv_b", [DEPTH, 256]),
                        ("lru_wa", [DEPTH, 4, 64, 64]), ("lru_ba", [DEPTH, 256]),
                        ("lru_wx", [DEPTH, 4, 64, 64]), ("lru_bx", [DEPTH, 256]), ("lru_a_param", [DEPTH, 256]),
                        ("s5_lam_re", [DEPTH, 16, 64]), ("s5_lam_im", [DEPTH, 16, 64]),
                        ("s5_b_re", [DEPTH, 16, 64, 16]), ("s5_b_im", [DEPTH, 16, 64, 16]),
                        ("s5_c_re", [DEPTH, 16, 16, 64]), ("s5_c_im", [DEPTH, 16, 16, 64]),
                        ("s5_d", [DEPTH, 256]), ("s5_log_dt", [DEPTH, 16]),
                        ("s5_glu_w", [DEPTH, 256, 256]), ("s5_glu_b", [DEPTH, 256])):
            self.pin[nm] = di(nm, shp)
        self.role_in = di("role", [128, 2])
        self.out = nc.dram_tensor("out", [self.seq_x, D], F32, kind="ExternalOutput").ap()
        ntile = len(self.tiles)
        self.send = [nc.dram_tensor(f"send{i}", [128, KT * NT], F32, kind="Internal").ap() for i in range(2)]
        self.recv = [nc.dram_tensor(f"recv{i}", [128, KT * NT], F32, kind="Internal").ap() for i in range(4)]
        self.groups = [[i, i + self.npairs] for i in range(self.npairs)]
        self.wbf = nc.dram_tensor("wbf", [NCHUNK, 128, WSLOT], BF16, kind="Internal").ap()
        self.tapout = {}
        for name, shape in self.taps:
            self.tapout[name] = nc.dram_tensor("tap_" + name, list(shape), F32, kind="ExternalOutput").ap()

        sb = lambda name, shape: es.enter_context(nc.sbuf_tensor(name, list(shape), F32))
        self.NSLAB = 46
        self.SL = sb("SL", [128, self.NSLAB, NT])
        self.W = [sb(f"W{i}", [128, WSLOT]) for i in range(NWSLOT)]
        self.ident = sb("ident", [128, 128])
        self.ones = sb("ones", [128, 128])
        self.PRM = sb("PRM", [128, 128])
        self.tmp = [sb(f"tmp{i}", [128, NT]) for i in range(4)]
        self.cmask = sb("cmask", [128, NT])
        self.bd64 = sb("bd64", [128, 128])
        self.lruw = sb("lruw", [128, 2, 2, 128])
        self.car_sc = sb("car_sc", [128, 2, 2])
        self.car_lx = sb("car_lx", [128, 2, 3])
        self.car_lh = sb("car_lh", [128, 2])
        self.SS = sb("SS", [128, 2, 9, 128])
        self.SM = [sb(f"SM{i}", [128, 2, 128]) for i in range(2)]
        self.mask2 = sb("mask2", [128, 128])
        self.smn = 0
        self.Q = 8
        self.BsT = sb("BsT", [128, 2, 8, 2, 128])
        self.CsT = sb("CsT", [128, 8, 8, 2, 32])
        self.TA = sb("TA", [128, 8, 2, 64])
        self.TB = sb("TB", [128, 8, 2, 65])
        self.RR = sb("RR", [128, 8])
        self.GW = sb("GW", [128, 2, 256])
        self.CARJ = sb("CARJ", [128, 8, 2])
        self.cmask8 = sb("cmask8", [128, NT])
        self.CH = sb("CH", [128, 8, 66])
        self.ROLE = sb("ROLE", [128, 2])
        self.HM0 = sb("HM0", [128, KT, NMETA])
        self.SNAP = sb("SNAP", [128, 2 * 2 + 2 * 3 + 2 + 2 * 128 + 8 * 2])
        self.ps = [es.enter_context(nc.psum_tensor(f"ps{i}", [128, NT], F32)) for i in range(8)]
        self.psn = 0
        self.tmpn = 0
        self.S_H = list(range(0, 8))
        self.S_Y = list(range(8, 16))
        self.S_Z = list(range(16, 24))
        self.S_U = list(range(24, 46))
        self.S5S = self.SL[:, 24:29, :].rearrange("p a b -> p (a b)")[:, 0:2176]
        self.S5S_keys = [("sl", i) for i in range(24, 29)]
        self.CB = es.enter_context(nc.sbuf_tensor("CB", [128, WSLOT], BF16))
        self.WBv = [self.W[i // 2][:, :].bitcast(BF16)[:, (i % 2) * WSLOT:(i % 2 + 1) * WSLOT] for i in range(NBSLOT)]
        self.bf = False

        t.op("pool", lambda e: e.memset(self.ident[:], 0.0), writes=["ident"])
        t.op("pool", lambda e: e.affine_select(
            out=self.ident[:], in_=self.ident[:], compare_op=ALU.not_equal, fill=1.0,
            base=0, pattern=[[-1, 128]], channel_multiplier=1), writes=["ident"])
        t.op("pool", lambda e: e.memset(self.ones[:], 1.0), writes=["ones"])
        t.op("pool", lambda e: e.memset(self.cmask[:], 1.0), writes=["cmask"])
        t.op("pool", lambda e: e.affine_select(
            out=self.cmask[:].rearrange("p (c s) -> p c s", s=64), in_=self.cmask[:].rearrange("p (c s) -> p c s", s=64),
            compare_op=ALU.not_equal, fill=0.0, base=0, pattern=[[0, NT // 64], [1, 64]], channel_multiplier=0),
            writes=["cmask"])
        t.op("pool", lambda e: e.memset(self.cmask8[:], 1.0), writes=["cmask8"])
        t.op("pool", lambda e: e.affine_select(
            out=self.cmask8[:].rearrange("p (c s) -> p c s", s=8), in_=self.cmask8[:].rearrange("p (c s) -> p c s", s=8),
            compare_op=ALU.not_equal, fill=0.0, base=0, pattern=[[0, NT // 8], [1, 8]], channel_multiplier=0),
            writes=["cmask8"])
        t.op("pool", lambda e: e.memset(self.mask2[:], 1.0), writes=["mask2"])
        t.op("pool", lambda e: e.affine_select(
            out=self.mask2[:, :], in_=self.mask2[:, :],
            compare_op=ALU.is_ge, fill=0.0, base=0, pattern=[[1, 128]], channel_multiplier=-1),
            writes=["mask2"])
        t.op("pool", lambda e: e.memset(self.mask2[0:64, 64:128], 0.0), writes=["mask2"])
        t.op("pool", lambda e: e.memset(self.bd64[:], 0.0), writes=["bd64"])
        for hh in range(2):
            t.op("pool", lambda e, hh=hh: e.memset(self.bd64[hh * 64:(hh + 1) * 64, hh * 64:(hh + 1) * 64], 1.0 / 64),
                 writes=["bd64"])

        t.op("pool", lambda e: e.dma_start(out=self.ROLE[:, :], in_=self.role_in), writes=["role"],
             stream="dma_io", inc=16)
        self.fA = self.ROLE[:, 0:1]
        self.fB = self.ROLE[:, 1:2]
        nsteps = self.n_xtiles + SKEW
        self.nsteps = nsteps
        self.wq = []
        self.wq_issued = 0
        self.wq_used = 0
        self.wq.extend(self.chunk_seq(0))
        self.wq.extend(self.chunk_seq(1))
        for _ in range(nsteps):
            self.wq.extend(self.bf_chunk_seq())
        self._index_queue()

        try:
            self._program(nsteps)
        except _Stop:
            pass
        t.drain("pool")
        es.close()
        return nc

    def dbg(self, lvl):
        if int(os.environ.get("DBG_STOP", "99")) <= lvl:
            raise _Stop()

    def _program(self, nsteps):
        self.layer_setup(0, mine=False)
        self.tile_pass(0, "p0", NMETA)
        self.layer_setup(1, mine=True)
        self.tile_pass(1, "p1", NMETA)
        self.snapshot()
        self.dbg(1)
        self.bf = True
        for step in range(nsteps):
            self.tile_pass(1, "main", NT, step)
            if step == SKEW - 1:
                self.restore()

    def sl(self, idx, n=NT):
        return self.SL[:, idx, 0:n]

    def slk(self, idx):
        return ("sl", idx)

    def nextps(self):
        i = self.psn
        self.psn = (self.psn + 1) % 6
        return i

    def nexttmp(self):
        i = self.tmpn
        self.tmpn = (self.tmpn + 1) % 4
        return i

    def tap(self, name, src_ap, reads):
        if name in self.tapout:
            self.t.op("pool", lambda e: e.dma_start(out=self.tapout[name], in_=src_ap),
                      reads=reads, stream="dma_io", inc=16)

    def chunk_seq(self, l):
        seq = []
        wi = self.w_in[l].rearrange("(kt p) n -> p kt n", p=128)
        for c in range(5):
            seq.append(("win", [(wi[:, :, c * 512:(c + 1) * 512], 0, KT, 512)]))
        wo = self.w_out[l].rearrange("(kt p) n -> p kt n", p=128)
        for c in range(2):
            seq.append(("wout", [(wo[:, :, c * 512:(c + 1) * 512], 0, KT, 512)]))
        w1 = self.w_f1[l].rearrange("(kt p) n -> p kt n", p=128)
        for c in range(11):
            seq.append(("f1", [(w1[:, :, c * 256:(c + 1) * 256], 0, KT, 256),
                               (w1[:, :, DFF + c * 256:DFF + (c + 1) * 256], KT * 256, KT, 256)]))
        w2 = self.w_f2[l].rearrange("(kt p) n -> p kt n", p=128)
        for c in range(8):
            seq.append(("f2", [(w2[:, :, c * 128:(c + 1) * 128], 0, FT, 128)]))
        return seq

    def bf_chunk_seq(self):
        seq = []
        kinds = ["win"] * 5 + ["wout"] * 2 + ["f1"] * 11 + ["f2"] * 8
        for c, kd in enumerate(kinds):
            nel = FT * 128 if kd == "f2" else WSLOT
            seq.append((kd, [(self.wbf[c][:, 0:nel], 0, nel, 1)], c))
        return seq

    def _slot_of(self, idx):
        ent = self.wq[idx]
        if len(ent) == 3:
            return ("b", self.bcount[idx] % NBSLOT)
        return ("f", self.fcount[idx] % NWSLOT)

    def _slot_ap(self, slotid):
        return self.WBv[slotid[1]] if slotid[0] == "b" else self.W[slotid[1]]

    def _phys_keys(self, slotid):
        if slotid[0] == "b":
            return [("w", slotid[1] // 2, slotid[1] % 2, 0), ("w", slotid[1] // 2, slotid[1] % 2, 1)]
        return [("w", slotid[1], 0, 0), ("w", slotid[1], 1, 0), ("w", slotid[1], 0, 1), ("w", slotid[1], 1, 1)]

    def _issue_chunk(self, idx):
        ent = self.wq[idx]
        slotid = self._slot_of(idx)
        base = self._slot_ap(slotid)
        if len(ent) == 3:
            kind, parts, c = ent
            src, off, nel, _ = parts[0]
            self.t.op("sp", lambda e: e.dma_start(out=base[:, 0:nel], in_=src),
                      reads=[("wbf", c)], writes=self._phys_keys(slotid), stream=f"dma_w{slotid[1] // 2}", inc=16)
            return
        kind, parts = ent
        for pidx, (src, off, nk, ncol) in enumerate(parts):
            dst = base[:, off:off + nk * ncol].rearrange("p (k n) -> p k n", k=nk)
            wk = [("w", slotid[1], pidx, 0)] if len(parts) == 2 else [("w", slotid[1], 0, 0), ("w", slotid[1], 1, 0)]
            wk2 = [(a, b_, c_, 1) for (a, b_, c_, _) in wk]
            kh = nk // 2
            self.t.op("sp", lambda e, dst=dst, src=src: e.dma_start(out=dst[:, 0:kh, :], in_=src[:, 0:kh, :]),
                      writes=wk, stream=f"dma_w{slotid[1]}", inc=16)
            self.t.op("pool", lambda e, dst=dst, src=src: e.dma_start(out=dst[:, kh:nk, :], in_=src[:, kh:nk, :]),
                      writes=wk2, stream=f"dma_wp{slotid[1]}", inc=16)

    def _index_queue(self):
        self.bcount, self.fcount = {}, {}
        nb = nf = 0
        for i, ent in enumerate(self.wq):
            if len(ent) == 3:
                self.bcount[i] = nb
                nb += 1
            else:
                self.fcount[i] = nf
                nf += 1

    def wacquire(self, kind):
        idx = self.wq_used
        assert self.wq[idx][0] == kind, (self.wq[idx][0], kind)
        while self.wq_issued <= idx:
            self._issue_chunk(self.wq_issued)
            self.wq_issued += 1
        self.wq_used += 1
        self.cur_chunk = idx
        return self._slot_of(idx)

    def wprefetch(self):
        depth = (NBSLOT - 1) if len(self.wq[self.wq_used - 1]) == 3 else (NWSLOT - 1)
        lim = min(self.wq_used + depth, len(self.wq))
        while self.wq_issued < lim:
            nxt = self.wq[self.wq_issued]
            if (len(nxt) == 3) != (len(self.wq[self.wq_used - 1]) == 3):
                break
            self._issue_chunk(self.wq_issued)
            self.wq_issued += 1

    def wview(self, slotid, off, nk, ncol):
        return self._slot_ap(slotid)[:, off:off + nk * ncol].rearrange("p (k n) -> p k n", k=nk)

    def wkeys(self, slotid):
        return self._phys_keys(slotid)

    def cast_store(self, slotid, kind):
        c = self.cur_chunk - NCHUNK
        nel = FT * 128 if kind == "f2" else WSLOT
        self.t.op("act", lambda e: e.activation(out=self.CB[:, 0:nel], in_=self.W[slotid[1]][:, 0:nel], func=AF.Copy),
                  reads=self._phys_keys(slotid), writes=["CB"])
        self.t.op("pool", lambda e: e.dma_start(out=self.wbf[c][:, 0:nel], in_=self.CB[:, 0:nel]),
                  reads=["CB"], writes=[("wbf", c)], stream="dma_io", inc=16)

    def layer_setup(self, l, mine):
        t = self.t
        for i, n in enumerate(("ln1_g", "ln1_b", "ln2_g", "ln2_b")):
            src = self.ln[n][l].rearrange("(m p) -> p m", p=128)
            t.op("pool", lambda e, src=src, i=i: e.dma_start(
                out=self.PRM[:, i * 8:(i + 1) * 8], in_=src, allow_slow_non_contiguous=True),
                writes=[("prm", i)], stream="dma_io", inc=16)
        P = self.PRM
        pin = self.pin

        def vec(name, col, key):
            src = pin[name][l].rearrange("(c p) -> p c", p=128)
            t.op("pool", lambda e: e.dma_start(out=P[:, col:col + 2], in_=src, allow_slow_non_contiguous=True),
                 writes=[key], stream="dma_io", inc=16)

        for c in range(2):
            t.op("pool", lambda e, c=c: e.dma_start(
                out=P[:, 32 + 3 * c:35 + 3 * c],
                in_=pin["sc_conv_w"][l][:, c * 128:(c + 1) * 128].rearrange("k p -> p k"),
                allow_slow_non_contiguous=True), writes=["p_scw"], stream="dma_io", inc=16)
            t.op("pool", lambda e, c=c: e.dma_start(
                out=P[:, 38 + 4 * c:42 + 4 * c],
                in_=pin["lru_conv_w"][l][:, c * 128:(c + 1) * 128].rearrange("k p -> p k"),
                allow_slow_non_contiguous=True), writes=["p_lcw"], stream="dma_io", inc=16)
        vec("lru_conv_b", 46, "p_lcb")
        vec("lru_ba", 48, "p_lba")
        vec("lru_bx", 50, "p_lbx")
        vec("lru_a_param", 52, "p_cp")
        vec("hg_gnorm", 60, "p_gn")
        vec("s5_d", 62, "p_s5d")
        vec("s5_glu_b", 64, "p_glub")
        t.op("act", lambda e: e.activation(out=P[:, 52:54], in_=P[:, 52:54], func=AF.Exp, scale=-1.0),
             reads=["p_cp"], writes=["p_cp"])
        t.op("act", lambda e: e.activation(out=P[:, 52:54], in_=P[:, 52:54], func=AF.Ln, bias=1.0, scale=1.0),
             reads=["p_cp"], writes=["p_cp"])
        t.op("dve", lambda e: e.tensor_single_scalar(P[:, 54:56], P[:, 52:54], -16.0, ALU.mult),
             reads=["p_cp"], writes=["p_cp2"])
        t.op("dve", lambda e: e.tensor_single_scalar(P[:, 52:54], P[:, 52:54], -8.0, ALU.mult),
             reads=["p_cp", "p_cp2"], writes=["p_cp"])
        if not mine:
            t.op("dve", lambda e: e.memset(P[:, 56:58], 0.0), writes=["p_lb"])
        else:
            for i in range(2):
                src = pin["hg_lb_raw"][i].rearrange("(c p) -> p c", p=128)
                t.op("pool", lambda e, i=i, src=src: e.dma_start(out=P[:, 66 + 2 * i:68 + 2 * i], in_=src,
                                                                 allow_slow_non_contiguous=True),
                     writes=[("p_lbraw", i)], stream="dma_io", inc=16)
            t.op("dve", lambda e: e.tensor_tensor(P[:, 56:58], P[:, 68:70], P[:, 66:68], ALU.subtract),
                 reads=[("p_lbraw", 0), ("p_lbraw", 1)], writes=["p_lb"])
            t.op("act", lambda e: e.activation(out=P[:, 56:58], in_=P[:, 56:58], func=AF.Sigmoid),
                 reads=["p_lb"], writes=["p_lb"])
            t.op("dve", lambda e: e.tensor_single_scalar(P[:, 56:58], P[:, 56:58], self.fB, ALU.mult),
                 reads=["p_lb", "role"], writes=["p_lb"])
        t.op("dve", lambda e: e.tensor_scalar(P[:, 58:60], P[:, 56:58], -1.0, 1.0, ALU.mult, ALU.add),
             reads=["p_lb"], writes=["p_oml"])
        t.op("dve", lambda e: e.memset(self.lruw[:], 0.0), writes=["lruw"])
        for gi, nm in enumerate(("lru_wa", "lru_wx")):
            for h in range(4):
                c, hh = h // 2, h % 2
                t.op("pool", lambda e, gi=gi, nm=nm, h=h, c=c, hh=hh: e.dma_start(
                    out=self.lruw[hh * 64:(hh + 1) * 64, gi, c, hh * 64:(hh + 1) * 64], in_=pin[nm][l, h]),
                    writes=["lruw"], stream="dma_io", inc=16)
        t.op("dve", lambda e: e.memset(self.car_sc[:], 0.0), writes=[("car_sc", 0), ("car_sc", 1)])
        t.op("dve", lambda e: e.memset(self.car_lx[:], 0.0), writes=[("car_lx", 0), ("car_lx", 1)])
        t.op("dve", lambda e: e.memset(self.car_lh[:], 0.0), writes=[("car_lh", 0), ("car_lh", 1)])
        t.op("dve", lambda e: e.memset(self.SS[:], 0.0), writes=[("SS", 0), ("SS", 1)])
        if "s5" not in self.stub:
            self.s5_setup(l)

    def evac(self, eng, out_ap, in_ap, reads, writes):
        if eng == "act":
            return self.t.op("act", lambda e: e.activation(out=out_ap, in_=in_ap, func=AF.Copy),
                             reads=reads, writes=writes)
        return self.t.op("dve", lambda e: e.tensor_copy(out_ap, in_ap), reads=reads, writes=writes)

    def mm(self, ps_ap, pskey, pairs, reads):
        def emit(pe):
            n = len(pairs)
            ins = None
            for i, (lh, rh) in enumerate(pairs):
                ins = pe.matmul(ps_ap, lh, rh, start=(i == 0), stop=(i == n - 1))
            return ins
        return self.t.op("pe", emit, reads=reads, writes=[pskey])

    def states(self):
        o = [0]

        def sn(ncol, shape=None):
            ap = self.SNAP[:, o[0]:o[0] + ncol]
            o[0] += ncol
            return ap
        return [
            (self.car_sc[:].rearrange("p a b -> p (a b)"), sn(4), [("car_sc", 0), ("car_sc", 1)]),
            (self.car_lx[:].rearrange("p a b -> p (a b)"), sn(6), [("car_lx", 0), ("car_lx", 1)]),
            (self.car_lh[:, :], sn(2), [("car_lh", 0), ("car_lh", 1)]),
            (self.SS[:, :, 0, :], sn(256).rearrange("p (a b) -> p a b", a=2), [("SS", 0), ("SS", 1)]),
            (self.CARJ[:].rearrange("p a b -> p (a b)"), sn(16), [("CARJ", i) for i in range(8)]),
        ]

    def snapshot(self):
        for st, snp, keys in self.states():
            self.t.op("dve", lambda e, st=st, snp=snp: e.tensor_copy(snp, st), reads=keys, writes=["snap"])

    def restore(self):
        for st, snp, keys in self.states():
            self.t.op("dve", lambda e, st=st: e.tensor_single_scalar(st, st, self.fA, ALU.mult),
                      reads=keys + ["role"], writes=keys)
            self.t.op("dve", lambda e, st=st, snp=snp: e.scalar_tensor_tensor(st, snp, self.fB, st, ALU.mult, ALU.add),
                      reads=keys + ["role", "snap"], writes=keys)

    def xstage(self, mode, nb):
        base = self.S_Y[0] if mode in ("p0", "p1") else self.S_U[0]
        st = self.SL[:, base:base + 8, :].rearrange("p a b -> p (a b)")
        return st[:, 0:nb * D].rearrange("p (b f) -> p b f", b=nb), [self.slk(base + i) for i in range(8)]

    def load_x(self, mode, n, step):
        nb = (n + 127) // 128
        pb = min(n, 128)
        stage, skeys = self.xstage(mode, nb)
        if mode in ("p0", "p1"):
            src = self.meta.rearrange("(b p) f -> p b f", p=pb)
        else:
            xi = min(step, self.n_xtiles - 1)
            src = self.x[xi * NT:(xi + 1) * NT, :].rearrange("(b p) f -> p b f", p=pb)
        self.t.op("pool", lambda e: e.dma_start(out=stage[0:pb], in_=src),
                  writes=skeys, stream="dma_io", inc=16)

    def load_input(self, mode, n, step):
        t = self.t
        H, U = self.S_H, self.S_U
        nb = (n + 127) // 128
        pb = min(n, 128)
        stage, skeys = self.xstage(mode, nb)
        if mode in ("p0", "p1") or step == 0:
            self.load_x(mode, n, step)
        RS = 14
        if mode == "main" and step >= SKEW:
            par = (step - SKEW) % 4
            rsrc = self.recv[par][:, :].rearrange("p (k n) -> p k n", k=KT)
            t.op("pool", lambda e: e.dma_start(out=self.SL[:, U[RS]:U[RS] + 8, :], in_=rsrc),
                 reads=[("recv", par)], writes=[self.slk(U[RS + i]) for i in range(8)], stream="dma_io", inc=16)
        for k in range(KT):
            pi = self.nextps()

            def emit(pe, k=k, pi=pi):
                ins = None
                for b in range(nb):
                    ins = pe.transpose(self.ps[pi][:, b * pb:(b + 1) * pb],
                                       stage[0:pb, b, k * 128:(k + 1) * 128],
                                       self.ident[0:pb, 0:pb])
                return ins
            t.op("pe", emit, reads=skeys + ["ident"], writes=[("ps", pi)])
            hk = self.slk(H[k])
            if mode == "p0":
                self.evac("act" if k % 2 else "dve", self.sl(H[k], n), self.ps[pi][:, 0:n],
                          reads=[("ps", pi)], writes=[hk])
                continue
            if k % 2:
                t.op("act", lambda e, k=k, pi=pi: e.activation(out=self.sl(H[k], n), in_=self.ps[pi][:, 0:n],
                                                               func=AF.Copy, scale=self.fA),
                     reads=[("ps", pi), "role"], writes=[hk])
            else:
                t.op("dve", lambda e, k=k, pi=pi: e.tensor_single_scalar(self.sl(H[k], n), self.ps[pi][:, 0:n],
                                                                         self.fA, ALU.mult),
                     reads=[("ps", pi), "role"], writes=[hk])
            if mode == "p1":
                t.op("dve", lambda e, k=k: e.scalar_tensor_tensor(self.sl(H[k], n), self.HM0[:, k, :], self.fB,
                                                                  self.sl(H[k], n), ALU.mult, ALU.add),
                     reads=[hk, "role", "HM0"], writes=[hk])
        if mode == "main" and step >= SKEW:
            for k in range(KT):
                hk = self.slk(H[k])
                t.op("dve", lambda e, k=k: e.scalar_tensor_tensor(self.sl(H[k], n), self.sl(U[RS + k], n), self.fB,
                                                                  self.sl(H[k], n), ALU.mult, ALU.add),
                     reads=[hk, "role", self.slk(U[RS + k])], writes=[hk])

    def store_output(self, mode, n, step):
        t = self.t
        Z = self.S_Z
        if mode == "p0":
            t.op("act", lambda e: e.activation(out=self.HM0[:, :, :], in_=self.SL[:, Z[0]:Z[0] + 8, 0:NMETA],
                                               func=AF.Copy),
                 reads=[self.slk(i) for i in Z], writes=["HM0"])
            return
        if mode == "p1":
            return
        if True:
            par = step % 2
            rpar = step % 4
            U = self.S_U
            for k in range(KT):
                if k % 2:
                    t.op("act", lambda e, k=k: e.activation(out=self.sl(U[8 + k]), in_=self.sl(Z[k]), func=AF.Copy,
                                                            scale=self.fA),
                         reads=[self.slk(Z[k]), "role"], writes=[self.slk(U[8 + k])])
                else:
                    t.op("dve", lambda e, k=k: e.tensor_single_scalar(self.sl(U[8 + k]), self.sl(Z[k]), self.fA,
                                                                      ALU.mult),
                         reads=[self.slk(Z[k]), "role"], writes=[self.slk(U[8 + k])])
            t.op("pool", lambda e: e.dma_start(out=self.send[par].rearrange("p (k n) -> p k n", k=KT),
                                               in_=self.SL[:, U[8]:U[8] + 8, :]),
                 reads=[self.slk(U[8 + i]) for i in range(8)], writes=[("send", par)], stream="dma_io", inc=16)
            t.op("pool", lambda e: e.collective_compute("AllReduce", ALU.add, replica_groups=self.groups,
                                                        ins=[self.send[par]], outs=[self.recv[rpar]]),
                 reads=[("send", par)], writes=[("recv", rpar)], stream="cc", inc=1)
        if step < SKEW:
            return
        nb = n // 128
        stage = self.SL[:, self.S_Y[0]:self.S_Y[0] + 8, :].rearrange("p a b -> p (a b)")
        stage = stage[:, 0:nb * D].rearrange("p (b f) -> p b f", b=nb)
        for b in range(nb):
            for half in range(2):
                pi = self.nextps()

                def emit(pe, b=b, half=half, pi=pi):
                    ins = None
                    for kk in range(4):
                        k = half * 4 + kk
                        ins = pe.transpose(self.ps[pi][:, kk * 128:(kk + 1) * 128],
                                           self.SL[:, Z[k], b * 128:(b + 1) * 128],
                                           self.ident[:, :])
                    return ins
                t.op("pe", emit, reads=[self.slk(Z[half * 4 + kk]) for kk in range(4)] + ["ident"],
                     writes=[("ps", pi)])
                self.evac("act" if half else "dve", stage[:, b, half * 512:(half + 1) * 512],
                          self.ps[pi][:, :], reads=[("ps", pi)], writes=[self.slk(self.S_Y[2 * b + half])])
        r0 = (step - SKEW) * NT
        dst = self.out[r0:r0 + n, :].rearrange("(b p) f -> p b f", p=128)
        t.op("pool", lambda e: e.dma_start(out=dst, in_=stage),
             reads=[self.slk(i) for i in self.S_Y], stream="dma_io", inc=16)

    def layernorm(self, n, gcol, bcol):
        t = self.t
        Z = self.S_Z
        ps_s = self.nextps()
        ps_q = self.nextps()
        self.mm(self.ps[ps_s][:, 0:n], ("ps", ps_s),
                [(self.ones[:, :], self.sl(Z[m], n)) for m in range(KT)],
                reads=[self.slk(Z[m]) for m in range(KT)] + ["ones"])
        sq = []
        for m in range(KT):
            ti = self.nexttmp()
            t.op("act", lambda e, m=m, ti=ti: e.activation(out=self.tmp[ti][:, 0:n], in_=self.sl(Z[m], n),
                                                          func=AF.Square),
                 reads=[self.slk(Z[m])], writes=[("tmp", ti)])
            first = (m == 0)
            last = (m == KT - 1)
            t.op("pe", lambda e, ti=ti, first=first, last=last: e.matmul(
                self.ps[ps_q][:, 0:n], self.ones[:, :], self.tmp[ti][:, 0:n], start=first, stop=last),
                reads=[("tmp", ti), "ones"], writes=[("ps", ps_q)])
        mi = self.nexttmp()
        mean = self.tmp[mi]
        mk = ("tmp", mi)
        ri = self.nexttmp()
        rstd = self.tmp[ri]
        rk = ("tmp", ri)
        t.op("dve", lambda e: e.tensor_single_scalar(mean[:, 0:n], self.ps[ps_s][:, 0:n], 1.0 / D, ALU.mult),
             reads=[("ps", ps_s)], writes=[mk])
        t.op("dve", lambda e: e.tensor_tensor(rstd[:, 0:n], mean[:, 0:n], mean[:, 0:n], ALU.mult),
             reads=[mk], writes=[rk])
        t.op("dve", lambda e: e.scalar_tensor_tensor(rstd[:, 0:n], self.ps[ps_q][:, 0:n], 1.0 / D,
                                                     rstd[:, 0:n], ALU.mult, ALU.subtract),
             reads=[("ps", ps_q), rk], writes=[rk])
        t.op("act", lambda e: e.activation(out=rstd[:, 0:n], in_=rstd[:, 0:n], func=AF.Sqrt, bias=EPS, scale=1.0),
             reads=[rk], writes=[rk])
        t.op("dve", lambda e: e.reciprocal(rstd[:, 0:n], rstd[:, 0:n]), reads=[rk], writes=[rk])
        for m in range(KT):
            zk = self.slk(Z[m])
            t.op("dve", lambda e, m=m: e.tensor_tensor(self.sl(Z[m], n), self.sl(Z[m], n), mean[:, 0:n], ALU.subtract),
                 reads=[zk, mk], writes=[zk])
            t.op("dve", lambda e, m=m: e.tensor_tensor(self.sl(Z[m], n), self.sl(Z[m], n), rstd[:, 0:n], ALU.mult),
                 reads=[zk, rk], writes=[zk])
            t.op("act", lambda e, m=m: e.activation(out=self.sl(Z[m], n), in_=self.sl(Z[m], n), func=AF.Identity,
                                                    scale=self.PRM[:, gcol + m:gcol + m + 1],
                                                    bias=self.PRM[:, bcol + m:bcol + m + 1]),
                 reads=[zk, ("prm", gcol // 8), ("prm", bcol // 8)], writes=[zk])

    def bfslab(self, slab_idx, half, n):
        return self.SL[:, slab_idx, :].bitcast(BF16)[:, half * NT:half * NT + n]

    def hsrc(self, k, n0, n1):
        if self.bf:
            return self.tmp[k // 2][:, :].bitcast(BF16)[:, (k % 2) * NT + n0:(k % 2) * NT + n1], ("tmp", k // 2)
        return self.SL[:, self.S_H[k], n0:n1], self.slk(self.S_H[k])

    def ysl(self, k, n):
        if self.bf:
            return self.bfslab(self.S_Y[k // 2], k % 2, n)
        return self.sl(self.S_Y[k], n)

    def yk(self, k):
        return self.slk(self.S_Y[k // 2]) if self.bf else self.slk(self.S_Y[k])

    def zsrc(self, k, n):
        if self.bf:
            return self.bfslab(self.S_U[12 + k // 2], k % 2, n), self.slk(self.S_U[12 + k // 2])
        return self.sl(self.S_Z[k], n), self.slk(self.S_Z[k])

    def actsl(self, i, n):
        if self.bf:
            return self.bfslab(self.S_U[i // 2], i % 2, n), self.slk(self.S_U[i // 2])
        return self.sl(self.S_U[i], n), self.slk(self.S_U[i])

    def tile_pass(self, l, mode, n, step=0):
        t = self.t
        H, Y, Z, U = self.S_H, self.S_Y, self.S_Z, self.S_U
        self.load_input(mode, n, step)
        if self.bf:
            for k in range(KT):
                dst, dk = self.hsrc(k, 0, n)
                t.op("act", lambda e, k=k, dst=dst: e.activation(out=dst, in_=self.sl(H[k], n), func=AF.Copy),
                     reads=[self.slk(H[k])], writes=[dk])
        hsr = [self.hsrc(k, 0, n) for k in range(KT)]
        hreads = list({kk for _, kk in hsr})
        for c in range(5):
            slot = self.wacquire("win")
            if mode == "p1":
                self.cast_store(slot, "win")
            wv = self.wview(slot, 0, KT, 512)
            for jj in range(4):
                j = c * 4 + jj
                if j == 12 and "hg" not in self.stub:
                    nb = (n + 127) // 128
                    VT = self.SL[:, U[12]:U[12] + 2, :].rearrange("p a b -> p (a b)").rearrange(
                        "p (b f) -> p b f", f=256)
                    for blk in range(nb):
                        pb = min(128, n - blk * 128)
                        pi = self.nextps()
                        self.mm(self.ps[pi][0:pb, 0:256], ("ps", pi),
                                [(self.hsrc(k, blk * 128, blk * 128 + pb)[0], wv[:, k, 0:256]) for k in range(KT)],
                                reads=hreads + self.wkeys(slot))
                        self.evac("act" if blk % 2 else "dve", VT[0:pb, blk, :], self.ps[pi][0:pb, 0:256],
                                  reads=[("ps", pi)], writes=[self.slk(U[12]), self.slk(U[13])])
                    continue
                if j == 13 and "hg" not in self.stub:
                    continue
                pi = self.nextps()
                self.mm(self.ps[pi][:, 0:n], ("ps", pi),
                        [(wv[:, k, jj * 128:(jj + 1) * 128], hsr[k][0]) for k in range(KT)],
                        reads=hreads + self.wkeys(slot))
                self.evac("act" if j % 2 else "dve", self.sl(U[j], n), self.ps[pi][:, 0:n],
                          reads=[("ps", pi)], writes=[self.slk(U[j])])
            self.wprefetch()
        if mode == "main":
            self.dbg(2)
        self.mixers(l, 0, 0, n)
        if mode == "main":
            self.dbg(3)
        yreads = list({self.yk(k) for k in range(KT)})
        for c in range(2):
            slot = self.wacquire("wout")
            if mode == "p1":
                self.cast_store(slot, "wout")
            wv = self.wview(slot, 0, KT, 512)
            for jj in range(4):
                m = c * 4 + jj
                pi = self.nextps()
                self.mm(self.ps[pi][:, 0:n], ("ps", pi),
                        [(wv[:, k, jj * 128:(jj + 1) * 128], self.ysl(k, n)) for k in range(KT)],
                        reads=yreads + self.wkeys(slot))
                t.op("dve", lambda e, m=m, pi=pi: e.scalar_tensor_tensor(
                    self.sl(Z[m], n), self.sl(H[m], n), ALPHA, self.ps[pi][:, 0:n], ALU.mult, ALU.add),
                    reads=[("ps", pi), self.slk(H[m])], writes=[self.slk(Z[m])])
            self.wprefetch()
        self.layernorm(n, 0, 8)
        if self.bf:
            for k in range(KT):
                dst, dk = self.zsrc(k, n)
                t.op("dve", lambda e, k=k, dst=dst: e.tensor_copy(dst, self.sl(Z[k], n)),
                     reads=[self.slk(Z[k])], writes=[dk])
        if mode == "main":
            self.dbg(4)
        zsr = [self.zsrc(k, n) for k in range(KT)]
        zreads = list({kk for _, kk in zsr})
        for c in range(11):
            slot = self.wacquire("f1")
            if mode == "p1":
                self.cast_store(slot, "f1")
            wg = self.wview(slot, 0, KT, 256)
            wu = self.wview(slot, KT * 256, KT, 256)
            for jj in range(2):
                i = c * 2 + jj
                pg = self.nextps()
                pu = self.nextps()
                self.mm(self.ps[pg][:, 0:n], ("ps", pg),
                        [(wg[:, k, jj * 128:(jj + 1) * 128], zsr[k][0]) for k in range(KT)],
                        reads=zreads + self.wkeys(slot))
                self.mm(self.ps[pu][:, 0:n], ("ps", pu),
                        [(wu[:, k, jj * 128:(jj + 1) * 128], zsr[k][0]) for k in range(KT)],
                        reads=zreads + self.wkeys(slot))
                tk = self.nexttmp()
                t.op("act", lambda e, tk=tk, pg=pg: e.activation(out=self.tmp[tk][:, 0:n], in_=self.ps[pg][:, 0:n],
                                                                func=AF.Silu),
                     reads=[("ps", pg)], writes=[("tmp", tk)])
                adst, ak = self.actsl(i, n)
                t.op("dve", lambda e, tk=tk, pu=pu, adst=adst: e.tensor_tensor(
                    adst, self.tmp[tk][:, 0:n], self.ps[pu][:, 0:n], ALU.mult),
                    reads=[("tmp", tk), ("ps", pu)], writes=[ak])
            self.wprefetch()
        asr = [self.actsl(k, n) for k in range(FT)]
        ureads = list({kk for _, kk in asr})
        for m in range(KT):
            slot = self.wacquire("f2")
            if mode == "p1":
                self.cast_store(slot, "f2")
            wv = self.wview(slot, 0, FT, 128)
            pi = self.nextps()
            self.mm(self.ps[pi][:, 0:n], ("ps", pi),
                    [(wv[:, k, :], asr[k][0]) for k in range(FT)],
                    reads=ureads + self.wkeys(slot))
            t.op("dve", lambda e, m=m, pi=pi: e.scalar_tensor_tensor(
                self.sl(Z[m], n), self.sl(Z[m], n), ALPHA, self.ps[pi][:, 0:n], ALU.mult, ALU.add),
                reads=[("ps", pi), self.slk(Z[m])], writes=[self.slk(Z[m])])
            self.wprefetch()
        if mode == "main":
            self.dbg(5)
            if step + 1 < self.nsteps:
                self.load_x("main", NT, step + 1)
        self.layernorm(n, 16, 24)
        if mode == "main":
            self.dbg(6)
        self.store_output(mode, n, step)

    def prm(self, col):
        return self.PRM[:, col:col + 1]

    def conv_acc(self, acc, x, carry, wcol, K, n, xk, acck, cark, wkey, first_bias=None):
        t = self.t
        if first_bias is None:
            t.op("dve", lambda e: e.tensor_single_scalar(acc[:, 0:n], x[:, 0:n], self.prm(wcol + K - 1), ALU.mult),
                 reads=[xk, wkey], writes=[acck])
        else:
            t.op("dve", lambda e: e.tensor_scalar(acc[:, 0:n], x[:, 0:n], self.prm(wcol + K - 1), self.prm(first_bias),
                                                  ALU.mult, ALU.add),
                 reads=[xk, wkey, "p_lcb"], writes=[acck])
        for k in range(K - 1):
            sh = K - 1 - k
            t.op("dve", lambda e, sh=sh, k=k: e.scalar_tensor_tensor(
                acc[:, sh:n], x[:, 0:n - sh], self.prm(wcol + k), acc[:, sh:n], ALU.mult, ALU.add),
                reads=[xk, wkey, acck], writes=[acck])
            t.op("dve", lambda e, sh=sh, k=k: e.scalar_tensor_tensor(
                acc[:, 0:sh], carry[:, K - 1 - sh:K - 1], self.prm(wcol + k), acc[:, 0:sh], ALU.mult, ALU.add),
                reads=[cark, wkey, acck], writes=[acck])
        t.op("dve", lambda e: e.tensor_copy(carry[:, 0:K - 1], x[:, n - (K - 1):n]),
             reads=[xk], writes=[cark])

    def mixers(self, l, ti, t0, n):
        t = self.t
        Y, U, Z = self.S_Y, self.S_U, self.S_Z
        if "s5" in self.stub:
            for k in range(2):
                self.evac("act" if k % 2 else "dve", self.ysl(k, n), self.sl(U[k], n),
                          reads=[self.slk(U[k])], writes=[self.yk(k)])
        else:
            self.mix_s5(l, ti, n)
        if "sc" in self.stub:
            for k in range(2):
                self.evac("act", self.ysl(2 + k, n), self.sl(U[2 + k], n),
                          reads=[self.slk(U[2 + k])], writes=[self.yk(2 + k)])
        else:
            self.mix_sc(n)
        if "hg" in self.stub:
            for k in range(2):
                self.evac("dve", self.ysl(4 + k, n), self.sl(U[4 + k], n),
                          reads=[self.slk(U[4 + k])], writes=[self.yk(4 + k)])
        else:
            self.mix_hg(n)
        if "lru" in self.stub:
            for k in range(2):
                self.evac("act", self.ysl(6 + k, n), self.sl(U[6 + k], n),
                          reads=[self.slk(U[6 + k])], writes=[self.yk(6 + k)])
        else:
            self.mix_lru(n)

    def mix_sc(self, n):
        t = self.t
        Y, U, Z = self.S_Y, self.S_U, self.S_Z
        for c in range(2):
            hs, bs, cs = U[2 + c], U[4 + c], U[6 + c]
            acc = Z[c]
            t.op("dve", lambda e: e.tensor_tensor(self.sl(hs, n), self.sl(hs, n), self.sl(cs, n), ALU.mult),
                 reads=[self.slk(hs), self.slk(cs)], writes=[self.slk(hs)])
            self.conv_acc(self.SL[:, acc, :], self.SL[:, hs, :], self.car_sc[:, c, :], 32 + 3 * c, 3, n,
                          self.slk(hs), self.slk(acc), ("car_sc", c), "p_scw")
            t.op("dve", lambda e: e.tensor_tensor(self.ysl(2 + c, n), self.sl(acc, n), self.sl(bs, n), ALU.mult),
                 reads=[self.slk(acc), self.slk(bs)], writes=[self.yk(2 + c)])

    def mix_lru(self, n):
        t = self.t
        Y, U, Z = self.S_Y, self.S_U, self.S_Z
        for c in range(2):
            xs, ys = U[16 + c], U[18 + c]
            xc, ga, gx, aa = Z[0], Z[1], Z[2], Z[3]
            k = self.slk
            self.conv_acc(self.SL[:, xc, :], self.SL[:, xs, :], self.car_lx[:, c, :], 38 + 4 * c, 4, n,
                          k(xs), k(xc), ("car_lx", c), "p_lcw", first_bias=46 + c)
            pa, px = self.nextps(), self.nextps()
            self.mm(self.ps[pa][:, 0:n], ("ps", pa), [(self.lruw[:, 0, c, :], self.sl(xc, n))], reads=[k(xc), "lruw"])
            self.mm(self.ps[px][:, 0:n], ("ps", px), [(self.lruw[:, 1, c, :], self.sl(xc, n))], reads=[k(xc), "lruw"])
            t.op("act", lambda e: e.activation(out=self.sl(ga, n), in_=self.ps[pa][:, 0:n], func=AF.Sigmoid,
                                               bias=self.prm(48 + c), scale=1.0),
                 reads=[("ps", pa), "p_lba"], writes=[k(ga)])
            t.op("act", lambda e: e.activation(out=self.sl(gx, n), in_=self.ps[px][:, 0:n], func=AF.Sigmoid,
                                               bias=self.prm(50 + c), scale=1.0),
                 reads=[("ps", px), "p_lbx"], writes=[k(gx)])
            t.op("act", lambda e: e.activation(out=self.sl(aa, n), in_=self.sl(ga, n), func=AF.Exp,
                                               scale=self.prm(52 + c)),
                 reads=[k(ga), "p_cp"], writes=[k(aa)])
            t.op("act", lambda e: e.activation(out=self.sl(ga, n), in_=self.sl(ga, n), func=AF.Exp,
                                               scale=self.prm(54 + c)),
                 reads=[k(ga), "p_cp2"], writes=[k(ga)])
            t.op("act", lambda e: e.activation(out=self.sl(ga, n), in_=self.sl(ga, n), func=AF.Sqrt,
                                               scale=-1.0, bias=1.0),
                 reads=[k(ga)], writes=[k(ga)])
            t.op("dve", lambda e: e.tensor_tensor(self.sl(gx, n), self.sl(gx, n), self.sl(xc, n), ALU.mult),
                 reads=[k(gx), k(xc)], writes=[k(gx)])
            t.op("dve", lambda e: e.tensor_tensor(self.sl(gx, n), self.sl(gx, n), self.sl(ga, n), ALU.mult),
                 reads=[k(gx), k(ga)], writes=[k(gx)])
            t.op("dve", lambda e: e.tensor_tensor_scan(self.sl(xc, n), self.sl(aa, n), self.sl(gx, n),
                                                       self.car_lh[:, c:c + 1], ALU.mult, ALU.add),
                 reads=[k(aa), k(gx), ("car_lh", c)], writes=[k(xc)])
            t.op("dve", lambda e: e.tensor_copy(self.car_lh[:, c:c + 1], self.SL[:, xc, n - 1:n]),
                 reads=[k(xc)], writes=[("car_lh", c)])
            t.op("act", lambda e: e.activation(out=self.sl(aa, n), in_=self.sl(ys, n), func=AF.Gelu_apprx_tanh),
                 reads=[k(ys)], writes=[k(aa)])
            t.op("dve", lambda e: e.tensor_tensor(self.ysl(6 + c, n), self.sl(xc, n), self.sl(aa, n), ALU.mult),
                 reads=[k(xc), k(aa)], writes=[self.yk(6 + c)])

    def mix_hg(self, n):
        t = self.t
        Y, U, Z = self.S_Y, self.S_U, self.S_Z
        k = self.slk
        CL = 64 if n >= 64 else n
        nch = n // CL
        mid = CL // 2
        nb = (n + 127) // 128
        VT = self.SL[:, U[12]:U[12] + 2, :].rearrange("p a b -> p (a b)").rearrange("p (b f) -> p b f", f=256)
        vtk = [k(U[12]), k(U[13])]
        X = [Z[0], Z[1], Z[2], Z[3], Z[4], Z[5], Z[6], Z[7]]
        c3 = lambda ap: ap.rearrange("p (c s) -> p c s", s=CL)
        for pr in range(2):
            qs, fs, gs = U[8 + pr], U[10 + pr], U[14 + pr]
            g_, b_, d_, e1, e2, e3, qp, ktk = X
            t.op("act", lambda e: e.activation(out=self.sl(qs, n), in_=self.sl(qs, n), func=AF.Silu),
                 reads=[k(qs)], writes=[k(qs)])
            t.op("act", lambda e: e.activation(out=self.sl(fs, n), in_=self.sl(fs, n), func=AF.Sigmoid),
                 reads=[k(fs)], writes=[k(fs)])
            t.op("dve", lambda e: e.tensor_scalar(self.sl(fs, n), self.sl(fs, n), self.prm(58 + pr), self.prm(56 + pr),
                                                  ALU.mult, ALU.add),
                 reads=[k(fs), "p_lb", "p_oml"], writes=[k(fs)])
            t.op("act", lambda e: e.activation(out=self.sl(g_, n), in_=self.sl(fs, n), func=AF.Ln),
                 reads=[k(fs)], writes=[k(g_)])
            t.op("dve", lambda e: e.tensor_scalar(self.sl(fs, n), self.sl(fs, n), -1.0, 1.0, ALU.mult, ALU.add),
                 reads=[k(fs), k(g_)], writes=[k(fs)])
            t.op("dve", lambda e: e.tensor_tensor_scan(self.sl(b_, n), self.cmask[:, 0:n], self.sl(g_, n), 0.0,
                                                       ALU.mult, ALU.add),
                 reads=[k(g_), "cmask"], writes=[k(b_)])
            b3 = c3(self.sl(b_, n))
            t.op("dve", lambda e: e.tensor_tensor(c3(self.sl(d_, n)), b3, b3[:, :, mid:mid + 1].to_broadcast([128, nch, CL]),
                                                  ALU.subtract),
                 reads=[k(b_)], writes=[k(d_)])
            t.op("act", lambda e: e.activation(out=self.sl(e1, n), in_=self.sl(d_, n), func=AF.Exp),
                 reads=[k(d_)], writes=[k(e1)])
            t.op("act", lambda e: e.activation(out=self.sl(e2, n), in_=self.sl(d_, n), func=AF.Exp, scale=-1.0),
                 reads=[k(d_)], writes=[k(e2)])
            t.op("act", lambda e: e.activation(out=self.sl(e3, n), in_=self.sl(b_, n), func=AF.Exp),
                 reads=[k(b_)], writes=[k(e3)])
            t.op("dve", lambda e: e.tensor_tensor(c3(self.sl(d_, n)), b3, b3[:, :, CL - 1:CL].to_broadcast([128, nch, CL]),
                                                  ALU.subtract),
                 reads=[k(b_), k(e1), k(e2)], writes=[k(d_)])
            t.op("act", lambda e: e.activation(out=self.sl(d_, n), in_=self.sl(d_, n), func=AF.Exp, scale=-1.0),
                 reads=[k(d_)], writes=[k(d_)])
            t.op("dve", lambda e: e.scalar_tensor_tensor(self.sl(e1, n), self.sl(qs, n), 0.125, self.sl(e1, n),
                                                         ALU.mult, ALU.mult),
                 reads=[k(qs), k(e1)], writes=[k(e1)])
            t.op("dve", lambda e: e.tensor_tensor(self.sl(e2, n), self.sl(fs, n), self.sl(e2, n), ALU.mult),
                 reads=[k(fs), k(e2)], writes=[k(e2)])
            t.op("dve", lambda e: e.scalar_tensor_tensor(self.sl(qp, n), self.sl(qs, n), 0.125, self.sl(e3, n),
                                                         ALU.mult, ALU.mult),
                 reads=[k(qs), k(e3)], writes=[k(qp)])
            t.op("dve", lambda e: e.tensor_tensor(self.sl(d_, n), self.sl(fs, n), self.sl(d_, n), ALU.mult),
                 reads=[k(fs), k(d_)], writes=[k(d_)])
            KTv = self.SL[:, ktk, :].rearrange("p (b f) -> p b f", f=128)
            for blk in range(nb):
                pb = min(128, n - blk * 128)
                pi = self.nextps()
                t.op("pe", lambda e, blk=blk, pb=pb, pi=pi: e.transpose(
                    self.ps[pi][0:pb, 0:128], self.SL[:, d_, blk * 128:blk * 128 + pb], self.ident[:, :]),
                    reads=[k(d_), "ident"], writes=[("ps", pi)])
                self.evac("act", KTv[0:pb, blk, :], self.ps[pi][0:pb, 0:128], reads=[("ps", pi)], writes=[k(ktk)])
            for c in range(nch):
                blk, r0 = (c * CL) // 128, (c * CL) % 128
                pi = self.nextps()
                self.mm(self.ps[pi][:, 0:128], ("ps", pi),
                        [(KTv[r0:r0 + CL, blk, :], VT[r0:r0 + CL, blk, pr * 128:(pr + 1) * 128])],
                        reads=[k(ktk)] + vtk)
                for hh in range(2):
                    rs = slice(hh * 64, (hh + 1) * 64)
                    t.op("dve", lambda e, c=c, rs=rs, pi=pi: e.scalar_tensor_tensor(
                        self.SS[rs, pr, c + 1, rs], self.SS[rs, pr, c, rs],
                        self.SL[rs, e3, (c + 1) * CL - 1:(c + 1) * CL], self.ps[pi][rs, rs],
                        ALU.mult, ALU.add),
                        reads=[("ps", pi), k(e3), ("SS", pr)], writes=[("SS", pr)])
            OPS = self.ps[6]
            for blk in range(nb):
                pb = min(128, n - blk * 128)
                bs = slice(blk * 128, blk * 128 + pb)
                smi = self.smn
                self.smn = (self.smn + 1) % 2
                SMv = self.SM[smi]
                for hh in range(2):
                    rs = slice(hh * 64, (hh + 1) * 64)
                    pi = self.nextps()
                    t.op("pe", lambda e, pi=pi, rs=rs, bs=bs, pb=pb, hh=hh: e.matmul(
                        self.ps[pi][0:pb, 0:pb], self.SL[rs, e2, bs], self.SL[rs, e1, bs],
                        start=True, stop=True, tile_position=(hh * 64, 0)),
                        reads=[k(e1), k(e2)], writes=[("ps", pi)])
                    t.op("dve", lambda e, pi=pi, pb=pb, hh=hh, SMv=SMv: e.tensor_tensor(
                        SMv[0:pb, hh, 0:pb], self.ps[pi][0:pb, 0:pb], self.mask2[0:pb, 0:pb], ALU.mult),
                        reads=[("ps", pi), "mask2"], writes=[("SM", smi, hh)])

                def emit_o(pe, blk=blk, bs=bs, pb=pb, SMv=SMv):
                    ins = None
                    for hh in range(2):
                        rs = slice(hh * 64, (hh + 1) * 64)
                        pe.matmul(OPS[rs, bs], VT[0:pb, blk, pr * 128 + hh * 64:pr * 128 + (hh + 1) * 64],
                                  SMv[0:pb, hh, 0:pb], start=True, stop=False, tile_position=(0, hh * 64))
                    for c in range(blk * 128 // CL, (blk * 128 + pb) // CL):
                        cs = slice(c * CL, (c + 1) * CL)
                        ins = pe.matmul(OPS[:, cs], self.SS[:, pr, c, :], self.SL[:, qp, cs],
                                        start=False, stop=True)
                    return ins
                t.op("pe", emit_o, reads=[("SM", smi, 0), ("SM", smi, 1), ("SS", pr), k(qp)] + vtk,
                     writes=[("ps", 6)])
            for hh in range(2):
                rs = slice(hh * 64, (hh + 1) * 64)
                t.op("dve", lambda e, rs=rs: e.tensor_copy(self.SS[rs, pr, 0, rs], self.SS[rs, pr, nch, rs]),
                     reads=[("SS", pr)], writes=[("SS", pr)])
            osq, rst = g_, b_
            t.op("act", lambda e: e.activation(out=self.sl(osq, n), in_=OPS[:, 0:n], func=AF.Square),
                 reads=[("ps", 6)], writes=[k(osq)])
            pm = self.nextps()
            self.mm(self.ps[pm][:, 0:n], ("ps", pm), [(self.bd64[:, :], self.sl(osq, n))], reads=[k(osq), "bd64"])
            t.op("act", lambda e: e.activation(out=self.sl(rst, n), in_=self.ps[pm][:, 0:n], func=AF.Sqrt,
                                               bias=EPS, scale=1.0),
                 reads=[("ps", pm)], writes=[k(rst)])
            t.op("dve", lambda e: e.reciprocal(self.sl(rst, n), self.sl(rst, n)), reads=[k(rst)], writes=[k(rst)])
            t.op("dve", lambda e: e.tensor_tensor(self.sl(rst, n), self.sl(rst, n), OPS[:, 0:n], ALU.mult),
                 reads=[k(rst), ("ps", 6)], writes=[k(rst)])
            t.op("act", lambda e: e.activation(out=self.sl(gs, n), in_=self.sl(gs, n), func=AF.Silu),
                 reads=[k(gs)], writes=[k(gs)])
            t.op("dve", lambda e: e.scalar_tensor_tensor(self.ysl(4 + pr, n), self.sl(rst, n), self.prm(60 + pr),
                                                         self.sl(gs, n), ALU.mult, ALU.mult),
                 reads=[k(rst), k(gs), "p_gn"], writes=[self.yk(4 + pr)])

    def s5_setup(self, l):
        t = self.t
        pin = self.pin
        Q = self.Q
        S = self.S5S
        KEY = "s5setup"
        col = [0]

        def alloc(ncol):
            c0 = col[0]
            col[0] += ncol
            assert col[0] <= 2176
            return S[:, c0:c0 + ncol]

        def dv(fn):
            t.op("dve", fn, reads=[KEY], writes=[KEY])

        def ac(fn):
            t.op("act", fn, reads=[KEY], writes=[KEY])

        def dma(out, in_):
            t.op("pool", lambda e: e.dma_start(out=out, in_=in_, allow_slow_non_contiguous=True),
                 reads=[KEY], writes=[KEY], stream="dma_io", inc=16)

        TT = lambda o, a, b, op: dv(lambda e: e.tensor_tensor(o, a, b, op))
        TS = lambda o, a, sc, op: dv(lambda e: e.tensor_single_scalar(o, a, sc, op))

        def cmul(o_r, o_i, a_r, a_i, b_r, b_i, t1, t2):
            TT(t1, a_r, b_r, ALU.mult)
            TT(t2, a_i, b_i, ALU.mult)
            TT(o_r, t1, t2, ALU.subtract)
            TT(t1, a_r, b_i, ALU.mult)
            TT(t2, a_i, b_r, ALU.mult)
            TT(o_i, t1, t2, ALU.add)

        t.op("dve", lambda e: e.memset(S[:, 0:8], 0.0), reads=[KEY], writes=[KEY] + self.S5S_keys)
        V = lambda: alloc(8)
        LR, LI, DT, X, TH, Cc, Ss, T1, T2, T3, AR, AI, WR, WI, MM, IAR, IAI, RI = [V() for _ in range(18)]
        lre = pin["s5_lam_re"][l]
        lim = pin["s5_lam_im"][l]
        ldt = pin["s5_log_dt"][l]
        dma(LR, bass.AP(lre.tensor, lre.offset, [[1, 128], [128, 8]]))
        dma(LI, bass.AP(lim.tensor, lim.offset, [[1, 128], [128, 8]]))
        for gl in range(2):
            dma(DT[gl * 64:(gl + 1) * 64, :], bass.AP(ldt.tensor, ldt.offset + gl, [[0, 64], [2, 8]]))
        ac(lambda e: e.activation(out=DT, in_=DT, func=AF.Exp))
        TT(X, LR, DT, ALU.mult)
        TT(TH, LI, DT, ALU.mult)
        ac(lambda e: e.activation(out=Ss, in_=TH, func=AF.Sin, scale=1.0 / 16))
        TS(T3, TH, 1.0 / 16, ALU.mult)
        TS(T3, T3, math.pi / 2, ALU.add)
        ac(lambda e: e.activation(out=Cc, in_=T3, func=AF.Sin))
        for _ in range(4):
            TT(T1, Cc, Cc, ALU.mult)
            TT(T2, Ss, Ss, ALU.mult)
            TT(T3, Cc, Ss, ALU.mult)
            TT(Cc, T1, T2, ALU.subtract)
            TS(Ss, T3, 2.0, ALU.mult)
        ac(lambda e: e.activation(out=T3, in_=X, func=AF.Exp))
        TT(AR, T3, Cc, ALU.mult)
        TT(AI, T3, Ss, ALU.mult)
        ac(lambda e: e.activation(out=self.RR[:, :], in_=X, func=AF.Exp, scale=float(Q)))
        dv(lambda e: e.reciprocal(RI, self.RR[:, :]))
        TT(T1, T3, T3, ALU.mult)
        dv(lambda e: e.reciprocal(T1, T1))
        TT(IAR, AR, T1, ALU.mult)
        TT(IAI, AI, T1, ALU.mult)
        TS(IAI, IAI, -1.0, ALU.mult)
        TS(T3, AR, -1.0, ALU.add)
        TT(T1, LR, LR, ALU.mult)
        TT(T2, LI, LI, ALU.mult)
        TT(MM, T1, T2, ALU.add)
        dv(lambda e: e.reciprocal(MM, MM))
        TT(T1, T3, LR, ALU.mult)
        TT(T2, AI, LI, ALU.mult)
        TT(WR, T1, T2, ALU.add)
        TT(WR, WR, MM, ALU.mult)
        TT(T1, AI, LR, ALU.mult)
        TT(T2, T3, LI, ALU.mult)
        TT(WI, T1, T2, ALU.subtract)
        TT(WI, WI, MM, ALU.mult)
        PWr = alloc(8 * (Q + 1)).rearrange("p (a s) -> p a s", s=Q + 1)
        PWi = alloc(8 * (Q + 1)).rearrange("p (a s) -> p a s", s=Q + 1)
        IPr = alloc(8 * Q).rearrange("p (a s) -> p a s", s=Q)
        IPi = alloc(8 * Q).rearrange("p (a s) -> p a s", s=Q)
        dv(lambda e: e.memset(PWr[:, :, 0], 1.0))
        dv(lambda e: e.memset(PWi[:, :, 0], 0.0))
        dv(lambda e: e.memset(IPr[:, :, 0], 1.0))
        dv(lambda e: e.memset(IPi[:, :, 0], 0.0))
        for sidx in range(Q):
            cmul(PWr[:, :, sidx + 1], PWi[:, :, sidx + 1], PWr[:, :, sidx], PWi[:, :, sidx], AR, AI, T1, T2)
        for sidx in range(Q - 1):
            cmul(IPr[:, :, sidx + 1], IPi[:, :, sidx + 1], IPr[:, :, sidx], IPi[:, :, sidx], IAR, IAI, T1, T2)
        FBr = alloc(8 * Q).rearrange("p (a s) -> p a s", s=Q)
        FBi = alloc(8 * Q).rearrange("p (a s) -> p a s", s=Q)
        for sidx in range(Q):
            cmul(FBr[:, :, sidx], FBi[:, :, sidx], IPr[:, :, sidx], IPi[:, :, sidx], WR, WI, T1, T2)
        TBr, TBi = self.TB[:, :, 0, :], self.TB[:, :, 1, :]
        dv(lambda e: e.memset(TBr[:, :, 0], 1.0))
        dv(lambda e: e.memset(TBi[:, :, 0], 0.0))
        TT(TBr[:, :, 1], PWr[:, :, Q], RI, ALU.mult)
        TT(TBi[:, :, 1], PWi[:, :, Q], RI, ALU.mult)
        big = lambda: alloc(256).rearrange("p (a h) -> p a h", h=32)
        Br, Bi, Sr, Si, U1, U2 = [big() for _ in range(6)]
        TWs = (U1.rearrange("p a h -> p (a h)"), U2.rearrange("p a h -> p (a h)"))
        tw = lambda i, m: TWs[i][:, 0:8 * m].rearrange("p (a c) -> p a c", c=m)
        m = 1
        while m < 64:
            if m >= 2:
                h = m // 2
                cmul(TBr[:, :, m], TBi[:, :, m], TBr[:, :, h], TBi[:, :, h], TBr[:, :, h], TBi[:, :, h], T1, T2)
            if m >= 2:
                br = TBr[:, :, m:m + 1].to_broadcast([128, 8, m - 1])
                bi = TBi[:, :, m:m + 1].to_broadcast([128, 8, m - 1])
                cmul(TBr[:, :, m + 1:2 * m], TBi[:, :, m + 1:2 * m], TBr[:, :, 1:m], TBi[:, :, 1:m], br, bi,
                     tw(0, m - 1), tw(1, m - 1))
            m *= 2
        cmul(TBr[:, :, 64], TBi[:, :, 64], TBr[:, :, 32], TBi[:, :, 32], TBr[:, :, 32], TBi[:, :, 32], T1, T2)
        rb = self.RR[:, :].unsqueeze(2).to_broadcast([128, 8, 64])
        TT(self.TA[:, :, 0, :], TBr[:, :, 0:64], rb, ALU.mult)
        TT(self.TA[:, :, 1, :], TBi[:, :, 0:64], rb, ALU.mult)
        TS(self.TA[:, :, 1, :], self.TA[:, :, 1, :], -1.0, ALU.mult)
        for ap_ in (Br, Bi):
            dv(lambda e, ap_=ap_: e.memset(ap_, 0.0))
        for g in range(16):
            P_, gl = g // 2, g % 2
            dma(Br[gl * 64:(gl + 1) * 64, P_, gl * 16:(gl + 1) * 16], pin["s5_b_re"][l, g])
            dma(Bi[gl * 64:(gl + 1) * 64, P_, gl * 16:(gl + 1) * 16], pin["s5_b_im"][l, g])
        for sidx in range(Q):
            fr = FBr[:, :, sidx:sidx + 1].to_broadcast([128, 8, 32])
            fi = FBi[:, :, sidx:sidx + 1].to_broadcast([128, 8, 32])
            cmul(Sr, Si, Br, Bi, fr, fi, U1, U2)
            for ri, src in enumerate((Sr, Si)):
                for half in range(2):
                    pi = self.nextps()
                    t.op("pe", lambda e, src=src, half=half, pi=pi: e.transpose(
                        self.ps[pi][:, 0:128], src[:, half * 4:(half + 1) * 4, :], self.ident[:, :]),
                        reads=[KEY, "ident"], writes=[("ps", pi)])
                    t.op("act", lambda e, half=half, sidx=sidx, ri=ri, pi=pi: e.activation(
                        out=self.BsT[:, half, sidx, ri, :], in_=self.ps[pi][:, 0:128], func=AF.Copy),
                        reads=[("ps", pi)], writes=["BsT"])
        CNr = Br.rearrange("p a h -> p (a h)").rearrange("p (a q) -> p a q", q=128)
        CNi = Bi.rearrange("p a h -> p (a h)").rearrange("p (a q) -> p a q", q=128)
        for ap_ in (CNr, CNi):
            dv(lambda e, ap_=ap_: e.memset(ap_, 0.0))
        for g in range(16):
            P_, gl = g // 2, g % 2
            half, j = P_ // 4, P_ % 4
            r0 = 32 * j + 16 * gl
            dma(CNr[r0:r0 + 16, half, gl * 64:(gl + 1) * 64], pin["s5_c_re"][l, g])
            dma(CNi[r0:r0 + 16, half, gl * 64:(gl + 1) * 64], pin["s5_c_im"][l, g])
        for src, dst in ((CNr, Sr), (CNi, Si)):
            for half in range(2):
                pi = self.nextps()
                t.op("pe", lambda e, src=src, half=half, pi=pi: e.transpose(
                    self.ps[pi][:, 0:128], src[:, half, :], self.ident[:, :]),
                    reads=[KEY, "ident"], writes=[("ps", pi)])
                t.op("act", lambda e, dst=dst, half=half, pi=pi: e.activation(
                    out=dst[:, half * 4:(half + 1) * 4, :], in_=self.ps[pi][:, 0:128].rearrange("p (a h) -> p a h", h=32),
                    func=AF.Copy), reads=[("ps", pi), KEY], writes=[KEY])
        for sidx in range(Q):
            pr_ = PWr[:, :, sidx:sidx + 1].to_broadcast([128, 8, 32])
            pi_ = PWi[:, :, sidx:sidx + 1].to_broadcast([128, 8, 32])
            cmul(self.CsT[:, :, sidx, 0, :], self.CsT[:, :, sidx, 1, :], Sr, Si, pr_, pi_, U1, U2)
            dv(lambda e, sidx=sidx: e.tensor_single_scalar(self.CsT[:, :, sidx, 1, :], self.CsT[:, :, sidx, 1, :],
                                                           -1.0, ALU.mult))
        t.op("dve", lambda e: e.tensor_copy(self.CH[:, 0, 0:1], self.CH[:, 0, 0:1]), reads=[KEY],
             writes=["CsT", "TAB"] + self.S5S_keys)
        t.op("pool", lambda e: e.dma_start(out=self.GW[:, :, :],
                                           in_=pin["s5_glu_w"][l].rearrange("(kt p) n -> p kt n", p=128)),
             writes=["GW"], stream="dma_io", inc=16)
        t.op("dve", lambda e: e.memset(self.CARJ[:], 0.0), writes=[("CARJ", P_) for P_ in range(8)])

    def mix_s5(self, l, ti, n):
        t = self.t
        Y, U, Z = self.S_Y, self.S_U, self.S_Z
        k = self.slk
        Q = self.Q
        ncn = n // Q
        tm = lambda ap: ap.rearrange("p (c s) -> p s c", s=Q)
        sm = lambda ap: ap.rearrange("p (s c) -> p s c", s=Q)
        for half in range(2):
            t.op("act", lambda e, half=half: e.activation(out=sm(self.sl(Z[half], n)), in_=tm(self.sl(U[half], n)),
                                                          func=AF.Copy),
                 reads=[k(U[half])], writes=[k(Z[half])])
        YB = [self.ps[6], self.ps[7]]
        tv = lambda i: self.tmp[i][:, 0:8 * ncn].rearrange("p (a c) -> p a c", c=ncn)
        GLr, GLi, Vr, Vi = tv(0), tv(1), tv(2), tv(3)
        SY = self.SL[:, Y[0]:Y[0] + 8, :].rearrange("p a b -> p (a b)")
        n1 = ncn + 1
        JT = SY[:, 0:16 * n1].rearrange("p (a r c) -> p a r c", a=8, r=2)
        JJ = SY[:, 1040:1040 + 16 * n1].rearrange("p (a r c) -> p a r c", a=8, r=2)
        T1 = SY[:, 2080:2080 + 8 * n1].rearrange("p (a c) -> p a c", a=8)
        T2 = SY[:, 2600:2600 + 8 * n1].rearrange("p (a c) -> p a c", a=8)
        ck = [("tmp", i) for i in range(4)] + [k(i) for i in Y]
        Wk, Gk, Gs = (Z[2], Z[3]), (Z[4], Z[5]), (Z[6], Z[7])
        def do_b(P_):
            half, j = P_ // 4, P_ % 4
            rj = slice(32 * j, 32 * j + 32)
            pvs = []
            for ri in range(2):
                pv = self.nextps()
                pvs.append(pv)

                def emit_b(pe, ri=ri, pv=pv):
                    ins = None
                    for sidx in range(Q):
                        ins = pe.matmul(self.ps[pv][:, sidx * ncn:(sidx + 1) * ncn],
                                        self.BsT[rj, half, sidx, ri, :],
                                        self.SL[rj, Z[half], sidx * ncn:(sidx + 1) * ncn],
                                        start=True, stop=True, tile_position=(32 * j, 0))
                    return ins
                t.op("pe", emit_b, reads=["BsT", k(Z[half])], writes=[("ps", pv)])
            return pvs

        def do_chain(P_, pvs):
            for ri in range(2):
                pv = pvs[ri]
                t.op("act", lambda e, ri=ri, pv=pv: e.activation(out=tm(self.sl(Wk[ri], n)), in_=sm(self.ps[pv][:, 0:n]),
                                                                func=AF.Copy),
                     reads=[("ps", pv)], writes=[k(Wk[ri])])
                t.op("dve", lambda e, ri=ri: e.tensor_tensor_scan(self.sl(Gk[ri], n), self.cmask8[:, 0:n],
                                                                  self.sl(Wk[ri], n), 0.0, ALU.mult, ALU.add),
                     reads=[k(Wk[ri]), "cmask8"], writes=[k(Gk[ri])])
                t.op("pool", lambda e, ri=ri: e.tensor_copy(sm(self.sl(Gs[ri], n)), tm(self.sl(Gk[ri], n))),
                     reads=[k(Gk[ri])], writes=[k(Gs[ri])])
                gl = (GLr, GLi)[ri]
                t.op("pool", lambda e, ri=ri, gl=gl: e.tensor_copy(
                    gl[:, P_, :], self.SL[:, Gk[ri], 0:n].rearrange("p (c s) -> p c s", s=Q)[:, :, Q - 1]),
                    reads=[k(Gk[ri])], writes=[("tmp", ri)])

        def do_c(P_):
            half, j = P_ // 4, P_ % 4
            rj = slice(32 * j, 32 * j + 32)

            def emit_c(pe):
                ins = None
                for sidx in range(Q):
                    cs = slice(sidx * ncn, (sidx + 1) * ncn)
                    pe.matmul(YB[half][rj, cs], self.CsT[:, P_, sidx, 0, :], self.SL[:, Gs[0], cs],
                              start=(sidx == 0), stop=False, tile_position=(0, 32 * j))
                    ins = pe.matmul(YB[half][rj, cs], self.CsT[:, P_, sidx, 1, :], self.SL[:, Gs[1], cs],
                                    start=False, stop=False, tile_position=(0, 32 * j))
                return ins
            t.op("pe", emit_c, reads=["CsT", k(Gs[0]), k(Gs[1])], writes=[("ps", 6 + half)])

        pv_next = do_b(0)
        for P_ in range(8):
            pv_cur = pv_next
            do_chain(P_, pv_cur)
            if P_ + 1 < 8:
                pv_next = do_b(P_ + 1)
            do_c(P_)

        def dv(fn, extra_r=(), extra_w=()):
            t.op("dve", fn, reads=ck + ["TAB"] + list(extra_r), writes=ck + list(extra_w))
        TAr, TAi = self.TA[:, :, 0, 0:ncn], self.TA[:, :, 1, 0:ncn]
        TBr, TBi = self.TB[:, :, 0, 0:n1], self.TB[:, :, 1, 0:n1]
        t1, t2 = T1[:, :, 0:ncn], T2[:, :, 0:ncn]
        dv(lambda e: e.tensor_tensor(t1, GLr, TAr, ALU.mult))
        dv(lambda e: e.tensor_tensor(t2, GLi, TAi, ALU.mult))
        dv(lambda e: e.tensor_tensor(Vr, t1, t2, ALU.subtract))
        dv(lambda e: e.tensor_tensor(t1, GLr, TAi, ALU.mult))
        dv(lambda e: e.tensor_tensor(t2, GLi, TAr, ALU.mult))
        dv(lambda e: e.tensor_tensor(Vi, t1, t2, ALU.add))
        cjk = [("CARJ", i) for i in range(8)]
        for ri in range(2):
            dv(lambda e, ri=ri: e.tensor_copy(JT[:, :, ri, 0], self.CARJ[:, :, ri]), extra_r=cjk)
        for P_ in range(8):
            for ri, v_ in enumerate((Vr, Vi)):
                dv(lambda e, ri=ri, v_=v_, P_=P_: e.tensor_tensor_scan(
                    JT[:, P_, ri, 1:n1], self.RR[:, P_:P_ + 1].to_broadcast([128, ncn]), v_[:, P_, :],
                    self.CARJ[:, P_, ri:ri + 1], ALU.mult, ALU.add), extra_r=cjk)
        dv(lambda e: e.tensor_tensor(T1, JT[:, :, 0, :], TBr, ALU.mult))
        dv(lambda e: e.tensor_tensor(T2, JT[:, :, 1, :], TBi, ALU.mult))
        dv(lambda e: e.tensor_tensor(JJ[:, :, 0, :], T1, T2, ALU.subtract))
        dv(lambda e: e.tensor_tensor(T1, JT[:, :, 0, :], TBi, ALU.mult))
        dv(lambda e: e.tensor_tensor(T2, JT[:, :, 1, :], TBr, ALU.mult))
        dv(lambda e: e.tensor_tensor(JJ[:, :, 1, :], T1, T2, ALU.add))
        for ri in range(2):
            dv(lambda e, ri=ri: e.tensor_copy(self.CARJ[:, :, ri], JJ[:, :, ri, ncn]), extra_r=cjk, extra_w=cjk)
        for half in range(2):
            for j in range(4):
                P_ = half * 4 + j
                rj = slice(32 * j, 32 * j + 32)

                def emit_j(pe, P_=P_, half=half, rj=rj, j=j):
                    ins = None
                    for sidx in range(Q):
                        cs = slice(sidx * ncn, (sidx + 1) * ncn)
                        pe.matmul(YB[half][rj, cs], self.CsT[:, P_, sidx, 0, :], JJ[:, P_, 0, 0:ncn],
                                  start=False, stop=False, tile_position=(0, 32 * j))
                        ins = pe.matmul(YB[half][rj, cs], self.CsT[:, P_, sidx, 1, :], JJ[:, P_, 1, 0:ncn],
                                        start=False, stop=True, tile_position=(0, 32 * j))
                    return ins
                t.op("pe", emit_j, reads=["CsT"] + ck, writes=[("ps", 6 + half)])
            ys = U[20 + half]
            t.op("dve", lambda e, half=half, ys=ys: e.scalar_tensor_tensor(
                tm(self.sl(ys, n)), tm(self.sl(U[half], n)), self.prm(62 + half), sm(YB[half][:, 0:n]),
                ALU.mult, ALU.add),
                reads=[k(U[half]), ("ps", 6 + half), "p_s5d"], writes=[k(ys)])
            t.op("act", lambda e, ys=ys: e.activation(out=self.sl(ys, n), in_=self.sl(ys, n),
                                                      func=AF.Gelu_apprx_tanh),
                 reads=[k(ys)], writes=[k(ys)])
        for m in range(2):
            pi = self.nextps()
            self.mm(self.ps[pi][:, 0:n], ("ps", pi),
                    [(self.GW[:, kt, m * 128:(m + 1) * 128], self.sl(U[20 + kt], n)) for kt in range(2)],
                    reads=[k(U[20]), k(U[21]), "GW"])
            tk = self.nexttmp()
            t.op("act", lambda e, tk=tk, pi=pi, m=m: e.activation(out=self.tmp[tk][:, 0:n], in_=self.ps[pi][:, 0:n],
                                                                 func=AF.Sigmoid, bias=self.prm(64 + m), scale=1.0),
                 reads=[("ps", pi), "p_glub"], writes=[("tmp", tk)])
            t.op("dve", lambda e, tk=tk, m=m: e.tensor_tensor(self.ysl(m, n), self.sl(U[20 + m], n),
                                                              self.tmp[tk][:, 0:n], ALU.mult),
                 reads=[("tmp", tk), k(U[20 + m])], writes=[self.yk(m)])


_CACHE = {}
_BUILDERS = {}


def _get_nc(n_xtiles):
    if n_xtiles not in _CACHE:
        _BUILDERS[n_xtiles] = Builder(n_xtiles)
        _CACHE[n_xtiles] = _BUILDERS[n_xtiles].build()
    return _CACHE[n_xtiles]


def make_in_maps(inputs, names, xs):
    npairs = len(xs)
    role_a = np.zeros((128, 2), np.float32)
    role_a[:, 0] = 1.0
    role_b = np.zeros((128, 2), np.float32)
    role_b[:, 1] = 1.0
    pa, pb = {}, {}
    for k in names:
        if k in ("x", "role"):
            continue
        arr = np.ascontiguousarray(inputs[k], dtype=np.float32)
        if k in ("meta_tokens", "hg_lb_raw"):
            pa[k] = arr
            pb[k] = arr
        else:
            pa[k] = np.ascontiguousarray(arr[[0, 0]])
            pb[k] = arr
    maps = [dict(pa, x=xs[i], role=role_a) for i in range(npairs)]
    maps += [dict(pb, x=xs[i], role=role_b) for i in range(npairs)]
    return maps


def kernel(**inputs):
    x = np.ascontiguousarray(inputs["x"], dtype=np.float32)
    bsz, seq, _ = x.shape
    n_xtiles = seq // NT
    nc = _get_nc(n_xtiles)
    names = _BUILDERS[n_xtiles].in_names
    in_maps = make_in_maps(inputs, names, [x[b] for b in range(bsz)])
    res = run_bass_kernel_spmd(nc, in_maps, core_ids=list(range(2 * bsz)))
    return np.stack([res.results[bsz + b]["out"] for b in range(bsz)], axis=0)
```

```python
import math
import os
from contextlib import ExitStack

import numpy as np
import concourse.bass as bass
import concourse.mybir as mybir
from concourse.bass_utils import run_bass_kernel_spmd

F32 = mybir.dt.float32
BF16 = mybir.dt.bfloat16
AF = mybir.ActivationFunctionType
ALU = mybir.AluOpType

D = 1024
KT = 8
NIN = 2560
DFF = 2816
FT = 22
NMETA = 16
SEQ = 8192
DEPTH = 2
ALPHA = (2 * DEPTH) ** 0.25
EPS = 1e-5
NT = 512
SKEW = 2
WSLOT = 4096
NWSLOT = 2
NBSLOT = 4
NCHUNK = 26


class Trk:
    def __init__(self, nc, es):
        self.nc = nc
        self.es = es
        self.E = {"pe": nc.tensor, "act": nc.scalar, "dve": nc.vector,
                  "pool": nc.gpsimd, "sp": nc.sync}
        self.cur = {}
        self.waited = {}
        self.lastw = {}
        self.rd = {}
        self.nsem = 0
        self.LIM = 30000
        self.nins = 0

    def _sem(self, stream, inc):
        s = self.cur.get(stream)
        if s is None or s[1] + inc > self.LIM:
            name = f"s{self.nsem}"
            sem = self.es.enter_context(self.nc.semaphore(f"{name}_{stream}"))
            self.nsem += 1
            s = [sem, 0, name]
            self.cur[stream] = s
        s[1] += inc
        return (s[0], s[1], s[2])

    def wait(self, engine, tok):
        sem, val, name = tok
        k = (engine, name)
        if self.waited.get(k, 0) >= val:
            return
        self.waited[k] = val
        self.E[engine].wait_ge(sem, val)

    def op(self, engine, emit, reads=(), writes=(), stream=None, inc=1):
        deps = {}

        def add(tok):
            if tok is None:
                return
            n = tok[2]
            if n not in deps or deps[n][1] < tok[1]:
                deps[n] = tok

        for k in reads:
            add(self.lastw.get(k))
        for k in writes:
            add(self.lastw.get(k))
            for t in self.rd.get(k, {}).values():
                add(t)
        st = stream or engine
        own = self.cur.get(st)
        for tok in deps.values():
            if engine == "pe" and stream is None and own is not None and tok[2] == own[2]:
                continue
            self.wait(engine, tok)
        ins = emit(self.E[engine])
        tok = self._sem(st, inc)
        ins.then_inc(tok[0], inc)
        self.nins += 1
        for k in reads:
            self.rd.setdefault(k, {})[tok[2]] = tok
        for k in writes:
            self.lastw[k] = tok
            self.rd[k] = {}
        return tok

    def drain(self, engine):
        for st, s in self.cur.items():
            self.wait(engine, (s[0], s[1], s[2]))


class _Stop(Exception):
    pass


class Builder:
    def __init__(self, n_xtiles, stub=(), taps=(), npairs=4):
        self.npairs = npairs
        self.n_xtiles = n_xtiles
        self.stub = set(stub)
        self.taps = list(taps)
        self.tiles = [(0, NMETA)] + [(NMETA + i * NT, NT) for i in range(n_xtiles)]
        self.seq_x = n_xtiles * NT

    def build(self):
        nc = bass.Bass("TRN2", target_bir_lowering=False)
        self.nc = nc
        self.es = ExitStack()
        es = self.es
        self.t = Trk(nc, es)
        t = self.t
        self.in_names = []

        def di(name, shape):
            self.in_names.append(name)
            return nc.dram_tensor(name, list(shape), F32, kind="ExternalInput").ap()
        self.x = di("x", [self.seq_x, D])
        self.meta = di("meta_tokens", [NMETA, D])
        self.w_in = di("w_in", [DEPTH, D, NIN])
        self.w_out = di("w_out", [DEPTH, D, D])
        self.w_f1 = di("w_ffn_in", [DEPTH, D, 2 * DFF])
        self.w_f2 = di("w_ffn_out", [DEPTH, DFF, D])
        self.ln = {n: di(n, [DEPTH, D]) for n in ("ln1_g", "ln1_b", "ln2_g", "ln2_b")}
        self.pin = {}
        for nm, shp in (("hg_lb_raw", [DEPTH, 256]), ("sc_conv_w", [DEPTH, 3, 256]), ("hg_gnorm", [DEPTH, 256]),
                        ("lru_conv_w", [DEPTH, 4, 256]), ("lru_conv_b", [DEPTH, 256]),
                        ("lru_wa", [DEPTH, 4, 64, 64]), ("lru_ba", [DEPTH, 256]),
                        ("lru_wx", [DEPTH, 4, 64, 64]), ("lru_bx", [DEPTH, 256]), ("lru_a_param", [DEPTH, 256]),
                        ("s5_lam_re", [DEPTH, 16, 64]), ("s5_lam_im", [DEPTH, 16, 64]),
                        ("s5_b_re", [DEPTH, 16, 64, 16]), ("s5_b_im", [DEPTH, 16, 64, 16]),
                        ("s5_c_re", [DEPTH, 16, 16, 64]), ("s5_c_im", [DEPTH, 16, 16, 64]),
                        ("s5_d", [DEPTH, 256]), ("s5_log_dt", [DEPTH, 16]),
                        ("s5_glu_w", [DEPTH, 256, 256]), ("s5_glu_b", [DEPTH, 256])):
            self.pin[nm] = di(nm, shp)
        self.role_in = di("role", [128, 2])
        self.out = nc.dram_tensor("out", [self.seq_x, D], F32, kind="ExternalOutput").ap()
        ntile = len(self.tiles)
        self.send = [nc.dram_tensor(f"send{i}", [128, KT * NT], F32, kind="Internal").ap() for i in range(2)]
        self.recv = [nc.dram_tensor(f"recv{i}", [128, KT * NT], F32, kind="Internal").ap() for i in range(4)]
        self.groups = [[i, i + self.npairs] for i in range(self.npairs)]
        self.wbf = nc.dram_tensor("wbf", [NCHUNK, 128, WSLOT], BF16, kind="Internal").ap()
        self.tapout = {}
        for name, shape in self.taps:
            self.tapout[name] = nc.dram_tensor("tap_" + name, list(shape), F32, kind="ExternalOutput").ap()

        sb = lambda name, shape: es.enter_context(nc.sbuf_tensor(name, list(shape), F32))
        self.NSLAB = 46
        self.SL = sb("SL", [128, self.NSLAB, NT])
        self.W = [sb(f"W{i}", [128, WSLOT]) for i in range(NWSLOT)]
        self.ident = sb("ident", [128, 128])
        self.ones = sb("ones", [128, 128])
        self.PRM = sb("PRM", [128, 128])
        self.tmp = [sb(f"tmp{i}", [128, NT]) for i in range(4)]
        self.cmask = sb("cmask", [128, NT])
        self.bd64 = sb("bd64", [128, 128])
        self.lruw = sb("lruw", [128, 2, 2, 128])
        self.car_sc = sb("car_sc", [128, 2, 2])
        self.car_lx = sb("car_lx", [128, 2, 3])
        self.car_lh = sb("car_lh", [128, 2])
        self.SS = sb("SS", [128, 2, 9, 128])
        self.SM = [sb(f"SM{i}", [128, 2, 128]) for i in range(2)]
        self.mask2 = sb("mask2", [128, 128])
        self.smn = 0
        self.Q = 8
        self.BsT = sb("BsT", [128, 2, 8, 2, 128])
        self.CsT = sb("CsT", [128, 8, 8, 2, 32])
        self.TA = sb("TA", [128, 8, 2, 64])
        self.TB = sb("TB", [128, 8, 2, 65])
        self.RR = sb("RR", [128, 8])
        self.GW = sb("GW", [128, 2, 256])
        self.CARJ = sb("CARJ", [128, 8, 2])
        self.cmask8 = sb("cmask8", [128, NT])
        self.CH = sb("CH", [128, 8, 66])
        self.ROLE = sb("ROLE", [128, 2])
        self.HM0 = sb("HM0", [128, KT, NMETA])
        self.SNAP = sb("SNAP", [128, 2 * 2 + 2 * 3 + 2 + 2 * 128 + 8 * 2])
        self.ps = [es.enter_context(nc.psum_tensor(f"ps{i}", [128, NT], F32)) for i in range(8)]
        self.psn = 0
        self.tmpn = 0
        self.S_H = list(range(0, 8))
        self.S_Y = list(range(8, 16))
        self.S_Z = list(range(16, 24))
        self.S_U = list(range(24, 46))
        self.S5S = self.SL[:, 24:29, :].rearrange("p a b -> p (a b)")[:, 0:2176]
        self.S5S_keys = [("sl", i) for i in range(24, 29)]
        self.CB = es.enter_context(nc.sbuf_tensor("CB", [128, WSLOT], BF16))
        self.WBv = [self.W[i // 2][:, :].bitcast(BF16)[:, (i % 2) * WSLOT:(i % 2 + 1) * WSLOT] for i in range(NBSLOT)]
        self.bf = False

        t.op("pool", lambda e: e.memset(self.ident[:], 0.0), writes=["ident"])
        t.op("pool", lambda e: e.affine_select(
            out=self.ident[:], in_=self.ident[:], compare_op=ALU.not_equal, fill=1.0,
            base=0, pattern=[[-1, 128]], channel_multiplier=1), writes=["ident"])
        t.op("pool", lambda e: e.memset(self.ones[:], 1.0), writes=["ones"])
        t.op("pool", lambda e: e.memset(self.cmask[:], 1.0), writes=["cmask"])
        t.op("pool", lambda e: e.affine_select(
            out=self.cmask[:].rearrange("p (c s) -> p c s", s=64), in_=self.cmask[:].rearrange("p (c s) -> p c s", s=64),
            compare_op=ALU.not_equal, fill=0.0, base=0, pattern=[[0, NT // 64], [1, 64]], channel_multiplier=0),
            writes=["cmask"])
        t.op("pool", lambda e: e.memset(self.cmask8[:], 1.0), writes=["cmask8"])
        t.op("pool", lambda e: e.affine_select(
            out=self.cmask8[:].rearrange("p (c s) -> p c s", s=8), in_=self.cmask8[:].rearrange("p (c s) -> p c s", s=8),
            compare_op=ALU.not_equal, fill=0.0, base=0, pattern=[[0, NT // 8], [1, 8]], channel_multiplier=0),
            writes=["cmask8"])
        t.op("pool", lambda e: e.memset(self.mask2[:], 1.0), writes=["mask2"])
        t.op("pool", lambda e: e.affine_select(
            out=self.mask2[:, :], in_=self.mask2[:, :],
            compare_op=ALU.is_ge, fill=0.0, base=0, pattern=[[1, 128]], channel_multiplier=-1),
            writes=["mask2"])
        t.op("pool", lambda e: e.memset(self.mask2[0:64, 64:128], 0.0), writes=["mask2"])
        t.op("pool", lambda e: e.memset(self.bd64[:], 0.0), writes=["bd64"])
        for hh in range(2):
            t.op("pool", lambda e, hh=hh: e.memset(self.bd64[hh * 64:(hh + 1) * 64, hh * 64:(hh + 1) * 64], 1.0 / 64),
                 writes=["bd64"])

        t.op("pool", lambda e: e.dma_start(out=self.ROLE[:, :], in_=self.role_in), writes=["role"],
             stream="dma_io", inc=16)
        self.fA = self.ROLE[:, 0:1]
        self.fB = self.ROLE[:, 1:2]
        nsteps = self.n_xtiles + SKEW
        self.nsteps = nsteps
        self.wq = []
        self.wq_issued = 0
        self.wq_used = 0
        self.wq.extend(self.chunk_seq(0))
        self.wq.extend(self.chunk_seq(1))
        for _ in range(nsteps):
            self.wq.extend(self.bf_chunk_seq())
        self._index_queue()

        try:
            self._program(nsteps)
        except _Stop:
            pass
        t.drain("pool")
        es.close()
        return nc

    def dbg(self, lvl):
        if int(os.environ.get("DBG_STOP", "99")) <= lvl:
            raise _Stop()

    def _program(self, nsteps):
        self.layer_setup(0, mine=False)
        self.tile_pass(0, "p0", NMETA)
        self.layer_setup(1, mine=True)
        self.tile_pass(1, "p1", NMETA)
        self.snapshot()
        self.dbg(1)
        self.bf = True
        for step in range(nsteps):
            self.tile_pass(1, "main", NT, step)
            if step == SKEW - 1:
                self.restore()

    def sl(self, idx, n=NT):
        return self.SL[:, idx, 0:n]

    def slk(self, idx):
        return ("sl", idx)

    def nextps(self):
        i = self.psn
        self.psn = (self.psn + 1) % 6
        return i

    def nexttmp(self):
        i = self.tmpn
        self.tmpn = (self.tmpn + 1) % 4
        return i

    def tap(self, name, src_ap, reads):
        if name in self.tapout:
            self.t.op("pool", lambda e: e.dma_start(out=self.tapout[name], in_=src_ap),
                      reads=reads, stream="dma_io", inc=16)

    def chunk_seq(self, l):
        seq = []
        wi = self.w_in[l].rearrange("(kt p) n -> p kt n", p=128)
        for c in range(5):
            seq.append(("win", [(wi[:, :, c * 512:(c + 1) * 512], 0, KT, 512)]))
        wo = self.w_out[l].rearrange("(kt p) n -> p kt n", p=128)
        for c in range(2):
            seq.append(("wout", [(wo[:, :, c * 512:(c + 1) * 512], 0, KT, 512)]))
        w1 = self.w_f1[l].rearrange("(kt p) n -> p kt n", p=128)
        for c in range(11):
            seq.append(("f1", [(w1[:, :, c * 256:(c + 1) * 256], 0, KT, 256),
                               (w1[:, :, DFF + c * 256:DFF + (c + 1) * 256], KT * 256, KT, 256)]))
        w2 = self.w_f2[l].rearrange("(kt p) n -> p kt n", p=128)
        for c in range(8):
            seq.append(("f2", [(w2[:, :, c * 128:(c + 1) * 128], 0, FT, 128)]))
        return seq

    def bf_chunk_seq(self):
        seq = []
        kinds = ["win"] * 5 + ["wout"] * 2 + ["f1"] * 11 + ["f2"] * 8
        for c, kd in enumerate(kinds):
            nel = FT * 128 if kd == "f2" else WSLOT
            seq.append((kd, [(self.wbf[c][:, 0:nel], 0, nel, 1)], c))
        return seq

    def _slot_of(self, idx):
        ent = self.wq[idx]
        if len(ent) == 3:
            return ("b", self.bcount[idx] % NBSLOT)
        return ("f", self.fcount[idx] % NWSLOT)

    def _slot_ap(self, slotid):
        return self.WBv[slotid[1]] if slotid[0] == "b" else self.W[slotid[1]]

    def _phys_keys(self, slotid):
        if slotid[0] == "b":
            return [("w", slotid[1] // 2, slotid[1] % 2, 0), ("w", slotid[1] // 2, slotid[1] % 2, 1)]
        return [("w", slotid[1], 0, 0), ("w", slotid[1], 1, 0), ("w", slotid[1], 0, 1), ("w", slotid[1], 1, 1)]

    def _issue_chunk(self, idx):
        ent = self.wq[idx]
        slotid = self._slot_of(idx)
        base = self._slot_ap(slotid)
        if len(ent) == 3:
            kind, parts, c = ent
            src, off, nel, _ = parts[0]
            self.t.op("sp", lambda e: e.dma_start(out=base[:, 0:nel], in_=src),
                      reads=[("wbf", c)], writes=self._phys_keys(slotid), stream=f"dma_w{slotid[1] // 2}", inc=16)
            return
        kind, parts = ent
        for pidx, (src, off, nk, ncol) in enumerate(parts):
            dst = base[:, off:off + nk * ncol].rearrange("p (k n) -> p k n", k=nk)
            wk = [("w", slotid[1], pidx, 0)] if len(parts) == 2 else [("w", slotid[1], 0, 0), ("w", slotid[1], 1, 0)]
            wk2 = [(a, b_, c_, 1) for (a, b_, c_, _) in wk]
            kh = nk // 2
            self.t.op("sp", lambda e, dst=dst, src=src: e.dma_start(out=dst[:, 0:kh, :], in_=src[:, 0:kh, :]),
                      writes=wk, stream=f"dma_w{slotid[1]}", inc=16)
            self.t.op("pool", lambda e, dst=dst, src=src: e.dma_start(out=dst[:, kh:nk, :], in_=src[:, kh:nk, :]),
                      writes=wk2, stream=f"dma_wp{slotid[1]}", inc=16)

    def _index_queue(self):
        self.bcount, self.fcount = {}, {}
        nb = nf = 0
        for i, ent in enumerate(self.wq):
            if len(ent) == 3:
                self.bcount[i] = nb
                nb += 1
            else:
                self.fcount[i] = nf
                nf += 1

    def wacquire(self, kind):
        idx = self.wq_used
        assert self.wq[idx][0] == kind, (self.wq[idx][0], kind)
        while self.wq_issued <= idx:
            self._issue_chunk(self.wq_issued)
            self.wq_issued += 1
        self.wq_used += 1
        self.cur_chunk = idx
        return self._slot_of(idx)

    def wprefetch(self):
        depth = (NBSLOT - 1) if len(self.wq[self.wq_used - 1]) == 3 else (NWSLOT - 1)
        lim = min(self.wq_used + depth, len(self.wq))
        while self.wq_issued < lim:
            nxt = self.wq[self.wq_issued]
            if (len(nxt) == 3) != (len(self.wq[self.wq_used - 1]) == 3):
                break
            self._issue_chunk(self.wq_issued)
            self.wq_issued += 1

    def wview(self, slotid, off, nk, ncol):
        return self._slot_ap(slotid)[:, off:off + nk * ncol].rearrange("p (k n) -> p k n", k=nk)

    def wkeys(self, slotid):
        return self._phys_keys(slotid)

    def cast_store(self, slotid, kind):
        c = self.cur_chunk - NCHUNK
        nel = FT * 128 if kind == "f2" else WSLOT
        self.t.op("act", lambda e: e.activation(out=self.CB[:, 0:nel], in_=self.W[slotid[1]][:, 0:nel], func=AF.Copy),
                  reads=self._phys_keys(slotid), writes=["CB"])
        self.t.op("pool", lambda e: e.dma_start(out=self.wbf[c][:, 0:nel], in_=self.CB[:, 0:nel]),
                  reads=["CB"], writes=[("wbf", c)], stream="dma_io", inc=16)

    def layer_setup(self, l, mine):
        t = self.t
        for i, n in enumerate(("ln1_g", "ln1_b", "ln2_g", "ln2_b")):
            src = self.ln[n][l].rearrange("(m p) -> p m", p=128)
            t.op("pool", lambda e, src=src, i=i: e.dma_start(
                out=self.PRM[:, i * 8:(i + 1) * 8], in_=src, allow_slow_non_contiguous=True),
                writes=[("prm", i)], stream="dma_io", inc=16)
        P = self.PRM
        pin = self.pin

        def vec(name, col, key):
            src = pin[name][l].rearrange("(c p) -> p c", p=128)
            t.op("pool", lambda e: e.dma_start(out=P[:, col:col + 2], in_=src, allow_slow_non_contiguous=True),
                 writes=[key], stream="dma_io", inc=16)

        for c in range(2):
            t.op("pool", lambda e, c=c: e.dma_start(
                out=P[:, 32 + 3 * c:35 + 3 * c],
                in_=pin["sc_conv_w"][l][:, c * 128:(c + 1) * 128].rearrange("k p -> p k"),
                allow_slow_non_contiguous=True), writes=["p_scw"], stream="dma_io", inc=16)
            t.op("pool", lambda e, c=c: e.dma_start(
                out=P[:, 38 + 4 * c:42 + 4 * c],
                in_=pin["lru_conv_w"][l][:, c * 128:(c + 1) * 128].rearrange("k p -> p k"),
                allow_slow_non_contiguous=True), writes=["p_lcw"], stream="dma_io", inc=16)
        vec("lru_conv_b", 46, "p_lcb")
        vec("lru_ba", 48, "p_lba")
        vec("lru_bx", 50, "p_lbx")
        vec("lru_a_param", 52, "p_cp")
        vec("hg_gnorm", 60, "p_gn")
        vec("s5_d", 62, "p_s5d")
        vec("s5_glu_b", 64, "p_glub")
        t.op("act", lambda e: e.activation(out=P[:, 52:54], in_=P[:, 52:54], func=AF.Exp, scale=-1.0),
             reads=["p_cp"], writes=["p_cp"])
        t.op("act", lambda e: e.activation(out=P[:, 52:54], in_=P[:, 52:54], func=AF.Ln, bias=1.0, scale=1.0),
             reads=["p_cp"], writes=["p_cp"])
        t.op("dve", lambda e: e.tensor_single_scalar(P[:, 54:56], P[:, 52:54], -16.0, ALU.mult),
             reads=["p_cp"], writes=["p_cp2"])
        t.op("dve", lambda e: e.tensor_single_scalar(P[:, 52:54], P[:, 52:54], -8.0, ALU.mult),
             reads=["p_cp", "p_cp2"], writes=["p_cp"])
        if not mine:
            t.op("dve", lambda e: e.memset(P[:, 56:58], 0.0), writes=["p_lb"])
        else:
            for i in range(2):
                src = pin["hg_lb_raw"][i].rearrange("(c p) -> p c", p=128)
                t.op("pool", lambda e, i=i, src=src: e.dma_start(out=P[:, 66 + 2 * i:68 + 2 * i], in_=src,
                                                                 allow_slow_non_contiguous=True),
                     writes=[("p_lbraw", i)], stream="dma_io", inc=16)
            t.op("dve", lambda e: e.tensor_tensor(P[:, 56:58], P[:, 68:70], P[:, 66:68], ALU.subtract),
                 reads=[("p_lbraw", 0), ("p_lbraw", 1)], writes=["p_lb"])
            t.op("act", lambda e: e.activation(out=P[:, 56:58], in_=P[:, 56:58], func=AF.Sigmoid),
                 reads=["p_lb"], writes=["p_lb"])
            t.op("dve", lambda e: e.tensor_single_scalar(P[:, 56:58], P[:, 56:58], self.fB, ALU.mult),
                 reads=["p_lb", "role"], writes=["p_lb"])
        t.op("dve", lambda e: e.tensor_scalar(P[:, 58:60], P[:, 56:58], -1.0, 1.0, ALU.mult, ALU.add),
             reads=["p_lb"], writes=["p_oml"])
        t.op("dve", lambda e: e.memset(self.lruw[:], 0.0), writes=["lruw"])
        for gi, nm in enumerate(("lru_wa", "lru_wx")):
            for h in range(4):
                c, hh = h // 2, h % 2
                t.op("pool", lambda e, gi=gi, nm=nm, h=h, c=c, hh=hh: e.dma_start(
                    out=self.lruw[hh * 64:(hh + 1) * 64, gi, c, hh * 64:(hh + 1) * 64], in_=pin[nm][l, h]),
                    writes=["lruw"], stream="dma_io", inc=16)
        t.op("dve", lambda e: e.memset(self.car_sc[:], 0.0), writes=[("car_sc", 0), ("car_sc", 1)])
        t.op("dve", lambda e: e.memset(self.car_lx[:], 0.0), writes=[("car_lx", 0), ("car_lx", 1)])
        t.op("dve", lambda e: e.memset(self.car_lh[:], 0.0), writes=[("car_lh", 0), ("car_lh", 1)])
        t.op("dve", lambda e: e.memset(self.SS[:], 0.0), writes=[("SS", 0), ("SS", 1)])
        if "s5" not in self.stub:
            self.s5_setup(l)

    def evac(self, eng, out_ap, in_ap, reads, writes):
        if eng == "act":
            return self.t.op("act", lambda e: e.activation(out=out_ap, in_=in_ap, func=AF.Copy),
                             reads=reads, writes=writes)
        return self.t.op("dve", lambda e: e.tensor_copy(out_ap, in_ap), reads=reads, writes=writes)

    def mm(self, ps_ap, pskey, pairs, reads):
        def emit(pe):
            n = len(pairs)
            ins = None
            for i, (lh, rh) in enumerate(pairs):
                ins = pe.matmul(ps_ap, lh, rh, start=(i == 0), stop=(i == n - 1))
            return ins
        return self.t.op("pe", emit, reads=reads, writes=[pskey])

    def states(self):
        o = [0]

        def sn(ncol, shape=None):
            ap = self.SNAP[:, o[0]:o[0] + ncol]
            o[0] += ncol
            return ap
        return [
            (self.car_sc[:].rearrange("p a b -> p (a b)"), sn(4), [("car_sc", 0), ("car_sc", 1)]),
            (self.car_lx[:].rearrange("p a b -> p (a b)"), sn(6), [("car_lx", 0), ("car_lx", 1)]),
            (self.car_lh[:, :], sn(2), [("car_lh", 0), ("car_lh", 1)]),
            (self.SS[:, :, 0, :], sn(256).rearrange("p (a b) -> p a b", a=2), [("SS", 0), ("SS", 1)]),
            (self.CARJ[:].rearrange("p a b -> p (a b)"), sn(16), [("CARJ", i) for i in range(8)]),
        ]

    def snapshot(self):
        for st, snp, keys in self.states():
            self.t.op("dve", lambda e, st=st, snp=snp: e.tensor_copy(snp, st), reads=keys, writes=["snap"])

    def restore(self):
        for st, snp, keys in self.states():
            self.t.op("dve", lambda e, st=st: e.tensor_single_scalar(st, st, self.fA, ALU.mult),
                      reads=keys + ["role"], writes=keys)
            self.t.op("dve", lambda e, st=st, snp=snp: e.scalar_tensor_tensor(st, snp, self.fB, st, ALU.mult, ALU.add),
                      reads=keys + ["role", "snap"], writes=keys)

    def xstage(self, mode, nb):
        base = self.S_Y[0] if mode in ("p0", "p1") else self.S_U[0]
        st = self.SL[:, base:base + 8, :].rearrange("p a b -> p (a b)")
        return st[:, 0:nb * D].rearrange("p (b f) -> p b f", b=nb), [self.slk(base + i) for i in range(8)]

    def load_x(self, mode, n, step):
        nb = (n + 127) // 128
        pb = min(n, 128)
        stage, skeys = self.xstage(mode, nb)
        if mode in ("p0", "p1"):
            src = self.meta.rearrange("(b p) f -> p b f", p=pb)
        else:
            xi = min(step, self.n_xtiles - 1)
            src = self.x[xi * NT:(xi + 1) * NT, :].rearrange("(b p) f -> p b f", p=pb)
        self.t.op("pool", lambda e: e.dma_start(out=stage[0:pb], in_=src),
                  writes=skeys, stream="dma_io", inc=16)

    def load_recv(self, step):
        U = self.S_U
        RS = 14
        par = (step - SKEW) % 4
        rsrc = self.recv[par][:, :].rearrange("p (k n) -> p k n", k=KT)
        self.t.op("pool", lambda e: e.dma_start(out=self.SL[:, U[RS]:U[RS] + 8, :], in_=rsrc),
                  reads=[("recv", par)], writes=[self.slk(U[RS + i]) for i in range(8)], stream="dma_io", inc=16)

    def load_input(self, mode, n, step):
        t = self.t
        H, U = self.S_H, self.S_U
        nb = (n + 127) // 128
        pb = min(n, 128)
        stage, skeys = self.xstage(mode, nb)
        if mode in ("p0", "p1") or step == 0:
            self.load_x(mode, n, step)
        RS = 14
        if mode == "main" and step >= SKEW and not (SKEW >= 2 and step >= 1):
            self.load_recv(step)
        for k in range(KT):
            pi = self.nextps()

            def emit(pe, k=k, pi=pi):
                ins = None
                for b in range(nb):
                    ins = pe.transpose(self.ps[pi][:, b * pb:(b + 1) * pb],
                                       stage[0:pb, b, k * 128:(k + 1) * 128],
                                       self.ident[0:pb, 0:pb])
                return ins
            t.op("pe", emit, reads=skeys + ["ident"], writes=[("ps", pi)])
            hk = self.slk(H[k])
            if mode == "p0":
                self.evac("act" if k % 2 else "dve", self.sl(H[k], n), self.ps[pi][:, 0:n],
                          reads=[("ps", pi)], writes=[hk])
                continue
            if k % 2:
                t.op("act", lambda e, k=k, pi=pi: e.activation(out=self.sl(H[k], n), in_=self.ps[pi][:, 0:n],
                                                               func=AF.Copy, scale=self.fA),
                     reads=[("ps", pi), "role"], writes=[hk])
            else:
                t.op("dve", lambda e, k=k, pi=pi: e.tensor_single_scalar(self.sl(H[k], n), self.ps[pi][:, 0:n],
                                                                         self.fA, ALU.mult),
                     reads=[("ps", pi), "role"], writes=[hk])
            if mode == "p1":
                t.op("dve", lambda e, k=k: e.scalar_tensor_tensor(self.sl(H[k], n), self.HM0[:, k, :], self.fB,
                                                                  self.sl(H[k], n), ALU.mult, ALU.add),
                     reads=[hk, "role", "HM0"], writes=[hk])
        if mode == "main" and step >= SKEW:
            for k in range(KT):
                hk = self.slk(H[k])
                t.op("dve", lambda e, k=k: e.scalar_tensor_tensor(self.sl(H[k], n), self.sl(U[RS + k], n), self.fB,
                                                                  self.sl(H[k], n), ALU.mult, ALU.add),
                     reads=[hk, "role", self.slk(U[RS + k])], writes=[hk])

    def store_output(self, mode, n, step):
        t = self.t
        Z = self.S_Z
        if mode == "p0":
            t.op("act", lambda e: e.activation(out=self.HM0[:, :, :], in_=self.SL[:, Z[0]:Z[0] + 8, 0:NMETA],
                                               func=AF.Copy),
                 reads=[self.slk(i) for i in Z], writes=["HM0"])
            return
        if mode == "p1":
            return
        if step >= SKEW:
            nb = n // 128
            stage = self.SL[:, self.S_Y[0]:self.S_Y[0] + 8, :].rearrange("p a b -> p (a b)")
            stage = stage[:, 0:nb * D].rearrange("p (b f) -> p b f", b=nb)
            for b in range(nb):
                for half in range(2):
                    pi = self.nextps()

                    def emit(pe, b=b, half=half, pi=pi):
                        ins = None
                        for kk in range(4):
                            k = half * 4 + kk
                            ins = pe.transpose(self.ps[pi][:, kk * 128:(kk + 1) * 128],
                                               self.SL[:, Z[k], b * 128:(b + 1) * 128],
                                               self.ident[:, :])
                        return ins
                    t.op("pe", emit, reads=[self.slk(Z[half * 4 + kk]) for kk in range(4)] + ["ident"],
                         writes=[("ps", pi)])
                    self.evac("act" if half else "dve", stage[:, b, half * 512:(half + 1) * 512],
                              self.ps[pi][:, :], reads=[("ps", pi)], writes=[self.slk(self.S_Y[2 * b + half])])
            r0 = (step - SKEW) * NT
            dst = self.out[r0:r0 + n, :].rearrange("(b p) f -> p b f", p=128)
            t.op("pool", lambda e: e.dma_start(out=dst, in_=stage),
                 reads=[self.slk(i) for i in self.S_Y], stream="dma_io", inc=16)
        par = step % 2
        rpar = step % 4
        for k in range(KT):
            if k % 2:
                t.op("act", lambda e, k=k: e.activation(out=self.sl(Z[k]), in_=self.sl(Z[k]), func=AF.Copy,
                                                        scale=self.fA),
                     reads=[self.slk(Z[k]), "role"], writes=[self.slk(Z[k])])
            else:
                t.op("dve", lambda e, k=k: e.tensor_single_scalar(self.sl(Z[k]), self.sl(Z[k]), self.fA, ALU.mult),
                     reads=[self.slk(Z[k]), "role"], writes=[self.slk(Z[k])])
        t.op("pool", lambda e: e.dma_start(out=self.send[par].rearrange("p (k n) -> p k n", k=KT),
                                           in_=self.SL[:, Z[0]:Z[0] + 8, :]),
             reads=[self.slk(Z[i]) for i in range(8)], writes=[("send", par)], stream="dma_io", inc=16)
        t.op("pool", lambda e: e.collective_compute("AllReduce", ALU.add, replica_groups=self.groups,
                                                    ins=[self.send[par]], outs=[self.recv[rpar]]),
             reads=[("send", par)], writes=[("recv", rpar)], stream="cc", inc=1)

    def layernorm(self, n, gcol, bcol):
        t = self.t
        Z = self.S_Z
        ps_s = self.nextps()
        ps_q = self.nextps()
        self.mm(self.ps[ps_s][:, 0:n], ("ps", ps_s),
                [(self.ones[:, :], self.sl(Z[m], n)) for m in range(KT)],
                reads=[self.slk(Z[m]) for m in range(KT)] + ["ones"])
        sq = []
        for m in range(KT):
            ti = self.nexttmp()
            t.op("act", lambda e, m=m, ti=ti: e.activation(out=self.tmp[ti][:, 0:n], in_=self.sl(Z[m], n),
                                                          func=AF.Square),
                 reads=[self.slk(Z[m])], writes=[("tmp", ti)])
            first = (m == 0)
            last = (m == KT - 1)
            t.op("pe", lambda e, ti=ti, first=first, last=last: e.matmul(
                self.ps[ps_q][:, 0:n], self.ones[:, :], self.tmp[ti][:, 0:n], start=first, stop=last),
                reads=[("tmp", ti), "ones"], writes=[("ps", ps_q)])
        mi = self.nexttmp()
        mean = self.tmp[mi]
        mk = ("tmp", mi)
        ri = self.nexttmp()
        rstd = self.tmp[ri]
        rk = ("tmp", ri)
        t.op("dve", lambda e: e.tensor_single_scalar(mean[:, 0:n], self.ps[ps_s][:, 0:n], 1.0 / D, ALU.mult),
             reads=[("ps", ps_s)], writes=[mk])
        t.op("dve", lambda e: e.tensor_tensor(rstd[:, 0:n], mean[:, 0:n], mean[:, 0:n], ALU.mult),
             reads=[mk], writes=[rk])
        t.op("dve", lambda e: e.scalar_tensor_tensor(rstd[:, 0:n], self.ps[ps_q][:, 0:n], 1.0 / D,
                                                     rstd[:, 0:n], ALU.mult, ALU.subtract),
             reads=[("ps", ps_q), rk], writes=[rk])
        t.op("act", lambda e: e.activation(out=rstd[:, 0:n], in_=rstd[:, 0:n], func=AF.Sqrt, bias=EPS, scale=1.0),
             reads=[rk], writes=[rk])
        t.op("dve", lambda e: e.reciprocal(rstd[:, 0:n], rstd[:, 0:n]), reads=[rk], writes=[rk])
        for m in range(KT):
            zk = self.slk(Z[m])
            t.op("dve", lambda e, m=m: e.tensor_tensor(self.sl(Z[m], n), self.sl(Z[m], n), mean[:, 0:n], ALU.subtract),
                 reads=[zk, mk], writes=[zk])
            t.op("dve", lambda e, m=m: e.tensor_tensor(self.sl(Z[m], n), self.sl(Z[m], n), rstd[:, 0:n], ALU.mult),
                 reads=[zk, rk], writes=[zk])
            t.op("act", lambda e, m=m: e.activation(out=self.sl(Z[m], n), in_=self.sl(Z[m], n), func=AF.Identity,
                                                    scale=self.PRM[:, gcol + m:gcol + m + 1],
                                                    bias=self.PRM[:, bcol + m:bcol + m + 1]),
                 reads=[zk, ("prm", gcol // 8), ("prm", bcol // 8)], writes=[zk])

    def bfslab(self, slab_idx, half, n):
        return self.SL[:, slab_idx, :].bitcast(BF16)[:, half * NT:half * NT + n]

    def hsrc(self, k, n0, n1):
        if self.bf:
            return self.tmp[k // 2][:, :].bitcast(BF16)[:, (k % 2) * NT + n0:(k % 2) * NT + n1], ("tmp", k // 2)
        return self.SL[:, self.S_H[k], n0:n1], self.slk(self.S_H[k])

    def ysl(self, k, n):
        if self.bf:
            return self.bfslab(self.S_Y[k // 2], k % 2, n)
        return self.sl(self.S_Y[k], n)

    def yk(self, k):
        return self.slk(self.S_Y[k // 2]) if self.bf else self.slk(self.S_Y[k])

    def zsrc(self, k, n):
        if self.bf:
            return self.bfslab(self.S_U[12 + k // 2], k % 2, n), self.slk(self.S_U[12 + k // 2])
        return self.sl(self.S_Z[k], n), self.slk(self.S_Z[k])

    def actsl(self, i, n):
        if self.bf:
            return self.bfslab(self.S_U[i // 2], i % 2, n), self.slk(self.S_U[i // 2])
        return self.sl(self.S_U[i], n), self.slk(self.S_U[i])

    def tile_pass(self, l, mode, n, step=0):
        t = self.t
        H, Y, Z, U = self.S_H, self.S_Y, self.S_Z, self.S_U
        self.load_input(mode, n, step)
        if self.bf:
            for k in range(KT):
                dst, dk = self.hsrc(k, 0, n)
                t.op("act", lambda e, k=k, dst=dst: e.activation(out=dst, in_=self.sl(H[k], n), func=AF.Copy),
                     reads=[self.slk(H[k])], writes=[dk])
        hsr = [self.hsrc(k, 0, n) for k in range(KT)]
        hreads = list({kk for _, kk in hsr})
        for c in range(5):
            slot = self.wacquire("win")
            if mode == "p1":
                self.cast_store(slot, "win")
            wv = self.wview(slot, 0, KT, 512)
            for jj in range(4):
                j = c * 4 + jj
                if j == 12 and "hg" not in self.stub:
                    nb = (n + 127) // 128
                    VT = self.SL[:, U[12]:U[12] + 2, :].rearrange("p a b -> p (a b)").rearrange(
                        "p (b f) -> p b f", f=256)
                    for blk in range(nb):
                        pb = min(128, n - blk * 128)
                        pi = self.nextps()
                        self.mm(self.ps[pi][0:pb, 0:256], ("ps", pi),
                                [(self.hsrc(k, blk * 128, blk * 128 + pb)[0], wv[:, k, 0:256]) for k in range(KT)],
                                reads=hreads + self.wkeys(slot))
                        self.evac("act" if blk % 2 else "dve", VT[0:pb, blk, :], self.ps[pi][0:pb, 0:256],
                                  reads=[("ps", pi)], writes=[self.slk(U[12]), self.slk(U[13])])
                    continue
                if j == 13 and "hg" not in self.stub:
                    continue
                pi = self.nextps()
                self.mm(self.ps[pi][:, 0:n], ("ps", pi),
                        [(wv[:, k, jj * 128:(jj + 1) * 128], hsr[k][0]) for k in range(KT)],
                        reads=hreads + self.wkeys(slot))
                self.evac("act" if j % 2 else "dve", self.sl(U[j], n), self.ps[pi][:, 0:n],
                          reads=[("ps", pi)], writes=[self.slk(U[j])])
            self.wprefetch()
        if mode == "main":
            self.dbg(2)
        self.mixers(l, 0, 0, n)
        if mode == "main":
            self.dbg(3)
        yreads = list({self.yk(k) for k in range(KT)})
        for c in range(2):
            slot = self.wacquire("wout")
            if mode == "p1":
                self.cast_store(slot, "wout")
            wv = self.wview(slot, 0, KT, 512)
            for jj in range(4):
                m = c * 4 + jj
                pi = self.nextps()
                self.mm(self.ps[pi][:, 0:n], ("ps", pi),
                        [(wv[:, k, jj * 128:(jj + 1) * 128], self.ysl(k, n)) for k in range(KT)],
                        reads=yreads + self.wkeys(slot))
                t.op("dve", lambda e, m=m, pi=pi: e.scalar_tensor_tensor(
                    self.sl(Z[m], n), self.sl(H[m], n), ALPHA, self.ps[pi][:, 0:n], ALU.mult, ALU.add),
                    reads=[("ps", pi), self.slk(H[m])], writes=[self.slk(Z[m])])
            self.wprefetch()
        self.layernorm(n, 0, 8)
        if self.bf:
            for k in range(KT):
                dst, dk = self.zsrc(k, n)
                t.op("dve", lambda e, k=k, dst=dst: e.tensor_copy(dst, self.sl(Z[k], n)),
                     reads=[self.slk(Z[k])], writes=[dk])
        if mode == "main":
            self.dbg(4)
        zsr = [self.zsrc(k, n) for k in range(KT)]
        zreads = list({kk for _, kk in zsr})
        for c in range(11):
            slot = self.wacquire("f1")
            if mode == "p1":
                self.cast_store(slot, "f1")
            wg = self.wview(slot, 0, KT, 256)
            wu = self.wview(slot, KT * 256, KT, 256)
            for jj in range(2):
                i = c * 2 + jj
                pg = self.nextps()
                pu = self.nextps()
                self.mm(self.ps[pg][:, 0:n], ("ps", pg),
                        [(wg[:, k, jj * 128:(jj + 1) * 128], zsr[k][0]) for k in range(KT)],
                        reads=zreads + self.wkeys(slot))
                self.mm(self.ps[pu][:, 0:n], ("ps", pu),
                        [(wu[:, k, jj * 128:(jj + 1) * 128], zsr[k][0]) for k in range(KT)],
                        reads=zreads + self.wkeys(slot))
                tk = self.nexttmp()
                t.op("act", lambda e, tk=tk, pg=pg: e.activation(out=self.tmp[tk][:, 0:n], in_=self.ps[pg][:, 0:n],
                                                                func=AF.Silu),
                     reads=[("ps", pg)], writes=[("tmp", tk)])
                adst, ak = self.actsl(i, n)
                t.op("dve", lambda e, tk=tk, pu=pu, adst=adst: e.tensor_tensor(
                    adst, self.tmp[tk][:, 0:n], self.ps[pu][:, 0:n], ALU.mult),
                    reads=[("tmp", tk), ("ps", pu)], writes=[ak])
            self.wprefetch()
        asr = [self.actsl(k, n) for k in range(FT)]
        ureads = list({kk for _, kk in asr})
        for m in range(KT):
            slot = self.wacquire("f2")
            if mode == "p1":
                self.cast_store(slot, "f2")
            wv = self.wview(slot, 0, FT, 128)
            pi = self.nextps()
            self.mm(self.ps[pi][:, 0:n], ("ps", pi),
                    [(wv[:, k, :], asr[k][0]) for k in range(FT)],
                    reads=ureads + self.wkeys(slot))
            t.op("dve", lambda e, m=m, pi=pi: e.scalar_tensor_tensor(
                self.sl(Z[m], n), self.sl(Z[m], n), ALPHA, self.ps[pi][:, 0:n], ALU.mult, ALU.add),
                reads=[("ps", pi), self.slk(Z[m])], writes=[self.slk(Z[m])])
            self.wprefetch()
        if mode == "main":
            self.dbg(5)
            if step + 1 < self.nsteps:
                self.load_x("main", NT, step + 1)
                if SKEW >= 2 and step + 1 >= SKEW:
                    self.load_recv(step + 1)
        self.layernorm(n, 16, 24)
        if mode == "main":
            self.dbg(6)
        self.store_output(mode, n, step)

    def prm(self, col):
        return self.PRM[:, col:col + 1]

    def conv_acc(self, acc, x, carry, wcol, K, n, xk, acck, cark, wkey, first_bias=None):
        t = self.t
        if first_bias is None:
            t.op("dve", lambda e: e.tensor_single_scalar(acc[:, 0:n], x[:, 0:n], self.prm(wcol + K - 1), ALU.mult),
                 reads=[xk, wkey], writes=[acck])
        else:
            t.op("dve", lambda e: e.tensor_scalar(acc[:, 0:n], x[:, 0:n], self.prm(wcol + K - 1), self.prm(first_bias),
                                                  ALU.mult, ALU.add),
                 reads=[xk, wkey, "p_lcb"], writes=[acck])
        for k in range(K - 1):
            sh = K - 1 - k
            t.op("dve", lambda e, sh=sh, k=k: e.scalar_tensor_tensor(
                acc[:, sh:n], x[:, 0:n - sh], self.prm(wcol + k), acc[:, sh:n], ALU.mult, ALU.add),
                reads=[xk, wkey, acck], writes=[acck])
            t.op("dve", lambda e, sh=sh, k=k: e.scalar_tensor_tensor(
                acc[:, 0:sh], carry[:, K - 1 - sh:K - 1], self.prm(wcol + k), acc[:, 0:sh], ALU.mult, ALU.add),
                reads=[cark, wkey, acck], writes=[acck])
        t.op("dve", lambda e: e.tensor_copy(carry[:, 0:K - 1], x[:, n - (K - 1):n]),
             reads=[xk], writes=[cark])

    def mixers(self, l, ti, t0, n):
        t = self.t
        Y, U, Z = self.S_Y, self.S_U, self.S_Z
        if "s5" in self.stub:
            for k in range(2):
                self.evac("act" if k % 2 else "dve", self.ysl(k, n), self.sl(U[k], n),
                          reads=[self.slk(U[k])], writes=[self.yk(k)])
        else:
            self.mix_s5(l, ti, n)
        if "sc" in self.stub:
            for k in range(2):
                self.evac("act", self.ysl(2 + k, n), self.sl(U[2 + k], n),
                          reads=[self.slk(U[2 + k])], writes=[self.yk(2 + k)])
        else:
            self.mix_sc(n)
        if "hg" in self.stub:
            for k in range(2):
                self.evac("dve", self.ysl(4 + k, n), self.sl(U[4 + k], n),
                          reads=[self.slk(U[4 + k])], writes=[self.yk(4 + k)])
        else:
            self.mix_hg(n)
        if "lru" in self.stub:
            for k in range(2):
                self.evac("act", self.ysl(6 + k, n), self.sl(U[6 + k], n),
                          reads=[self.slk(U[6 + k])], writes=[self.yk(6 + k)])
        else:
            self.mix_lru(n)

    def mix_sc(self, n):
        t = self.t
        Y, U, Z = self.S_Y, self.S_U, self.S_Z
        for c in range(2):
            hs, bs, cs = U[2 + c], U[4 + c], U[6 + c]
            acc = Z[c]
            t.op("dve", lambda e: e.tensor_tensor(self.sl(hs, n), self.sl(hs, n), self.sl(cs, n), ALU.mult),
                 reads=[self.slk(hs), self.slk(cs)], writes=[self.slk(hs)])
            self.conv_acc(self.SL[:, acc, :], self.SL[:, hs, :], self.car_sc[:, c, :], 32 + 3 * c, 3, n,
                          self.slk(hs), self.slk(acc), ("car_sc", c), "p_scw")
            t.op("dve", lambda e: e.tensor_tensor(self.ysl(2 + c, n), self.sl(acc, n), self.sl(bs, n), ALU.mult),
                 reads=[self.slk(acc), self.slk(bs)], writes=[self.yk(2 + c)])

    def mix_lru(self, n):
        t = self.t
        Y, U, Z = self.S_Y, self.S_U, self.S_Z
        for c in range(2):
            xs, ys = U[16 + c], U[18 + c]
            xc, ga, gx, aa = Z[4 * c], Z[4 * c + 1], Z[4 * c + 2], Z[4 * c + 3]
            k = self.slk
            self.conv_acc(self.SL[:, xc, :], self.SL[:, xs, :], self.car_lx[:, c, :], 38 + 4 * c, 4, n,
                          k(xs), k(xc), ("car_lx", c), "p_lcw", first_bias=46 + c)
            pa, px = self.nextps(), self.nextps()
            self.mm(self.ps[pa][:, 0:n], ("ps", pa), [(self.lruw[:, 0, c, :], self.sl(xc, n))], reads=[k(xc), "lruw"])
            self.mm(self.ps[px][:, 0:n], ("ps", px), [(self.lruw[:, 1, c, :], self.sl(xc, n))], reads=[k(xc), "lruw"])
            t.op("act", lambda e: e.activation(out=self.sl(ga, n), in_=self.ps[pa][:, 0:n], func=AF.Sigmoid,
                                               bias=self.prm(48 + c), scale=1.0),
                 reads=[("ps", pa), "p_lba"], writes=[k(ga)])
            t.op("act", lambda e: e.activation(out=self.sl(gx, n), in_=self.ps[px][:, 0:n], func=AF.Sigmoid,
                                               bias=self.prm(50 + c), scale=1.0),
                 reads=[("ps", px), "p_lbx"], writes=[k(gx)])
            t.op("act", lambda e: e.activation(out=self.sl(aa, n), in_=self.sl(ga, n), func=AF.Exp,
                                               scale=self.prm(52 + c)),
                 reads=[k(ga), "p_cp"], writes=[k(aa)])
            t.op("act", lambda e: e.activation(out=self.sl(ga, n), in_=self.sl(ga, n), func=AF.Exp,
                                               scale=self.prm(54 + c)),
                 reads=[k(ga), "p_cp2"], writes=[k(ga)])
            t.op("act", lambda e: e.activation(out=self.sl(ga, n), in_=self.sl(ga, n), func=AF.Sqrt,
                                               scale=-1.0, bias=1.0),
                 reads=[k(ga)], writes=[k(ga)])
            t.op("dve", lambda e: e.tensor_tensor(self.sl(gx, n), self.sl(gx, n), self.sl(xc, n), ALU.mult),
                 reads=[k(gx), k(xc)], writes=[k(gx)])
            t.op("dve", lambda e: e.tensor_tensor(self.sl(gx, n), self.sl(gx, n), self.sl(ga, n), ALU.mult),
                 reads=[k(gx), k(ga)], writes=[k(gx)])
            t.op("dve", lambda e: e.tensor_tensor_scan(self.sl(xc, n), self.sl(aa, n), self.sl(gx, n),
                                                       self.car_lh[:, c:c + 1], ALU.mult, ALU.add),
                 reads=[k(aa), k(gx), ("car_lh", c)], writes=[k(xc)])
            t.op("dve", lambda e: e.tensor_copy(self.car_lh[:, c:c + 1], self.SL[:, xc, n - 1:n]),
                 reads=[k(xc)], writes=[("car_lh", c)])
            t.op("act", lambda e: e.activation(out=self.sl(aa, n), in_=self.sl(ys, n), func=AF.Gelu_apprx_tanh),
                 reads=[k(ys)], writes=[k(aa)])
            t.op("dve", lambda e: e.tensor_tensor(self.ysl(6 + c, n), self.sl(xc, n), self.sl(aa, n), ALU.mult),
                 reads=[k(xc), k(aa)], writes=[self.yk(6 + c)])

    def mix_hg(self, n):
        t = self.t
        Y, U, Z = self.S_Y, self.S_U, self.S_Z
        k = self.slk
        CL = 64 if n >= 64 else n
        nch = n // CL
        mid = CL // 2
        nb = (n + 127) // 128
        VT = self.SL[:, U[12]:U[12] + 2, :].rearrange("p a b -> p (a b)").rearrange("p (b f) -> p b f", f=256)
        vtk = [k(U[12]), k(U[13])]
        X = [Z[0], Z[1], Z[2], Z[3], Z[4], Z[5], Z[6], Z[7]]
        c3 = lambda ap: ap.rearrange("p (c s) -> p c s", s=CL)
        for pr in range(2):
            qs, fs, gs = U[8 + pr], U[10 + pr], U[14 + pr]
            g_, b_, d_, e1, e2, e3, qp, ktk = X
            t.op("act", lambda e: e.activation(out=self.sl(qs, n), in_=self.sl(qs, n), func=AF.Silu),
                 reads=[k(qs)], writes=[k(qs)])
            t.op("act", lambda e: e.activation(out=self.sl(fs, n), in_=self.sl(fs, n), func=AF.Sigmoid),
                 reads=[k(fs)], writes=[k(fs)])
            t.op("dve", lambda e: e.tensor_scalar(self.sl(fs, n), self.sl(fs, n), self.prm(58 + pr), self.prm(56 + pr),
                                                  ALU.mult, ALU.add),
                 reads=[k(fs), "p_lb", "p_oml"], writes=[k(fs)])
            t.op("act", lambda e: e.activation(out=self.sl(g_, n), in_=self.sl(fs, n), func=AF.Ln),
                 reads=[k(fs)], writes=[k(g_)])
            t.op("dve", lambda e: e.tensor_scalar(self.sl(fs, n), self.sl(fs, n), -1.0, 1.0, ALU.mult, ALU.add),
                 reads=[k(fs), k(g_)], writes=[k(fs)])
            t.op("dve", lambda e: e.tensor_tensor_scan(self.sl(b_, n), self.cmask[:, 0:n], self.sl(g_, n), 0.0,
                                                       ALU.mult, ALU.add),
                 reads=[k(g_), "cmask"], writes=[k(b_)])
            b3 = c3(self.sl(b_, n))
            t.op("dve", lambda e: e.tensor_tensor(c3(self.sl(d_, n)), b3, b3[:, :, mid:mid + 1].to_broadcast([128, nch, CL]),
                                                  ALU.subtract),
                 reads=[k(b_)], writes=[k(d_)])
            t.op("act", lambda e: e.activation(out=self.sl(e1, n), in_=self.sl(d_, n), func=AF.Exp),
                 reads=[k(d_)], writes=[k(e1)])
            t.op("act", lambda e: e.activation(out=self.sl(e2, n), in_=self.sl(d_, n), func=AF.Exp, scale=-1.0),
                 reads=[k(d_)], writes=[k(e2)])
            t.op("act", lambda e: e.activation(out=self.sl(e3, n), in_=self.sl(b_, n), func=AF.Exp),
                 reads=[k(b_)], writes=[k(e3)])
            t.op("dve", lambda e: e.tensor_tensor(c3(self.sl(d_, n)), b3, b3[:, :, CL - 1:CL].to_broadcast([128, nch, CL]),
                                                  ALU.subtract),
                 reads=[k(b_), k(e1), k(e2)], writes=[k(d_)])
            t.op("act", lambda e: e.activation(out=self.sl(d_, n), in_=self.sl(d_, n), func=AF.Exp, scale=-1.0),
                 reads=[k(d_)], writes=[k(d_)])
            t.op("dve", lambda e: e.scalar_tensor_tensor(self.sl(e1, n), self.sl(qs, n), 0.125, self.sl(e1, n),
                                                         ALU.mult, ALU.mult),
                 reads=[k(qs), k(e1)], writes=[k(e1)])
            t.op("dve", lambda e: e.tensor_tensor(self.sl(e2, n), self.sl(fs, n), self.sl(e2, n), ALU.mult),
                 reads=[k(fs), k(e2)], writes=[k(e2)])
            t.op("dve", lambda e: e.scalar_tensor_tensor(self.sl(qp, n), self.sl(qs, n), 0.125, self.sl(e3, n),
                                                         ALU.mult, ALU.mult),
                 reads=[k(qs), k(e3)], writes=[k(qp)])
            t.op("dve", lambda e: e.tensor_tensor(self.sl(d_, n), self.sl(fs, n), self.sl(d_, n), ALU.mult),
                 reads=[k(fs), k(d_)], writes=[k(d_)])
            KTv = self.SL[:, ktk, :].rearrange("p (b f) -> p b f", f=128)
            for blk in range(nb):
                pb = min(128, n - blk * 128)
                pi = self.nextps()
                t.op("pe", lambda e, blk=blk, pb=pb, pi=pi: e.transpose(
                    self.ps[pi][0:pb, 0:128], self.SL[:, d_, blk * 128:blk * 128 + pb], self.ident[:, :]),
                    reads=[k(d_), "ident"], writes=[("ps", pi)])
                self.evac("act", KTv[0:pb, blk, :], self.ps[pi][0:pb, 0:128], reads=[("ps", pi)], writes=[k(ktk)])
            for c in range(nch):
                blk, r0 = (c * CL) // 128, (c * CL) % 128
                pi = self.nextps()
                self.mm(self.ps[pi][:, 0:128], ("ps", pi),
                        [(KTv[r0:r0 + CL, blk, :], VT[r0:r0 + CL, blk, pr * 128:(pr + 1) * 128])],
                        reads=[k(ktk)] + vtk)
                for hh in range(2):
                    rs = slice(hh * 64, (hh + 1) * 64)
                    t.op("dve", lambda e, c=c, rs=rs, pi=pi: e.scalar_tensor_tensor(
                        self.SS[rs, pr, c + 1, rs], self.SS[rs, pr, c, rs],
                        self.SL[rs, e3, (c + 1) * CL - 1:(c + 1) * CL], self.ps[pi][rs, rs],
                        ALU.mult, ALU.add),
                        reads=[("ps", pi), k(e3), ("SS", pr)], writes=[("SS", pr)])
            OPS = self.ps[6]
            for blk in range(nb):
                pb = min(128, n - blk * 128)
                bs = slice(blk * 128, blk * 128 + pb)
                smi = self.smn
                self.smn = (self.smn + 1) % 2
                SMv = self.SM[smi]
                for hh in range(2):
                    rs = slice(hh * 64, (hh + 1) * 64)
                    pi = self.nextps()
                    t.op("pe", lambda e, pi=pi, rs=rs, bs=bs, pb=pb, hh=hh: e.matmul(
                        self.ps[pi][0:pb, 0:pb], self.SL[rs, e2, bs], self.SL[rs, e1, bs],
                        start=True, stop=True, tile_position=(hh * 64, 0)),
                        reads=[k(e1), k(e2)], writes=[("ps", pi)])
                    t.op("dve", lambda e, pi=pi, pb=pb, hh=hh, SMv=SMv: e.tensor_tensor(
                        SMv[0:pb, hh, 0:pb], self.ps[pi][0:pb, 0:pb], self.mask2[0:pb, 0:pb], ALU.mult),
                        reads=[("ps", pi), "mask2"], writes=[("SM", smi, hh)])

                def emit_o(pe, blk=blk, bs=bs, pb=pb, SMv=SMv):
                    ins = None
                    for hh in range(2):
                        rs = slice(hh * 64, (hh + 1) * 64)
                        pe.matmul(OPS[rs, bs], VT[0:pb, blk, pr * 128 + hh * 64:pr * 128 + (hh + 1) * 64],
                                  SMv[0:pb, hh, 0:pb], start=True, stop=False, tile_position=(0, hh * 64))
                    for c in range(blk * 128 // CL, (blk * 128 + pb) // CL):
                        cs = slice(c * CL, (c + 1) * CL)
                        ins = pe.matmul(OPS[:, cs], self.SS[:, pr, c, :], self.SL[:, qp, cs],
                                        start=False, stop=True)
                    return ins
                t.op("pe", emit_o, reads=[("SM", smi, 0), ("SM", smi, 1), ("SS", pr), k(qp)] + vtk,
                     writes=[("ps", 6)])
            for hh in range(2):
                rs = slice(hh * 64, (hh + 1) * 64)
                t.op("dve", lambda e, rs=rs: e.tensor_copy(self.SS[rs, pr, 0, rs], self.SS[rs, pr, nch, rs]),
                     reads=[("SS", pr)], writes=[("SS", pr)])
            osq, rst = g_, b_
            t.op("act", lambda e: e.activation(out=self.sl(osq, n), in_=OPS[:, 0:n], func=AF.Square),
                 reads=[("ps", 6)], writes=[k(osq)])
            pm = self.nextps()
            self.mm(self.ps[pm][:, 0:n], ("ps", pm), [(self.bd64[:, :], self.sl(osq, n))], reads=[k(osq), "bd64"])
            t.op("act", lambda e: e.activation(out=self.sl(rst, n), in_=self.ps[pm][:, 0:n], func=AF.Sqrt,
                                               bias=EPS, scale=1.0),
                 reads=[("ps", pm)], writes=[k(rst)])
            t.op("dve", lambda e: e.reciprocal(self.sl(rst, n), self.sl(rst, n)), reads=[k(rst)], writes=[k(rst)])
            t.op("dve", lambda e: e.tensor_tensor(self.sl(rst, n), self.sl(rst, n), OPS[:, 0:n], ALU.mult),
                 reads=[k(rst), ("ps", 6)], writes=[k(rst)])
            t.op("act", lambda e: e.activation(out=self.sl(gs, n), in_=self.sl(gs, n), func=AF.Silu),
                 reads=[k(gs)], writes=[k(gs)])
            t.op("dve", lambda e: e.scalar_tensor_tensor(self.ysl(4 + pr, n), self.sl(rst, n), self.prm(60 + pr),
                                                         self.sl(gs, n), ALU.mult, ALU.mult),
                 reads=[k(rst), k(gs), "p_gn"], writes=[self.yk(4 + pr)])

    def s5_setup(self, l):
        t = self.t
        pin = self.pin
        Q = self.Q
        S = self.S5S
        KEY = "s5setup"
        col = [0]

        def alloc(ncol):
            c0 = col[0]
            col[0] += ncol
            assert col[0] <= 2176
            return S[:, c0:c0 + ncol]

        def dv(fn):
            t.op("dve", fn, reads=[KEY], writes=[KEY])

        def ac(fn):
            t.op("act", fn, reads=[KEY], writes=[KEY])

        def dma(out, in_):
            t.op("pool", lambda e: e.dma_start(out=out, in_=in_, allow_slow_non_contiguous=True),
                 reads=[KEY], writes=[KEY], stream="dma_io", inc=16)

        TT = lambda o, a, b, op: dv(lambda e: e.tensor_tensor(o, a, b, op))
        TS = lambda o, a, sc, op: dv(lambda e: e.tensor_single_scalar(o, a, sc, op))

        def cmul(o_r, o_i, a_r, a_i, b_r, b_i, t1, t2):
            TT(t1, a_r, b_r, ALU.mult)
            TT(t2, a_i, b_i, ALU.mult)
            TT(o_r, t1, t2, ALU.subtract)
            TT(t1, a_r, b_i, ALU.mult)
            TT(t2, a_i, b_r, ALU.mult)
            TT(o_i, t1, t2, ALU.add)

        t.op("dve", lambda e: e.memset(S[:, 0:8], 0.0), reads=[KEY], writes=[KEY] + self.S5S_keys)
        V = lambda: alloc(8)
        LR, LI, DT, X, TH, Cc, Ss, T1, T2, T3, AR, AI, WR, WI, MM, IAR, IAI, RI = [V() for _ in range(18)]
        lre = pin["s5_lam_re"][l]
        lim = pin["s5_lam_im"][l]
        ldt = pin["s5_log_dt"][l]
        dma(LR, bass.AP(lre.tensor, lre.offset, [[1, 128], [128, 8]]))
        dma(LI, bass.AP(lim.tensor, lim.offset, [[1, 128], [128, 8]]))
        for gl in range(2):
            dma(DT[gl * 64:(gl + 1) * 64, :], bass.AP(ldt.tensor, ldt.offset + gl, [[0, 64], [2, 8]]))
        ac(lambda e: e.activation(out=DT, in_=DT, func=AF.Exp))
        TT(X, LR, DT, ALU.mult)
        TT(TH, LI, DT, ALU.mult)
        ac(lambda e: e.activation(out=Ss, in_=TH, func=AF.Sin, scale=1.0 / 16))
        TS(T3, TH, 1.0 / 16, ALU.mult)
        TS(T3, T3, math.pi / 2, ALU.add)
        ac(lambda e: e.activation(out=Cc, in_=T3, func=AF.Sin))
        for _ in range(4):
            TT(T1, Cc, Cc, ALU.mult)
            TT(T2, Ss, Ss, ALU.mult)
            TT(T3, Cc, Ss, ALU.mult)
            TT(Cc, T1, T2, ALU.subtract)
            TS(Ss, T3, 2.0, ALU.mult)
        ac(lambda e: e.activation(out=T3, in_=X, func=AF.Exp))
        TT(AR, T3, Cc, ALU.mult)
        TT(AI, T3, Ss, ALU.mult)
        ac(lambda e: e.activation(out=self.RR[:, :], in_=X, func=AF.Exp, scale=float(Q)))
        dv(lambda e: e.reciprocal(RI, self.RR[:, :]))
        TT(T1, T3, T3, ALU.mult)
        dv(lambda e: e.reciprocal(T1, T1))
        TT(IAR, AR, T1, ALU.mult)
        TT(IAI, AI, T1, ALU.mult)
        TS(IAI, IAI, -1.0, ALU.mult)
        TS(T3, AR, -1.0, ALU.add)
        TT(T1, LR, LR, ALU.mult)
        TT(T2, LI, LI, ALU.mult)
        TT(MM, T1, T2, ALU.add)
        dv(lambda e: e.reciprocal(MM, MM))
        TT(T1, T3, LR, ALU.mult)
        TT(T2, AI, LI, ALU.mult)
        TT(WR, T1, T2, ALU.add)
        TT(WR, WR, MM, ALU.mult)
        TT(T1, AI, LR, ALU.mult)
        TT(T2, T3, LI, ALU.mult)
        TT(WI, T1, T2, ALU.subtract)
        TT(WI, WI, MM, ALU.mult)
        PWr = alloc(8 * (Q + 1)).rearrange("p (a s) -> p a s", s=Q + 1)
        PWi = alloc(8 * (Q + 1)).rearrange("p (a s) -> p a s", s=Q + 1)
        IPr = alloc(8 * Q).rearrange("p (a s) -> p a s", s=Q)
        IPi = alloc(8 * Q).rearrange("p (a s) -> p a s", s=Q)
        dv(lambda e: e.memset(PWr[:, :, 0], 1.0))
        dv(lambda e: e.memset(PWi[:, :, 0], 0.0))
        dv(lambda e: e.memset(IPr[:, :, 0], 1.0))
        dv(lambda e: e.memset(IPi[:, :, 0], 0.0))
        for sidx in range(Q):
            cmul(PWr[:, :, sidx + 1], PWi[:, :, sidx + 1], PWr[:, :, sidx], PWi[:, :, sidx], AR, AI, T1, T2)
        for sidx in range(Q - 1):
            cmul(IPr[:, :, sidx + 1], IPi[:, :, sidx + 1], IPr[:, :, sidx], IPi[:, :, sidx], IAR, IAI, T1, T2)
        FBr = alloc(8 * Q).rearrange("p (a s) -> p a s", s=Q)
        FBi = alloc(8 * Q).rearrange("p (a s) -> p a s", s=Q)
        for sidx in range(Q):
            cmul(FBr[:, :, sidx], FBi[:, :, sidx], IPr[:, :, sidx], IPi[:, :, sidx], WR, WI, T1, T2)
        TBr, TBi = self.TB[:, :, 0, :], self.TB[:, :, 1, :]
        dv(lambda e: e.memset(TBr[:, :, 0], 1.0))
        dv(lambda e: e.memset(TBi[:, :, 0], 0.0))
        TT(TBr[:, :, 1], PWr[:, :, Q], RI, ALU.mult)
        TT(TBi[:, :, 1], PWi[:, :, Q], RI, ALU.mult)
        big = lambda: alloc(256).rearrange("p (a h) -> p a h", h=32)
        Br, Bi, Sr, Si, U1, U2 = [big() for _ in range(6)]
        TWs = (U1.rearrange("p a h -> p (a h)"), U2.rearrange("p a h -> p (a h)"))
        tw = lambda i, m: TWs[i][:, 0:8 * m].rearrange("p (a c) -> p a c", c=m)
        m = 1
        while m < 64:
            if m >= 2:
                h = m // 2
                cmul(TBr[:, :, m], TBi[:, :, m], TBr[:, :, h], TBi[:, :, h], TBr[:, :, h], TBi[:, :, h], T1, T2)
            if m >= 2:
                br = TBr[:, :, m:m + 1].to_broadcast([128, 8, m - 1])
                bi = TBi[:, :, m:m + 1].to_broadcast([128, 8, m - 1])
                cmul(TBr[:, :, m + 1:2 * m], TBi[:, :, m + 1:2 * m], TBr[:, :, 1:m], TBi[:, :, 1:m], br, bi,
                     tw(0, m - 1), tw(1, m - 1))
            m *= 2
        cmul(TBr[:, :, 64], TBi[:, :, 64], TBr[:, :, 32], TBi[:, :, 32], TBr[:, :, 32], TBi[:, :, 32], T1, T2)
        rb = self.RR[:, :].unsqueeze(2).to_broadcast([128, 8, 64])
        TT(self.TA[:, :, 0, :], TBr[:, :, 0:64], rb, ALU.mult)
        TT(self.TA[:, :, 1, :], TBi[:, :, 0:64], rb, ALU.mult)
        TS(self.TA[:, :, 1, :], self.TA[:, :, 1, :], -1.0, ALU.mult)
        for ap_ in (Br, Bi):
            dv(lambda e, ap_=ap_: e.memset(ap_, 0.0))
        for g in range(16):
            P_, gl = g // 2, g % 2
            dma(Br[gl * 64:(gl + 1) * 64, P_, gl * 16:(gl + 1) * 16], pin["s5_b_re"][l, g])
            dma(Bi[gl * 64:(gl + 1) * 64, P_, gl * 16:(gl + 1) * 16], pin["s5_b_im"][l, g])
        for sidx in range(Q):
            fr = FBr[:, :, sidx:sidx + 1].to_broadcast([128, 8, 32])
            fi = FBi[:, :, sidx:sidx + 1].to_broadcast([128, 8, 32])
            cmul(Sr, Si, Br, Bi, fr, fi, U1, U2)
            for ri, src in enumerate((Sr, Si)):
                for half in range(2):
                    pi = self.nextps()
                    t.op("pe", lambda e, src=src, half=half, pi=pi: e.transpose(
                        self.ps[pi][:, 0:128], src[:, half * 4:(half + 1) * 4, :], self.ident[:, :]),
                        reads=[KEY, "ident"], writes=[("ps", pi)])
                    t.op("act", lambda e, half=half, sidx=sidx, ri=ri, pi=pi: e.activation(
                        out=self.BsT[:, half, sidx, ri, :], in_=self.ps[pi][:, 0:128], func=AF.Copy),
                        reads=[("ps", pi)], writes=["BsT"])
        CNr = Br.rearrange("p a h -> p (a h)").rearrange("p (a q) -> p a q", q=128)
        CNi = Bi.rearrange("p a h -> p (a h)").rearrange("p (a q) -> p a q", q=128)
        for ap_ in (CNr, CNi):
            dv(lambda e, ap_=ap_: e.memset(ap_, 0.0))
        for g in range(16):
            P_, gl = g // 2, g % 2
            half, j = P_ // 4, P_ % 4
            r0 = 32 * j + 16 * gl
            dma(CNr[r0:r0 + 16, half, gl * 64:(gl + 1) * 64], pin["s5_c_re"][l, g])
            dma(CNi[r0:r0 + 16, half, gl * 64:(gl + 1) * 64], pin["s5_c_im"][l, g])
        for src, dst in ((CNr, Sr), (CNi, Si)):
            for half in range(2):
                pi = self.nextps()
                t.op("pe", lambda e, src=src, half=half, pi=pi: e.transpose(
                    self.ps[pi][:, 0:128], src[:, half, :], self.ident[:, :]),
                    reads=[KEY, "ident"], writes=[("ps", pi)])
                t.op("act", lambda e, dst=dst, half=half, pi=pi: e.activation(
                    out=dst[:, half * 4:(half + 1) * 4, :], in_=self.ps[pi][:, 0:128].rearrange("p (a h) -> p a h", h=32),
                    func=AF.Copy), reads=[("ps", pi), KEY], writes=[KEY])
        for sidx in range(Q):
            pr_ = PWr[:, :, sidx:sidx + 1].to_broadcast([128, 8, 32])
            pi_ = PWi[:, :, sidx:sidx + 1].to_broadcast([128, 8, 32])
            cmul(self.CsT[:, :, sidx, 0, :], self.CsT[:, :, sidx, 1, :], Sr, Si, pr_, pi_, U1, U2)
            dv(lambda e, sidx=sidx: e.tensor_single_scalar(self.CsT[:, :, sidx, 1, :], self.CsT[:, :, sidx, 1, :],
                                                           -1.0, ALU.mult))
        t.op("dve", lambda e: e.tensor_copy(self.CH[:, 0, 0:1], self.CH[:, 0, 0:1]), reads=[KEY],
             writes=["CsT", "TAB"] + self.S5S_keys)
        t.op("pool", lambda e: e.dma_start(out=self.GW[:, :, :],
                                           in_=pin["s5_glu_w"][l].rearrange("(kt p) n -> p kt n", p=128)),
             writes=["GW"], stream="dma_io", inc=16)
        t.op("dve", lambda e: e.memset(self.CARJ[:], 0.0), writes=[("CARJ", P_) for P_ in range(8)])

    def mix_s5(self, l, ti, n):
        t = self.t
        Y, U, Z = self.S_Y, self.S_U, self.S_Z
        k = self.slk
        Q = self.Q
        ncn = n // Q
        tm = lambda ap: ap.rearrange("p (c s) -> p s c", s=Q)
        sm = lambda ap: ap.rearrange("p (s c) -> p s c", s=Q)
        for half in range(2):
            t.op("act", lambda e, half=half: e.activation(out=sm(self.sl(Z[half], n)), in_=tm(self.sl(U[half], n)),
                                                          func=AF.Copy),
                 reads=[k(U[half])], writes=[k(Z[half])])
        YB = [self.ps[6], self.ps[7]]
        tv = lambda i: self.tmp[i][:, 0:8 * ncn].rearrange("p (a c) -> p a c", c=ncn)
        GLr, GLi, Vr, Vi = tv(0), tv(1), tv(2), tv(3)
        SY = self.SL[:, Y[0]:Y[0] + 8, :].rearrange("p a b -> p (a b)")
        n1 = ncn + 1
        JT = SY[:, 0:16 * n1].rearrange("p (a r c) -> p a r c", a=8, r=2)
        JJ = SY[:, 1040:1040 + 16 * n1].rearrange("p (a r c) -> p a r c", a=8, r=2)
        T1 = SY[:, 2080:2080 + 8 * n1].rearrange("p (a c) -> p a c", a=8)
        T2 = SY[:, 2600:2600 + 8 * n1].rearrange("p (a c) -> p a c", a=8)
        ck = [("tmp", i) for i in range(4)] + [k(i) for i in Y]
        Wk, Gk, Gs = (Z[2], Z[3]), (Z[4], Z[5]), (Z[6], Z[7])
        def do_b(P_):
            half, j = P_ // 4, P_ % 4
            rj = slice(32 * j, 32 * j + 32)
            pvs = []
            for ri in range(2):
                pv = self.nextps()
                pvs.append(pv)

                def emit_b(pe, ri=ri, pv=pv):
                    ins = None
                    for sidx in range(Q):
                        ins = pe.matmul(self.ps[pv][:, sidx * ncn:(sidx + 1) * ncn],
                                        self.BsT[rj, half, sidx, ri, :],
                                        self.SL[rj, Z[half], sidx * ncn:(sidx + 1) * ncn],
                                        start=True, stop=True, tile_position=(32 * j, 0))
                    return ins
                t.op("pe", emit_b, reads=["BsT", k(Z[half])], writes=[("ps", pv)])
            return pvs

        def do_chain(P_, pvs):
            for ri in range(2):
                pv = pvs[ri]
                t.op("act", lambda e, ri=ri, pv=pv: e.activation(out=tm(self.sl(Wk[ri], n)), in_=sm(self.ps[pv][:, 0:n]),
                                                                func=AF.Copy),
                     reads=[("ps", pv)], writes=[k(Wk[ri])])
                t.op("dve", lambda e, ri=ri: e.tensor_tensor_scan(self.sl(Gk[ri], n), self.cmask8[:, 0:n],
                                                                  self.sl(Wk[ri], n), 0.0, ALU.mult, ALU.add),
                     reads=[k(Wk[ri]), "cmask8"], writes=[k(Gk[ri])])
                t.op("pool", lambda e, ri=ri: e.tensor_copy(sm(self.sl(Gs[ri], n)), tm(self.sl(Gk[ri], n))),
                     reads=[k(Gk[ri])], writes=[k(Gs[ri])])
                gl = (GLr, GLi)[ri]
                t.op("pool", lambda e, ri=ri, gl=gl: e.tensor_copy(
                    gl[:, P_, :], self.SL[:, Gk[ri], 0:n].rearrange("p (c s) -> p c s", s=Q)[:, :, Q - 1]),
                    reads=[k(Gk[ri])], writes=[("tmp", ri)])

        def do_c(P_):
            half, j = P_ // 4, P_ % 4
            rj = slice(32 * j, 32 * j + 32)

            def emit_c(pe):
                ins = None
                for sidx in range(Q):
                    cs = slice(sidx * ncn, (sidx + 1) * ncn)
                    pe.matmul(YB[half][rj, cs], self.CsT[:, P_, sidx, 0, :], self.SL[:, Gs[0], cs],
                              start=(sidx == 0), stop=False, tile_position=(0, 32 * j))
                    ins = pe.matmul(YB[half][rj, cs], self.CsT[:, P_, sidx, 1, :], self.SL[:, Gs[1], cs],
                                    start=False, stop=False, tile_position=(0, 32 * j))
                return ins
            t.op("pe", emit_c, reads=["CsT", k(Gs[0]), k(Gs[1])], writes=[("ps", 6 + half)])

        pv_next = do_b(0)
        for P_ in range(8):
            pv_cur = pv_next
            do_chain(P_, pv_cur)
            if P_ + 1 < 8:
                pv_next = do_b(P_ + 1)
            do_c(P_)

        def dv(fn, extra_r=(), extra_w=()):
            t.op("dve", fn, reads=ck + ["TAB"] + list(extra_r), writes=ck + list(extra_w))
        TAr, TAi = self.TA[:, :, 0, 0:ncn], self.TA[:, :, 1, 0:ncn]
        TBr, TBi = self.TB[:, :, 0, 0:n1], self.TB[:, :, 1, 0:n1]
        t1, t2 = T1[:, :, 0:ncn], T2[:, :, 0:ncn]
        dv(lambda e: e.tensor_tensor(t1, GLr, TAr, ALU.mult))
        dv(lambda e: e.tensor_tensor(t2, GLi, TAi, ALU.mult))
        dv(lambda e: e.tensor_tensor(Vr, t1, t2, ALU.subtract))
        dv(lambda e: e.tensor_tensor(t1, GLr, TAi, ALU.mult))
        dv(lambda e: e.tensor_tensor(t2, GLi, TAr, ALU.mult))
        dv(lambda e: e.tensor_tensor(Vi, t1, t2, ALU.add))
        cjk = [("CARJ", i) for i in range(8)]
        for ri in range(2):
            dv(lambda e, ri=ri: e.tensor_copy(JT[:, :, ri, 0], self.CARJ[:, :, ri]), extra_r=cjk)
        for P_ in range(8):
            for ri, v_ in enumerate((Vr, Vi)):
                dv(lambda e, ri=ri, v_=v_, P_=P_: e.tensor_tensor_scan(
                    JT[:, P_, ri, 1:n1], self.RR[:, P_:P_ + 1].to_broadcast([128, ncn]), v_[:, P_, :],
                    self.CARJ[:, P_, ri:ri + 1], ALU.mult, ALU.add), extra_r=cjk)
        dv(lambda e: e.tensor_tensor(T1, JT[:, :, 0, :], TBr, ALU.mult))
        dv(lambda e: e.tensor_tensor(T2, JT[:, :, 1, :], TBi, ALU.mult))
        dv(lambda e: e.tensor_tensor(JJ[:, :, 0, :], T1, T2, ALU.subtract))
        dv(lambda e: e.tensor_tensor(T1, JT[:, :, 0, :], TBi, ALU.mult))
        dv(lambda e: e.tensor_tensor(T2, JT[:, :, 1, :], TBr, ALU.mult))
        dv(lambda e: e.tensor_tensor(JJ[:, :, 1, :], T1, T2, ALU.add))
        for ri in range(2):
            dv(lambda e, ri=ri: e.tensor_copy(self.CARJ[:, :, ri], JJ[:, :, ri, ncn]), extra_r=cjk, extra_w=cjk)
        for half in range(2):
            for j in range(4):
                P_ = half * 4 + j
                rj = slice(32 * j, 32 * j + 32)

                def emit_j(pe, P_=P_, half=half, rj=rj, j=j):
                    ins = None
                    for sidx in range(Q):
                        cs = slice(sidx * ncn, (sidx + 1) * ncn)
                        pe.matmul(YB[half][rj, cs], self.CsT[:, P_, sidx, 0, :], JJ[:, P_, 0, 0:ncn],
                                  start=False, stop=False, tile_position=(0, 32 * j))
                        ins = pe.matmul(YB[half][rj, cs], self.CsT[:, P_, sidx, 1, :], JJ[:, P_, 1, 0:ncn],
                                        start=False, stop=True, tile_position=(0, 32 * j))
                    return ins
                t.op("pe", emit_j, reads=["CsT"] + ck, writes=[("ps", 6 + half)])
            ys = U[20 + half]
            t.op("dve", lambda e, half=half, ys=ys: e.scalar_tensor_tensor(
                tm(self.sl(ys, n)), tm(self.sl(U[half], n)), self.prm(62 + half), sm(YB[half][:, 0:n]),
                ALU.mult, ALU.add),
                reads=[k(U[half]), ("ps", 6 + half), "p_s5d"], writes=[k(ys)])
            t.op("act", lambda e, ys=ys: e.activation(out=self.sl(ys, n), in_=self.sl(ys, n),
                                                      func=AF.Gelu_apprx_tanh),
                 reads=[k(ys)], writes=[k(ys)])
        for m in range(2):
            pi = self.nextps()
            self.mm(self.ps[pi][:, 0:n], ("ps", pi),
                    [(self.GW[:, kt, m * 128:(m + 1) * 128], self.sl(U[20 + kt], n)) for kt in range(2)],
                    reads=[k(U[20]), k(U[21]), "GW"])
            tk = self.nexttmp()
            t.op("act", lambda e, tk=tk, pi=pi, m=m: e.activation(out=self.tmp[tk][:, 0:n], in_=self.ps[pi][:, 0:n],
                                                                 func=AF.Sigmoid, bias=self.prm(64 + m), scale=1.0),
                 reads=[("ps", pi), "p_glub"], writes=[("tmp", tk)])
            t.op("dve", lambda e, tk=tk, m=m: e.tensor_tensor(self.ysl(m, n), self.sl(U[20 + m], n),
                                                              self.tmp[tk][:, 0:n], ALU.mult),
                 reads=[("tmp", tk), k(U[20 + m])], writes=[self.yk(m)])


_CACHE = {}
_BUILDERS = {}


def _get_nc(n_xtiles):
    if n_xtiles not in _CACHE:
        _BUILDERS[n_xtiles] = Builder(n_xtiles)
        _CACHE[n_xtiles] = _BUILDERS[n_xtiles].build()
    return _CACHE[n_xtiles]


def make_in_maps(inputs, names, xs):
    npairs = len(xs)
    role_a = np.zeros((128, 2), np.float32)
    role_a[:, 0] = 1.0
    role_b = np.zeros((128, 2), np.float32)
    role_b[:, 1] = 1.0
    pa, pb = {}, {}
    for k in names:
        if k in ("x", "role"):
            continue
        arr = np.ascontiguousarray(inputs[k], dtype=np.float32)
        if k in ("meta_tokens", "hg_lb_raw"):
            pa[k] = arr
            pb[k] = arr
        else:
            pa[k] = np.ascontiguousarray(arr[[0, 0]])
            pb[k] = arr
    maps = [dict(pa, x=xs[i], role=role_a) for i in range(npairs)]
    maps += [dict(pb, x=xs[i], role=role_b) for i in range(npairs)]
    return maps


def kernel(**inputs):
    x = np.ascontiguousarray(inputs["x"], dtype=np.float32)
    bsz, seq, _ = x.shape
    n_xtiles = seq // NT
    nc = _get_nc(n_xtiles)
    names = _BUILDERS[n_xtiles].in_names
    in_maps = make_in_maps(inputs, names, [x[b] for b in range(bsz)])
    res = run_bass_kernel_spmd(nc, in_maps, core_ids=list(range(2 * bsz)))
    return np.stack([res.results[bsz + b]["out"] for b in range(bsz)], axis=0)
```

```python
import math
import os
from contextlib import ExitStack

import numpy as np
import concourse.bass as bass
import concourse.mybir as mybir
from concourse.bass_utils import run_bass_kernel_spmd

F32 = mybir.dt.float32
BF16 = mybir.dt.bfloat16
AF = mybir.ActivationFunctionType
ALU = mybir.AluOpType

D = 1024
KT = 8
NIN = 2560
DFF = 2816
FT = 22
NMETA = 16
SEQ = 8192
DEPTH = 2
ALPHA = (2 * DEPTH) ** 0.25
EPS = 1e-5
NT = 512
SKEW = 2
WSLOT = 4096
NWSLOT = 2
NBSLOT = 4
NCHUNK = 26


class Trk:
    def __init__(self, nc, es):
        self.nc = nc
        self.es = es
        self.E = {"pe": nc.tensor, "act": nc.scalar, "dve": nc.vector,
                  "pool": nc.gpsimd, "sp": nc.sync}
        self.cur = {}
        self.waited = {}
        self.lastw = {}
        self.rd = {}
        self.nsem = 0
        self.LIM = 30000
        self.nins = 0

    def _sem(self, stream, inc):
        s = self.cur.get(stream)
        if s is None or s[1] + inc > self.LIM:
            name = f"s{self.nsem}"
            sem = self.es.enter_context(self.nc.semaphore(f"{name}_{stream}"))
            self.nsem += 1
            s = [sem, 0, name]
            self.cur[stream] = s
        s[1] += inc
        return (s[0], s[1], s[2])

    def wait(self, engine, tok):
        sem, val, name = tok
        k = (engine, name)
        if self.waited.get(k, 0) >= val:
            return
        self.waited[k] = val
        self.E[engine].wait_ge(sem, val)

    def op(self, engine, emit, reads=(), writes=(), stream=None, inc=1):
        deps = {}

        def add(tok):
            if tok is None:
                return
            n = tok[2]
            if n not in deps or deps[n][1] < tok[1]:
                deps[n] = tok

        for k in reads:
            add(self.lastw.get(k))
        for k in writes:
            add(self.lastw.get(k))
            for t in self.rd.get(k, {}).values():
                add(t)
        st = stream or engine
        own = self.cur.get(st)
        for tok in deps.values():
            if engine == "pe" and stream is None and own is not None and tok[2] == own[2]:
                continue
            self.wait(engine, tok)
        ins = emit(self.E[engine])
        tok = self._sem(st, inc)
        ins.then_inc(tok[0], inc)
        self.nins += 1
        for k in reads:
            self.rd.setdefault(k, {})[tok[2]] = tok
        for k in writes:
            self.lastw[k] = tok
            self.rd[k] = {}
        return tok

    def drain(self, engine):
        for st, s in self.cur.items():
            self.wait(engine, (s[0], s[1], s[2]))


class _Stop(Exception):
    pass


class Builder:
    def __init__(self, n_xtiles, stub=(), taps=(), npairs=4):
        self.npairs = npairs
        self.n_xtiles = n_xtiles
        self.stub = set(stub)
        self.taps = list(taps)
        self.tiles = [(0, NMETA)] + [(NMETA + i * NT, NT) for i in range(n_xtiles)]
        self.seq_x = n_xtiles * NT

    def build(self):
        nc = bass.Bass("TRN2", target_bir_lowering=False)
        self.nc = nc
        self.es = ExitStack()
        es = self.es
        self.t = Trk(nc, es)
        t = self.t
        self.in_names = []

        def di(name, shape):
            self.in_names.append(name)
            return nc.dram_tensor(name, list(shape), F32, kind="ExternalInput").ap()
        self.x = di("x", [self.seq_x, D])
        self.meta = di("meta_tokens", [NMETA, D])
        self.w_in = di("w_in", [DEPTH, D, NIN])
        self.w_out = di("w_out", [DEPTH, D, D])
        self.w_f1 = di("w_ffn_in", [DEPTH, D, 2 * DFF])
        self.w_f2 = di("w_ffn_out", [DEPTH, DFF, D])
        self.ln = {n: di(n, [DEPTH, D]) for n in ("ln1_g", "ln1_b", "ln2_g", "ln2_b")}
        self.pin = {}
        for nm, shp in (("hg_lb_raw", [DEPTH, 256]), ("sc_conv_w", [DEPTH, 3, 256]), ("hg_gnorm", [DEPTH, 256]),
                        ("lru_conv_w", [DEPTH, 4, 256]), ("lru_conv_b", [DEPTH, 256]),
                        ("lru_wa", [DEPTH, 4, 64, 64]), ("lru_ba", [DEPTH, 256]),
                        ("lru_wx", [DEPTH, 4, 64, 64]), ("lru_bx", [DEPTH, 256]), ("lru_a_param", [DEPTH, 256]),
                        ("s5_lam_re", [DEPTH, 16, 64]), ("s5_lam_im", [DEPTH, 16, 64]),
                        ("s5_b_re", [DEPTH, 16, 64, 16]), ("s5_b_im", [DEPTH, 16, 64, 16]),
                        ("s5_c_re", [DEPTH, 16, 16, 64]), ("s5_c_im", [DEPTH, 16, 16, 64]),
                        ("s5_d", [DEPTH, 256]), ("s5_log_dt", [DEPTH, 16]),
                        ("s5_glu_w", [DEPTH, 256, 256]), ("s5_glu_b", [DEPTH, 256])):
            self.pin[nm] = di(nm, shp)
        self.role_in = di("role", [128, 2])
        self.out = nc.dram_tensor("out", [self.seq_x, D], F32, kind="ExternalOutput").ap()
        ntile = len(self.tiles)
        self.send = [nc.dram_tensor(f"send{i}", [128, KT * NT], F32, kind="Internal").ap() for i in range(2)]
        self.recv = [nc.dram_tensor(f"recv{i}", [128, KT * NT], F32, kind="Internal").ap() for i in range(4)]
        self.groups = [[i, i + self.npairs] for i in range(self.npairs)]
        self.wbf = nc.dram_tensor("wbf", [NCHUNK, 128, WSLOT], BF16, kind="Internal").ap()
        self.tapout = {}
        for name, shape in self.taps:
            self.tapout[name] = nc.dram_tensor("tap_" + name, list(shape), F32, kind="ExternalOutput").ap()

        sb = lambda name, shape: es.enter_context(nc.sbuf_tensor(name, list(shape), F32))
        self.NSLAB = 46
        self.SL = sb("SL", [128, self.NSLAB, NT])
        self.W = [sb(f"W{i}", [128, WSLOT]) for i in range(NWSLOT)]
        self.ident = sb("ident", [128, 128])
        self.ones = sb("ones", [128, 128])
        self.PRM = sb("PRM", [128, 128])
        self.tmp = [sb(f"tmp{i}", [128, NT]) for i in range(4)]
        self.cmask = sb("cmask", [128, NT])
        self.bd64 = sb("bd64", [128, 128])
        self.lruw = sb("lruw", [128, 2, 2, 128])
        self.car_sc = sb("car_sc", [128, 2, 2])
        self.car_lx = sb("car_lx", [128, 2, 3])
        self.car_lh = sb("car_lh", [128, 2])
        self.SS = sb("SS", [128, 2, 9, 128])
        self.SM = [sb(f"SM{i}", [128, 2, 128]) for i in range(2)]
        self.mask2 = sb("mask2", [128, 128])
        self.smn = 0
        self.Q = 8
        self.BsT = sb("BsT", [128, 2, 8, 2, 128])
        self.CsT = sb("CsT", [128, 8, 8, 2, 32])
        self.TA = sb("TA", [128, 8, 2, 64])
        self.TB = sb("TB", [128, 8, 2, 65])
        self.RR = sb("RR", [128, 8])
        self.GW = sb("GW", [128, 2, 256])
        self.CARJ = sb("CARJ", [128, 8, 2])
        self.cmask8 = sb("cmask8", [128, NT])
        self.CH = sb("CH", [128, 8, 66])
        self.ROLE = sb("ROLE", [128, 2])
        self.HM0 = sb("HM0", [128, KT, NMETA])
        self.SNAP = sb("SNAP", [128, 2 * 2 + 2 * 3 + 2 + 2 * 128 + 8 * 2])
        self.ps = [es.enter_context(nc.psum_tensor(f"ps{i}", [128, NT], F32)) for i in range(8)]
        self.psn = 0
        self.tmpn = 0
        self.S_H = list(range(0, 8))
        self.S_Y = list(range(8, 16))
        self.S_Z = list(range(16, 24))
        self.S_U = list(range(24, 46))
        self.S5S = self.SL[:, 24:29, :].rearrange("p a b -> p (a b)")[:, 0:2176]
        self.S5S_keys = [("sl", i) for i in range(24, 29)]
        self.CB = es.enter_context(nc.sbuf_tensor("CB", [128, WSLOT], BF16))
        self.WBv = [self.W[i // 2][:, :].bitcast(BF16)[:, (i % 2) * WSLOT:(i % 2 + 1) * WSLOT] for i in range(NBSLOT)]
        self.bf = False

        t.op("pool", lambda e: e.memset(self.ident[:], 0.0), writes=["ident"])
        t.op("pool", lambda e: e.affine_select(
            out=self.ident[:], in_=self.ident[:], compare_op=ALU.not_equal, fill=1.0,
            base=0, pattern=[[-1, 128]], channel_multiplier=1), writes=["ident"])
        t.op("pool", lambda e: e.memset(self.ones[:], 1.0), writes=["ones"])
        t.op("pool", lambda e: e.memset(self.cmask[:], 1.0), writes=["cmask"])
        t.op("pool", lambda e: e.affine_select(
            out=self.cmask[:].rearrange("p (c s) -> p c s", s=64), in_=self.cmask[:].rearrange("p (c s) -> p c s", s=64),
            compare_op=ALU.not_equal, fill=0.0, base=0, pattern=[[0, NT // 64], [1, 64]], channel_multiplier=0),
            writes=["cmask"])
        t.op("pool", lambda e: e.memset(self.cmask8[:], 1.0), writes=["cmask8"])
        t.op("pool", lambda e: e.affine_select(
            out=self.cmask8[:].rearrange("p (c s) -> p c s", s=8), in_=self.cmask8[:].rearrange("p (c s) -> p c s", s=8),
            compare_op=ALU.not_equal, fill=0.0, base=0, pattern=[[0, NT // 8], [1, 8]], channel_multiplier=0),
            writes=["cmask8"])
        t.op("pool", lambda e: e.memset(self.mask2[:], 1.0), writes=["mask2"])
        t.op("pool", lambda e: e.affine_select(
            out=self.mask2[:, :], in_=self.mask2[:, :],
            compare_op=ALU.is_ge, fill=0.0, base=0, pattern=[[1, 128]], channel_multiplier=-1),
            writes=["mask2"])
        t.op("pool", lambda e: e.memset(self.mask2[0:64, 64:128], 0.0), writes=["mask2"])
        t.op("pool", lambda e: e.memset(self.bd64[:], 0.0), writes=["bd64"])
        for hh in range(2):
            t.op("pool", lambda e, hh=hh: e.memset(self.bd64[hh * 64:(hh + 1) * 64, hh * 64:(hh + 1) * 64], 1.0 / 64),
                 writes=["bd64"])

        t.op("pool", lambda e: e.dma_start(out=self.ROLE[:, :], in_=self.role_in), writes=["role"],
             stream="dma_io", inc=16)
        self.fA = self.ROLE[:, 0:1]
        self.fB = self.ROLE[:, 1:2]
        nsteps = self.n_xtiles + SKEW
        self.nsteps = nsteps
        self.wq = []
        self.wq_issued = 0
        self.wq_used = 0
        self.wq.extend(self.chunk_seq(0))
        self.wq.extend(self.chunk_seq(1))
        for _ in range(nsteps):
            self.wq.extend(self.bf_chunk_seq())
        self._index_queue()

        try:
            self._program(nsteps)
        except _Stop:
            pass
        t.drain("pool")
        es.close()
        return nc

    def dbg(self, lvl):
        if int(os.environ.get("DBG_STOP", "99")) <= lvl:
            raise _Stop()

    def _program(self, nsteps):
        self.layer_setup(0, mine=False)
        self.tile_pass(0, "p0", NMETA)
        self.layer_setup(1, mine=True)
        self.tile_pass(1, "p1", NMETA)
        self.snapshot()
        self.dbg(1)
        self.bf = True
        for step in range(nsteps):
            self.tile_pass(1, "main", NT, step)
            if step == SKEW - 1:
                self.restore()

    def sl(self, idx, n=NT):
        return self.SL[:, idx, 0:n]

    def slk(self, idx):
        return ("sl", idx)

    def nextps(self):
        i = self.psn
        self.psn = (self.psn + 1) % 6
        return i

    def nexttmp(self):
        i = self.tmpn
        self.tmpn = (self.tmpn + 1) % 4
        return i

    def tap(self, name, src_ap, reads):
        if name in self.tapout:
            self.t.op("pool", lambda e: e.dma_start(out=self.tapout[name], in_=src_ap),
                      reads=reads, stream="dma_io", inc=16)

    def chunk_seq(self, l):
        seq = []
        wi = self.w_in[l].rearrange("(kt p) n -> p kt n", p=128)
        for c in range(5):
            seq.append(("win", [(wi[:, :, c * 512:(c + 1) * 512], 0, KT, 512)]))
        wo = self.w_out[l].rearrange("(kt p) n -> p kt n", p=128)
        for c in range(2):
            seq.append(("wout", [(wo[:, :, c * 512:(c + 1) * 512], 0, KT, 512)]))
        w1 = self.w_f1[l].rearrange("(kt p) n -> p kt n", p=128)
        for c in range(11):
            seq.append(("f1", [(w1[:, :, c * 256:(c + 1) * 256], 0, KT, 256),
                               (w1[:, :, DFF + c * 256:DFF + (c + 1) * 256], KT * 256, KT, 256)]))
        w2 = self.w_f2[l].rearrange("(kt p) n -> p kt n", p=128)
        for c in range(8):
            seq.append(("f2", [(w2[:, :, c * 128:(c + 1) * 128], 0, FT, 128)]))
        return seq

    def bf_chunk_seq(self):
        seq = []
        kinds = ["win"] * 5 + ["wout"] * 2 + ["f1"] * 11 + ["f2"] * 8
        for c, kd in enumerate(kinds):
            nel = FT * 128 if kd == "f2" else WSLOT
            seq.append((kd, [(self.wbf[c][:, 0:nel], 0, nel, 1)], c))
        return seq

    def _slot_of(self, idx):
        ent = self.wq[idx]
        if len(ent) == 3:
            return ("b", self.bcount[idx] % NBSLOT)
        return ("f", self.fcount[idx] % NWSLOT)

    def _slot_ap(self, slotid):
        return self.WBv[slotid[1]] if slotid[0] == "b" else self.W[slotid[1]]

    def _phys_keys(self, slotid):
        if slotid[0] == "b":
            return [("w", slotid[1] // 2, slotid[1] % 2, 0), ("w", slotid[1] // 2, slotid[1] % 2, 1)]
        return [("w", slotid[1], 0, 0), ("w", slotid[1], 1, 0), ("w", slotid[1], 0, 1), ("w", slotid[1], 1, 1)]

    def _issue_chunk(self, idx):
        ent = self.wq[idx]
        slotid = self._slot_of(idx)
        base = self._slot_ap(slotid)
        if len(ent) == 3:
            kind, parts, c = ent
            src, off, nel, _ = parts[0]
            self.t.op("sp", lambda e: e.dma_start(out=base[:, 0:nel], in_=src),
                      reads=[("wbf", c)], writes=self._phys_keys(slotid), stream=f"dma_w{slotid[1] // 2}", inc=16)
            return
        kind, parts = ent
        for pidx, (src, off, nk, ncol) in enumerate(parts):
            dst = base[:, off:off + nk * ncol].rearrange("p (k n) -> p k n", k=nk)
            wk = [("w", slotid[1], pidx, 0)] if len(parts) == 2 else [("w", slotid[1], 0, 0), ("w", slotid[1], 1, 0)]
            wk2 = [(a, b_, c_, 1) for (a, b_, c_, _) in wk]
            kh = nk // 2
            self.t.op("sp", lambda e, dst=dst, src=src: e.dma_start(out=dst[:, 0:kh, :], in_=src[:, 0:kh, :]),
                      writes=wk, stream=f"dma_w{slotid[1]}", inc=16)
            self.t.op("pool", lambda e, dst=dst, src=src: e.dma_start(out=dst[:, kh:nk, :], in_=src[:, kh:nk, :]),
                      writes=wk2, stream=f"dma_wp{slotid[1]}", inc=16)

    def _index_queue(self):
        self.bcount, self.fcount = {}, {}
        nb = nf = 0
        for i, ent in enumerate(self.wq):
            if len(ent) == 3:
                self.bcount[i] = nb
                nb += 1
            else:
                self.fcount[i] = nf
                nf += 1

    def wacquire(self, kind):
        idx = self.wq_used
        assert self.wq[idx][0] == kind, (self.wq[idx][0], kind)
        while self.wq_issued <= idx:
            self._issue_chunk(self.wq_issued)
            self.wq_issued += 1
        self.wq_used += 1
        self.cur_chunk = idx
        return self._slot_of(idx)

    def wprefetch(self):
        depth = (NBSLOT - 1) if len(self.wq[self.wq_used - 1]) == 3 else (NWSLOT - 1)
        lim = min(self.wq_used + depth, len(self.wq))
        while self.wq_issued < lim:
            nxt = self.wq[self.wq_issued]
            if (len(nxt) == 3) != (len(self.wq[self.wq_used - 1]) == 3):
                break
            self._issue_chunk(self.wq_issued)
            self.wq_issued += 1

    def wview(self, slotid, off, nk, ncol):
        return self._slot_ap(slotid)[:, off:off + nk * ncol].rearrange("p (k n) -> p k n", k=nk)

    def wkeys(self, slotid):
        return self._phys_keys(slotid)

    def cast_store(self, slotid, kind):
        c = self.cur_chunk - NCHUNK
        nel = FT * 128 if kind == "f2" else WSLOT
        self.t.op("act", lambda e: e.activation(out=self.CB[:, 0:nel], in_=self.W[slotid[1]][:, 0:nel], func=AF.Copy),
                  reads=self._phys_keys(slotid), writes=["CB"])
        self.t.op("pool", lambda e: e.dma_start(out=self.wbf[c][:, 0:nel], in_=self.CB[:, 0:nel]),
                  reads=["CB"], writes=[("wbf", c)], stream="dma_io", inc=16)

    def layer_setup(self, l, mine):
        t = self.t
        for i, n in enumerate(("ln1_g", "ln1_b", "ln2_g", "ln2_b")):
            src = self.ln[n][l].rearrange("(m p) -> p m", p=128)
            t.op("pool", lambda e, src=src, i=i: e.dma_start(
                out=self.PRM[:, i * 8:(i + 1) * 8], in_=src, allow_slow_non_contiguous=True),
                writes=[("prm", i)], stream="dma_io", inc=16)
        P = self.PRM
        pin = self.pin

        def vec(name, col, key):
            src = pin[name][l].rearrange("(c p) -> p c", p=128)
            t.op("pool", lambda e: e.dma_start(out=P[:, col:col + 2], in_=src, allow_slow_non_contiguous=True),
                 writes=[key], stream="dma_io", inc=16)

        for c in range(2):
            t.op("pool", lambda e, c=c: e.dma_start(
                out=P[:, 32 + 3 * c:35 + 3 * c],
                in_=pin["sc_conv_w"][l][:, c * 128:(c + 1) * 128].rearrange("k p -> p k"),
                allow_slow_non_contiguous=True), writes=["p_scw"], stream="dma_io", inc=16)
            t.op("pool", lambda e, c=c: e.dma_start(
                out=P[:, 38 + 4 * c:42 + 4 * c],
                in_=pin["lru_conv_w"][l][:, c * 128:(c + 1) * 128].rearrange("k p -> p k"),
                allow_slow_non_contiguous=True), writes=["p_lcw"], stream="dma_io", inc=16)
        vec("lru_conv_b", 46, "p_lcb")
        vec("lru_ba", 48, "p_lba")
        vec("lru_bx", 50, "p_lbx")
        vec("lru_a_param", 52, "p_cp")
        vec("hg_gnorm", 60, "p_gn")
        vec("s5_d", 62, "p_s5d")
        vec("s5_glu_b", 64, "p_glub")
        t.op("act", lambda e: e.activation(out=P[:, 52:54], in_=P[:, 52:54], func=AF.Exp, scale=-1.0),
             reads=["p_cp"], writes=["p_cp"])
        t.op("act", lambda e: e.activation(out=P[:, 52:54], in_=P[:, 52:54], func=AF.Ln, bias=1.0, scale=1.0),
             reads=["p_cp"], writes=["p_cp"])
        t.op("dve", lambda e: e.tensor_single_scalar(P[:, 54:56], P[:, 52:54], -16.0, ALU.mult),
             reads=["p_cp"], writes=["p_cp2"])
        t.op("dve", lambda e: e.tensor_single_scalar(P[:, 52:54], P[:, 52:54], -8.0, ALU.mult),
             reads=["p_cp", "p_cp2"], writes=["p_cp"])
        if not mine:
            t.op("dve", lambda e: e.memset(P[:, 56:58], 0.0), writes=["p_lb"])
        else:
            for i in range(2):
                src = pin["hg_lb_raw"][i].rearrange("(c p) -> p c", p=128)
                t.op("pool", lambda e, i=i, src=src: e.dma_start(out=P[:, 66 + 2 * i:68 + 2 * i], in_=src,
                                                                 allow_slow_non_contiguous=True),
                     writes=[("p_lbraw", i)], stream="dma_io", inc=16)
            t.op("dve", lambda e: e.tensor_tensor(P[:, 56:58], P[:, 68:70], P[:, 66:68], ALU.subtract),
                 reads=[("p_lbraw", 0), ("p_lbraw", 1)], writes=["p_lb"])
            t.op("act", lambda e: e.activation(out=P[:, 56:58], in_=P[:, 56:58], func=AF.Sigmoid),
                 reads=["p_lb"], writes=["p_lb"])
            t.op("dve", lambda e: e.tensor_single_scalar(P[:, 56:58], P[:, 56:58], self.fB, ALU.mult),
                 reads=["p_lb", "role"], writes=["p_lb"])
        t.op("dve", lambda e: e.tensor_scalar(P[:, 58:60], P[:, 56:58], -1.0, 1.0, ALU.mult, ALU.add),
             reads=["p_lb"], writes=["p_oml"])
        t.op("dve", lambda e: e.memset(self.lruw[:], 0.0), writes=["lruw"])
        for gi, nm in enumerate(("lru_wa", "lru_wx")):
            for h in range(4):
                c, hh = h // 2, h % 2
                t.op("pool", lambda e, gi=gi, nm=nm, h=h, c=c, hh=hh: e.dma_start(
                    out=self.lruw[hh * 64:(hh + 1) * 64, gi, c, hh * 64:(hh + 1) * 64], in_=pin[nm][l, h]),
                    reads=["lruw"], writes=[("lruw", gi, h)], stream="dma_io", inc=16)
        t.op("dve", lambda e: e.memset(self.car_sc[:], 0.0), writes=[("car_sc", 0), ("car_sc", 1)])
        t.op("dve", lambda e: e.memset(self.car_lx[:], 0.0), writes=[("car_lx", 0), ("car_lx", 1)])
        t.op("dve", lambda e: e.memset(self.car_lh[:], 0.0), writes=[("car_lh", 0), ("car_lh", 1)])
        t.op("dve", lambda e: e.memset(self.SS[:], 0.0), writes=[("SS", 0), ("SS", 1)])
        if "s5" not in self.stub:
            self.s5_setup(l)

    def evac(self, eng, out_ap, in_ap, reads, writes):
        if eng == "act":
            return self.t.op("act", lambda e: e.activation(out=out_ap, in_=in_ap, func=AF.Copy),
                             reads=reads, writes=writes)
        return self.t.op("dve", lambda e: e.tensor_copy(out_ap, in_ap), reads=reads, writes=writes)

    def mm(self, ps_ap, pskey, pairs, reads):
        def emit(pe):
            n = len(pairs)
            ins = None
            for i, (lh, rh) in enumerate(pairs):
                ins = pe.matmul(ps_ap, lh, rh, start=(i == 0), stop=(i == n - 1))
            return ins
        return self.t.op("pe", emit, reads=reads, writes=[pskey])

    def states(self):
        o = [0]

        def sn(ncol, shape=None):
            ap = self.SNAP[:, o[0]:o[0] + ncol]
            o[0] += ncol
            return ap
        return [
            (self.car_sc[:].rearrange("p a b -> p (a b)"), sn(4), [("car_sc", 0), ("car_sc", 1)]),
            (self.car_lx[:].rearrange("p a b -> p (a b)"), sn(6), [("car_lx", 0), ("car_lx", 1)]),
            (self.car_lh[:, :], sn(2), [("car_lh", 0), ("car_lh", 1)]),
            (self.SS[:, :, 0, :], sn(256).rearrange("p (a b) -> p a b", a=2), [("SS", 0), ("SS", 1)]),
            (self.CARJ[:].rearrange("p a b -> p (a b)"), sn(16), [("CARJ", i) for i in range(8)]),
        ]

    def snapshot(self):
        for st, snp, keys in self.states():
            self.t.op("dve", lambda e, st=st, snp=snp: e.tensor_copy(snp, st), reads=keys, writes=["snap"])

    def restore(self):
        for st, snp, keys in self.states():
            self.t.op("dve", lambda e, st=st: e.tensor_single_scalar(st, st, self.fA, ALU.mult),
                      reads=keys + ["role"], writes=keys)
            self.t.op("dve", lambda e, st=st, snp=snp: e.scalar_tensor_tensor(st, snp, self.fB, st, ALU.mult, ALU.add),
                      reads=keys + ["role", "snap"], writes=keys)

    def xstage(self, mode, nb):
        base = self.S_Y[0] if mode in ("p0", "p1") else self.S_U[0]
        st = self.SL[:, base:base + 8, :].rearrange("p a b -> p (a b)")
        return st[:, 0:nb * D].rearrange("p (b f) -> p b f", b=nb), [self.slk(base + i) for i in range(8)]

    def load_x(self, mode, n, step):
        nb = (n + 127) // 128
        pb = min(n, 128)
        stage, skeys = self.xstage(mode, nb)
        if mode in ("p0", "p1"):
            src = self.meta.rearrange("(b p) f -> p b f", p=pb)
        else:
            xi = min(step, self.n_xtiles - 1)
            src = self.x[xi * NT:(xi + 1) * NT, :].rearrange("(b p) f -> p b f", p=pb)
        self.t.op("pool", lambda e: e.dma_start(out=stage[0:pb], in_=src),
                  writes=skeys, stream="dma_io", inc=16)

    def load_recv(self, step):
        U = self.S_U
        RS = 14
        par = (step - SKEW) % 4
        rsrc = self.recv[par][:, :].rearrange("p (k n) -> p k n", k=KT)
        self.t.op("pool", lambda e: e.dma_start(out=self.SL[:, U[RS]:U[RS] + 8, :], in_=rsrc),
                  reads=[("recv", par)], writes=[self.slk(U[RS + i]) for i in range(8)], stream="dma_io", inc=16)

    def load_input(self, mode, n, step):
        t = self.t
        H, U = self.S_H, self.S_U
        nb = (n + 127) // 128
        pb = min(n, 128)
        stage, skeys = self.xstage(mode, nb)
        if mode in ("p0", "p1") or step == 0:
            self.load_x(mode, n, step)
        RS = 14
        if mode == "main" and step >= SKEW and not (SKEW >= 2 and step >= 1):
            self.load_recv(step)
        for k in range(KT):
            pi = self.nextps()

            def emit(pe, k=k, pi=pi):
                ins = None
                for b in range(nb):
                    ins = pe.transpose(self.ps[pi][:, b * pb:(b + 1) * pb],
                                       stage[0:pb, b, k * 128:(k + 1) * 128],
                                       self.ident[0:pb, 0:pb])
                return ins
            t.op("pe", emit, reads=skeys + ["ident"], writes=[("ps", pi)])
            hk = self.slk(H[k])
            if mode == "p0":
                self.evac("act" if k % 2 else "dve", self.sl(H[k], n), self.ps[pi][:, 0:n],
                          reads=[("ps", pi)], writes=[hk])
                continue
            if k % 2:
                t.op("act", lambda e, k=k, pi=pi: e.activation(out=self.sl(H[k], n), in_=self.ps[pi][:, 0:n],
                                                               func=AF.Copy, scale=self.fA),
                     reads=[("ps", pi), "role"], writes=[hk])
            else:
                t.op("dve", lambda e, k=k, pi=pi: e.tensor_single_scalar(self.sl(H[k], n), self.ps[pi][:, 0:n],
                                                                         self.fA, ALU.mult),
                     reads=[("ps", pi), "role"], writes=[hk])
            if mode == "p1":
                t.op("dve", lambda e, k=k: e.scalar_tensor_tensor(self.sl(H[k], n), self.HM0[:, k, :], self.fB,
                                                                  self.sl(H[k], n), ALU.mult, ALU.add),
                     reads=[hk, "role", "HM0"], writes=[hk])
        if mode == "main" and step >= SKEW:
            for k in range(KT):
                hk = self.slk(H[k])
                t.op("dve", lambda e, k=k: e.scalar_tensor_tensor(self.sl(H[k], n), self.sl(U[RS + k], n), self.fB,
                                                                  self.sl(H[k], n), ALU.mult, ALU.add),
                     reads=[hk, "role", self.slk(U[RS + k])], writes=[hk])

    def store_output(self, mode, n, step):
        t = self.t
        Z = self.S_Z
        if mode == "p0":
            t.op("act", lambda e: e.activation(out=self.HM0[:, :, :], in_=self.SL[:, Z[0]:Z[0] + 8, 0:NMETA],
                                               func=AF.Copy),
                 reads=[self.slk(i) for i in Z], writes=["HM0"])
            return
        if mode == "p1":
            return
        if step >= SKEW:
            nb = n // 128
            stage = self.SL[:, self.S_Y[0]:self.S_Y[0] + 8, :].rearrange("p a b -> p (a b)")
            stage = stage[:, 0:nb * D].rearrange("p (b f) -> p b f", b=nb)
            for b in range(nb):
                for half in range(2):
                    pi = self.nextps()

                    def emit(pe, b=b, half=half, pi=pi):
                        ins = None
                        for kk in range(4):
                            k = half * 4 + kk
                            ins = pe.transpose(self.ps[pi][:, kk * 128:(kk + 1) * 128],
                                               self.SL[:, Z[k], b * 128:(b + 1) * 128],
                                               self.ident[:, :])
                        return ins
                    t.op("pe", emit, reads=[self.slk(Z[half * 4 + kk]) for kk in range(4)] + ["ident"],
                         writes=[("ps", pi)])
                    self.evac("act" if half else "dve", stage[:, b, half * 512:(half + 1) * 512],
                              self.ps[pi][:, :], reads=[("ps", pi)], writes=[self.slk(self.S_Y[2 * b + half])])
            r0 = (step - SKEW) * NT
            dst = self.out[r0:r0 + n, :].rearrange("(b p) f -> p b f", p=128)
            t.op("pool", lambda e: e.dma_start(out=dst, in_=stage),
                 reads=[self.slk(i) for i in self.S_Y], stream="dma_io", inc=16)
        par = step % 2
        rpar = step % 4
        for k in range(KT):
            if k % 2:
                t.op("act", lambda e, k=k: e.activation(out=self.sl(Z[k]), in_=self.sl(Z[k]), func=AF.Copy,
                                                        scale=self.fA),
                     reads=[self.slk(Z[k]), "role"], writes=[self.slk(Z[k])])
            else:
                t.op("dve", lambda e, k=k: e.tensor_single_scalar(self.sl(Z[k]), self.sl(Z[k]), self.fA, ALU.mult),
                     reads=[self.slk(Z[k]), "role"], writes=[self.slk(Z[k])])
        t.op("pool", lambda e: e.dma_start(out=self.send[par].rearrange("p (k n) -> p k n", k=KT),
                                           in_=self.SL[:, Z[0]:Z[0] + 8, :]),
             reads=[self.slk(Z[i]) for i in range(8)], writes=[("send", par)], stream="dma_io", inc=16)
        t.op("pool", lambda e: e.collective_compute("AllReduce", ALU.add, replica_groups=self.groups,
                                                    ins=[self.send[par]], outs=[self.recv[rpar]]),
             reads=[("send", par)], writes=[("recv", rpar)], stream="cc", inc=1)

    def layernorm(self, n, gcol, bcol):
        t = self.t
        Z = self.S_Z
        ps_s = self.nextps()
        ps_q = self.nextps()
        self.mm(self.ps[ps_s][:, 0:n], ("ps", ps_s),
                [(self.ones[:, :], self.sl(Z[m], n)) for m in range(KT)],
                reads=[self.slk(Z[m]) for m in range(KT)] + ["ones"])
        sq = []
        for m in range(KT):
            ti = self.nexttmp()
            t.op("act", lambda e, m=m, ti=ti: e.activation(out=self.tmp[ti][:, 0:n], in_=self.sl(Z[m], n),
                                                          func=AF.Square),
                 reads=[self.slk(Z[m])], writes=[("tmp", ti)])
            first = (m == 0)
            last = (m == KT - 1)
            t.op("pe", lambda e, ti=ti, first=first, last=last: e.matmul(
                self.ps[ps_q][:, 0:n], self.ones[:, :], self.tmp[ti][:, 0:n], start=first, stop=last),
                reads=[("tmp", ti), "ones"], writes=[("ps", ps_q)])
        mi = self.nexttmp()
        mean = self.tmp[mi]
        mk = ("tmp", mi)
        ri = self.nexttmp()
        rstd = self.tmp[ri]
        rk = ("tmp", ri)
        t.op("dve", lambda e: e.tensor_single_scalar(mean[:, 0:n], self.ps[ps_s][:, 0:n], 1.0 / D, ALU.mult),
             reads=[("ps", ps_s)], writes=[mk])
        t.op("dve", lambda e: e.tensor_tensor(rstd[:, 0:n], mean[:, 0:n], mean[:, 0:n], ALU.mult),
             reads=[mk], writes=[rk])
        t.op("dve", lambda e: e.scalar_tensor_tensor(rstd[:, 0:n], self.ps[ps_q][:, 0:n], 1.0 / D,
                                                     rstd[:, 0:n], ALU.mult, ALU.subtract),
             reads=[("ps", ps_q), rk], writes=[rk])
        t.op("act", lambda e: e.activation(out=rstd[:, 0:n], in_=rstd[:, 0:n], func=AF.Sqrt, bias=EPS, scale=1.0),
             reads=[rk], writes=[rk])
        t.op("dve", lambda e: e.reciprocal(rstd[:, 0:n], rstd[:, 0:n]), reads=[rk], writes=[rk])
        for m in range(KT):
            zk = self.slk(Z[m])
            t.op("dve", lambda e, m=m: e.tensor_tensor(self.sl(Z[m], n), self.sl(Z[m], n), mean[:, 0:n], ALU.subtract),
                 reads=[zk, mk], writes=[zk])
            t.op("dve", lambda e, m=m: e.tensor_tensor(self.sl(Z[m], n), self.sl(Z[m], n), rstd[:, 0:n], ALU.mult),
                 reads=[zk, rk], writes=[zk])
            t.op("act", lambda e, m=m: e.activation(out=self.sl(Z[m], n), in_=self.sl(Z[m], n), func=AF.Identity,
                                                    scale=self.PRM[:, gcol + m:gcol + m + 1],
                                                    bias=self.PRM[:, bcol + m:bcol + m + 1]),
                 reads=[zk, ("prm", gcol // 8), ("prm", bcol // 8)], writes=[zk])

    def bfslab(self, slab_idx, half, n):
        return self.SL[:, slab_idx, :].bitcast(BF16)[:, half * NT:half * NT + n]

    def hsrc(self, k, n0, n1):
        if self.bf:
            return self.tmp[k // 2][:, :].bitcast(BF16)[:, (k % 2) * NT + n0:(k % 2) * NT + n1], ("tmp", k // 2)
        return self.SL[:, self.S_H[k], n0:n1], self.slk(self.S_H[k])

    def ysl(self, k, n):
        if self.bf:
            return self.bfslab(self.S_Y[k // 2], k % 2, n)
        return self.sl(self.S_Y[k], n)

    def yk(self, k):
        return self.slk(self.S_Y[k // 2]) if self.bf else self.slk(self.S_Y[k])

    def zsrc(self, k, n):
        if self.bf:
            return self.bfslab(self.S_U[12 + k // 2], k % 2, n), self.slk(self.S_U[12 + k // 2])
        return self.sl(self.S_Z[k], n), self.slk(self.S_Z[k])

    def actsl(self, i, n):
        if self.bf:
            return self.bfslab(self.S_U[i // 2], i % 2, n), self.slk(self.S_U[i // 2])
        return self.sl(self.S_U[i], n), self.slk(self.S_U[i])

    def tile_pass(self, l, mode, n, step=0):
        t = self.t
        H, Y, Z, U = self.S_H, self.S_Y, self.S_Z, self.S_U
        self.load_input(mode, n, step)
        if self.bf:
            for k in range(KT):
                dst, dk = self.hsrc(k, 0, n)
                t.op("act", lambda e, k=k, dst=dst: e.activation(out=dst, in_=self.sl(H[k], n), func=AF.Copy),
                     reads=[self.slk(H[k])], writes=[dk])
        hsr = [self.hsrc(k, 0, n) for k in range(KT)]
        hreads = list({kk for _, kk in hsr})
        for c in range(5):
            slot = self.wacquire("win")
            if mode == "p1":
                self.cast_store(slot, "win")
            wv = self.wview(slot, 0, KT, 512)
            for jj in range(4):
                j = c * 4 + jj
                if j == 12 and "hg" not in self.stub:
                    nb = (n + 127) // 128
                    VT = self.SL[:, U[12]:U[12] + 2, :].rearrange("p a b -> p (a b)").rearrange(
                        "p (b f) -> p b f", f=256)
                    for blk in range(nb):
                        pb = min(128, n - blk * 128)
                        pi = self.nextps()
                        self.mm(self.ps[pi][0:pb, 0:256], ("ps", pi),
                                [(self.hsrc(k, blk * 128, blk * 128 + pb)[0], wv[:, k, 0:256]) for k in range(KT)],
                                reads=hreads + self.wkeys(slot))
                        self.evac("act" if blk % 2 else "dve", VT[0:pb, blk, :], self.ps[pi][0:pb, 0:256],
                                  reads=[("ps", pi)], writes=[self.slk(U[12]), self.slk(U[13])])
                    continue
                if j == 13 and "hg" not in self.stub:
                    continue
                pi = self.nextps()
                self.mm(self.ps[pi][:, 0:n], ("ps", pi),
                        [(wv[:, k, jj * 128:(jj + 1) * 128], hsr[k][0]) for k in range(KT)],
                        reads=hreads + self.wkeys(slot))
                self.evac("act" if j % 2 else "dve", self.sl(U[j], n), self.ps[pi][:, 0:n],
                          reads=[("ps", pi)], writes=[self.slk(U[j])])
            self.wprefetch()
        if mode == "main":
            self.dbg(2)
        self.mixers(l, 0, 0, n)
        if mode == "main":
            self.dbg(3)
        yreads = list({self.yk(k) for k in range(KT)})
        for c in range(2):
            slot = self.wacquire("wout")
            if mode == "p1":
                self.cast_store(slot, "wout")
            wv = self.wview(slot, 0, KT, 512)
            for jj in range(4):
                m = c * 4 + jj
                pi = self.nextps()
                self.mm(self.ps[pi][:, 0:n], ("ps", pi),
                        [(wv[:, k, jj * 128:(jj + 1) * 128], self.ysl(k, n)) for k in range(KT)],
                        reads=yreads + self.wkeys(slot))
                t.op("dve", lambda e, m=m, pi=pi: e.scalar_tensor_tensor(
                    self.sl(Z[m], n), self.sl(H[m], n), ALPHA, self.ps[pi][:, 0:n], ALU.mult, ALU.add),
                    reads=[("ps", pi), self.slk(H[m])], writes=[self.slk(Z[m])])
            self.wprefetch()
        self.layernorm(n, 0, 8)
        if self.bf:
            for k in range(KT):
                dst, dk = self.zsrc(k, n)
                t.op("dve", lambda e, k=k, dst=dst: e.tensor_copy(dst, self.sl(Z[k], n)),
                     reads=[self.slk(Z[k])], writes=[dk])
        if mode == "main":
            self.dbg(4)
        zsr = [self.zsrc(k, n) for k in range(KT)]
        zreads = list({kk for _, kk in zsr})
        for c in range(11):
            slot = self.wacquire("f1")
            if mode == "p1":
                self.cast_store(slot, "f1")
            wg = self.wview(slot, 0, KT, 256)
            wu = self.wview(slot, KT * 256, KT, 256)
            for jj in range(2):
                i = c * 2 + jj
                pg = self.nextps()
                pu = self.nextps()
                self.mm(self.ps[pg][:, 0:n], ("ps", pg),
                        [(wg[:, k, jj * 128:(jj + 1) * 128], zsr[k][0]) for k in range(KT)],
                        reads=zreads + self.wkeys(slot))
                self.mm(self.ps[pu][:, 0:n], ("ps", pu),
                        [(wu[:, k, jj * 128:(jj + 1) * 128], zsr[k][0]) for k in range(KT)],
                        reads=zreads + self.wkeys(slot))
                tk = self.nexttmp()
                t.op("act", lambda e, tk=tk, pg=pg: e.activation(out=self.tmp[tk][:, 0:n], in_=self.ps[pg][:, 0:n],
                                                                func=AF.Silu),
                     reads=[("ps", pg)], writes=[("tmp", tk)])
                adst, ak = self.actsl(i, n)
                t.op("dve", lambda e, tk=tk, pu=pu, adst=adst: e.tensor_tensor(
                    adst, self.tmp[tk][:, 0:n], self.ps[pu][:, 0:n], ALU.mult),
                    reads=[("tmp", tk), ("ps", pu)], writes=[ak])
            self.wprefetch()
        asr = [self.actsl(k, n) for k in range(FT)]
        ureads = list({kk for _, kk in asr})
        for m in range(KT):
            slot = self.wacquire("f2")
            if mode == "p1":
                self.cast_store(slot, "f2")
            wv = self.wview(slot, 0, FT, 128)
            pi = self.nextps()
            self.mm(self.ps[pi][:, 0:n], ("ps", pi),
                    [(wv[:, k, :], asr[k][0]) for k in range(FT)],
                    reads=ureads + self.wkeys(slot))
            t.op("dve", lambda e, m=m, pi=pi: e.scalar_tensor_tensor(
                self.sl(Z[m], n), self.sl(Z[m], n), ALPHA, self.ps[pi][:, 0:n], ALU.mult, ALU.add),
                reads=[("ps", pi), self.slk(Z[m])], writes=[self.slk(Z[m])])
            self.wprefetch()
        if mode == "main":
            self.dbg(5)
            if step + 1 < self.nsteps:
                self.load_x("main", NT, step + 1)
                if SKEW >= 2 and step + 1 >= SKEW:
                    self.load_recv(step + 1)
        self.layernorm(n, 16, 24)
        if mode == "main":
            self.dbg(6)
        self.store_output(mode, n, step)

    def prm(self, col):
        return self.PRM[:, col:col + 1]

    def conv_acc(self, acc, x, carry, wcol, K, n, xk, acck, cark, wkey, first_bias=None):
        t = self.t
        if first_bias is None:
            t.op("dve", lambda e: e.tensor_single_scalar(acc[:, 0:n], x[:, 0:n], self.prm(wcol + K - 1), ALU.mult),
                 reads=[xk, wkey], writes=[acck])
        else:
            t.op("dve", lambda e: e.tensor_scalar(acc[:, 0:n], x[:, 0:n], self.prm(wcol + K - 1), self.prm(first_bias),
                                                  ALU.mult, ALU.add),
                 reads=[xk, wkey, "p_lcb"], writes=[acck])
        for k in range(K - 1):
            sh = K - 1 - k
            t.op("dve", lambda e, sh=sh, k=k: e.scalar_tensor_tensor(
                acc[:, sh:n], x[:, 0:n - sh], self.prm(wcol + k), acc[:, sh:n], ALU.mult, ALU.add),
                reads=[xk, wkey, acck], writes=[acck])
            t.op("dve", lambda e, sh=sh, k=k: e.scalar_tensor_tensor(
                acc[:, 0:sh], carry[:, K - 1 - sh:K - 1], self.prm(wcol + k), acc[:, 0:sh], ALU.mult, ALU.add),
                reads=[cark, wkey, acck], writes=[acck])
        t.op("dve", lambda e: e.tensor_copy(carry[:, 0:K - 1], x[:, n - (K - 1):n]),
             reads=[xk], writes=[cark])

    def mixers(self, l, ti, t0, n):
        t = self.t
        Y, U, Z = self.S_Y, self.S_U, self.S_Z
        if "s5" in self.stub:
            for k in range(2):
                self.evac("act" if k % 2 else "dve", self.ysl(k, n), self.sl(U[k], n),
                          reads=[self.slk(U[k])], writes=[self.yk(k)])
        else:
            self.mix_s5(l, ti, n)
        if "sc" in self.stub:
            for k in range(2):
                self.evac("act", self.ysl(2 + k, n), self.sl(U[2 + k], n),
                          reads=[self.slk(U[2 + k])], writes=[self.yk(2 + k)])
        else:
            self.mix_sc(n)
        if "hg" in self.stub:
            for k in range(2):
                self.evac("dve", self.ysl(4 + k, n), self.sl(U[4 + k], n),
                          reads=[self.slk(U[4 + k])], writes=[self.yk(4 + k)])
        else:
            self.mix_hg(n)
        if "lru" in self.stub:
            for k in range(2):
                self.evac("act", self.ysl(6 + k, n), self.sl(U[6 + k], n),
                          reads=[self.slk(U[6 + k])], writes=[self.yk(6 + k)])
        else:
            self.mix_lru(n)

    def mix_sc(self, n):
        t = self.t
        Y, U, Z = self.S_Y, self.S_U, self.S_Z
        for c in range(2):
            hs, bs, cs = U[2 + c], U[4 + c], U[6 + c]
            acc = Z[c]
            t.op("dve", lambda e: e.tensor_tensor(self.sl(hs, n), self.sl(hs, n), self.sl(cs, n), ALU.mult),
                 reads=[self.slk(hs), self.slk(cs)], writes=[self.slk(hs)])
            self.conv_acc(self.SL[:, acc, :], self.SL[:, hs, :], self.car_sc[:, c, :], 32 + 3 * c, 3, n,
                          self.slk(hs), self.slk(acc), ("car_sc", c), "p_scw")
            t.op("dve", lambda e: e.tensor_tensor(self.ysl(2 + c, n), self.sl(acc, n), self.sl(bs, n), ALU.mult),
                 reads=[self.slk(acc), self.slk(bs)], writes=[self.yk(2 + c)])

    def mix_lru(self, n):
        t = self.t
        Y, U, Z = self.S_Y, self.S_U, self.S_Z
        for c in range(2):
            xs, ys = U[16 + c], U[18 + c]
            xc, ga, gx, aa = Z[4 * c], Z[4 * c + 1], Z[4 * c + 2], Z[4 * c + 3]
            k = self.slk
            self.conv_acc(self.SL[:, xc, :], self.SL[:, xs, :], self.car_lx[:, c, :], 38 + 4 * c, 4, n,
                          k(xs), k(xc), ("car_lx", c), "p_lcw", first_bias=46 + c)
            pa, px = self.nextps(), self.nextps()
            self.mm(self.ps[pa][:, 0:n], ("ps", pa), [(self.lruw[:, 0, c, :], self.sl(xc, n))], reads=[k(xc), "lruw"] + [("lruw", g_, h_) for g_ in range(2) for h_ in range(4)])
            self.mm(self.ps[px][:, 0:n], ("ps", px), [(self.lruw[:, 1, c, :], self.sl(xc, n))], reads=[k(xc), "lruw"] + [("lruw", g_, h_) for g_ in range(2) for h_ in range(4)])
            t.op("act", lambda e: e.activation(out=self.sl(ga, n), in_=self.ps[pa][:, 0:n], func=AF.Sigmoid,
                                               bias=self.prm(48 + c), scale=1.0),
                 reads=[("ps", pa), "p_lba"], writes=[k(ga)])
            t.op("act", lambda e: e.activation(out=self.sl(gx, n), in_=self.ps[px][:, 0:n], func=AF.Sigmoid,
                                               bias=self.prm(50 + c), scale=1.0),
                 reads=[("ps", px), "p_lbx"], writes=[k(gx)])
            t.op("act", lambda e: e.activation(out=self.sl(aa, n), in_=self.sl(ga, n), func=AF.Exp,
                                               scale=self.prm(52 + c)),
                 reads=[k(ga), "p_cp"], writes=[k(aa)])
            t.op("act", lambda e: e.activation(out=self.sl(ga, n), in_=self.sl(ga, n), func=AF.Exp,
                                               scale=self.prm(54 + c)),
                 reads=[k(ga), "p_cp2"], writes=[k(ga)])
            t.op("act", lambda e: e.activation(out=self.sl(ga, n), in_=self.sl(ga, n), func=AF.Sqrt,
                                               scale=-1.0, bias=1.0),
                 reads=[k(ga)], writes=[k(ga)])
            t.op("dve", lambda e: e.tensor_tensor(self.sl(gx, n), self.sl(gx, n), self.sl(xc, n), ALU.mult),
                 reads=[k(gx), k(xc)], writes=[k(gx)])
            t.op("dve", lambda e: e.tensor_tensor(self.sl(gx, n), self.sl(gx, n), self.sl(ga, n), ALU.mult),
                 reads=[k(gx), k(ga)], writes=[k(gx)])
            t.op("dve", lambda e: e.tensor_tensor_scan(self.sl(xc, n), self.sl(aa, n), self.sl(gx, n),
                                                       self.car_lh[:, c:c + 1], ALU.mult, ALU.add),
                 reads=[k(aa), k(gx), ("car_lh", c)], writes=[k(xc)])
            t.op("dve", lambda e: e.tensor_copy(self.car_lh[:, c:c + 1], self.SL[:, xc, n - 1:n]),
                 reads=[k(xc)], writes=[("car_lh", c)])
            t.op("act", lambda e: e.activation(out=self.sl(aa, n), in_=self.sl(ys, n), func=AF.Gelu_apprx_tanh),
                 reads=[k(ys)], writes=[k(aa)])
            t.op("dve", lambda e: e.tensor_tensor(self.ysl(6 + c, n), self.sl(xc, n), self.sl(aa, n), ALU.mult),
                 reads=[k(xc), k(aa)], writes=[self.yk(6 + c)])

    def mix_hg(self, n):
        t = self.t
        Y, U, Z = self.S_Y, self.S_U, self.S_Z
        k = self.slk
        CL = 64 if n >= 64 else n
        nch = n // CL
        mid = CL // 2
        nb = (n + 127) // 128
        VT = self.SL[:, U[12]:U[12] + 2, :].rearrange("p a b -> p (a b)").rearrange("p (b f) -> p b f", f=256)
        vtk = [k(U[12]), k(U[13])]
        X = [Z[0], Z[1], Z[2], Z[3], Z[4], Z[5], Z[6], Z[7]]
        c3 = lambda ap: ap.rearrange("p (c s) -> p c s", s=CL)
        for pr in range(2):
            qs, fs, gs = U[8 + pr], U[10 + pr], U[14 + pr]
            g_, b_, d_, e1, e2, e3, qp, ktk = X
            t.op("act", lambda e: e.activation(out=self.sl(qs, n), in_=self.sl(qs, n), func=AF.Silu),
                 reads=[k(qs)], writes=[k(qs)])
            t.op("act", lambda e: e.activation(out=self.sl(fs, n), in_=self.sl(fs, n), func=AF.Sigmoid),
                 reads=[k(fs)], writes=[k(fs)])
            t.op("dve", lambda e: e.tensor_scalar(self.sl(fs, n), self.sl(fs, n), self.prm(58 + pr), self.prm(56 + pr),
                                                  ALU.mult, ALU.add),
                 reads=[k(fs), "p_lb", "p_oml"], writes=[k(fs)])
            t.op("act", lambda e: e.activation(out=self.sl(g_, n), in_=self.sl(fs, n), func=AF.Ln),
                 reads=[k(fs)], writes=[k(g_)])
            t.op("dve", lambda e: e.tensor_scalar(self.sl(fs, n), self.sl(fs, n), -1.0, 1.0, ALU.mult, ALU.add),
                 reads=[k(fs), k(g_)], writes=[k(fs)])
            t.op("dve", lambda e: e.tensor_tensor_scan(self.sl(b_, n), self.cmask[:, 0:n], self.sl(g_, n), 0.0,
                                                       ALU.mult, ALU.add),
                 reads=[k(g_), "cmask"], writes=[k(b_)])
            b3 = c3(self.sl(b_, n))
            t.op("dve", lambda e: e.tensor_tensor(c3(self.sl(d_, n)), b3, b3[:, :, mid:mid + 1].to_broadcast([128, nch, CL]),
                                                  ALU.subtract),
                 reads=[k(b_)], writes=[k(d_)])
            t.op("act", lambda e: e.activation(out=self.sl(e1, n), in_=self.sl(d_, n), func=AF.Exp),
                 reads=[k(d_)], writes=[k(e1)])
            t.op("act", lambda e: e.activation(out=self.sl(e2, n), in_=self.sl(d_, n), func=AF.Exp, scale=-1.0),
                 reads=[k(d_)], writes=[k(e2)])
            t.op("act", lambda e: e.activation(out=self.sl(e3, n), in_=self.sl(b_, n), func=AF.Exp),
                 reads=[k(b_)], writes=[k(e3)])
            t.op("dve", lambda e: e.tensor_tensor(c3(self.sl(d_, n)), b3, b3[:, :, CL - 1:CL].to_broadcast([128, nch, CL]),
                                                  ALU.subtract),
                 reads=[k(b_), k(e1), k(e2)], writes=[k(d_)])
            t.op("act", lambda e: e.activation(out=self.sl(d_, n), in_=self.sl(d_, n), func=AF.Exp, scale=-1.0),
                 reads=[k(d_)], writes=[k(d_)])
            t.op("dve", lambda e: e.scalar_tensor_tensor(self.sl(e1, n), self.sl(qs, n), 0.125, self.sl(e1, n),
                                                         ALU.mult, ALU.mult),
                 reads=[k(qs), k(e1)], writes=[k(e1)])
            t.op("dve", lambda e: e.tensor_tensor(self.sl(e2, n), self.sl(fs, n), self.sl(e2, n), ALU.mult),
                 reads=[k(fs), k(e2)], writes=[k(e2)])
            t.op("dve", lambda e: e.scalar_tensor_tensor(self.sl(qp, n), self.sl(qs, n), 0.125, self.sl(e3, n),
                                                         ALU.mult, ALU.mult),
                 reads=[k(qs), k(e3)], writes=[k(qp)])
            t.op("dve", lambda e: e.tensor_tensor(self.sl(d_, n), self.sl(fs, n), self.sl(d_, n), ALU.mult),
                 reads=[k(fs), k(d_)], writes=[k(d_)])
            KTv = self.SL[:, ktk, :].rearrange("p (b f) -> p b f", f=128)
            for blk in range(nb):
                pb = min(128, n - blk * 128)
                pi = self.nextps()
                t.op("pe", lambda e, blk=blk, pb=pb, pi=pi: e.transpose(
                    self.ps[pi][0:pb, 0:128], self.SL[:, d_, blk * 128:blk * 128 + pb], self.ident[:, :]),
                    reads=[k(d_), "ident"], writes=[("ps", pi)])
                self.evac("act", KTv[0:pb, blk, :], self.ps[pi][0:pb, 0:128], reads=[("ps", pi)], writes=[k(ktk)])
            for c in range(nch):
                blk, r0 = (c * CL) // 128, (c * CL) % 128
                pi = self.nextps()
                self.mm(self.ps[pi][:, 0:128], ("ps", pi),
                        [(KTv[r0:r0 + CL, blk, :], VT[r0:r0 + CL, blk, pr * 128:(pr + 1) * 128])],
                        reads=[k(ktk)] + vtk)
                for hh in range(2):
                    rs = slice(hh * 64, (hh + 1) * 64)
                    t.op("dve", lambda e, c=c, rs=rs, pi=pi: e.scalar_tensor_tensor(
                        self.SS[rs, pr, c + 1, rs], self.SS[rs, pr, c, rs],
                        self.SL[rs, e3, (c + 1) * CL - 1:(c + 1) * CL], self.ps[pi][rs, rs],
                        ALU.mult, ALU.add),
                        reads=[("ps", pi), k(e3), ("SS", pr)], writes=[("SS", pr)])
            OPS = self.ps[6]
            for blk in range(nb):
                pb = min(128, n - blk * 128)
                bs = slice(blk * 128, blk * 128 + pb)
                smi = self.smn
                self.smn = (self.smn + 1) % 2
                SMv = self.SM[smi]
                for hh in range(2):
                    rs = slice(hh * 64, (hh + 1) * 64)
                    pi = self.nextps()
                    t.op("pe", lambda e, pi=pi, rs=rs, bs=bs, pb=pb, hh=hh: e.matmul(
                        self.ps[pi][0:pb, 0:pb], self.SL[rs, e2, bs], self.SL[rs, e1, bs],
                        start=True, stop=True, tile_position=(hh * 64, 0)),
                        reads=[k(e1), k(e2)], writes=[("ps", pi)])
                    t.op("dve", lambda e, pi=pi, pb=pb, hh=hh, SMv=SMv: e.tensor_tensor(
                        SMv[0:pb, hh, 0:pb], self.ps[pi][0:pb, 0:pb], self.mask2[0:pb, 0:pb], ALU.mult),
                        reads=[("ps", pi), "mask2"], writes=[("SM", smi, hh)])

                def emit_o(pe, blk=blk, bs=bs, pb=pb, SMv=SMv):
                    ins = None
                    for hh in range(2):
                        rs = slice(hh * 64, (hh + 1) * 64)
                        pe.matmul(OPS[rs, bs], VT[0:pb, blk, pr * 128 + hh * 64:pr * 128 + (hh + 1) * 64],
                                  SMv[0:pb, hh, 0:pb], start=True, stop=False, tile_position=(0, hh * 64))
                    for c in range(blk * 128 // CL, (blk * 128 + pb) // CL):
                        cs = slice(c * CL, (c + 1) * CL)
                        ins = pe.matmul(OPS[:, cs], self.SS[:, pr, c, :], self.SL[:, qp, cs],
                                        start=False, stop=True)
                    return ins
                t.op("pe", emit_o, reads=[("SM", smi, 0), ("SM", smi, 1), ("SS", pr), k(qp)] + vtk,
                     writes=[("ps", 6)])
            for hh in range(2):
                rs = slice(hh * 64, (hh + 1) * 64)
                t.op("dve", lambda e, rs=rs: e.tensor_copy(self.SS[rs, pr, 0, rs], self.SS[rs, pr, nch, rs]),
                     reads=[("SS", pr)], writes=[("SS", pr)])
            osq, rst = g_, b_
            t.op("act", lambda e: e.activation(out=self.sl(osq, n), in_=OPS[:, 0:n], func=AF.Square),
                 reads=[("ps", 6)], writes=[k(osq)])
            pm = self.nextps()
            self.mm(self.ps[pm][:, 0:n], ("ps", pm), [(self.bd64[:, :], self.sl(osq, n))], reads=[k(osq), "bd64"])
            t.op("act", lambda e: e.activation(out=self.sl(rst, n), in_=self.ps[pm][:, 0:n], func=AF.Sqrt,
                                               bias=EPS, scale=1.0),
                 reads=[("ps", pm)], writes=[k(rst)])
            t.op("dve", lambda e: e.reciprocal(self.sl(rst, n), self.sl(rst, n)), reads=[k(rst)], writes=[k(rst)])
            t.op("dve", lambda e: e.tensor_tensor(self.sl(rst, n), self.sl(rst, n), OPS[:, 0:n], ALU.mult),
                 reads=[k(rst), ("ps", 6)], writes=[k(rst)])
            t.op("act", lambda e: e.activation(out=self.sl(gs, n), in_=self.sl(gs, n), func=AF.Silu),
                 reads=[k(gs)], writes=[k(gs)])
            t.op("dve", lambda e: e.scalar_tensor_tensor(self.ysl(4 + pr, n), self.sl(rst, n), self.prm(60 + pr),
                                                         self.sl(gs, n), ALU.mult, ALU.mult),
                 reads=[k(rst), k(gs), "p_gn"], writes=[self.yk(4 + pr)])

    def s5_setup(self, l):
        t = self.t
        pin = self.pin
        Q = self.Q
        S = self.S5S
        KEY = "s5setup"
        col = [0]

        def alloc(ncol):
            c0 = col[0]
            col[0] += ncol
            assert col[0] <= 2176
            return S[:, c0:c0 + ncol]

        pending = []
        self._s5dma = getattr(self, "_s5dma", 0)

        def consume():
            r = [KEY] + list(pending)
            pending.clear()
            return r

        def dv(fn):
            t.op("dve", fn, reads=consume(), writes=[KEY])

        def ac(fn):
            t.op("act", fn, reads=consume(), writes=[KEY])

        def dma(out, in_):
            self._s5dma += 1
            key = ("s5dma", self._s5dma)
            t.op("pool", lambda e: e.dma_start(out=out, in_=in_, allow_slow_non_contiguous=True),
                 reads=[KEY], writes=[key], stream="dma_io", inc=16)
            pending.append(key)

        TT = lambda o, a, b, op: dv(lambda e: e.tensor_tensor(o, a, b, op))
        TS = lambda o, a, sc, op: dv(lambda e: e.tensor_single_scalar(o, a, sc, op))

        def cmul(o_r, o_i, a_r, a_i, b_r, b_i, t1, t2):
            TT(t1, a_r, b_r, ALU.mult)
            TT(t2, a_i, b_i, ALU.mult)
            TT(o_r, t1, t2, ALU.subtract)
            TT(t1, a_r, b_i, ALU.mult)
            TT(t2, a_i, b_r, ALU.mult)
            TT(o_i, t1, t2, ALU.add)

        t.op("dve", lambda e: e.memset(S[:, 0:8], 0.0), reads=[KEY], writes=[KEY] + self.S5S_keys)
        V = lambda: alloc(8)
        LR, LI, DT, X, TH, Cc, Ss, T1, T2, T3, AR, AI, WR, WI, MM, IAR, IAI, RI = [V() for _ in range(18)]
        lre = pin["s5_lam_re"][l]
        lim = pin["s5_lam_im"][l]
        ldt = pin["s5_log_dt"][l]
        dma(LR, bass.AP(lre.tensor, lre.offset, [[1, 128], [128, 8]]))
        dma(LI, bass.AP(lim.tensor, lim.offset, [[1, 128], [128, 8]]))
        for gl in range(2):
            dma(DT[gl * 64:(gl + 1) * 64, :], bass.AP(ldt.tensor, ldt.offset + gl, [[0, 64], [2, 8]]))
        ac(lambda e: e.activation(out=DT, in_=DT, func=AF.Exp))
        TT(X, LR, DT, ALU.mult)
        TT(TH, LI, DT, ALU.mult)
        ac(lambda e: e.activation(out=Ss, in_=TH, func=AF.Sin, scale=1.0 / 16))
        TS(T3, TH, 1.0 / 16, ALU.mult)
        TS(T3, T3, math.pi / 2, ALU.add)
        ac(lambda e: e.activation(out=Cc, in_=T3, func=AF.Sin))
        for _ in range(4):
            TT(T1, Cc, Cc, ALU.mult)
            TT(T2, Ss, Ss, ALU.mult)
            TT(T3, Cc, Ss, ALU.mult)
            TT(Cc, T1, T2, ALU.subtract)
            TS(Ss, T3, 2.0, ALU.mult)
        ac(lambda e: e.activation(out=T3, in_=X, func=AF.Exp))
        TT(AR, T3, Cc, ALU.mult)
        TT(AI, T3, Ss, ALU.mult)
        ac(lambda e: e.activation(out=self.RR[:, :], in_=X, func=AF.Exp, scale=float(Q)))
        dv(lambda e: e.reciprocal(RI, self.RR[:, :]))
        TT(T1, T3, T3, ALU.mult)
        dv(lambda e: e.reciprocal(T1, T1))
        TT(IAR, AR, T1, ALU.mult)
        TT(IAI, AI, T1, ALU.mult)
        TS(IAI, IAI, -1.0, ALU.mult)
        TS(T3, AR, -1.0, ALU.add)
        TT(T1, LR, LR, ALU.mult)
        TT(T2, LI, LI, ALU.mult)
        TT(MM, T1, T2, ALU.add)
        dv(lambda e: e.reciprocal(MM, MM))
        TT(T1, T3, LR, ALU.mult)
        TT(T2, AI, LI, ALU.mult)
        TT(WR, T1, T2, ALU.add)
        TT(WR, WR, MM, ALU.mult)
        TT(T1, AI, LR, ALU.mult)
        TT(T2, T3, LI, ALU.mult)
        TT(WI, T1, T2, ALU.subtract)
        TT(WI, WI, MM, ALU.mult)
        PWr = alloc(8 * (Q + 1)).rearrange("p (a s) -> p a s", s=Q + 1)
        PWi = alloc(8 * (Q + 1)).rearrange("p (a s) -> p a s", s=Q + 1)
        IPr = alloc(8 * Q).rearrange("p (a s) -> p a s", s=Q)
        IPi = alloc(8 * Q).rearrange("p (a s) -> p a s", s=Q)
        dv(lambda e: e.memset(PWr[:, :, 0], 1.0))
        dv(lambda e: e.memset(PWi[:, :, 0], 0.0))
        dv(lambda e: e.memset(IPr[:, :, 0], 1.0))
        dv(lambda e: e.memset(IPi[:, :, 0], 0.0))
        for sidx in range(Q):
            cmul(PWr[:, :, sidx + 1], PWi[:, :, sidx + 1], PWr[:, :, sidx], PWi[:, :, sidx], AR, AI, T1, T2)
        for sidx in range(Q - 1):
            cmul(IPr[:, :, sidx + 1], IPi[:, :, sidx + 1], IPr[:, :, sidx], IPi[:, :, sidx], IAR, IAI, T1, T2)
        FBr = alloc(8 * Q).rearrange("p (a s) -> p a s", s=Q)
        FBi = alloc(8 * Q).rearrange("p (a s) -> p a s", s=Q)
        for sidx in range(Q):
            cmul(FBr[:, :, sidx], FBi[:, :, sidx], IPr[:, :, sidx], IPi[:, :, sidx], WR, WI, T1, T2)
        TBr, TBi = self.TB[:, :, 0, :], self.TB[:, :, 1, :]
        dv(lambda e: e.memset(TBr[:, :, 0], 1.0))
        dv(lambda e: e.memset(TBi[:, :, 0], 0.0))
        TT(TBr[:, :, 1], PWr[:, :, Q], RI, ALU.mult)
        TT(TBi[:, :, 1], PWi[:, :, Q], RI, ALU.mult)
        big = lambda: alloc(256).rearrange("p (a h) -> p a h", h=32)
        Br, Bi, Sr, Si, U1, U2 = [big() for _ in range(6)]
        TWs = (U1.rearrange("p a h -> p (a h)"), U2.rearrange("p a h -> p (a h)"))
        tw = lambda i, m: TWs[i][:, 0:8 * m].rearrange("p (a c) -> p a c", c=m)
        m = 1
        while m < 64:
            if m >= 2:
                h = m // 2
                cmul(TBr[:, :, m], TBi[:, :, m], TBr[:, :, h], TBi[:, :, h], TBr[:, :, h], TBi[:, :, h], T1, T2)
            if m >= 2:
                br = TBr[:, :, m:m + 1].to_broadcast([128, 8, m - 1])
                bi = TBi[:, :, m:m + 1].to_broadcast([128, 8, m - 1])
                cmul(TBr[:, :, m + 1:2 * m], TBi[:, :, m + 1:2 * m], TBr[:, :, 1:m], TBi[:, :, 1:m], br, bi,
                     tw(0, m - 1), tw(1, m - 1))
            m *= 2
        cmul(TBr[:, :, 64], TBi[:, :, 64], TBr[:, :, 32], TBi[:, :, 32], TBr[:, :, 32], TBi[:, :, 32], T1, T2)
        rb = self.RR[:, :].unsqueeze(2).to_broadcast([128, 8, 64])
        TT(self.TA[:, :, 0, :], TBr[:, :, 0:64], rb, ALU.mult)
        TT(self.TA[:, :, 1, :], TBi[:, :, 0:64], rb, ALU.mult)
        TS(self.TA[:, :, 1, :], self.TA[:, :, 1, :], -1.0, ALU.mult)
        for ap_ in (Br, Bi):
            dv(lambda e, ap_=ap_: e.memset(ap_, 0.0))
        for g in range(16):
            P_, gl = g // 2, g % 2
            dma(Br[gl * 64:(gl + 1) * 64, P_, gl * 16:(gl + 1) * 16], pin["s5_b_re"][l, g])
            dma(Bi[gl * 64:(gl + 1) * 64, P_, gl * 16:(gl + 1) * 16], pin["s5_b_im"][l, g])
        for sidx in range(Q):
            fr = FBr[:, :, sidx:sidx + 1].to_broadcast([128, 8, 32])
            fi = FBi[:, :, sidx:sidx + 1].to_broadcast([128, 8, 32])
            cmul(Sr, Si, Br, Bi, fr, fi, U1, U2)
            for ri, src in enumerate((Sr, Si)):
                for half in range(2):
                    pi = self.nextps()
                    t.op("pe", lambda e, src=src, half=half, pi=pi: e.transpose(
                        self.ps[pi][:, 0:128], src[:, half * 4:(half + 1) * 4, :], self.ident[:, :]),
                        reads=[KEY, "ident"], writes=[("ps", pi)])
                    t.op("act", lambda e, half=half, sidx=sidx, ri=ri, pi=pi: e.activation(
                        out=self.BsT[:, half, sidx, ri, :], in_=self.ps[pi][:, 0:128], func=AF.Copy),
                        reads=[("ps", pi)], writes=["BsT"])
        CNr = Br.rearrange("p a h -> p (a h)").rearrange("p (a q) -> p a q", q=128)
        CNi = Bi.rearrange("p a h -> p (a h)").rearrange("p (a q) -> p a q", q=128)
        for ap_ in (CNr, CNi):
            dv(lambda e, ap_=ap_: e.memset(ap_, 0.0))
        for g in range(16):
            P_, gl = g // 2, g % 2
            half, j = P_ // 4, P_ % 4
            r0 = 32 * j + 16 * gl
            dma(CNr[r0:r0 + 16, half, gl * 64:(gl + 1) * 64], pin["s5_c_re"][l, g])
            dma(CNi[r0:r0 + 16, half, gl * 64:(gl + 1) * 64], pin["s5_c_im"][l, g])
        cn_reads = consume()
        for src, dst in ((CNr, Sr), (CNi, Si)):
            for half in range(2):
                pi = self.nextps()
                t.op("pe", lambda e, src=src, half=half, pi=pi: e.transpose(
                    self.ps[pi][:, 0:128], src[:, half, :], self.ident[:, :]),
                    reads=cn_reads + ["ident"], writes=[("ps", pi)])
                t.op("act", lambda e, dst=dst, half=half, pi=pi: e.activation(
                    out=dst[:, half * 4:(half + 1) * 4, :], in_=self.ps[pi][:, 0:128].rearrange("p (a h) -> p a h", h=32),
                    func=AF.Copy), reads=[("ps", pi), KEY], writes=[KEY])
        for sidx in range(Q):
            pr_ = PWr[:, :, sidx:sidx + 1].to_broadcast([128, 8, 32])
            pi_ = PWi[:, :, sidx:sidx + 1].to_broadcast([128, 8, 32])
            cmul(self.CsT[:, :, sidx, 0, :], self.CsT[:, :, sidx, 1, :], Sr, Si, pr_, pi_, U1, U2)
            dv(lambda e, sidx=sidx: e.tensor_single_scalar(self.CsT[:, :, sidx, 1, :], self.CsT[:, :, sidx, 1, :],
                                                           -1.0, ALU.mult))
        t.op("dve", lambda e: e.tensor_copy(self.CH[:, 0, 0:1], self.CH[:, 0, 0:1]), reads=[KEY],
             writes=["CsT", "TAB"] + self.S5S_keys)
        t.op("pool", lambda e: e.dma_start(out=self.GW[:, :, :],
                                           in_=pin["s5_glu_w"][l].rearrange("(kt p) n -> p kt n", p=128)),
             writes=["GW"], stream="dma_io", inc=16)
        t.op("dve", lambda e: e.memset(self.CARJ[:], 0.0), writes=[("CARJ", P_) for P_ in range(8)])

    def mix_s5(self, l, ti, n):
        t = self.t
        Y, U, Z = self.S_Y, self.S_U, self.S_Z
        k = self.slk
        Q = self.Q
        ncn = n // Q
        tm = lambda ap: ap.rearrange("p (c s) -> p s c", s=Q)
        sm = lambda ap: ap.rearrange("p (s c) -> p s c", s=Q)
        for half in range(2):
            t.op("act", lambda e, half=half: e.activation(out=sm(self.sl(Z[half], n)), in_=tm(self.sl(U[half], n)),
                                                          func=AF.Copy),
                 reads=[k(U[half])], writes=[k(Z[half])])
        YB = [self.ps[6], self.ps[7]]
        tv = lambda i: self.tmp[i][:, 0:8 * ncn].rearrange("p (a c) -> p a c", c=ncn)
        GLr, GLi, Vr, Vi = tv(0), tv(1), tv(2), tv(3)
        SY = self.SL[:, Y[0]:Y[0] + 8, :].rearrange("p a b -> p (a b)")
        n1 = ncn + 1
        JT = SY[:, 0:16 * n1].rearrange("p (a r c) -> p a r c", a=8, r=2)
        JJ = SY[:, 1040:1040 + 16 * n1].rearrange("p (a r c) -> p a r c", a=8, r=2)
        T1 = SY[:, 2080:2080 + 8 * n1].rearrange("p (a c) -> p a c", a=8)
        T2 = SY[:, 2600:2600 + 8 * n1].rearrange("p (a c) -> p a c", a=8)
        ck = [("tmp", i) for i in range(4)] + [k(i) for i in Y]
        Wk, Gk, Gs = (Z[2], Z[3]), (Z[4], Z[5]), (Z[6], Z[7])
        def do_b(P_):
            half, j = P_ // 4, P_ % 4
            rj = slice(32 * j, 32 * j + 32)
            pvs = []
            for ri in range(2):
                pv = self.nextps()
                pvs.append(pv)

                def emit_b(pe, ri=ri, pv=pv):
                    ins = None
                    for sidx in range(Q):
                        ins = pe.matmul(self.ps[pv][:, sidx * ncn:(sidx + 1) * ncn],
                                        self.BsT[rj, half, sidx, ri, :],
                                        self.SL[rj, Z[half], sidx * ncn:(sidx + 1) * ncn],
                                        start=True, stop=True, tile_position=(32 * j, 0))
                    return ins
                t.op("pe", emit_b, reads=["BsT", k(Z[half])], writes=[("ps", pv)])
            return pvs

        def do_chain(P_, pvs):
            for ri in range(2):
                pv = pvs[ri]
                t.op("act", lambda e, ri=ri, pv=pv: e.activation(out=tm(self.sl(Wk[ri], n)), in_=sm(self.ps[pv][:, 0:n]),
                                                                func=AF.Copy),
                     reads=[("ps", pv)], writes=[k(Wk[ri])])
                t.op("dve", lambda e, ri=ri: e.tensor_tensor_scan(self.sl(Gk[ri], n), self.cmask8[:, 0:n],
                                                                  self.sl(Wk[ri], n), 0.0, ALU.mult, ALU.add),
                     reads=[k(Wk[ri]), "cmask8"], writes=[k(Gk[ri])])
                t.op("pool", lambda e, ri=ri: e.tensor_copy(sm(self.sl(Gs[ri], n)), tm(self.sl(Gk[ri], n))),
                     reads=[k(Gk[ri])], writes=[k(Gs[ri])])
                gl = (GLr, GLi)[ri]
                t.op("pool", lambda e, ri=ri, gl=gl: e.tensor_copy(
                    gl[:, P_, :], self.SL[:, Gk[ri], 0:n].rearrange("p (c s) -> p c s", s=Q)[:, :, Q - 1]),
                    reads=[k(Gk[ri])], writes=[("tmp", ri)])

        def do_c(P_):
            half, j = P_ // 4, P_ % 4
            rj = slice(32 * j, 32 * j + 32)

            def emit_c(pe):
                ins = None
                for sidx in range(Q):
                    cs = slice(sidx * ncn, (sidx + 1) * ncn)
                    pe.matmul(YB[half][rj, cs], self.CsT[:, P_, sidx, 0, :], self.SL[:, Gs[0], cs],
                              start=(sidx == 0), stop=False, tile_position=(0, 32 * j))
                    ins = pe.matmul(YB[half][rj, cs], self.CsT[:, P_, sidx, 1, :], self.SL[:, Gs[1], cs],
                                    start=False, stop=False, tile_position=(0, 32 * j))
                return ins
            t.op("pe", emit_c, reads=["CsT", k(Gs[0]), k(Gs[1])], writes=[("ps", 6 + half)])

        pv_next = do_b(0)
        for P_ in range(8):
            pv_cur = pv_next
            do_chain(P_, pv_cur)
            if P_ + 1 < 8:
                pv_next = do_b(P_ + 1)
            do_c(P_)

        def dv(fn, extra_r=(), extra_w=()):
            t.op("dve", fn, reads=ck + ["TAB"] + list(extra_r), writes=ck + list(extra_w))
        TAr, TAi = self.TA[:, :, 0, 0:ncn], self.TA[:, :, 1, 0:ncn]
        TBr, TBi = self.TB[:, :, 0, 0:n1], self.TB[:, :, 1, 0:n1]
        t1, t2 = T1[:, :, 0:ncn], T2[:, :, 0:ncn]
        dv(lambda e: e.tensor_tensor(t1, GLr, TAr, ALU.mult))
        dv(lambda e: e.tensor_tensor(t2, GLi, TAi, ALU.mult))
        dv(lambda e: e.tensor_tensor(Vr, t1, t2, ALU.subtract))
        dv(lambda e: e.tensor_tensor(t1, GLr, TAi, ALU.mult))
        dv(lambda e: e.tensor_tensor(t2, GLi, TAr, ALU.mult))
        dv(lambda e: e.tensor_tensor(Vi, t1, t2, ALU.add))
        cjk = [("CARJ", i) for i in range(8)]
        for ri in range(2):
            dv(lambda e, ri=ri: e.tensor_copy(JT[:, :, ri, 0], self.CARJ[:, :, ri]), extra_r=cjk)
        for P_ in range(8):
            for ri, v_ in enumerate((Vr, Vi)):
                dv(lambda e, ri=ri, v_=v_, P_=P_: e.tensor_tensor_scan(
                    JT[:, P_, ri, 1:n1], self.RR[:, P_:P_ + 1].to_broadcast([128, ncn]), v_[:, P_, :],
                    self.CARJ[:, P_, ri:ri + 1], ALU.mult, ALU.add), extra_r=cjk)
        dv(lambda e: e.tensor_tensor(T1, JT[:, :, 0, :], TBr, ALU.mult))
        dv(lambda e: e.tensor_tensor(T2, JT[:, :, 1, :], TBi, ALU.mult))
        dv(lambda e: e.tensor_tensor(JJ[:, :, 0, :], T1, T2, ALU.subtract))
        dv(lambda e: e.tensor_tensor(T1, JT[:, :, 0, :], TBi, ALU.mult))
        dv(lambda e: e.tensor_tensor(T2, JT[:, :, 1, :], TBr, ALU.mult))
        dv(lambda e: e.tensor_tensor(JJ[:, :, 1, :], T1, T2, ALU.add))
        for ri in range(2):
            dv(lambda e, ri=ri: e.tensor_copy(self.CARJ[:, :, ri], JJ[:, :, ri, ncn]), extra_r=cjk, extra_w=cjk)
        for half in range(2):
            for j in range(4):
                P_ = half * 4 + j
                rj = slice(32 * j, 32 * j + 32)

                def emit_j(pe, P_=P_, half=half, rj=rj, j=j):
                    ins = None
                    for sidx in range(Q):
                        cs = slice(sidx * ncn, (sidx + 1) * ncn)
                        pe.matmul(YB[half][rj, cs], self.CsT[:, P_, sidx, 0, :], JJ[:, P_, 0, 0:ncn],
                                  start=False, stop=False, tile_position=(0, 32 * j))
                        ins = pe.matmul(YB[half][rj, cs], self.CsT[:, P_, sidx, 1, :], JJ[:, P_, 1, 0:ncn],
                                        start=False, stop=True, tile_position=(0, 32 * j))
                    return ins
                t.op("pe", emit_j, reads=["CsT"] + ck, writes=[("ps", 6 + half)])
            ys = U[20 + half]
            t.op("dve", lambda e, half=half, ys=ys: e.scalar_tensor_tensor(
                tm(self.sl(ys, n)), tm(self.sl(U[half], n)), self.prm(62 + half), sm(YB[half][:, 0:n]),
                ALU.mult, ALU.add),
                reads=[k(U[half]), ("ps", 6 + half), "p_s5d"], writes=[k(ys)])
            t.op("act", lambda e, ys=ys: e.activation(out=self.sl(ys, n), in_=self.sl(ys, n),
                                                      func=AF.Gelu_apprx_tanh),
                 reads=[k(ys)], writes=[k(ys)])
        for m in range(2):
            pi = self.nextps()
            self.mm(self.ps[pi][:, 0:n], ("ps", pi),
                    [(self.GW[:, kt, m * 128:(m + 1) * 128], self.sl(U[20 + kt], n)) for kt in range(2)],
                    reads=[k(U[20]), k(U[21]), "GW"])
            tk = self.nexttmp()
            t.op("act", lambda e, tk=tk, pi=pi, m=m: e.activation(out=self.tmp[tk][:, 0:n], in_=self.ps[pi][:, 0:n],
                                                                 func=AF.Sigmoid, bias=self.prm(64 + m), scale=1.0),
                 reads=[("ps", pi), "p_glub"], writes=[("tmp", tk)])
            t.op("dve", lambda e, tk=tk, m=m: e.tensor_tensor(self.ysl(m, n), self.sl(U[20 + m], n),
                                                              self.tmp[tk][:, 0:n], ALU.mult),
                 reads=[("tmp", tk), k(U[20 + m])], writes=[self.yk(m)])


_CACHE = {}
_BUILDERS = {}


def _get_nc(n_xtiles):
    if n_xtiles not in _CACHE:
        _BUILDERS[n_xtiles] = Builder(n_xtiles)
        _CACHE[n_xtiles] = _BUILDERS[n_xtiles].build()
    return _CACHE[n_xtiles]


def make_in_maps(inputs, names, xs):
    npairs = len(xs)
    role_a = np.zeros((128, 2), np.float32)
    role_a[:, 0] = 1.0
    role_b = np.zeros((128, 2), np.float32)
    role_b[:, 1] = 1.0
    pa, pb = {}, {}
    for k in names:
        if k in ("x", "role"):
            continue
        arr = np.ascontiguousarray(inputs[k], dtype=np.float32)
        if k in ("meta_tokens", "hg_lb_raw"):
            pa[k] = arr
            pb[k] = arr
        else:
            pa[k] = np.ascontiguousarray(arr[[0, 0]])
            pb[k] = arr
    maps = [dict(pa, x=xs[i], role=role_a) for i in range(npairs)]
    maps += [dict(pb, x=xs[i], role=role_b) for i in range(npairs)]
    return maps


def kernel(**inputs):
    x = np.ascontiguousarray(inputs["x"], dtype=np.float32)
    bsz, seq, _ = x.shape
    n_xtiles = seq // NT
    nc = _get_nc(n_xtiles)
    names = _BUILDERS[n_xtiles].in_names
    in_maps = make_in_maps(inputs, names, [x[b] for b in range(bsz)])
    res = run_bass_kernel_spmd(nc, in_maps, core_ids=list(range(2 * bsz)))
    return np.stack([res.results[bsz + b]["out"] for b in range(bsz)], axis=0)
```
